# Optimizing a Trainium2 kernel written in Bass

```python
import numpy as np
import jax
import jax.numpy as jnp
from jax import lax

D_MODEL = 1024
BATCH = 8
SEQ = 2048
DEPTH = 4
DEC_BATCH = 32
DEC_SEQ = 8
PAST_LEN = 8192
PAGE_SIZE = 128

HEAD_DIM = 64
ATTN_W = D_MODEL // 2
CONV_W = D_MODEL // 4
POOL_W = D_MODEL - ATTN_W - CONV_W
MIX_W = ATTN_W + CONV_W + POOL_W
N_Q_HEADS = ATTN_W // HEAD_DIM
N_KV_HEADS = 2
Q_PER_KV = N_Q_HEADS // N_KV_HEADS
KV_W = N_KV_HEADS * HEAD_DIM
N_BRANCH = 3
CONV_HEADS = CONV_W // HEAD_DIM
CONV_K = 3
CONV_BUF = CONV_K - 1
POOL_WINDOWS = (2, 4, 8, 16)
POOL_GROUPS = len(POOL_WINDOWS)
POOL_GW = POOL_W // POOL_GROUPS
POOL_BUF = max(POOL_WINDOWS) - 1
CMP_BLOCK = 32
CMP_STRIDE = 16
SLC_BLOCK = 64
N_SLC = 16
WINDOW = 512
SLC_QBLOCK = 64
WIN_QBLOCK = 128
FFN_HIDDEN = -(-(8 * D_MODEL) // (3 * 256)) * 256
IN_WIDTHS = (ATTN_W,) + (KV_W,) * 6 + (N_Q_HEADS * N_BRANCH,) + (CONV_W,) * 3 + (POOL_W,)
IN_W = sum(IN_WIDTHS)
IN_SPLITS = tuple(np.cumsum(IN_WIDTHS)[:-1].tolist())
EPS = 1e-6
NEG = -1e30
FORCE = 1e4
ATTN_SCALE = HEAD_DIM ** -0.5

kernel_name = 'hymba_nsa_conv_pool_decoder_step'


def rms_norm(x, g):
    xf = x.astype(jnp.float32)
    y = xf * lax.rsqrt(jnp.mean(xf * xf, axis=-1, keepdims=True) + EPS)
    return (y * g.astype(jnp.float32)).astype(x.dtype)


def masked_softmax(s, mask):
    p = jax.nn.softmax(jnp.where(mask, s, NEG), axis=-1)
    return jnp.where(mask, p, 0.0)


def compress(raw, pe, w1, w2):
    L = raw.shape[1]
    n_cmp = (L - CMP_BLOCK) // CMP_STRIDE + 1
    idx = np.arange(n_cmp)[:, None] * CMP_STRIDE + np.arange(CMP_BLOCK)[None, :]
    blk = raw[:, idx] + pe[:, None, :]
    blk = jnp.moveaxis(blk, 3, 2)
    flat = blk.reshape(blk.shape[:3] + (CMP_BLOCK * HEAD_DIM,))
    return jax.nn.silu(flat @ w1) @ w2


def cmp_attention(q, k_cmp, v_cmp, q_pos):
    n_cmp = k_cmp.shape[1]
    blk_end = np.arange(n_cmp) * CMP_STRIDE + CMP_BLOCK - 1
    mask = blk_end[None, :] <= q_pos[:, None]
    s = jnp.einsum('btgrd,bcgd->bgrtc', q, k_cmp).astype(jnp.float32) * ATTN_SCALE
    p = masked_softmax(s, mask)
    o = jnp.einsum('bgrtc,bcgd->btgrd', p.astype(v_cmp.dtype), v_cmp)
    return o, p


def cmp_to_slc_matrix(n_cmp, n_sel):
    m = np.zeros((n_cmp, n_sel), np.float32)
    i = np.arange(n_cmp)
    for part in range(CMP_BLOCK // CMP_STRIDE):
        j = np.minimum((i + part) * CMP_STRIDE // SLC_BLOCK, n_sel - 1)
        np.add.at(m, (i, j), 1.0)
    return m


def select_blocks(p_cmp, q_pos, n_sel):
    overlap = jnp.asarray(cmp_to_slc_matrix(p_cmp.shape[-1], n_sel))
    imp = jnp.einsum('bgrtc,cj->bgtj', p_cmp, overlap)
    j = np.arange(n_sel)[None, :]
    cur = (q_pos // SLC_BLOCK)[:, None]
    valid = j <= cur
    forced = valid & ((j == 0) | (j == cur) | (j == cur - 1))
    score = jnp.where(forced, FORCE, jnp.where(valid, imp, NEG))
    top_s, top_i = lax.top_k(score, min(N_SLC, n_sel))
    return top_i, top_s > 0.5 * NEG


def slc_block_attention(q_b, idx_b, ok_b, pos_b, k_blocks, v_blocks):
    b_i = jnp.arange(k_blocks.shape[0])[:, None, None, None]
    g_i = jnp.arange(k_blocks.shape[1])[None, :, None, None]
    k_sel = k_blocks[b_i, g_i, idx_b]
    v_sel = v_blocks[b_i, g_i, idx_b]
    s = jnp.einsum('btgrd,bgtkpd->bgrtkp', q_b, k_sel).astype(jnp.float32) * ATTN_SCALE
    k_pos = idx_b[..., None] * SLC_BLOCK + jnp.arange(SLC_BLOCK)
    mask = (ok_b[..., None] & (k_pos <= pos_b[None, None, :, None, None]))[:, :, None]
    shp = s.shape
    p = masked_softmax(s.reshape(shp[:4] + (-1,)), mask.reshape(mask.shape[:4] + (-1,))).reshape(shp)
    return jnp.einsum('bgrtkp,bgtkpd->btgrd', p.astype(v_sel.dtype), v_sel)


def slc_attention(q, top_i, ok, q_pos, k_slc, v_slc):
    B, T = q.shape[:2]
    L = k_slc.shape[1]
    n_sel = -(-L // SLC_BLOCK)

    def to_blocks(a):
        a = jnp.pad(a, ((0, 0), (0, n_sel * SLC_BLOCK - L), (0, 0), (0, 0)))
        return a.reshape(B, n_sel, SLC_BLOCK, N_KV_HEADS, HEAD_DIM).transpose(0, 3, 1, 2, 4)

    k_blocks, v_blocks = to_blocks(k_slc), to_blocks(v_slc)
    qb = SLC_QBLOCK if T % SLC_QBLOCK == 0 else T
    nb = T // qb
    kk = top_i.shape[-1]
    xs = (jnp.moveaxis(q.reshape((B, nb, qb) + q.shape[2:]), 1, 0),
          jnp.moveaxis(top_i.reshape(B, N_KV_HEADS, nb, qb, kk), 2, 0),
          jnp.moveaxis(ok.reshape(B, N_KV_HEADS, nb, qb, kk), 2, 0),
          jnp.asarray(q_pos.reshape(nb, qb), jnp.int32))
    out = lax.map(lambda a: slc_block_attention(a[0], a[1], a[2], a[3], k_blocks, v_blocks), xs)
    return jnp.moveaxis(out, 0, 1).reshape(q.shape)


def window_attention(q, k, v, q_pos, k_pos):
    s = jnp.einsum('bnqgrd,bnkgd->bngrqk', q, k).astype(jnp.float32) * ATTN_SCALE
    diff = q_pos[:, :, None] - k_pos[:, None, :]
    mask = (k_pos[:, None, :] >= 0) & (diff >= 0) & (diff < WINDOW)
    p = masked_softmax(s, mask[None, :, None, None])
    return jnp.einsum('bngrqk,bnkgd->bnqgrd', p.astype(v.dtype), v)


def window_prompt(q, kw, vw):
    B, T = q.shape[:2]
    nb = T // WIN_QBLOCK
    n_prev = WINDOW // WIN_QBLOCK

    def band(a):
        a = jnp.pad(a, ((0, 0), (WINDOW, 0), (0, 0), (0, 0)))
        a = a.reshape(B, nb + n_prev, WIN_QBLOCK, N_KV_HEADS, HEAD_DIM)
        return jnp.concatenate([a[:, j:j + nb] for j in range(n_prev + 1)], axis=2)

    q_pos = np.arange(T).reshape(nb, WIN_QBLOCK)
    k_pos = np.arange(nb)[:, None] * WIN_QBLOCK - WINDOW + np.arange((n_prev + 1) * WIN_QBLOCK)[None, :]
    qb = q.reshape((B, nb, WIN_QBLOCK) + q.shape[2:])
    return window_attention(qb, band(kw), band(vw), q_pos, k_pos).reshape(q.shape)


def window_sample(q, kw_ext, vw_ext, pos0):
    T = q.shape[1]
    n_k = kw_ext.shape[1]
    q_pos = (pos0 + np.arange(T))[None]
    k_pos = (pos0 + T - n_k + np.arange(n_k))[None]
    return window_attention(q[:, None], kw_ext[:, None], vw_ext[:, None], q_pos, k_pos)[:, 0]


def short_conv(u_ext, w, bias):
    T = u_ext.shape[1] - CONV_BUF
    out = bias
    for j in range(CONV_K):
        out = out + u_ext[:, j:j + T] * w[j]
    return out


def multi_pool(p_ext, pos0, pool_w, pool_scale):
    B, n_ext, _ = p_ext.shape
    T = n_ext - POOL_BUF
    cs = jnp.cumsum(jnp.pad(p_ext.astype(jnp.float32), ((0, 0), (1, 0), (0, 0))), axis=1)
    pos = pos0 + np.arange(T)
    hi = cs[:, POOL_BUF + 1:]
    means = []
    for g, w in enumerate(POOL_WINDOWS):
        ch = slice(g * POOL_GW, (g + 1) * POOL_GW)
        lo = cs[:, POOL_BUF + 1 - w: POOL_BUF + 1 - w + T, ch]
        cnt = np.minimum(w, pos + 1).astype(np.float32)[None, :, None]
        means.append((hi[:, :, ch] - lo) / cnt)
    d = jnp.concatenate(means, axis=-1) - p_ext[:, POOL_BUF:].astype(jnp.float32)
    d = d.astype(p_ext.dtype).reshape(B, T, POOL_GROUPS, POOL_GW)
    return jnp.einsum('btgc,gcd->btgd', d, pool_w).reshape(B, T, POOL_W) * pool_scale


def swiglu(h, w_up, w_down):
    a, b = jnp.split(h @ w_up, 2, axis=-1)
    return (jax.nn.silu(a) * b) @ w_down


def trunk_layer(x, c, kv_past, win_past, conv_past, pool_past, lw):
    (norm1, norm2, w_ada, b_ada, w_in, w_out, q_gain, k_gain, cmp_pe, cmp_w1, cmp_w2,
     conv_w, conv_bias, pool_w, pool_scale, w_up, w_down) = lw
    B, T, _ = x.shape
    pos0 = 0 if kv_past is None else kv_past.shape[1]
    q_pos = pos0 + np.arange(T)
    mod = (jax.nn.silu(c) @ w_ada + b_ada)[:, None, :]
    shift1, scale1, gate1, shift2, scale2, gate2 = jnp.split(mod, 6, axis=-1)

    h = rms_norm(x, norm1) * (1 + scale1) + shift1
    z = h @ w_in
    (q, kc, vc, ks, vs, kw, vw, gates, conv_bg, conv_cg, conv_h, pool_in) = jnp.split(z, IN_SPLITS, axis=-1)

    heads = lambda a: a.reshape(B, T, N_KV_HEADS, HEAD_DIM)
    q = rms_norm(q.reshape(B, T, N_KV_HEADS, Q_PER_KV, HEAD_DIM), q_gain)
    ks = rms_norm(heads(ks), k_gain[1])
    kw = rms_norm(heads(kw), k_gain[2])
    new_kv = jnp.stack([heads(kc), heads(vc), ks, heads(vs)], axis=2)
    new_win = jnp.stack([kw, heads(vw)], axis=2)
    full = new_kv if kv_past is None else jnp.concatenate([kv_past, new_kv], axis=1)
    k_cmp = rms_norm(compress(full[:, :, 0], cmp_pe[0], cmp_w1[0], cmp_w2[0]), k_gain[0])
    v_cmp = compress(full[:, :, 1], cmp_pe[1], cmp_w1[1], cmp_w2[1])
    o_cmp, p_cmp = cmp_attention(q, k_cmp, v_cmp, q_pos)
    top_i, sel_ok = select_blocks(p_cmp, q_pos, -(-full.shape[1] // SLC_BLOCK))
    o_slc = slc_attention(q, top_i, sel_ok, q_pos, full[:, :, 2], full[:, :, 3])
    if win_past is None:
        o_win = window_prompt(q, new_win[:, :, 0], new_win[:, :, 1])
        win_state = new_win[:, -min(WINDOW, T):]
    else:
        win_ext = jnp.concatenate([win_past, new_win], axis=1)
        o_win = window_sample(q, win_ext[:, :, 0], win_ext[:, :, 1], pos0)
        win_state = win_ext[:, -win_past.shape[1]:]
    g = jax.nn.sigmoid(gates.astype(jnp.float32)).astype(x.dtype).reshape(B, T, N_KV_HEADS, Q_PER_KV, N_BRANCH)
    o_attn = (g[..., 0:1] * o_cmp + g[..., 1:2] * o_slc + g[..., 2:3] * o_win).reshape(B, T, ATTN_W)

    u_ext = jnp.concatenate([conv_past, conv_cg * conv_h], axis=1)
    y_conv = conv_bg * short_conv(u_ext, conv_w, conv_bias)

    p_ext = jnp.concatenate([pool_past, pool_in], axis=1)
    y_pool = multi_pool(p_ext, pos0, pool_w, pool_scale)

    x = x + gate1 * (jnp.concatenate([o_attn, y_conv, y_pool], axis=-1) @ w_out)
    h = rms_norm(x, norm2) * (1 + scale2) + shift2
    x = x + gate2 * swiglu(h, w_up, w_down)
    return x, new_kv, win_state, u_ext[:, -CONV_BUF:], p_ext[:, -POOL_BUF:]


def setup_inputs(seed: int = 0) -> dict:
    key = jax.random.key(seed)
    k = jax.random.split(key, 26)
    n_pages = PAST_LEN // PAGE_SIZE
    n_pool = (DEC_BATCH * n_pages * 5) // 4
    win_buf = min(WINDOW, PAST_LEN)
    D = D_MODEL

    def nrm(kk, shape, scale=1.0):
        return jax.random.normal(kk, shape, jnp.float32) * scale

    def gain(kk, shape):
        return 1.0 + 0.1 * jax.random.normal(kk, shape, jnp.float32)

    page_table = jax.random.permutation(k[6], n_pool)[: DEC_BATCH * n_pages].reshape(DEC_BATCH, n_pages).astype(jnp.int32)
    return {
        'x_prompt': nrm(k[0], (BATCH, SEQ, D)),
        'x_sample': nrm(k[1], (DEC_BATCH, DEC_SEQ, D)),
        'cache_nsa_kv': nrm(k[2], (DEPTH, n_pool, PAGE_SIZE, 4, N_KV_HEADS, HEAD_DIM)),
        'state_win_kv': nrm(k[3], (DEPTH, DEC_BATCH, win_buf, 2, N_KV_HEADS, HEAD_DIM)),
        'state_conv': nrm(k[4], (DEPTH, DEC_BATCH, CONV_BUF, CONV_W)),
        'state_pool': nrm(k[5], (DEPTH, DEC_BATCH, POOL_BUF, POOL_W)),
        'page_table': page_table,
        'c_prompt': nrm(k[7], (BATCH, D)),
        'c_sample': nrm(k[8], (DEC_BATCH, D)),
        'norm_mix': gain(k[9], (DEPTH, D)),
        'norm_ffn': gain(k[10], (DEPTH, D)),
        'w_ada': nrm(k[11], (DEPTH, D, 6 * D), 0.5 * D ** -0.5),
        'b_ada': nrm(k[12], (DEPTH, 6 * D), 0.01),
        'w_in': nrm(k[13], (DEPTH, D, IN_W), D ** -0.5),
        'w_out': nrm(k[14], (DEPTH, MIX_W, D), MIX_W ** -0.5),
        'q_norm': gain(k[15], (DEPTH, HEAD_DIM)),
        'k_norm': gain(k[16], (DEPTH, N_BRANCH, HEAD_DIM)),
        'cmp_pe': nrm(k[17], (DEPTH, 2, CMP_BLOCK, HEAD_DIM), 0.1),
        'cmp_w1': nrm(k[18], (DEPTH, 2, CMP_BLOCK * HEAD_DIM, HEAD_DIM), (CMP_BLOCK * HEAD_DIM) ** -0.5),
        'cmp_w2': nrm(k[19], (DEPTH, 2, HEAD_DIM, HEAD_DIM), HEAD_DIM ** -0.5),
        'conv_w': nrm(k[20], (DEPTH, CONV_K, CONV_W), CONV_K ** -0.5),
        'conv_bias': nrm(k[21], (DEPTH, CONV_W), 0.01),
        'pool_w': nrm(k[22], (DEPTH, POOL_GROUPS, POOL_GW, POOL_GW), POOL_GW ** -0.5),
        'pool_scale': gain(k[23], (DEPTH, POOL_W)),
        'w_up': nrm(k[24], (DEPTH, D, 2 * FFN_HIDDEN), D ** -0.5),
        'w_down': nrm(k[25], (DEPTH, FFN_HIDDEN, D), FFN_HIDDEN ** -0.5),
    }


def reference(x_prompt, x_sample, cache_nsa_kv, state_win_kv, state_conv, state_pool, page_table,
              c_prompt, c_sample, norm_mix, norm_ffn, w_ada, b_ada, w_in, w_out, q_norm, k_norm,
              cmp_pe, cmp_w1, cmp_w2, conv_w, conv_bias, pool_w, pool_scale, w_up, w_down):
    n_dec, n_pages = page_table.shape
    page = cache_nsa_kv.shape[2]
    yp, ys = x_prompt, x_sample
    conv0 = jnp.zeros((x_prompt.shape[0], CONV_BUF, CONV_W), x_prompt.dtype)
    pool0 = jnp.zeros((x_prompt.shape[0], POOL_BUF, POOL_W), x_prompt.dtype)
    kv_p, kv_s, win_p, win_s, conv_p, conv_s, pool_p, pool_s = [], [], [], [], [], [], [], []
    for l in range(DEPTH):
        lw = (norm_mix[l], norm_ffn[l], w_ada[l], b_ada[l], w_in[l], w_out[l], q_norm[l], k_norm[l],
              cmp_pe[l], cmp_w1[l], cmp_w2[l], conv_w[l], conv_bias[l], pool_w[l], pool_scale[l],
              w_up[l], w_down[l])
        yp, a, b, cst, pst = trunk_layer(yp, c_prompt, None, None, conv0, pool0, lw)
        kv_p.append(a); win_p.append(b); conv_p.append(cst); pool_p.append(pst)
        kv_past = cache_nsa_kv[l, page_table].reshape(n_dec, n_pages * page, 4, N_KV_HEADS, HEAD_DIM)
        ys, a, b, cst, pst = trunk_layer(ys, c_sample, kv_past, state_win_kv[l], state_conv[l], state_pool[l], lw)
        kv_s.append(a); win_s.append(b); conv_s.append(cst); pool_s.append(pst)
    return (yp, ys, jnp.stack(kv_p), jnp.stack(kv_s), jnp.stack(win_p), jnp.stack(win_s),
            jnp.stack(conv_p), jnp.stack(conv_s), jnp.stack(pool_p), jnp.stack(pool_s))
```

```python
import numpy as np
import ml_dtypes
import concourse.bass as bass
import concourse.mybir as mybir
from concourse.bass_utils import run_bass_kernel_spmd

F32 = mybir.dt.float32
BF16 = mybir.dt.bfloat16
I32 = mybir.dt.int32
AF = mybir.ActivationFunctionType
ALU = mybir.AluOpType
AX = mybir.AxisListType

D = 1024
HD = 64
IN_W = 2328
FFN_H = 2816
EPS = 1e-6
NEGB = -30000.0
SCALE = 0.125
CFG = dict(T=2048, NPG=64, DEPTH=4, NS=4, DS=8, NPOOL=2560)


class Tok:
    __slots__ = ("w", "r", "excl", "wl")

    def __init__(self, excl=False):
        self.w = None
        self.r = {}
        self.excl = excl
        self.wl = []


ENGS = ("pe", "act", "dve", "pool", "sp")
MARKS = []


class _Rec:
    def __getattr__(self, name):
        return lambda *a, **k: (name, a, k)


_REC = _Rec()


class Sched:
    def __init__(self):
        self.ops = {e: [] for e in ENGS}
        self.cnt = {e: 0 for e in ENGS}
        self.seen = {e: {} for e in ENGS}
        self.dnext = {e: 0 for e in ENGS}
        self.dval = {}
        self.NDS = {"sp": 24, "pool": 24, "act": 4}

    def _need(self, eng, dep, waits):
        if dep is None:
            return
        k, v = dep
        if eng == "pe" and k == "pe":
            return
        if self.seen[eng].get(k, 0) >= v:
            return
        if waits.get(k, 0) < v:
            waits[k] = v

    def _deps(self, eng, reads, writes, join=False):
        waits = {}
        for t in reads:
            self._need(eng, t.w, waits)
            for d_ in t.wl:
                self._need(eng, d_, waits)
        for t in writes:
            if join and t.w is not None and isinstance(t.w[0], tuple):
                continue
            if not (t.w is not None and t.w[0] == eng):
                self._need(eng, t.w, waits)
            for d_ in t.wl:
                self._need(eng, d_, waits)
            for k, v in t.r.items():
                if k != eng:
                    self._need(eng, (k, v), waits)
        for k, v in waits.items():
            self.seen[eng][k] = v
        return list(waits.items())

    def _mark(self, me, reads, writes, join=False):
        k, v = me
        for t in reads:
            if t.r.get(k, 0) < v:
                t.r[k] = v
        for t in writes:
            if join and t.w is not None and isinstance(t.w[0], tuple):
                t.wl.append(t.w)
            else:
                t.wl = []
                t.r = {}
            t.w = me

    def op(self, eng, fn, reads=(), writes=()):
        writes = list(writes) + [t for t in reads if t.excl]
        reads = [t for t in reads if not t.excl]
        waits = self._deps(eng, reads, writes)
        self.cnt[eng] += 1
        me = (eng, self.cnt[eng])
        self.ops[eng].append((waits, fn(_REC), eng, 1))
        self._mark(me, reads, writes)

    def dma(self, q, fn, reads=(), writes=(), join=False):
        i = self.dnext[q] % self.NDS[q]
        self.dnext[q] += 1
        key = ("d", q, i)
        prev = self.dval.get(key, 0)
        waits = self._deps(q, reads, writes, join)
        if prev and self.seen[q].get(key, 0) < prev:
            waits.append((key, prev))
            self.seen[q][key] = prev
        val = prev + 16
        self.dval[key] = val
        self.ops[q].append((waits, fn(_REC), key, 16))
        self._mark((key, val), reads, writes, join)

    def barrier(self):
        snap = dict(self.cnt)
        dsn = dict(self.dval)
        for eng in ENGS:
            waits = [(k, v) for k, v in dsn.items() if self.seen[eng].get(k, 0) < v]
            for e in ENGS:
                if e != eng and snap[e] and self.seen[eng].get(e, 0) < snap[e]:
                    waits.append((e, snap[e]))
            for k, v in waits:
                self.seen[eng][k] = v
            if waits:
                self.ops[eng].append((waits, None, None, 0))

    def final_wait(self, eng):
        waits = [(k, v) for k, v in self.dval.items() if self.seen[eng].get(k, 0) < v]
        for e in ENGS:
            if e != eng and self.cnt[e]:
                waits.append((e, self.cnt[e]))
        self.ops[eng].append((waits, None, None, 0))


def build(cfg):
    T, NPG, DEPTH, NS, DS = cfg["T"], cfg["NPG"], cfg["DEPTH"], cfg["NS"], cfg["DS"]
    NPOOL = cfg["NPOOL"]
    NT = T // 128
    NTT = NT + 1
    TS = NS * DS
    TT = T + TS
    PAST = NPG * 128
    NCP = (T - 32) // 16 + 1
    NCS = (PAST + DS - 32) // 16 + 1
    NSELP = T // 64
    NSELS = -(-(PAST + DS) // 64)
    NKS = (NCS + 127) // 128
    WINT = min(512, T) // 128
    GT = 256
    GTL = GT // 128
    NG = (NT + GTL - 1) // GTL

    nc = bass.Bass("TRN2", target_bir_lowering=False)
    S = Sched()

    def din(name, shape, dt=F32):
        return nc.dram_tensor(name, list(shape), dt, kind="ExternalInput").ap()

    def dout(name, shape, dt=F32):
        return nc.dram_tensor(name, list(shape), dt, kind="ExternalOutput").ap()

    x_p = din("x_p", [T, D]); x_s = din("x_s", [TS, D])
    cache = din("cache", [DEPTH, NPOOL * 128, 512])
    cacheR = cache.rearrange("l r (q c) -> (l r q) c", c=128)
    st_win = din("st_win", [DEPTH, NS, 512, 256])
    st_cp = din("st_cp", [DEPTH, NS * 17, 256])
    ptab = din("ptab", [NS, NPG], I32)
    c_all = din("c_all", [1 + NS, D])
    spar = din("spar", [DEPTH, 80, 128])
    kgbc = din("kgbc", [DEPTH, 2, 128])
    w_ada = din("w_ada", [DEPTH, D, 6 * D]); b_ada = din("b_ada", [DEPTH, 6 * D])
    w_in = din("w_in", [DEPTH, D, IN_W]); w_out = din("w_out", [DEPTH, D, D])
    w1bd = din("w1bd", [DEPTH, 2, 128, 32, 128])
    w2bd = din("w2bd", [DEPTH, 2, 128, 128])
    pwbd = din("pwbd", [DEPTH, 2, 128, 128])
    pet = din("pet", [DEPTH, 2, 128, 32])
    w_up = din("w_up", [DEPTH, D, 2 * FFN_H]); w_down = din("w_down", [DEPTH, FFN_H, D])
    k_idf = din("k_idf", [128, 128]); k_idb = din("k_idb", [128, 128], BF16)
    k_caus = din("k_caus", [128, 128], BF16); k_anti = din("k_anti", [128, 128], BF16)
    k_cmpb = din("k_cmpb", [128, T], BF16)
    k_e64 = din("k_e64", [128, 32, 128], BF16)
    k_fbp = din("k_fbp", [128, NT, NSELP], BF16); k_fbs = din("k_fbs", [TS, NSELS])
    k_newb = din("k_newb", [TS, NS, 4 * DS], BF16); k_winb0 = din("k_winb0", [128, 4 * DS], BF16)
    k_ovp = din("k_ovp", [128, 1 + NSELP], BF16); k_ovs = din("k_ovs", [128, NKS, 1 + NSELS], BF16)
    k_rc = din("k_rc", [128, 2, 16]); k_bones = din("k_bones", [128, 128], BF16)
    k_cts = din("k_cts", [1 + NS, 128 + TS])

    y_p = dout("y_p", [T, D]); y_s = dout("y_s", [TS, D])
    kv_p = dout("kv_p", [DEPTH, T, 512]); kv_s = dout("kv_s", [DEPTH, TS, 512])
    win_p = dout("win_p", [DEPTH, WINT * 128, 256]); win_s = dout("win_s", [DEPTH, NS, 512, 256])
    conv_p = dout("conv_p", [DEPTH, 2, 256]); conv_s = dout("conv_s", [DEPTH, NS * 2, 256])
    pool_p = dout("pool_p", [DEPTH, 15, 256]); pool_s = dout("pool_s", [DEPTH, NS * 15, 256])

    import contextlib
    es = contextlib.ExitStack()
    with es:
        def sb(name, shape, dt=F32):
            return es.enter_context(nc.sbuf_tensor(name, list(shape), dt))

        X = sb("X", [128, NTT, D]); tX = [Tok() for _ in range(NTT)]
        GATE = sb("GATE", [128, 2, 2, D], BF16); tGATE = [[Tok(), Tok()], [Tok(), Tok()]]
        MODC = sb("MODC", [128, 4, 8, 1 + NS]); tMODC = Tok()
        IDF = sb("IDF", [128, 128]); IDB = sb("IDB", [128, 128], BF16)
        CAUS = sb("CAUS", [128, 128], BF16); ANTI = sb("ANTI", [128, 128], BF16)
        CMPB = sb("CMPB", [128, T], BF16); E64 = sb("E64", [128, 32, 128], BF16)
        FBP = sb("FBP", [128, NT, NSELP], BF16); FBS = sb("FBS", [TS, NSELS])
        NEWB = sb("NEWB", [128, NS, 4 * DS], BF16); WINB0 = sb("WINB0", [128, 4 * DS], BF16)
        RC16 = sb("RC16", [128, 2, 16]); BONES = sb("BONES", [128, 128], BF16)
        CTS = sb("CTS", [128, 8, 128 + TS], BF16)
        tK = Tok()
        PETB = sb("PETB", [128, 32], BF16); tPET = Tok()
        tKVP = [Tok() for _ in range(NT)]
        PH = sb("PH", [128, 4768])
        def phf(off, n):
            return PH[:, off:off + n]
        def phb(off, n):
            return PH[:, off:off + n].bitcast(BF16)
        SCR = sb("SCR", [128, 1056]); tSCR = Tok()
        SA = SCR[:].rearrange("p (a c) -> p a c", a=2); tSA = tSCR
        oA = [0]
        def takeA(n):
            o_ = oA[0]; oA[0] += n; return o_
        HT = phb(takeA(4 * GT), 4 * GT).rearrange("p (k c) -> p k c", k=8); tHT = Tok()
        _o = takeA(768); KVO = phf(_o, 768).rearrange("p (a c) -> p a c", a=1); tKVO = [Tok(), Tok()]
        OUTT = phf(_o, 256); tOUTT = tKVO[0]
        UG = phf(takeA(2 * (2 + GT)), 2 * (2 + GT)).rearrange("p (a c) -> p a c", a=2); tUG = Tok()
        PG = phf(takeA(2 * (16 + GT)), 2 * (16 + GT)).rearrange("p (a c) -> p a c", a=2)[:, :, 0:15 + GT]; tPG = Tok()
        YCP = phb(takeA(2 * GT), 2 * GT).rearrange("p (a c) -> p a c", a=4); tYCP = Tok()
        DPB = phb(takeA(GT // 2), GT // 2); tDPB = Tok()
        UGS = phf(takeA(80), 80).rearrange("p (a s c) -> p a s c", a=2, s=NS); PGS = phf(takeA(184), 184).rearrange("p (a s c) -> p a s c", a=2, s=NS); tUGS = Tok()
        JNK = phb(takeA(512), 512); tJNK = Tok()
        WST = phf(takeA(256), 256)[0:126]; tWST = Tok()
        CTF = phf(0, 1024)[0:1 + NS]; tCTF = Tok()
        PT = phb(0, 768).rearrange("p (a c) -> p a c", a=3); tPT = [Tok(), Tok(), Tok()]
        OACC = phf(768, 512); tOACC = Tok()
        OACS = phf(1280, 512)[0:TS]; tOACS = Tok()
        PTS = phb(1792, 768).rearrange("p (s i c) -> p s i c", s=NS, i=3); tPTS = [[Tok() for _ in range(3)] for _ in range(NS)]
        SC2 = phf(2560, 408).rearrange("p (a c) -> p a c", a=3); tSC2 = Tok()
        BT = phb(2968, 128).rearrange("p (a c) -> p a c", a=2); tBT = Tok()
        BTS = phb(3096, 64).rearrange("p (a g c) -> p a g c", a=2, g=2); tBTS = Tok()
        SH = phb(3160, 256); SQ = phb(3416, 256); tSH = Tok()
        OT = phb(3672, 256).rearrange("p (a c) -> p a c", a=4); tOT = Tok()
        VN0 = phb(3928, 66)[0:TS].rearrange("p (a c) -> p a c", a=2)[:, :, 0:65] if False else phb(3928, 65)[0:TS].rearrange("p (a c) -> p a c", a=2); tVN0 = Tok()
        IDXL = phf(3994, 2 * NS * NPG).bitcast(I32).rearrange("p (a c) -> p a c", a=2); tIDXL = Tok()
        IDXG = phf(4506, NS * NPG); tIDXG = Tok()
        UT = phb(0, 1024).rearrange("p (a c) -> p a c", a=4); tUT = [Tok(), Tok()]
        RSC = phf(1024, 1024).rearrange("p (a c) -> p a c", a=2); tRSC = [Tok(), Tok()]
        WA = sb("WA", [128, 24576], BF16); tWA = [Tok() for _ in range(8)]
        R = sb("R", [128, 17472], BF16)
        GSIG = sb("GSIG", [128, NTT, 24]); tGS = [Tok() for _ in range(NTT)]
        SPT = sb("SPT", [128, 80]); tSPT = Tok()
        KGB = sb("KGB", [128, 2, 128]); tKGB = Tok()
        ST1 = sb("ST1", [128, 64]); tST1 = Tok()
        QN = sb("QN", [128, 768], BF16); tQN = Tok()
        SPR = sb("SPR", [80, 128]); tSPR = Tok()
        BST = SCR[:, 0:512]; tBST = tSCR
        IDX = sb("IDX", [128, NS, NPG], I32); tIDX = Tok()
        PSB = [es.enter_context(nc.psum_tensor(f"ps{i}", [128, 512], F32)) for i in range(8)]
        tPS = [Tok(True) for _ in range(8)]
        psn = [0]

        def ps():
            i = psn[0] % 5
            psn[0] += 1
            return PSB[i], tPS[i]

        WIN = WA[:, 0:8 * IN_W].rearrange("p (k c) -> p k c", k=8)
        WOCP = WA[:, 18688:18688 + 4096].rearrange("p (k c) -> p k c", k=4)
        tWIN, tWOCP = tWA[0], tWA[1]
        WOA = WA[:, 0:4096].rearrange("p (k c) -> p k c", k=4); tWOA = tWA[2]
        RAWT = WA[:, 4096:4096 + 16 * 513].rearrange("p (r m) -> p r m", r=16); tRAWT = tWA[3]
        KTC = WA[:, 4096:4096 + 512]
        STG = WA[:, 12304:12304 + 2 * 1152].rearrange("p (b c) -> p b c", b=2); tSTG = [tWA[4], tWA[5]]
        W1B = WA[:, 14608:14608 + 4096].rearrange("p (q c) -> p q c", q=32); tW1B = tWA[6]
        KCS = WA[:, 18704:18704 + (1 + NS) * 512].rearrange("p (s c) -> p s c", c=512); tKCS = tWA[7]
        VCS = WA[:, 21264:21264 + (1 + NS) * 512].rearrange("p (s k c) -> p s k c", k=4, c=128); tVCS = Tok()
        W2B = WA[:, 23824:23824 + 256].rearrange("p (a c) -> p a c", a=2); tW2B = Tok()
        PWB = WA[:, 24080:24080 + 256].rearrange("p (a c) -> p a c", a=2)
        QT = R[:, 0:4 * TT].rearrange("p (a t) -> p a t", a=4); tQT = Tok()
        o = 4 * TT
        KST = R[:, o:o + TT]; tKST = Tok(); o += TT
        KWT = R[:, o:o + TT]; tKWT = Tok(); o += TT
        VS = R[:, o:o + NTT * 130].rearrange("p (i g c) -> p i g c", g=2, c=65); tVS = Tok(); o += NTT * 130
        VW = R[:, o:o + NTT * 130].rearrange("p (i g c) -> p i g c", g=2, c=65); tVW = Tok(); o += NTT * 130
        OVP = R[:, o:o + 1 + NSELP]; o += 1 + NSELP + (1 + NSELP) % 2
        OVS = R[:, o:o + NKS * (1 + NSELS)].rearrange("p (k c) -> p k c", k=NKS); o += NKS * (1 + NSELS)
        tOV = Tok()
        assert o <= 17472, o
        H2T = R[:, 0:8 * TT].rearrange("p (k t) -> p k t", k=8); tH2T = Tok()
        WADA = R[:, 0:16384].rearrange("p (b k c) -> p b k c", b=4, k=8); tWADA = [Tok() for _ in range(4)]
        WUP = [WA[:, b * 12288:b * 12288 + 8192].rearrange("p (k c) -> p k c", k=8) for b in range(2)]
        WDN = [WA[:, b * 12288 + 8192:b * 12288 + 12288].rearrange("p (j c) -> p j c", j=4) for b in range(2)]
        tWF = [tWA[0], tWA[1]]

        allR = [tQT, tKST, tKWT, tVS, tVW, tOV, tH2T] + tWADA
        allWA = tWA + [tVCS, tW2B]

        def V(e):
            return e

        def ld(q, out, in_, w, r=()):
            S.dma(q, lambda e: e.dma_start(out=out, in_=in_), reads=list(r), writes=list(w))

        ld("sp", IDF[:], k_idf, [tK]); ld("sp", IDB[:], k_idb, [tK]); ld("sp", CAUS[:], k_caus, [tK])
        ld("sp", ANTI[:], k_anti, [tK]); ld("sp", CMPB[:], k_cmpb, [tK]); ld("sp", E64[:], k_e64, [tK])
        ld("sp", FBP[:], k_fbp, [tK]); ld("sp", FBS[:], k_fbs, [tK]); pass
        ld("sp", WINB0[:], k_winb0, [tK]); ld("sp", RC16[:], k_rc, [tK]); ld("sp", BONES[:], k_bones, [tK])
        for i in range(NT):
            ld("sp", X[:, i, :], x_p[i * 128:(i + 1) * 128, :], [tX[i]])
        ld("sp", X[0:TS, NT, :], x_s, [tX[NT]])
        ld("sp", CTF[:], c_all, [tCTF])
        S.dma("pool", lambda e: e.dma_start(out=IDX[:].rearrange("p s j -> p (s j)"),
                                            in_=ptab.rearrange("s j -> (s j)").partition_broadcast(128)), writes=[tIDX])
        PIO = sb("PIO", [128, 2], I32); IDXF = sb("IDXF", [128, NS * NPG])
        S.op("pool", lambda e: e.iota(PIO[:, 0:1], [[0, 1]], base=0, channel_multiplier=1), writes=[tSCR])
        S.op("dve", lambda e: e.tensor_copy(out=SCR[:, 0:1], in_=PIO[:, 0:1]), reads=[tSCR], writes=[tSCR])
        S.op("dve", lambda e: e.tensor_copy(out=IDXF[:], in_=IDX[:].rearrange("p s j -> p (s j)")), reads=[tIDX], writes=[tIDX])
        S.op("dve", lambda e: e.tensor_scalar(out=IDXF[:], in0=IDXF[:], scalar1=128.0, scalar2=SCR[:, 0:1], op0=ALU.mult, op1=ALU.add),
             reads=[tIDX, tSCR], writes=[tIDX])
        S.op("dve", lambda e: e.tensor_copy(out=IDX[:].rearrange("p s j -> p (s j)"), in_=IDXF[:]), reads=[tIDX], writes=[tIDX])
        S.op("pool", lambda e: e.memset(NEWB[:], 0.0), writes=[tK])
        ld("sp", NEWB[0:TS], k_newb, [tK])
        S.op("act", lambda e: e.activation(out=CTF[:], in_=CTF[:], func=AF.Silu), reads=[tCTF], writes=[tCTF])
        SEL = sb("SEL", [1 + NS, 128 + TS]); tSEL = Tok()
        ld("sp", SEL[:], k_cts, [tSEL])
        for k in range(8):
            pb, tp = ps()
            S.op("pe", lambda e, k=k, pb=pb: e.matmul(pb[:, 0:128 + TS], lhsT=CTF[:, k * 128:(k + 1) * 128], rhs=SEL[:],
                                                      start=True, stop=True), reads=[tCTF, tSEL], writes=[tp])
            S.op("dve", lambda e, k=k, pb=pb: e.tensor_copy(out=CTS[:, k, :], in_=pb[:, 0:128 + TS]), reads=[tp], writes=[tK])
        CT5 = sb("CT5", [128, 8, 1 + NS], BF16)
        S.op("dve", lambda e: e.tensor_copy(out=CT5[:, :, 0:1], in_=CTS[:, :, 0:1]), reads=[tK], writes=[tK])
        S.op("dve", lambda e: e.tensor_copy(out=CT5[:, :, 1:1 + NS], in_=CTS[:, :, 128:128 + TS:DS]), reads=[tK], writes=[tK])

        def rstd_from_ss(ss_ap, n, inv, tss):
            S.op("dve", lambda e: e.tensor_scalar(out=ss_ap, in0=ss_ap, scalar1=inv, scalar2=EPS, op0=ALU.mult, op1=ALU.add),
                 reads=[tss], writes=[tss])
            S.op("act", lambda e: e.activation(out=ss_ap, in_=ss_ap, func=AF.Sqrt), reads=[tss], writes=[tss])
            S.op("dve", lambda e: e.reciprocal(out=ss_ap, in_=ss_ap), reads=[tss], writes=[tss])

        def norm_to_HT(tiles, modS, modG, dest, tdest, doff):
            for j, i in enumerate(tiles):
                rows = 128 if i < NT else TS
                S.op("act", lambda e, i=i, rows=rows: e.activation(out=JNK[0:rows, :], in_=X[0:rows, i, :], func=AF.Square,
                                                                   accum_out=ST1[0:rows, 0:1]), reads=[tX[i]], writes=[tJNK, tST1])
                rstd_from_ss(ST1[0:rows, 0:1], 1, 1.0 / D, tST1)
                S.op("dve", lambda e, i=i, rows=rows: e.tensor_scalar(out=SCR[0:rows, 0:1024], in0=X[0:rows, i, :], scalar1=ST1[0:rows, 0:1],
                                                                      scalar2=None, op0=ALU.mult), reads=[tX[i], tST1], writes=[tSCR])
                for hf in range(2):
                    pb, tp = ps()
                    for kk in range(4):
                        k = hf * 4 + kk
                        S.op("pe", lambda e, k=k, kk=kk, pb=pb, rows=rows: e.transpose(pb[:, kk * 128:kk * 128 + rows], SCR[0:rows, k * 128:(k + 1) * 128],
                                                                                       IDF[0:rows, 0:rows]), reads=[tSCR, tK], writes=[tp])
                    for kk in range(4):
                        k = hf * 4 + kk
                        if i < NT:
                            eng = "act" if kk % 2 == 0 else "dve"
                            if eng == "act":
                                S.op("act", lambda e, k=k, kk=kk, pb=pb, j=j: e.activation(out=dest[:, k, doff + j * 128:doff + (j + 1) * 128], in_=pb[:, kk * 128:(kk + 1) * 128],
                                                                                           func=AF.Identity, scale=MODC[:, modG, k, 0:1], bias=MODC[:, modS, k, 0:1]),
                                     reads=[tp, tMODC], writes=[tdest])
                            else:
                                S.op("dve", lambda e, k=k, kk=kk, pb=pb, j=j: e.tensor_scalar(out=dest[:, k, doff + j * 128:doff + (j + 1) * 128], in0=pb[:, kk * 128:(kk + 1) * 128],
                                                                                              scalar1=MODC[:, modG, k, 0:1], scalar2=MODC[:, modS, k, 0:1], op0=ALU.mult, op1=ALU.add),
                                     reads=[tp, tMODC], writes=[tdest])
                        else:
                            for s in range(NS):
                                S.op("dve", lambda e, k=k, kk=kk, pb=pb, j=j, s=s: e.tensor_scalar(
                                    out=dest[:, k, doff + j * 128 + s * DS:doff + j * 128 + (s + 1) * DS], in0=pb[:, kk * 128 + s * DS:kk * 128 + (s + 1) * DS],
                                    scalar1=MODC[:, modG, k, 1 + s:2 + s], scalar2=MODC[:, modS, k, 1 + s:2 + s], op0=ALU.mult, op1=ALU.add),
                                     reads=[tp, tMODC], writes=[tdest])

        def resid_add(i, pb, tp, half, gi):
            rows = 128 if i < NT else TS
            gsel = 0 if i < NT else 1
            S.op("dve", lambda e: e.tensor_tensor(out=SCR[0:rows, 0:512], in0=pb[0:rows, :], in1=GATE[0:rows, gi, gsel, half * 512:(half + 1) * 512], op=ALU.mult),
                 reads=[tp, tGATE[gi][gsel]], writes=[tSCR])
            S.op("dve", lambda e: e.tensor_tensor(out=X[0:rows, i, half * 512:(half + 1) * 512], in0=X[0:rows, i, half * 512:(half + 1) * 512], in1=SCR[0:rows, 0:512], op=ALU.add),
                 reads=[tSCR, tX[i]], writes=[tX[i]])

        def cast_load(out, in_, w, r=()):
            S.dma("pool", lambda e: e.dma_start(out=out, in_=in_), reads=list(r), writes=list(w))

        def LAYER_BODY(l):
            S.barrier()
            MARKS.append(("A", l, dict(S.cnt)))
            for hh in range(2):
                cast_load(WIN[:, :, hh * 1164:(hh + 1) * 1164], w_in[l, :, hh * 1164:(hh + 1) * 1164].rearrange("(k p) c -> p k c", p=128), [tWIN] + allWA)
            cast_load(WOCP, w_out[l, 512:1024, :].rearrange("(k p) c -> p k c", p=128), [tWOCP])
            cast_load(PWB, pwbd[l].rearrange("a p c -> p a c"), [tW2B])
            S.op("pool", lambda e: e.memset(VS[:, :, :, 64:65], 1.0), writes=[tVS] + tWADA)
            S.op("pool", lambda e: e.memset(VW[:, :, :, 64:65], 1.0), writes=[tVW])
            ld("sp", OVP, k_ovp, [tOV]); ld("sp", OVS, k_ovs, [tOV])
            groups = [list(range(g * GTL, min(NT, g * GTL + GTL))) for g in range(NG)] + [[NT]]
            for tiles in groups:
                samp = tiles[0] == NT
                ncol = TS if samp else 128 * len(tiles)
                norm_to_HT(tiles, 0, 1, HT, tHT, 0)
                for j, i in enumerate(tiles):
                    rows = TS if samp else 128
                    c0 = j * 128
                    pq, tq = ps()
                    for k in range(8):
                        S.op("pe", lambda e, k=k, pq=pq, c0=c0, rows=rows: e.matmul(pq[0:rows, :], lhsT=HT[:, k, c0:c0 + rows], rhs=WIN[:, k, 0:512],
                                                                                  start=(k == 0), stop=(k == 7)), reads=[tHT, tWIN], writes=[tq])
                    pk, tk = ps()
                    for k in range(8):
                        S.op("pe", lambda e, k=k, pk=pk, c0=c0, rows=rows: e.matmul(pk[0:rows, :], lhsT=HT[:, k, c0:c0 + rows], rhs=WIN[:, k, 512:1024],
                                                                                  start=(k == 0), stop=(k == 7)), reads=[tHT, tWIN], writes=[tk])
                    pw, tw = ps()
                    for k in range(8):
                        S.op("pe", lambda e, k=k, pw=pw, c0=c0, rows=rows: e.matmul(pw[0:rows, 0:280], lhsT=HT[:, k, c0:c0 + rows], rhs=WIN[:, k, 1024:1304],
                                                                                  start=(k == 0), stop=(k == 7)), reads=[tHT, tWIN], writes=[tw])
                    kb = 0
                    KV = KVO[:, kb, :]
                    S.op("act", lambda e, pk=pk, KV=KV, rows=rows: e.activation(out=KV[0:rows, 0:256], in_=pk[0:rows, 0:256], func=AF.Identity), reads=[tk], writes=[tKVO[kb]])
                    S.op("act", lambda e, pk=pk, KV=KV, rows=rows: e.activation(out=KV[0:rows, 384:512], in_=pk[0:rows, 384:512], func=AF.Identity), reads=[tk], writes=[tKVO[kb]])
                    S.op("act", lambda e, pw=pw, KV=KV, rows=rows: e.activation(out=KV[0:rows, 640:768], in_=pw[0:rows, 128:256], func=AF.Identity), reads=[tw], writes=[tKVO[kb]])
                    S.op("act", lambda e, pq=pq, rows=rows: e.activation(out=SCR[0:rows, 0:512], in_=pq[0:rows, :], func=AF.Square), reads=[tq], writes=[tSCR])
                    S.op("act", lambda e, pk=pk, rows=rows: e.activation(out=SCR[0:rows, 512:640], in_=pk[0:rows, 256:384], func=AF.Square), reads=[tk], writes=[tSCR])
                    S.op("act", lambda e, pw=pw, rows=rows: e.activation(out=SCR[0:rows, 640:768], in_=pw[0:rows, 0:128], func=AF.Square), reads=[tw], writes=[tSCR])
                    S.op("dve", lambda e, rows=rows: e.tensor_reduce(out=ST1[0:rows, 0:12], in_=SCR[0:rows, 0:768].rearrange("p (h d) -> p h d", d=64), axis=AX.X, op=ALU.add),
                         reads=[tSCR], writes=[tST1])
                    rstd_from_ss(ST1[0:rows, 0:12], 12, 1.0 / 64, tST1)
                    S.op("dve", lambda e, pq=pq, rows=rows: e.tensor_tensor(
                        out=QN[0:rows, 0:512].rearrange("t (p a d) -> t a p d", a=2, p=4), in0=pq[0:rows, :].rearrange("t (a p d) -> t a p d", a=2, p=4),
                        in1=ST1[0:rows, 0:8].rearrange("t (a p) -> t a p", a=2).unsqueeze(3).to_broadcast([rows, 2, 4, 64]), op=ALU.mult), reads=[tq, tST1], writes=[tQN])
                    for (src, so, col, do, gi2) in ((pk, 256, 8, 256, 0), (pw, 0, 10, 512, 1)):
                        tsrc = tk if src is pk else tw
                        S.op("dve", lambda e, src=src, so=so, col=col, do=do, KV=KV, rows=rows: e.tensor_tensor(
                            out=KV[0:rows, do:do + 128].rearrange("t (g d) -> t g d", g=2), in0=src[0:rows, so:so + 128].rearrange("t (g d) -> t g d", g=2),
                            in1=ST1[0:rows, col:col + 2].unsqueeze(2).to_broadcast([rows, 2, 64]), op=ALU.mult), reads=[tsrc, tST1], writes=[tKVO[kb]])
                        S.op("dve", lambda e, do=do, gi2=gi2, KV=KV, rows=rows: e.tensor_tensor(out=KV[0:rows, do:do + 128], in0=KV[0:rows, do:do + 128], in1=KGB[0:rows, gi2, :], op=ALU.mult),
                             reads=[tKGB], writes=[tKVO[kb]])
                    if samp:
                        S.dma("sp", lambda e, KV=KV: e.dma_start(out=kv_s[l], in_=KV[0:TS, 0:512]), reads=[tKVO[kb]])
                        for s in range(NS):
                            S.dma("sp", lambda e, KV=KV, s=s: e.dma_start(out=win_s[l, s, 512 - DS:512, :], in_=KV[s * DS:(s + 1) * DS, 512:768]), reads=[tKVO[kb]])
                            for a_ in range(4):
                                S.dma("sp", lambda e, s=s, a_=a_: e.dma_start(out=WST[:, :], in_=st_win[l, s, DS:512, :].rearrange("(p a) c -> p a c", a=4)[:, a_, :]), writes=[tWST])
                                S.dma("sp", lambda e, s=s, a_=a_: e.dma_start(out=win_s[l, s, 0:512 - DS, :].rearrange("(p a) c -> p a c", a=4)[:, a_, :], in_=WST[:, :]), reads=[tWST])
                    else:
                        S.dma("sp", lambda e, KV=KV, i=i: e.dma_start(out=kv_p[l, i * 128:(i + 1) * 128, :], in_=KV[:, 0:512]), reads=[tKVO[kb]], writes=[tKVP[i]])
                        if i >= NT - WINT:
                            S.dma("sp", lambda e, KV=KV, i=i: e.dma_start(out=win_p[l, (i - NT + WINT) * 128:(i - NT + WINT + 1) * 128, :], in_=KV[:, 512:768]), reads=[tKVO[kb]])
                    S.op("act", lambda e, pw=pw, i=i, rows=rows: e.activation(out=GSIG[0:rows, i, :], in_=pw[0:rows, 256:280], func=AF.Exp, scale=-1.0), reads=[tw], writes=[tGS[i]])
                    S.op("dve", lambda e, i=i, rows=rows: e.tensor_scalar(out=GSIG[0:rows, i, :], in0=GSIG[0:rows, i, :], scalar1=1.0, scalar2=None, op0=ALU.add), reads=[tGS[i]], writes=[tGS[i]])
                    S.op("dve", lambda e, i=i, rows=rows: e.reciprocal(out=GSIG[0:rows, i, :], in_=GSIG[0:rows, i, :]), reads=[tGS[i]], writes=[tGS[i]])
                    S.op("act", lambda e, KV=KV, i=i, rows=rows: e.activation(out=VS[0:rows, i, :, 0:64], in_=KV[0:rows, 384:512].rearrange("t (g d) -> t g d", g=2), func=AF.Identity),
                         reads=[tKVO[kb]], writes=[tVS])
                    S.op("act", lambda e, KV=KV, i=i, rows=rows: e.activation(out=VW[0:rows, i, :, 0:64], in_=KV[0:rows, 640:768].rearrange("t (g d) -> t g d", g=2), func=AF.Identity),
                         reads=[tKVO[kb]], writes=[tVW])
                    S.op("act", lambda e, KV=KV, rows=rows: e.activation(out=QN[0:rows, 512:640], in_=KV[0:rows, 256:384], func=AF.Identity), reads=[tKVO[kb]], writes=[tQN])
                    S.op("act", lambda e, KV=KV, rows=rows: e.activation(out=QN[0:rows, 640:768], in_=KV[0:rows, 512:640], func=AF.Identity), reads=[tKVO[kb]], writes=[tQN])
                    pt_, tt_ = ps()
                    ptv = pt_[:].bitcast(BF16)
                    for b6 in range(6):
                        S.op("pe", lambda e, b6=b6, ptv=ptv, rows=rows: e.transpose(ptv[:, b6 * 128:b6 * 128 + rows], QN[0:rows, b6 * 128:(b6 + 1) * 128], IDB[0:rows, 0:rows]),
                             reads=[tQN, tK], writes=[tt_])
                    tc0 = i * 128
                    S.op("dve", lambda e, ptv=ptv, tc0=tc0, rows=rows: e.tensor_scalar(out=QT[:, :, tc0:tc0 + rows], in0=ptv[:, 0:512].rearrange("p (a t) -> p a t", a=4)[:, :, 0:rows],
                                                                                       scalar1=SPT[:, 74:75], scalar2=None, op0=ALU.mult), reads=[tt_, tSPT], writes=[tQT])
                    S.op("act", lambda e, ptv=ptv, tc0=tc0, rows=rows: e.activation(out=KST[:, tc0:tc0 + rows], in_=ptv[:, 512:512 + rows], func=AF.Identity), reads=[tt_], writes=[tKST])
                    S.op("act", lambda e, ptv=ptv, tc0=tc0, rows=rows: e.activation(out=KWT[:, tc0:tc0 + rows], in_=ptv[:, 640:640 + rows], func=AF.Identity), reads=[tt_], writes=[tKWT])


                n = ncol
                if samp:
                    S.dma("sp", lambda e: e.dma_start(out=OUTT[0:NS * 17, :], in_=st_cp[l]), writes=[tOUTT])
                    for c in range(2):
                        ph, th = ps()
                        S.op("pe", lambda e, c=c, ph=ph: e.transpose(ph[:, 0:NS * 17], OUTT[0:NS * 17, c * 128:(c + 1) * 128], IDF[0:NS * 17, 0:NS * 17]), reads=[tOUTT, tK], writes=[th])
                        S.op("dve", lambda e, c=c, ph=ph: e.tensor_copy(out=PGS[:, c, :, 0:15], in_=ph[:, 0:NS * 15].rearrange("p (s x) -> p s x", s=NS)), reads=[th], writes=[tUGS])
                        S.op("dve", lambda e, c=c, ph=ph: e.tensor_copy(out=UGS[:, c, :, 0:2], in_=ph[:, NS * 15:NS * 17].rearrange("p (s x) -> p s x", s=NS)), reads=[th], writes=[tUGS])
                    Uv = lambda c, a, b: UGS[:, c, :, a:b]
                    Pv = lambda c, a, b: PGS[:, c, :, a:b]
                    Sv = lambda sl_, a, b: SA[:, sl_, 0:NS * 23].rearrange("p (s x) -> p s x", s=NS)[:, :, a:b]
                    psv = lambda pb_: pb_[:, 0:TS].rearrange("p (s t) -> p s t", s=NS)
                    Yv = lambda k_: YCP[:, k_, 0:TS].rearrange("p (s t) -> p s t", s=NS)
                    Dv = lambda: DPB[:, 0:TS].rearrange("p (s t) -> p s t", s=NS)
                    nn = DS; tU = tUGS; tP = tUGS
                else:
                    Uv = lambda c, a, b: UG[:, c, a:b]
                    Pv = lambda c, a, b: PG[:, c, a:b]
                    Sv = lambda sl_, a, b: SA[:, sl_, a:b]
                    psv = lambda pb_: pb_[:, 0:n]
                    Yv = lambda k_: YCP[:, k_, 0:n]
                    Dv = lambda: DPB[:, 0:n]
                    nn = n; tU = tUG; tP = tPG
                    if tiles[0] == 0:
                        S.op("pool", lambda e: e.memset(UG[:, :, 0:2], 0.0), writes=[tUG])
                        S.op("pool", lambda e: e.memset(PG[:, :, 0:15], 0.0), writes=[tPG])

                def zT(cc):
                    pz, tz = ps()
                    for k in range(8):
                        S.op("pe", lambda e, k=k, pz=pz, cc=cc: e.matmul(pz[:, 0:n], lhsT=WIN[:, k, 1304 + cc * 128:1304 + (cc + 1) * 128], rhs=HT[:, k, 0:n],
                                                                       start=(k == 0), stop=(k == 7)), reads=[tWIN, tHT], writes=[tz])
                    return pz, tz

                hp = lambda ap, lo, hi: ap[lo:hi]
                for c in range(2):
                    pcg, tcg = zT(2 + c)
                    phn, thn = zT(4 + c)
                    S.op("act", lambda e, pcg=pcg: e.activation(out=Sv(0, 0, nn), in_=psv(pcg), func=AF.Identity), reads=[tcg], writes=[tSA])
                    S.op("dve", lambda e, phn=phn, c=c: e.tensor_tensor(out=Uv(c, 2, 2 + nn), in0=Sv(0, 0, nn), in1=psv(phn), op=ALU.mult), reads=[tSA, thn], writes=[tU])
                    S.op("dve", lambda e, c=c: e.tensor_scalar(out=Sv(1, 0, nn), in0=Uv(c, 2, 2 + nn), scalar1=SPT[:, 68 + c:69 + c], scalar2=SPT[:, 70 + c:71 + c], op0=ALU.mult, op1=ALU.add),
                         reads=[tU, tSPT], writes=[tSA])
                    S.op("dve", lambda e, c=c: e.scalar_tensor_tensor(out=Sv(1, 0, nn), in0=Uv(c, 1, 1 + nn), scalar=SPT[:, 66 + c:67 + c], in1=Sv(1, 0, nn), op0=ALU.mult, op1=ALU.add),
                         reads=[tU, tSPT, tSA], writes=[tSA])
                    S.op("dve", lambda e, c=c: e.scalar_tensor_tensor(out=Sv(1, 0, nn), in0=Uv(c, 0, nn), scalar=SPT[:, 64 + c:65 + c], in1=Sv(1, 0, nn), op0=ALU.mult, op1=ALU.add),
                         reads=[tU, tSPT, tSA], writes=[tSA])
                    pbg, tbg = zT(c)
                    S.op("dve", lambda e, pbg=pbg, c=c: e.tensor_tensor(out=Yv(c), in0=Sv(1, 0, nn), in1=psv(pbg), op=ALU.mult), reads=[tSA, tbg], writes=[tYCP])
                    ppi, tpi = zT(6 + c)
                    S.op("act", lambda e, ppi=ppi, c=c: e.activation(out=Pv(c, 15, 15 + nn), in_=psv(ppi), func=AF.Identity), reads=[tpi], writes=[tP])
                    L = 15 + nn
                    S.op("dve", lambda e, c=c: e.tensor_tensor(out=Sv(0, 1, L), in0=Pv(c, 1, L), in1=Pv(c, 0, L - 1), op=ALU.add), reads=[tP], writes=[tSA])
                    S.op("dve", lambda e, c=c: e.tensor_tensor(out=Sv(1, 3, L), in0=Sv(0, 3, L), in1=Sv(0, 1, L - 2), op=ALU.add), reads=[tSA], writes=[tSA])
                    if c == 0:
                        lo_s, lo_w, hi_s, hi_w = 0, 2.0, 1, 4.0
                    else:
                        S.op("dve", lambda e: e.tensor_tensor(out=Sv(0, 7, L), in0=Sv(1, 7, L), in1=Sv(1, 3, L - 4), op=ALU.add), reads=[tSA], writes=[tSA])
                        S.op("dve", lambda e: e.tensor_tensor(out=Sv(1, 15, L), in0=Sv(0, 15, L), in1=Sv(0, 7, L - 8), op=ALU.add), reads=[tSA], writes=[tSA])
                        lo_s, lo_w, hi_s, hi_w = 0, 8.0, 1, 16.0
                    for (plo, phi, ssl, ww) in ((0, 64, lo_s, lo_w), (64, 128, hi_s, hi_w)):
                        S.op("dve", lambda e, plo=plo, phi=phi, ssl=ssl, ww=ww, c=c: e.scalar_tensor_tensor(out=Dv()[plo:phi], in0=Sv(ssl, 15, L)[plo:phi], scalar=1.0 / ww, in1=Pv(c, 15, L)[plo:phi],
                                                                                                       op0=ALU.mult, op1=ALU.subtract), reads=[tSA, tP], writes=[tDPB])
                        if (not samp) and tiles[0] == 0:
                            S.op("dve", lambda e, plo=plo, phi=phi, ssl=ssl, c=c: e.tensor_tensor(out=ST1[plo:phi, 16:32], in0=SA[plo:phi, ssl, 15:31], in1=RC16[plo:phi, c, :], op=ALU.mult),
                                 reads=[tSA, tK], writes=[tST1])
                            S.op("dve", lambda e, plo=plo, phi=phi, c=c: e.tensor_tensor(out=DPB[plo:phi, 0:16], in0=ST1[plo:phi, 16:32], in1=PG[plo:phi, c, 15:31], op=ALU.subtract),
                                 reads=[tST1, tP], writes=[tDPB])
                    py, ty = ps()
                    S.op("pe", lambda e, py=py, c=c: e.matmul(py[:, 0:n], lhsT=PWB[:, c, :], rhs=DPB[:, 0:n], start=True, stop=True), reads=[tDPB, tW2B], writes=[ty])
                    S.op("dve", lambda e, py=py, c=c: e.tensor_scalar(out=YCP[:, 2 + c, 0:n], in0=py[:, 0:n], scalar1=SPT[:, 72 + c:73 + c], scalar2=None, op0=ALU.mult), reads=[ty, tSPT], writes=[tYCP])
                    last = samp or tiles[-1] == NT - 1
                    if not samp and not last:
                        S.op("pool", lambda e, c=c: e.tensor_copy(out=UG[:, c, 0:2], in_=UG[:, c, nn:nn + 2]), reads=[tUG], writes=[tUG])
                        S.op("pool", lambda e, c=c: e.tensor_copy(out=PG[:, c, 0:15], in_=PG[:, c, nn:nn + 15]), reads=[tPG], writes=[tPG])
                    if last:
                        nq = NS if samp else 1
                        pst, tst = ps()
                        if samp:
                            S.op("dve", lambda e, c=c: e.tensor_copy(out=SA[:, 0, 0:NS * 15].rearrange("p (s x) -> p s x", s=NS), in_=PGS[:, c, :, DS:DS + 15]), reads=[tUGS], writes=[tSA])
                            S.op("dve", lambda e, c=c: e.tensor_copy(out=SA[:, 0, NS * 15:NS * 17].rearrange("p (s x) -> p s x", s=NS), in_=UGS[:, c, :, DS:DS + 2]), reads=[tUGS], writes=[tSA])
                        else:
                            S.op("dve", lambda e, c=c: e.tensor_copy(out=SA[:, 0, 0:15], in_=PG[:, c, nn:nn + 15]), reads=[tPG], writes=[tSA])
                            S.op("dve", lambda e, c=c: e.tensor_copy(out=SA[:, 0, 15:17], in_=UG[:, c, nn:nn + 2]), reads=[tUG], writes=[tSA])
                        S.op("pe", lambda e, pst=pst, nq=nq: e.transpose(pst[0:nq * 17, 0:128], SA[:, 0, 0:nq * 17], IDF[:]), reads=[tSA, tK], writes=[tst])
                        S.op("dve", lambda e, pst=pst, nq=nq, c=c: e.tensor_copy(out=OUTT[0:nq * 17, c * 128:(c + 1) * 128], in_=pst[0:nq * 17, 0:128]), reads=[tst], writes=[tOUTT])
                if samp or tiles[-1] == NT - 1:
                    nq = NS if samp else 1
                    S.dma("sp", lambda e, nq=nq: e.dma_start(out=(pool_s if samp else pool_p)[l], in_=OUTT[0:nq * 15, :]), reads=[tOUTT])
                    S.dma("sp", lambda e, nq=nq: e.dma_start(out=(conv_s if samp else conv_p)[l], in_=OUTT[nq * 15:nq * 17, :]), reads=[tOUTT])
                for j, i in enumerate(tiles):
                    rows = TS if samp else 128
                    for half in range(2):
                        po, to = ps()
                        for k in range(4):
                            S.op("pe", lambda e, k=k, po=po, j=j, rows=rows, half=half: e.matmul(po[0:rows, :], lhsT=YCP[:, k, j * 128:j * 128 + rows], rhs=WOCP[:, k, half * 512:(half + 1) * 512],
                                                                                               start=(k == 0), stop=(k == 3)), reads=[tYCP, tWOCP], writes=[to])
                        resid_add(i, po, to, half, 0)


            ACC = [(PSB[5], tPS[5]), (PSB[6], tPS[6]), (PSB[7], tPS[7])]
            accn = [0]

            def acc():
                a = ACC[accn[0] % 3]
                accn[0] += 1
                return a
            PEND = [None]

            def push(fn):
                old = PEND[0]
                PEND[0] = fn
                if old is not None:
                    old()

            def flush():
                old = PEND[0]
                PEND[0] = None
                if old is not None:
                    old()
            ptn = [0]

            def ptbuf():
                i_ = ptn[0] % 3
                ptn[0] += 1
                return PT[:, i_, :], tPT[i_]

            S.barrier()
            MARKS.append(("B", l, dict(S.cnt)))
            S.op("pool", lambda e: e.memset(PTS[:], 0.0), writes=[t for r_ in tPTS for t in r_])
            S.op("pool", lambda e: e.memset(BTS[:], 0.0), writes=[tBTS])
            S.op("pool", lambda e: e.memset(BT[:], 0.0), writes=[tBT])
            S.op("pool", lambda e: e.memset(VN0[:, :, 0:1], 1.0), writes=[tVN0])
            cast_load(WOA, w_out[l, 0:512, :].rearrange("(k p) c -> p k c", p=128), [tWOA] + allWA)
            cast_load(W2B, w2bd[l].rearrange("a p c -> p a c"), [tW2B])

            def compress(slot, seq, nblk, loader, npages):
                for pg0 in range(0, npages, 8):
                    npc = min(8, npages - pg0)
                    b = (pg0 // 8) % 2
                    stg = STG[:, b, 0:1024].rearrange("p (j c) -> p j c", c=128)
                    loader(stg, tSTG[b], pg0, npc)
                    pt_, tt_ = ps()
                    ptv = pt_[:].bitcast(BF16)
                    for j in range(npc):
                        S.op("pe", lambda e, j=j: e.transpose(ptv[:, j * 128:(j + 1) * 128], stg[:, j, :], IDB[:]), reads=[tSTG[b], tK], writes=[tt_])
                    S.op("dve", lambda e: e.tensor_copy(out=RAWT[:, :, pg0 * 8:(pg0 + npc) * 8], in_=ptv[:, 0:npc * 128].rearrange("p (m r) -> p r m", r=16)),
                         reads=[tt_], writes=[tRAWT])
                ph, th = ps()
                for p in range(32):
                    S.op("pe", lambda e, p=p: e.matmul(ph[:, 0:nblk], lhsT=W1B[:, p, :], rhs=RAWT[:, p % 16, p // 16:p // 16 + nblk], start=(p == 0), stop=(p == 31)),
                         reads=[tW1B, tRAWT], writes=[th])
                S.op("act", lambda e: e.activation(out=SH[:, 0:nblk], in_=ph[:, 0:nblk], func=AF.Silu, bias=ST1[:, 32:33]), reads=[th, tST1], writes=[tSH])
                if slot == 0:
                    p2, t2 = ps()
                    S.op("pe", lambda e: e.matmul(p2[:, 0:nblk], lhsT=W2B[:, 0, :], rhs=SH[:, 0:nblk], start=True, stop=True), reads=[tSH, tW2B], writes=[t2])
                    S.op("act", lambda e: e.activation(out=SQ[:, 0:nblk], in_=p2[:, 0:nblk], func=AF.Square), reads=[t2], writes=[tSH])
                    p3, t3 = ps()
                    S.op("pe", lambda e: e.matmul(p3[:, 0:nblk], lhsT=BONES[:], rhs=SQ[:, 0:nblk], start=True, stop=True), reads=[tSH, tK], writes=[t3])
                    S.op("dve", lambda e: e.tensor_scalar(out=SCR[:, 0:nblk], in0=p3[:, 0:nblk], scalar1=1.0 / 64, scalar2=EPS, op0=ALU.mult, op1=ALU.add), reads=[t3], writes=[tSCR])
                    S.op("act", lambda e: e.activation(out=SCR[:, 0:nblk], in_=SCR[:, 0:nblk], func=AF.Sqrt), reads=[tSCR], writes=[tSCR])
                    S.op("dve", lambda e: e.reciprocal(out=SCR[:, 0:nblk], in_=SCR[:, 0:nblk]), reads=[tSCR], writes=[tSCR])
                    S.op("dve", lambda e: e.scalar_tensor_tensor(out=KCS[:, seq, 0:nblk], in0=p2[:, 0:nblk], scalar=SPT[:, 75:76], in1=SCR[:, 0:nblk], op0=ALU.mult, op1=ALU.mult),
                         reads=[t2, tSPT, tSCR], writes=[tKCS])
                else:
                    for kt in range((nblk + 127) // 128):
                        nb_ = min(128, nblk - kt * 128)
                        p2, t2 = ps()
                        S.op("pe", lambda e, kt=kt, nb_=nb_: e.matmul(p2[0:nb_, 0:128], lhsT=SH[:, kt * 128:kt * 128 + nb_], rhs=W2B[:, 1, :], start=True, stop=True),
                             reads=[tSH, tW2B], writes=[t2])
                        S.op("act", lambda e, kt=kt, nb_=nb_: e.activation(out=VCS[0:nb_, seq, kt, :], in_=p2[0:nb_, 0:128], func=AF.Identity), reads=[t2], writes=[tVCS])

            def mk_idxl(which, slot):
                S.op("dve", lambda e: e.tensor_scalar(out=IDXG[:], in0=IDXF[:], scalar1=float(l * NPOOL * 128), scalar2=4.0, op0=ALU.add, op1=ALU.mult), reads=[tIDX], writes=[tIDXG])
                S.op("dve", lambda e: e.tensor_scalar(out=IDXG[:], in0=IDXG[:], scalar1=float(slot), scalar2=None, op0=ALU.add), reads=[tIDXG], writes=[tIDXG])
                S.op("dve", lambda e: e.tensor_copy(out=IDXL[:, which, :], in_=IDXG[:]), reads=[tIDXG], writes=[tIDXL])

            def gather(out_ap, which, sq, pg, wtok, join=False):
                S.dma("pool", lambda e: e.indirect_dma_start(out=out_ap, out_offset=None, in_=cacheR,
                                                             in_offset=bass.IndirectOffsetOnAxis(ap=IDXL[:, which, sq * NPG + pg:sq * NPG + pg + 1], axis=0)),
                      reads=[tIDXL], writes=[wtok], join=join)

            def sample_loader(which, slot, sq):
                def f(stg, tstg, pg0, npc):
                    for j in range(npc):
                        gather(stg[:, j, :], which, sq, pg0 + j, tstg, join=(j > 0))
                return f

            def prompt_loader(slot):
                def f(stg, tstg, pg0, npc):
                    S.dma("pool", lambda e: e.dma_start(out=stg[:, 0:npc, :], in_=kv_p[l, pg0 * 128:(pg0 + npc) * 128, slot * 128:(slot + 1) * 128].rearrange("(j p) c -> p j c", p=128)),
                          reads=tKVP[pg0:pg0 + npc], writes=[tstg])
                return f

            for slot in range(2):
                for hh in range(2):
                    cast_load(W1B[:, hh * 16:(hh + 1) * 16, :], w1bd[l, slot, :, hh * 16:(hh + 1) * 16, :], [tW1B])
                cast_load(PETB[:], pet[l, slot], [tPET])
                pbias, tbias = ps()
                for p in range(32):
                    S.op("pe", lambda e, p=p: e.matmul(pbias[:, 0:2], lhsT=W1B[:, p, :], rhs=PETB[:, p:p + 1].to_broadcast([128, 2]), start=(p == 0), stop=(p == 31)),
                         reads=[tW1B, tPET], writes=[tbias])
                S.op("dve", lambda e: e.tensor_copy(out=ST1[:, 32:33], in_=pbias[:, 0:1]), reads=[tbias], writes=[tST1])
                compress(slot, 0, NCP, prompt_loader(slot), NT)
                mk_idxl(0, slot)
                import os
                for sq in range(NS if not os.environ.get('NOSCMP') else 0):
                    compress(slot, 1 + sq, NCS, sample_loader(0, slot, sq), NPG)

            MARKS.append(("Bp", l, dict(S.cnt)))
            def attn_epilogue(ac, tac, g, br, qt, first, stride, rows=128, nh=4, h0=0, ocol=0, scol=64, dest=None, tdest=None):
                dest = OACC if dest is None else dest
                tdest = tOACC if tdest is None else tdest
                hv = ac[0:rows, 0:nh * stride].rearrange("p (h c) -> p h c", c=stride)
                S.op("dve", lambda e: e.tensor_scalar(out=ST1[0:rows, 40:40 + nh], in0=hv[:, :, scol], scalar1=1e-30, scalar2=None, op0=ALU.add), reads=[tac], writes=[tST1])
                S.op("dve", lambda e: e.reciprocal(out=ST1[0:rows, 40:40 + nh], in_=ST1[0:rows, 40:40 + nh]), reads=[tST1], writes=[tST1])
                S.op("dve", lambda e: e.tensor_tensor(out=ST1[0:rows, 44:44 + nh], in0=ST1[0:rows, 40:40 + nh],
                                                      in1=GSIG[0:rows, qt, :].rearrange("p (h b) -> p h b", b=3)[:, 4 * g + h0:4 * g + h0 + nh, br], op=ALU.mult), reads=[tST1, tGS[qt]], writes=[tST1])
                for h in range(nh):
                    hh_ = 4 * g + h0 + h
                    if first:
                        S.op("dve", lambda e, h=h, hh_=hh_: e.tensor_scalar(out=dest[0:rows, hh_ * 64:(hh_ + 1) * 64], in0=hv[:, h, ocol:ocol + 64], scalar1=ST1[0:rows, 44 + h:45 + h], scalar2=None, op0=ALU.mult),
                             reads=[tac, tST1], writes=[tdest])
                    else:
                        S.op("dve", lambda e, h=h, hh_=hh_: e.scalar_tensor_tensor(out=dest[0:rows, hh_ * 64:(hh_ + 1) * 64], in0=hv[:, h, ocol:ocol + 64], scalar=ST1[0:rows, 44 + h:45 + h],
                                                                                  in1=dest[0:rows, hh_ * 64:(hh_ + 1) * 64], op0=ALU.mult, op1=ALU.add), reads=[tac, tST1, tdest], writes=[tdest])

            def select(score_ap, nsel, rows):
                if nsel > 16:
                    S.op("dve", lambda e: e.max(out=SC2[0:rows, 2, 0:8], in_=score_ap), reads=[tSC2], writes=[tSC2])
                    S.op("dve", lambda e: e.match_replace(out=SC2[0:rows, 1, 0:nsel], in_to_replace=SC2[0:rows, 2, 0:8], in_values=score_ap, imm_value=-3e38), reads=[tSC2], writes=[tSC2])
                    S.op("dve", lambda e: e.max(out=SC2[0:rows, 2, 8:16], in_=SC2[0:rows, 1, 0:nsel]), reads=[tSC2], writes=[tSC2])
                    S.op("dve", lambda e: e.tensor_scalar(out=SC2[0:rows, 1, 0:nsel], in0=score_ap, scalar1=SC2[0:rows, 2, 15:16], scalar2=None, op0=ALU.is_ge), reads=[tSC2], writes=[tSC2])
                    S.op("dve", lambda e: e.tensor_scalar(out=score_ap, in0=score_ap, scalar1=-5e29, scalar2=None, op0=ALU.is_gt), reads=[tSC2], writes=[tSC2])
                    S.op("dve", lambda e: e.tensor_tensor(out=score_ap, in0=score_ap, in1=SC2[0:rows, 1, 0:nsel], op=ALU.mult), reads=[tSC2], writes=[tSC2])
                else:
                    S.op("dve", lambda e: e.tensor_scalar(out=score_ap, in0=score_ap, scalar1=-5e29, scalar2=None, op0=ALU.is_gt), reads=[tSC2], writes=[tSC2])
                S.op("dve", lambda e: e.tensor_scalar(out=score_ap, in0=score_ap, scalar1=-1.0, scalar2=-NEGB, op0=ALU.add, op1=ALU.mult), reads=[tSC2], writes=[tSC2])

            def cmp_branch(qt, g):
                qc = slice(qt * 128, (qt + 1) * 128)
                gp = slice(64 * g, 64 * g + 64)
                pS, tS_ = ps()
                S.op("pe", lambda e: e.matmul(pS[0:NCP, :], lhsT=KCS[gp, 0, 0:NCP], rhs=QT[gp, :, qc], start=True, stop=False), reads=[tKCS, tQT], writes=[tS_])
                S.op("pe", lambda e: e.matmul(pS[0:NCP, :], lhsT=IDB[0:NCP, 0:NCP], rhs=CMPB[0:NCP, qc].unsqueeze(1).to_broadcast([NCP, 4, 128]), start=False, stop=True),
                     reads=[tK], writes=[tS_])
                pt, tpt = ptbuf()
                S.op("act", lambda e: e.activation(out=pt[0:NCP, :], in_=pS[0:NCP, :], func=AF.Exp, scale=SCALE), reads=[tS_], writes=[tpt])
                ac, tac = acc()

                def fin_cmp():
                    for h in range(4):
                        S.op("pe", lambda e, h=h: e.matmul(ac[:, h * 97:h * 97 + 64], lhsT=pt[0:NCP, h * 128:(h + 1) * 128], rhs=VCS[0:NCP, 0, 0, gp], start=True, stop=True),
                             reads=[tpt, tVCS], writes=[tac])
                        S.op("pe", lambda e, h=h: e.matmul(ac[:, h * 97 + 64:h * 97 + 65 + NSELP], lhsT=pt[0:NCP, h * 128:(h + 1) * 128], rhs=OVP[0:NCP, :], start=True, stop=True),
                             reads=[tpt, tOV], writes=[tac])
                    attn_epilogue(ac, tac, g, 0, qt, True, 97)
                    hv = ac[:, 0:388].rearrange("p (h c) -> p h c", c=97)
                    score = SC2[:, 0, 0:NSELP]
                    for h in range(4):
                        if h == 0:
                            S.op("dve", lambda e: e.tensor_scalar(out=score, in0=hv[:, 0, 65:65 + NSELP], scalar1=ST1[:, 40:41], scalar2=None, op0=ALU.mult), reads=[tac, tST1], writes=[tSC2])
                        else:
                            S.op("dve", lambda e, h=h: e.scalar_tensor_tensor(out=score, in0=hv[:, h, 65:65 + NSELP], scalar=ST1[:, 40 + h:41 + h], in1=score, op0=ALU.mult, op1=ALU.add),
                                 reads=[tac, tST1, tSC2], writes=[tSC2])
                    S.op("dve", lambda e: e.tensor_tensor(out=score, in0=score, in1=FBP[:, qt, :], op=ALU.add), reads=[tSC2, tK], writes=[tSC2])
                    select(score, NSELP, 128)
                    pB, tB = ps()
                    S.op("pe", lambda e: e.transpose(pB[0:NSELP, 0:128], score, IDF[:]), reads=[tSC2, tK], writes=[tB])
                    S.op("act", lambda e: e.activation(out=BT[0:NSELP, g, :], in_=pB[0:NSELP, 0:128], func=AF.Identity), reads=[tB], writes=[tBT])
                push(fin_cmp)

            def kv_branch(qt, g, br):
                qc = slice(qt * 128, (qt + 1) * 128)
                gp = slice(64 * g, 64 * g + 64)
                if br == 1:
                    KT_, VV, tKK, tVV, kts = KST, VS, tKST, tVS, list(range(0, qt + 1))
                else:
                    KT_, VV, tKK, tVV, kts = KWT, VW, tKWT, tVW, list(range(max(0, qt - 4), qt + 1))
                if True:
                    ac, tac = acc()
                    for ki, kt in enumerate(kts):
                        pS, tS_ = ps()
                        need_b = (kt == qt) or (br == 1) or (br == 2 and kt == qt - 4)
                        S.op("pe", lambda e, kt=kt: e.matmul(pS[:, :], lhsT=KT_[gp, kt * 128:(kt + 1) * 128], rhs=QT[gp, :, qc], start=True, stop=not need_b), reads=[tKK, tQT], writes=[tS_])
                        if kt == qt:
                            S.op("pe", lambda e: e.matmul(pS[:, :], lhsT=IDB[:], rhs=CAUS[:].unsqueeze(1).to_broadcast([128, 4, 128]), start=False, stop=True), reads=[tK], writes=[tS_])
                        elif br == 1:
                            S.op("pe", lambda e, kt=kt: e.matmul(pS[:, :], lhsT=E64[:, kt, :], rhs=BT[:, g, :].unsqueeze(1).to_broadcast([128, 4, 128]), start=False, stop=True),
                                 reads=[tK, tBT], writes=[tS_])
                        elif kt == qt - 4:
                            S.op("pe", lambda e: e.matmul(pS[:, :], lhsT=IDB[:], rhs=ANTI[:].unsqueeze(1).to_broadcast([128, 4, 128]), start=False, stop=True), reads=[tK], writes=[tS_])
                        pt, tpt = ptbuf()
                        S.op("act", lambda e: e.activation(out=pt, in_=pS[:, :], func=AF.Exp, scale=SCALE), reads=[tS_], writes=[tpt])
                        def fin_kv(ki=ki, kt=kt, pt=pt, tpt=tpt):
                            for h in range(4):
                                S.op("pe", lambda e, h=h: e.matmul(ac[:, h * 65:(h + 1) * 65], lhsT=pt[:, h * 128:(h + 1) * 128], rhs=VV[:, kt, g, :],
                                                                   start=(ki == 0 and h == 0), stop=(ki == len(kts) - 1), skip_group_check=True), reads=[tpt, tVV], writes=[tac])
                            if ki == len(kts) - 1:
                                attn_epilogue(ac, tac, g, br, qt, False, 65)
                        push(fin_kv)

            for qt in range(NT):
                for g in range(2):
                    cmp_branch(qt, g)
                for g in range(2):
                    kv_branch(qt, g, 2)
                for g in range(2):
                    kv_branch(qt, g, 1)
                flush()
                S.op("act", lambda e: e.activation(out=QN[:, 0:512], in_=OACC[:, :], func=AF.Identity), reads=[tOACC], writes=[tQN])
                pt_, tt_ = ps()
                ptv = pt_[:].bitcast(BF16)
                for k in range(4):
                    S.op("pe", lambda e, k=k: e.transpose(ptv[:, k * 128:(k + 1) * 128], QN[:, k * 128:(k + 1) * 128], IDB[:]), reads=[tQN, tK], writes=[tt_])
                S.op("dve", lambda e: e.tensor_copy(out=OT[:], in_=ptv[:, 0:512].rearrange("p (k t) -> p k t", k=4)), reads=[tt_], writes=[tOT])
                for half in range(2):
                    po, to = ps()
                    for k in range(4):
                        S.op("pe", lambda e, k=k, half=half: e.matmul(po[:, :], lhsT=OT[:, k, :], rhs=WOA[:, k, half * 512:(half + 1) * 512], start=(k == 0), stop=(k == 3)),
                             reads=[tOT, tWOA], writes=[to])
                    resid_add(qt, po, to, half, 0)


            MARKS.append(("Bs", l, dict(S.cnt)))
            S.op("pool", lambda e: e.memset(OACS[:], 0.0), writes=[tOACS])
            S.op("dve", lambda e: e.tensor_copy(out=VN0[:, 0, 1:65], in_=VS[0:TS, NT, 0, 0:64]), reads=[tVS], writes=[tVN0])
            S.op("dve", lambda e: e.tensor_copy(out=VN0[:, 1, 1:65], in_=VW[0:TS, NT, 0, 0:64]), reads=[tVW], writes=[tVN0])
            mk_idxl(0, 2); mk_idxl(1, 3)
            for b in range(2):
                stgv = STG[:, b, :].rearrange("p (j c) -> p j c", c=288)
                S.op("pool", lambda e, stgv=stgv: e.memset(stgv[:, :, 128:144], 1.0), writes=[tSTG[b]])
                S.op("pool", lambda e, stgv=stgv: e.memset(stgv[:, :, 272:288], 1.0), writes=[tSTG[b]])
            ptsn = [0]
            NBK = 2 * NPG
            CS = 65 + NSELS
            import os
            for sq in range(NS if not os.environ.get('NOSAMP') else 0):
                sc = slice(sq * DS, (sq + 1) * DS)
                qcs = slice(T + sq * DS, T + (sq + 1) * DS)

                def pts_next():
                    i_ = ptsn[0] % 3
                    ptsn[0] += 1
                    return PTS[:, sq, i_, :].rearrange("p (h t) -> p h t", h=4), tPTS[sq][i_]

                def score_tile(pS, tS_, nk, lhsT_k, tk_, g, bias=None):
                    gp_ = slice(64 * g, 64 * g + 64)
                    S.op("pe", lambda e: e.matmul(pS[0:nk, 0:4 * DS], lhsT=lhsT_k, rhs=QT[gp_, :, qcs], start=True, stop=(bias is None)), reads=[tk_, tQT], writes=[tS_])
                    if bias is not None:
                        bl, br_, tb_ = bias
                        S.op("pe", lambda e: e.matmul(pS[0:nk, 0:4 * DS], lhsT=bl, rhs=br_, start=False, stop=True), reads=[tK, tb_], writes=[tS_])
                    pts, tpts = pts_next()
                    S.op("act", lambda e: e.activation(out=pts[0:nk, :, sc], in_=pS[0:nk, 0:4 * DS].rearrange("p (h t) -> p h t", h=4), func=AF.Exp, scale=SCALE), reads=[tS_], writes=[tpts])
                    return pts, tpts

                for g in range(2):
                    gp = slice(64 * g, 64 * g + 64)
                    accs = [acc(), acc()]
                    for kt in range(NKS):
                        nk = min(128, NCS - kt * 128)
                        pS, tS_ = ps()
                        pts, tpts = score_tile(pS, tS_, nk, KCS[gp, 1 + sq, kt * 128:kt * 128 + nk], tKCS, g)
                        def fin_scmp(kt=kt, nk=nk, pts=pts, tpts=tpts):
                          for h in range(4):
                              ac, tac = accs[h // 2]
                              c0_ = (h % 2) * CS
                              S.op("pe", lambda e, h=h, ac=ac, c0_=c0_: e.matmul(ac[0:TS, c0_:c0_ + 64], lhsT=pts[0:nk, h, :], rhs=VCS[0:nk, 1 + sq, kt, gp],
                                                                                start=(kt == 0 and h % 2 == 0), stop=(kt == NKS - 1), skip_group_check=True), reads=[tpts, tVCS], writes=[tac])
                              S.op("pe", lambda e, h=h, ac=ac, c0_=c0_: e.matmul(ac[0:TS, c0_ + 64:c0_ + CS], lhsT=pts[0:nk, h, :], rhs=OVS[0:nk, kt, :],
                                                                                start=False, stop=(kt == NKS - 1), skip_group_check=True), reads=[tpts, tOV], writes=[tac])
                        push(fin_scmp)
                    flush()
                    score = SC2[0:TS, 0, 0:NSELS]
                    for hb in range(2):
                        ac, tac = accs[hb]
                        attn_epilogue(ac, tac, g, 0, NT, False, CS, rows=TS, nh=2, h0=2 * hb, dest=OACS, tdest=tOACS)
                        hv = ac[0:TS, 0:2 * CS].rearrange("p (h c) -> p h c", c=CS)
                        for h in range(2):
                            if hb == 0 and h == 0:
                                S.op("dve", lambda e: e.tensor_scalar(out=score, in0=hv[:, 0, 65:CS], scalar1=ST1[0:TS, 40:41], scalar2=None, op0=ALU.mult), reads=[tac, tST1], writes=[tSC2])
                            else:
                                S.op("dve", lambda e, h=h: e.scalar_tensor_tensor(out=score, in0=hv[:, h, 65:CS], scalar=ST1[0:TS, 40 + h:41 + h], in1=score, op0=ALU.mult, op1=ALU.add),
                                     reads=[tac, tST1, tSC2], writes=[tSC2])
                    S.op("dve", lambda e: e.tensor_tensor(out=score, in0=score, in1=FBS[:, :], op=ALU.add), reads=[tSC2, tK], writes=[tSC2])
                    select(score, NSELS, TS)
                    pB, tB = ps()
                    S.op("pe", lambda e: e.transpose(pB[0:NBK, 0:TS], SC2[0:TS, 0, 0:NBK], IDF[0:TS, 0:TS]), reads=[tSC2, tK], writes=[tB])
                    for hf_ in range((NBK + 63) // 64):
                        r0, r1 = hf_ * 64, min(NBK, hf_ * 64 + 64)
                        S.op("act", lambda e, hf_=hf_, r0=r0, r1=r1: e.activation(out=BTS[r0:r1, hf_, g, :], in_=pB[r0:r1, 0:TS], func=AF.Identity), reads=[tB], writes=[tBTS])

                def past_pass(npages, loadfn, biasfn, KN, tKN, VNg1, tVNg1, vn0_idx, br):
                    accg = [acc(), acc()]
                    first = [True, True]
                    for pg0 in range(0, npages, 4):
                        npc = min(4, npages - pg0)
                        b = (pg0 // 4) % 2
                        stgv = STG[:, b, :].rearrange("p (j c) -> p j c", c=288)
                        loadfn(stgv, tSTG[b], pg0, npc)
                        pt_, tt_ = ps()
                        ptv = pt_[:].bitcast(BF16)
                        for j in range(npc):
                            S.op("pe", lambda e, j=j: e.transpose(ptv[:, j * 128:(j + 1) * 128], stgv[:, j, 0:128], IDB[:]), reads=[tSTG[b], tK], writes=[tt_])
                        S.op("dve", lambda e: e.tensor_copy(out=KTC[:, 0:npc * 128], in_=ptv[:, 0:npc * 128]), reads=[tt_], writes=[tRAWT])
                        SLCV = int(os.environ.get('SLCV', '9'))
                        KK_ = int(os.environ.get('KK', '128'))
                        for g in range(2 if SLCV >= 2 else 0):
                            gp = slice(64 * g, 64 * g + 64)
                            ac, tac = accg[g]
                            if os.environ.get('OB') == '1':
                                ac, tac = ps()
                            for j in range(npc):
                                pg = pg0 + j
                                pS, tS_ = ps()
                                pts, tpts = score_tile(pS, tS_, 128, KTC[gp, j * 128:(j + 1) * 128], tRAWT, g, bias=biasfn(pg, g))
                                vr = stgv[:, j, 143:208] if g == 0 else stgv[:, j, 208:273]
                                if os.environ.get('VRT') == '1':
                                    vr = stgv[:, j, 144:209] if g == 0 else stgv[:, j, 208:273]
                                if os.environ.get('VRT') == '2':
                                    vr = VS[:, 0, g, :]
                                if os.environ.get('VRT') == '3':
                                    vr = VS[:, 0, g, 0:64]
                                if os.environ.get('E4') == '1':
                                    for hb in range(2):
                                        S.op("pe", lambda e, hb=hb, vr=vr, ac=ac: e.matmul(ac[0:2 * TS, hb * 65:(hb + 1) * 65], lhsT=pts[:, 2 * hb:2 * hb + 2, :], rhs=vr, start=(first[g] and hb == 0), stop=False, skip_group_check=True),
                                             reads=[tpts, tSTG[b]], writes=[tac])
                                def fin_pp(pts=pts, tpts=tpts, vr=vr, ac=ac, tac=tac, fg=first[g], b=b):
                                    for h in range(4 if (SLCV >= 3 and os.environ.get('E4') != '1') else 0):
                                        S.op("pe", lambda e, h=h, vr=vr, ac=ac: e.matmul(ac[0:TS, h * 65:h * 65 + vr.shape[-1]], lhsT=(IDB[0:KK_, 0:32] if os.environ.get('LT') == '1' else pts[0:KK_, h, :]), rhs=vr[0:KK_], start=((fg and h == 0) or os.environ.get('E3') == '1'), stop=False, skip_group_check=True),
                                             reads=([] if os.environ.get('ND') == '1' else [tpts, tSTG[b]]), writes=[tac])
                                push(fin_pp)
                                first[g] = False
                    flush()
                    for g in range(2 if SLCV >= 4 else 0):
                        gp = slice(64 * g, 64 * g + 64)
                        ac, tac = accg[g]
                        pS, tS_ = ps()
                        pts, tpts = score_tile(pS, tS_, TS, KN[gp, T:T + TS], tKN, g, bias=(IDB[:, 0:TS], NEWB[:, sq, :], tK))
                        vr = VN0[0:TS, vn0_idx, :] if g == 0 else VNg1[0:TS, NT, 1, :]
                        for h in range(4):
                            S.op("pe", lambda e, h=h, vr=vr, ac=ac: e.matmul(ac[0:TS, h * 65:(h + 1) * 65], lhsT=pts[0:TS, h, :], rhs=vr, start=False, stop=True, skip_group_check=True),
                                 reads=[tpts, tVN0, tVNg1], writes=[tac])
                        attn_epilogue(ac, tac, g, br, NT, False, 65, rows=TS, ocol=(1 if g == 0 else 0), scol=(0 if g == 0 else 64), dest=OACS, tdest=tOACS)

                def slc_load(stgv, tstg, pg0, npc):
                    for j in range(npc):
                        gather(stgv[:, j, 0:128], 0, sq, pg0 + j, tstg, join=(j > 0))
                        gather(stgv[:, j, 144:272], 1, sq, pg0 + j, tstg, join=True)

                def slc_bias(pg, g):
                    half = pg // 32
                    return (E64[:, pg % 32, :], BTS[:, half, g, sc].unsqueeze(1).to_broadcast([128, 4, DS]), tBTS)

                if int(os.environ.get('SST', '9')) >= 2:
                    past_pass(NPG, slc_load, slc_bias, KST, tKST, VS, tVS, 0, 1)

                def win_load(stgv, tstg, pg0, npc):
                    S.dma("pool", lambda e: e.dma_start(out=stgv[:, 0:npc, 0:128], in_=st_win[l, sq, pg0 * 128:(pg0 + npc) * 128, 0:128].rearrange("(j p) c -> p j c", p=128)), writes=[tstg])
                    S.dma("pool", lambda e: e.dma_start(out=stgv[:, 0:npc, 144:272], in_=st_win[l, sq, pg0 * 128:(pg0 + npc) * 128, 128:256].rearrange("(j p) c -> p j c", p=128)), writes=[tstg], join=True)

                def win_bias(pg, g):
                    if pg == 0:
                        return (IDB[:], WINB0[:, :], tK)
                    return None

                if int(os.environ.get('SST', '9')) >= 3:
                    past_pass(4, win_load, win_bias, KWT, tKWT, VW, tVW, 1, 2)

            S.op("act", lambda e: e.activation(out=QN[0:TS, 0:512], in_=OACS[:, :], func=AF.Identity), reads=[tOACS], writes=[tQN])
            pt_, tt_ = ps()
            ptv = pt_[:].bitcast(BF16)
            for k in range(4):
                S.op("pe", lambda e, k=k: e.transpose(ptv[:, k * 128:k * 128 + TS], QN[0:TS, k * 128:(k + 1) * 128], IDB[0:TS, 0:TS]), reads=[tQN, tK], writes=[tt_])
            S.op("dve", lambda e: e.tensor_copy(out=OT[:, :, 0:TS], in_=ptv[:, 0:512].rearrange("p (k t) -> p k t", k=4)[:, :, 0:TS]), reads=[tt_], writes=[tOT])
            for half in range(2):
                po, to = ps()
                for k in range(4):
                    S.op("pe", lambda e, k=k, half=half: e.matmul(po[0:TS, :], lhsT=OT[:, k, 0:TS], rhs=WOA[:, k, half * 512:(half + 1) * 512], start=(k == 0), stop=(k == 3)),
                         reads=[tOT, tWOA], writes=[to])
                resid_add(NT, po, to, half, 0)

            S.barrier()
            MARKS.append(("C", l, dict(S.cnt)))
            for i in range(NTT):
                norm_to_HT([i], 2, 3, H2T, tH2T, i * 128)
            colgroups = [(c0, min(512, T - c0), list(range(c0 // 128, (c0 + min(512, T - c0)) // 128))) for c0 in range(0, T, 512)] + [(T, TS, [NT])]
            NCH = FFN_H // 128
            blocks = [(j0, min(4, NCH - j0)) for j0 in range(0, NCH, 4)]
            for bi, (j0, nb) in enumerate(blocks):
                sl = bi % 2
                cast_load(WUP[sl][:, :, 0:nb * 128], w_up[l, :, j0 * 128:(j0 + nb) * 128].rearrange("(k p) c -> p k c", p=128), [tWF[sl]] + (allWA if bi < 2 else []))
                cast_load(WUP[sl][:, :, 512:512 + nb * 128], w_up[l, :, FFN_H + j0 * 128:FFN_H + (j0 + nb) * 128].rearrange("(k p) c -> p k c", p=128), [tWF[sl]])
                cast_load(WDN[sl][:, 0:nb, :], w_down[l, j0 * 128:(j0 + nb) * 128, :].rearrange("(j p) c -> p j c", p=128), [tWF[sl]])
                for (c0, ncol, tiles) in colgroups:
                    for j in range(nb):
                        pa, ta = ps()
                        for k in range(8):
                            S.op("pe", lambda e, k=k, j=j, pa=pa, c0=c0, ncol=ncol, sl=sl: e.matmul(pa[:, 0:ncol], lhsT=WUP[sl][:, k, j * 128:(j + 1) * 128], rhs=H2T[:, k, c0:c0 + ncol],
                                                                                               start=(k == 0), stop=(k == 7)), reads=[tWF[sl], tH2T], writes=[ta])
                        pb2, tb2 = ps()
                        for k in range(8):
                            S.op("pe", lambda e, k=k, j=j, pb2=pb2, c0=c0, ncol=ncol, sl=sl: e.matmul(pb2[:, 0:ncol], lhsT=WUP[sl][:, k, 512 + j * 128:512 + (j + 1) * 128], rhs=H2T[:, k, c0:c0 + ncol],
                                                                                                 start=(k == 0), stop=(k == 7)), reads=[tWF[sl], tH2T], writes=[tb2])
                        S.op("act", lambda e, pa=pa, ncol=ncol: e.activation(out=SCR[:, 0:ncol], in_=pa[:, 0:ncol], func=AF.Silu), reads=[ta], writes=[tSCR])
                        S.op("dve", lambda e, pb2=pb2, j=j, ncol=ncol: e.tensor_tensor(out=UT[:, j, 0:ncol], in0=SCR[:, 0:ncol], in1=pb2[:, 0:ncol], op=ALU.mult), reads=[tSCR, tb2], writes=[tUT[0]])
                    for ti, i in enumerate(tiles):
                        rows = 128 if i < NT else TS
                        for half in range(2):
                            po, to = ps()
                            for j in range(nb):
                                S.op("pe", lambda e, j=j, po=po, ti=ti, rows=rows, half=half, sl=sl: e.matmul(po[0:rows, :], lhsT=UT[:, j, ti * 128:ti * 128 + rows], rhs=WDN[sl][:, j, half * 512:(half + 1) * 512],
                                                                                                       start=(j == 0), stop=(j == nb - 1)), reads=[tUT[0], tWF[sl]], writes=[to])
                            gsel_ = 0 if i < NT else 1
                            S.op("dve", lambda e, po=po, rows=rows, half=half, gsel_=gsel_: e.tensor_tensor(out=RSC[0:rows, half, :], in0=po[0:rows, :], in1=GATE[0:rows, 1, gsel_, half * 512:(half + 1) * 512], op=ALU.mult),
                                 reads=[to, tGATE[1][gsel_]], writes=[tRSC[half]])
                            S.op("pool", lambda e, i=i, rows=rows, half=half: e.tensor_tensor(out=X[0:rows, i, half * 512:(half + 1) * 512], in0=X[0:rows, i, half * 512:(half + 1) * 512], in1=RSC[0:rows, half, :], op=ALU.add),
                                 reads=[tRSC[half], tX[i]], writes=[tX[i]])

        for l in range(DEPTH):
            S.barrier()
            MARKS.append(("L", l, dict(S.cnt)))
            ld("sp", SPR[:], spar[l], [tSPR])
            pb, tp = ps()
            S.op("pe", lambda e, pb=pb: e.transpose(pb[:, 0:80], SPR[:], IDF[0:80, 0:80]), reads=[tSPR, tK], writes=[tp])
            S.op("dve", lambda e, pb=pb: e.tensor_copy(out=SPT[:], in_=pb[:, 0:80]), reads=[tp], writes=[tSPT])
            ld("sp", KGB[:].rearrange("p a c -> p (a c)"), kgbc[l].rearrange("a c -> (a c)").partition_broadcast(128), [tKGB])
            for n in range(12):
                kind, half = n // 2, n % 2
                wb = n % 4
                cast_load(WADA[:, wb], w_ada[l, :, n * 512:(n + 1) * 512].rearrange("(k p) c -> p k c", p=128), [tWADA[wb]] + ([tWIN] if False else []),
                          r=[])
                if kind in (2, 5):
                    gi = 0 if kind == 2 else 1
                    ld("sp", BST[:], b_ada[l, n * 512:(n + 1) * 512].partition_broadcast(128), [tBST])
                    for gsel, (c0, cw) in enumerate(((0, 128), (128, TS))):
                        pb, tp = ps()
                        for k in range(8):
                            S.op("pe", lambda e, k=k, pb=pb, c0=c0, cw=cw, wb=wb: e.matmul(pb[0:cw, :], lhsT=CTS[:, k, c0:c0 + cw], rhs=WADA[:, wb, k, :],
                                                                                            start=(k == 0), stop=(k == 7)), reads=[tK, tWADA[wb]], writes=[tp])
                        S.op("dve", lambda e, pb=pb, cw=cw, gi=gi, gsel=gsel, half=half: e.tensor_tensor(
                            out=GATE[0:cw, gi, gsel, half * 512:(half + 1) * 512], in0=pb[0:cw, :], in1=BST[0:cw, :], op=ALU.add),
                             reads=[tp, tBST], writes=[tGATE[gi][gsel]])
                else:
                    mk = {0: 0, 1: 1, 3: 2, 4: 3}[kind]
                    pb, tp = ps()
                    for j in range(4):
                        for k in range(8):
                            S.op("pe", lambda e, k=k, j=j, pb=pb, wb=wb: e.matmul(pb[:, j * 8:j * 8 + 1 + NS], lhsT=WADA[:, wb, k, j * 128:(j + 1) * 128], rhs=CT5[:, k, :],
                                                                                   start=(k == 0), stop=(k == 7)), reads=[tK, tWADA[wb]], writes=[tp])
                    for j in range(4):
                        c = half * 4 + j
                        bcol = SPT[:, n * 4 + j:n * 4 + j + 1]
                        if kind in (0, 3):
                            S.op("dve", lambda e, j=j, pb=pb, c=c, bcol=bcol, mk=mk: e.tensor_scalar(out=MODC[:, mk, c, :], in0=pb[:, j * 8:j * 8 + 1 + NS], scalar1=bcol, scalar2=None, op0=ALU.add),
                                 reads=[tp, tSPT], writes=[tMODC])
                        else:
                            ncol = SPT[:, 48 + c:49 + c] if kind == 1 else SPT[:, 56 + c:57 + c]
                            S.op("dve", lambda e, j=j, pb=pb, c=c, bcol=bcol, mk=mk: e.tensor_scalar(out=MODC[:, mk, c, :], in0=pb[:, j * 8:j * 8 + 1 + NS], scalar1=bcol, scalar2=1.0, op0=ALU.add, op1=ALU.add),
                                 reads=[tp, tSPT], writes=[tMODC])
                            S.op("dve", lambda e, c=c, ncol=ncol, mk=mk: e.tensor_scalar(out=MODC[:, mk, c, :], in0=MODC[:, mk, c, :], scalar1=ncol, scalar2=None, op0=ALU.mult),
                                 reads=[tMODC, tSPT], writes=[tMODC])
            LAYER_BODY(l)
        for i in range(NT):
            S.dma("sp", lambda e, i=i: e.dma_start(out=y_p[i * 128:(i + 1) * 128, :], in_=X[:, i, :]), reads=[tX[i]])
        S.dma("sp", lambda e: e.dma_start(out=y_s, in_=X[0:TS, NT, :]), reads=[tX[NT]])
        S.final_wait("sp")
        print('CNT', S.cnt, {k: v for k, v in S.dval.items() if v > 1000})

        sems = {e: es.enter_context(nc.semaphore("s_" + e)) for e in ENGS}
        for q, n in S.NDS.items():
            for i in range(n):
                sems[("d", q, i)] = es.enter_context(nc.semaphore(f"d_{q}_{i}"))

        def run(e, lst):
            for waits, fn, key, inc in lst:
                for k, v in waits:
                    e.wait_ge(sems[k], v)
                if fn is not None:
                    name, a, k = fn
                    getattr(e, name)(*a, **k).then_inc(sems[key], inc)

        with nc.Block() as block:
            @block.sync
            def _(e):
                run(e, S.ops["sp"])

            @block.scalar
            def _(e):
                run(e, S.ops["act"])

            @block.vector
            def _(e):
                run(e, S.ops["dve"])

            @block.gpsimd
            def _(e):
                run(e, S.ops["pool"])

            @block.tensor
            def _(e):
                run(e, S.ops["pe"])
    return nc


def _consts(cfg):
    T, NPG, NS, DS = cfg["T"], cfg["NPG"], cfg["NS"], cfg["DS"]
    NT = T // 128; TS = NS * DS; PAST = NPG * 128
    NCS = (PAST + DS - 32) // 16 + 1; NSELP = T // 64; NSELS = -(-(PAST + DS) // 64); NKS = (NCS + 127) // 128
    bf = ml_dtypes.bfloat16
    k = {}
    k["k_idf"] = np.eye(128, dtype=np.float32); k["k_idb"] = np.eye(128).astype(bf)
    kk, qq = np.meshgrid(np.arange(128), np.arange(128), indexing="ij")
    k["k_caus"] = np.where(kk > qq, NEGB, 0.0).astype(bf); k["k_anti"] = np.where(kk <= qq, NEGB, 0.0).astype(bf)
    c = np.arange(128)[:, None]; t = np.arange(T)[None, :]
    k["k_cmpb"] = np.where(16 * c + 31 <= t, 0.0, NEGB).astype(bf)
    e64 = np.zeros((128, 32, 128), np.float32)
    for m in range(32):
        for key in range(128):
            j = 2 * m + key // 64
            e64[j % 64, m, key] = 1.0
            e64[64 + j % 64, m, key] = 1.0
    k["k_e64"] = e64.astype(bf)
    fbp = np.zeros((128, NT, NSELP), np.float32)
    for i in range(NT):
        for p in range(128):
            cur = (i * 128 + p) // 64
            for j in range(NSELP):
                if j > cur: fbp[p, i, j] = -1e30
                elif j == 0 or j == cur or j == cur - 1: fbp[p, i, j] = 1e4
    k["k_fbp"] = fbp.astype(bf)
    fbs = np.zeros((TS, NSELS), np.float32); cur = NSELS - 1
    fbs[:, 0] = 1e4; fbs[:, cur] = 1e4; fbs[:, cur - 1] = 1e4
    k["k_fbs"] = fbs
    newb = np.full((TS, NS, 4, DS), NEGB, np.float32)
    for s in range(NS):
        for t2 in range(DS):
            for tq in range(DS):
                if t2 <= tq: newb[s * DS + t2, s, :, tq] = 0.0
    k["k_newb"] = newb.reshape(TS, NS, 4 * DS).astype(bf)
    wb0 = np.zeros((128, 4, DS), np.float32)
    for i in range(128):
        for tq in range(DS):
            if not (i > tq): wb0[i, :, tq] = NEGB
    k["k_winb0"] = wb0.reshape(128, 4 * DS).astype(bf)

    def ov(ncmp, nsel):
        m = np.zeros((ncmp, nsel), np.float32); i = np.arange(ncmp)
        for part in range(2):
            j = np.minimum((i + part) * 16 // 64, nsel - 1); np.add.at(m, (i, j), 1.0)
        return m
    NCP = (T - 32) // 16 + 1
    ovp = np.zeros((128, 1 + NSELP), np.float32); ovp[:, 0] = 1.0; ovp[:NCP, 1:] = ov(NCP, NSELP)
    k["k_ovp"] = ovp.astype(bf)
    ovs = np.zeros((NKS * 128, 1 + NSELS), np.float32); ovs[:, 0] = 1.0; ovs[:NCS, 1:] = ov(NCS, NSELS)
    k["k_ovs"] = ovs.reshape(NKS, 128, 1 + NSELS).transpose(1, 0, 2).astype(bf)
    rc = np.zeros((128, 2, 16), np.float32)
    for ch in range(2):
        for p in range(128):
            w = (2, 4, 8, 16)[ch * 2 + p // 64]
            rc[p, ch, :] = 1.0 / np.minimum(w, np.arange(16) + 1)
    k["k_rc"] = rc
    bo = np.zeros((128, 128), np.float32); bo[:64, :64] = 1; bo[64:, 64:] = 1
    k["k_bones"] = bo.astype(bf)
    sel = np.zeros((1 + NS, 128 + TS), np.float32); sel[0, :128] = 1
    for s in range(NS): sel[1 + s, 128 + s * DS:128 + (s + 1) * DS] = 1
    k["k_cts"] = sel
    return k


_NC_CACHE = {}


def kernel(x_prompt, x_sample, cache_nsa_kv, state_win_kv, state_conv, state_pool, page_table,
           c_prompt, c_sample, norm_mix, norm_ffn, w_ada, b_ada, w_in, w_out, q_norm, k_norm,
           cmp_pe, cmp_w1, cmp_w2, conv_w, conv_bias, pool_w, pool_scale, w_up, w_down, cfg=None):
    cfg = dict(CFG) if cfg is None else cfg
    T, NPG, DEPTH, NS, DS = cfg["T"], cfg["NPG"], cfg["DEPTH"], cfg["NS"], cfg["DS"]
    f = lambda a: np.ascontiguousarray(np.asarray(a))
    NB = x_prompt.shape[0]; TS = NS * DS
    key = tuple(sorted(cfg.items()))
    if key not in _NC_CACHE:
        _NC_CACHE[key] = build(cfg)
    nc = _NC_CACHE[key]
    consts = _consts(cfg)
    spar = np.zeros((DEPTH, 80, 128), np.float32)
    spar[:, 0:48] = f(b_ada).reshape(DEPTH, 48, 128)
    spar[:, 48:56] = f(norm_mix).reshape(DEPTH, 8, 128); spar[:, 56:64] = f(norm_ffn).reshape(DEPTH, 8, 128)
    spar[:, 64:70] = f(conv_w).reshape(DEPTH, 6, 128); spar[:, 70:72] = f(conv_bias).reshape(DEPTH, 2, 128)
    spar[:, 72:74] = f(pool_scale).reshape(DEPTH, 2, 128)
    spar[:, 74, 0:64] = f(q_norm); spar[:, 74, 64:128] = f(q_norm)
    spar[:, 75:78, 0:64] = f(k_norm); spar[:, 75:78, 64:128] = f(k_norm)
    kgbc = np.concatenate([f(k_norm)[:, 1:3], f(k_norm)[:, 1:3]], axis=-1).astype(np.float32)
    w1 = f(cmp_w1).reshape(DEPTH, 2, 32, 64, 64)
    w1bd = np.zeros((DEPTH, 2, 128, 32, 128), np.float32)
    w1bd[:, :, 0:64, :, 0:64] = w1.transpose(0, 1, 3, 2, 4); w1bd[:, :, 64:, :, 64:] = w1.transpose(0, 1, 3, 2, 4)
    w2bd = np.zeros((DEPTH, 2, 128, 128), np.float32)
    w2bd[:, :, :64, :64] = f(cmp_w2); w2bd[:, :, 64:, 64:] = f(cmp_w2)
    pw = f(pool_w); pwbd = np.zeros((DEPTH, 2, 128, 128), np.float32)
    pwbd[:, 0, :64, :64] = pw[:, 0]; pwbd[:, 0, 64:, 64:] = pw[:, 1]; pwbd[:, 1, :64, :64] = pw[:, 2]; pwbd[:, 1, 64:, 64:] = pw[:, 3]
    pe_t = f(cmp_pe).transpose(0, 1, 3, 2)
    pet = np.concatenate([pe_t, pe_t], axis=2).astype(np.float32)
    cache2 = f(cache_nsa_kv).reshape(DEPTH, -1, 512)
    in_maps = []
    for c in range(NB):
        sl = slice(c * NS, (c + 1) * NS)
        st_cp = np.concatenate([f(state_pool)[:, sl].reshape(DEPTH, NS * 15, 256), f(state_conv)[:, sl].reshape(DEPTH, NS * 2, 256)], axis=1)
        m = dict(x_p=f(x_prompt[c]), x_s=f(x_sample[sl]).reshape(TS, D), cache=cache2,
                 st_win=f(state_win_kv)[:, sl].reshape(DEPTH, NS, 512, 256), st_cp=st_cp,
                 ptab=f(page_table[sl]).astype(np.int32), c_all=np.concatenate([f(c_prompt)[c:c + 1], f(c_sample)[sl]], 0),
                 spar=spar, kgbc=kgbc, w_ada=f(w_ada), b_ada=f(b_ada), w_in=f(w_in), w_out=f(w_out), w1bd=w1bd, w2bd=w2bd,
                 pwbd=pwbd, pet=pet, w_up=f(w_up), w_down=f(w_down))
        m.update(consts)
        in_maps.append(m)
    res = run_bass_kernel_spmd(nc, in_maps, core_ids=list(range(NB)))
    R_ = res.results
    cat = lambda k, ax=0: np.stack([r[k] for r in R_], axis=ax)
    yp = cat("y_p"); ys = np.concatenate([r["y_s"].reshape(NS, DS, D) for r in R_], 0)
    kvp = cat("kv_p", 1).reshape(DEPTH, NB, T, 4, 2, 64)
    kvs = np.concatenate([r["kv_s"].reshape(DEPTH, NS, DS, 4, 2, 64) for r in R_], 1)
    wp = cat("win_p", 1).reshape(DEPTH, NB, -1, 2, 2, 64)
    ws = np.concatenate([r["win_s"].reshape(DEPTH, NS, 512, 2, 2, 64) for r in R_], 1)
    cp = cat("conv_p", 1); cs = np.concatenate([r["conv_s"].reshape(DEPTH, NS, 2, 256) for r in R_], 1)
    pp = cat("pool_p", 1); pls = np.concatenate([r["pool_s"].reshape(DEPTH, NS, 15, 256) for r in R_], 1)
    return (yp, ys, kvp, kvs, wp, ws, cp, cs, pp, pls)
```

```python
import numpy as np
import ml_dtypes
import concourse.bass as bass
import concourse.mybir as mybir
from concourse.bass_utils import run_bass_kernel_spmd

F32 = mybir.dt.float32
BF16 = mybir.dt.bfloat16
I32 = mybir.dt.int32
AF = mybir.ActivationFunctionType
ALU = mybir.AluOpType
AX = mybir.AxisListType

D = 1024
HD = 64
IN_W = 2328
FFN_H = 2816
EPS = 1e-6
NEGB = -30000.0
SCALE = 0.125
CFG = dict(T=2048, NPG=64, DEPTH=4, NS=4, DS=8, NPOOL=2560)


class Tok:
    __slots__ = ("w", "r", "excl", "wl")

    def __init__(self, excl=False):
        self.w = None
        self.r = {}
        self.excl = excl
        self.wl = []


ENGS = ("pe", "act", "dve", "pool", "sp")
MARKS = []


class _Rec:
    def __getattr__(self, name):
        return lambda *a, **k: (name, a, k)


_REC = _Rec()


class Sched:
    def __init__(self):
        self.ops = {e: [] for e in ENGS}
        self.cnt = {e: 0 for e in ENGS}
        self.seen = {e: {} for e in ENGS}
        self.dnext = {e: 0 for e in ENGS}
        self.dval = {}
        self.NDS = {"sp": 24, "pool": 24, "act": 4}

    def _need(self, eng, dep, waits):
        if dep is None:
            return
        k, v = dep
        if eng == "pe" and k == "pe":
            return
        if self.seen[eng].get(k, 0) >= v:
            return
        if waits.get(k, 0) < v:
            waits[k] = v

    def _deps(self, eng, reads, writes, join=False):
        waits = {}
        for t in reads:
            self._need(eng, t.w, waits)
            for d_ in t.wl:
                self._need(eng, d_, waits)
        for t in writes:
            if join and t.w is not None and isinstance(t.w[0], tuple):
                continue
            if not (t.w is not None and t.w[0] == eng):
                self._need(eng, t.w, waits)
            for d_ in t.wl:
                self._need(eng, d_, waits)
            for k, v in t.r.items():
                if k != eng:
                    self._need(eng, (k, v), waits)
        for k, v in waits.items():
            self.seen[eng][k] = v
        return list(waits.items())

    def _mark(self, me, reads, writes, join=False):
        k, v = me
        for t in reads:
            if t.r.get(k, 0) < v:
                t.r[k] = v
        for t in writes:
            if join and t.w is not None and isinstance(t.w[0], tuple):
                t.wl.append(t.w)
            else:
                t.wl = []
                t.r = {}
            t.w = me

    def op(self, eng, fn, reads=(), writes=()):
        writes = list(writes) + [t for t in reads if t.excl]
        reads = [t for t in reads if not t.excl]
        waits = self._deps(eng, reads, writes)
        self.cnt[eng] += 1
        me = (eng, self.cnt[eng])
        self.ops[eng].append((waits, fn(_REC), eng, 1))
        self._mark(me, reads, writes)

    def dma(self, q, fn, reads=(), writes=(), join=False):
        i = self.dnext[q] % self.NDS[q]
        self.dnext[q] += 1
        key = ("d", q, i)
        prev = self.dval.get(key, 0)
        waits = self._deps(q, reads, writes, join)
        if prev and self.seen[q].get(key, 0) < prev:
            waits.append((key, prev))
            self.seen[q][key] = prev
        val = prev + 16
        self.dval[key] = val
        self.ops[q].append((waits, fn(_REC), key, 16))
        self._mark((key, val), reads, writes, join)

    def barrier(self):
        snap = dict(self.cnt)
        dsn = dict(self.dval)
        for eng in ENGS:
            waits = [(k, v) for k, v in dsn.items() if self.seen[eng].get(k, 0) < v]
            for e in ENGS:
                if e != eng and snap[e] and self.seen[eng].get(e, 0) < snap[e]:
                    waits.append((e, snap[e]))
            for k, v in waits:
                self.seen[eng][k] = v
            if waits:
                self.ops[eng].append((waits, None, None, 0))

    def final_wait(self, eng):
        waits = [(k, v) for k, v in self.dval.items() if self.seen[eng].get(k, 0) < v]
        for e in ENGS:
            if e != eng and self.cnt[e]:
                waits.append((e, self.cnt[e]))
        self.ops[eng].append((waits, None, None, 0))


def build(cfg):
    T, NPG, DEPTH, NS, DS = cfg["T"], cfg["NPG"], cfg["DEPTH"], cfg["NS"], cfg["DS"]
    NPOOL = cfg["NPOOL"]
    NT = T // 128
    NTT = NT + 1
    TS = NS * DS
    TT = T + TS
    PAST = NPG * 128
    NCP = (T - 32) // 16 + 1
    NCS = (PAST + DS - 32) // 16 + 1
    NSELP = T // 64
    NSELS = -(-(PAST + DS) // 64)
    NKS = (NCS + 127) // 128
    WINT = min(512, T) // 128
    GT = 256
    GTL = GT // 128
    NG = (NT + GTL - 1) // GTL

    nc = bass.Bass("TRN2", target_bir_lowering=False)
    S = Sched()

    def din(name, shape, dt=F32):
        return nc.dram_tensor(name, list(shape), dt, kind="ExternalInput").ap()

    def dout(name, shape, dt=F32):
        return nc.dram_tensor(name, list(shape), dt, kind="ExternalOutput").ap()

    x_p = din("x_p", [T, D]); x_s = din("x_s", [TS, D])
    cache = din("cache", [DEPTH, NPOOL * 128, 512])
    cacheR = cache.rearrange("l r (q c) -> (l r q) c", c=128)
    st_win = din("st_win", [DEPTH, NS, 512, 256])
    st_cp = din("st_cp", [DEPTH, NS * 17, 256])
    ptab = din("ptab", [NS, NPG], I32)
    c_all = din("c_all", [1 + NS, D])
    spar = din("spar", [DEPTH, 80, 128])
    kgbc = din("kgbc", [DEPTH, 2, 128])
    w_ada = din("w_ada", [DEPTH, D, 6 * D]); b_ada = din("b_ada", [DEPTH, 6 * D])
    w_in = din("w_in", [DEPTH, D, IN_W]); w_out = din("w_out", [DEPTH, D, D])
    w1bd = din("w1bd", [DEPTH, 2, 128, 32, 128])
    w2bd = din("w2bd", [DEPTH, 2, 128, 128])
    pwbd = din("pwbd", [DEPTH, 2, 128, 128])
    pet = din("pet", [DEPTH, 2, 128, 32])
    w_up = din("w_up", [DEPTH, D, 2 * FFN_H]); w_down = din("w_down", [DEPTH, FFN_H, D])
    k_idf = din("k_idf", [128, 128]); k_idb = din("k_idb", [128, 128], BF16)
    k_caus = din("k_caus", [128, 128], BF16); k_anti = din("k_anti", [128, 128], BF16)
    k_cmpb = din("k_cmpb", [128, T], BF16)
    k_e64 = din("k_e64", [128, 32, 128], BF16)
    k_fbp = din("k_fbp", [128, NT, NSELP], BF16); k_fbs = din("k_fbs", [TS, NSELS])
    k_newb = din("k_newb", [TS, NS, 4 * DS], BF16); k_winb0 = din("k_winb0", [128, 4 * DS], BF16)
    k_ovp = din("k_ovp", [128, 1 + NSELP], BF16); k_ovs = din("k_ovs", [128, NKS, 1 + NSELS], BF16)
    k_rc = din("k_rc", [128, 2, 16]); k_bones = din("k_bones", [128, 128], BF16)
    k_cts = din("k_cts", [1 + NS, 128 + TS])

    y_p = dout("y_p", [T, D]); y_s = dout("y_s", [TS, D])
    kv_p = dout("kv_p", [DEPTH, T, 512]); kv_s = dout("kv_s", [DEPTH, TS, 512])
    win_p = dout("win_p", [DEPTH, WINT * 128, 256]); win_s = dout("win_s", [DEPTH, NS, 512, 256])
    conv_p = dout("conv_p", [DEPTH, 2, 256]); conv_s = dout("conv_s", [DEPTH, NS * 2, 256])
    pool_p = dout("pool_p", [DEPTH, 15, 256]); pool_s = dout("pool_s", [DEPTH, NS * 15, 256])

    import contextlib
    es = contextlib.ExitStack()
    with es:
        def sb(name, shape, dt=F32):
            return es.enter_context(nc.sbuf_tensor(name, list(shape), dt))

        X = sb("X", [128, NTT, D]); tX = [Tok() for _ in range(NTT)]
        GATE = sb("GATE", [128, 2, 2, D], BF16); tGATE = [[Tok(), Tok()], [Tok(), Tok()]]
        MODC = sb("MODC", [128, 4, 8, 1 + NS]); tMODC = Tok()
        IDF = sb("IDF", [128, 128]); IDB = sb("IDB", [128, 128], BF16)
        CAUS = sb("CAUS", [128, 128], BF16); ANTI = sb("ANTI", [128, 128], BF16)
        CMPB = sb("CMPB", [128, T], BF16); E64 = sb("E64", [128, 32, 128], BF16)
        FBP = sb("FBP", [128, NT, NSELP], BF16); FBS = sb("FBS", [TS, NSELS])
        NEWB = sb("NEWB", [128, NS, 4 * DS], BF16); WINB0 = sb("WINB0", [128, 4 * DS], BF16)
        RC16 = sb("RC16", [128, 2, 16]); BONES = sb("BONES", [128, 128], BF16)
        CTS = sb("CTS", [128, 8, 128 + TS], BF16)
        tK = Tok()
        PETB = sb("PETB", [128, 32], BF16); tPET = Tok()
        tKVP = [Tok() for _ in range(NT)]
        PH = sb("PH", [128, 4768])
        def phf(off, n):
            return PH[:, off:off + n]
        def phb(off, n):
            return PH[:, off:off + n].bitcast(BF16)
        SCR = sb("SCR", [128, 1056]); tSCR = Tok()
        SA = SCR[:].rearrange("p (a c) -> p a c", a=2); tSA = tSCR
        oA = [0]
        def takeA(n):
            o_ = oA[0]; oA[0] += n; return o_
        HT = phb(takeA(4 * GT), 4 * GT).rearrange("p (k c) -> p k c", k=8); tHT = Tok()
        _o = takeA(768); KVO = phf(_o, 768).rearrange("p (a c) -> p a c", a=1); tKVO = [Tok(), Tok()]
        OUTT = phf(_o, 256); tOUTT = tKVO[0]
        UG = phf(takeA(2 * (2 + GT)), 2 * (2 + GT)).rearrange("p (a c) -> p a c", a=2); tUG = Tok()
        PG = phf(takeA(2 * (16 + GT)), 2 * (16 + GT)).rearrange("p (a c) -> p a c", a=2)[:, :, 0:15 + GT]; tPG = Tok()
        YCP = phb(takeA(2 * GT), 2 * GT).rearrange("p (a c) -> p a c", a=4); tYCP = Tok()
        DPB = phb(takeA(GT // 2), GT // 2); tDPB = Tok()
        UGS = phf(takeA(80), 80).rearrange("p (a s c) -> p a s c", a=2, s=NS); PGS = phf(takeA(184), 184).rearrange("p (a s c) -> p a s c", a=2, s=NS); tUGS = Tok()
        JNK = phb(takeA(512), 512); tJNK = Tok()
        WST = phf(takeA(256), 256)[0:126]; tWST = Tok()
        CTF = phf(0, 1024)[0:1 + NS]; tCTF = Tok()
        PT = phb(0, 768).rearrange("p (a c) -> p a c", a=3); tPT = [Tok(), Tok(), Tok()]
        OACC = phf(768, 512); tOACC = Tok()
        OACS = phf(1280, 512)[0:TS]; tOACS = Tok()
        PTS = phb(1792, 768).rearrange("p (s i c) -> p s i c", s=NS, i=3); tPTS = [[Tok() for _ in range(3)] for _ in range(NS)]
        SC2 = phf(2560, 408).rearrange("p (a c) -> p a c", a=3); tSC2 = Tok()
        BT = phb(2968, 128).rearrange("p (a c) -> p a c", a=2); tBT = Tok()
        BTS = phb(3096, 64).rearrange("p (a g c) -> p a g c", a=2, g=2); tBTS = Tok()
        SH = phb(3160, 256); SQ = phb(3416, 256); tSH = Tok()
        OT = phb(3672, 256).rearrange("p (a c) -> p a c", a=4); tOT = Tok()
        VN0 = phb(3928, 66)[0:TS].rearrange("p (a c) -> p a c", a=2)[:, :, 0:65] if False else phb(3928, 65)[0:TS].rearrange("p (a c) -> p a c", a=2); tVN0 = Tok()
        IDXL = phf(3994, 2 * NS * NPG).bitcast(I32).rearrange("p (a c) -> p a c", a=2); tIDXL = Tok()
        IDXG = phf(4506, NS * NPG); tIDXG = Tok()
        UT = phb(0, 1024).rearrange("p (a c) -> p a c", a=4); tUT = [Tok(), Tok()]
        RSC = phf(1024, 1024).rearrange("p (a c) -> p a c", a=2); tRSC = [Tok(), Tok()]
        WA = sb("WA", [128, 24576], BF16); tWA = [Tok() for _ in range(8)]
        R = sb("R", [128, 17472], BF16)
        GSIG = sb("GSIG", [128, NTT, 24]); tGS = [Tok() for _ in range(NTT)]
        SPT = sb("SPT", [128, 80]); tSPT = Tok()
        KGB = sb("KGB", [128, 2, 128]); tKGB = Tok()
        ST1 = sb("ST1", [128, 64]); tST1 = Tok()
        QN = sb("QN", [128, 768], BF16); tQN = Tok()
        SPR = sb("SPR", [80, 128]); tSPR = Tok()
        BST = SCR[:, 0:512]; tBST = tSCR
        IDX = sb("IDX", [128, NS, NPG], I32); tIDX = Tok()
        PSB = [es.enter_context(nc.psum_tensor(f"ps{i}", [128, 512], F32)) for i in range(8)]
        tPS = [Tok(True) for _ in range(8)]
        psn = [0]
        NPSR = [5]

        def ps():
            i = psn[0] % NPSR[0]
            psn[0] += 1
            return PSB[i], tPS[i]

        WIN = WA[:, 0:8 * IN_W].rearrange("p (k c) -> p k c", k=8)
        WOCP = WA[:, 18688:18688 + 4096].rearrange("p (k c) -> p k c", k=4)
        tWIN, tWOCP = tWA[0], tWA[1]
        WOA = WA[:, 0:4096].rearrange("p (k c) -> p k c", k=4); tWOA = tWA[2]
        RAWT = WA[:, 4096:4096 + 16 * 513].rearrange("p (r m) -> p r m", r=16); tRAWT = tWA[3]
        KTC = WA[:, 4096:4096 + 512]
        STG = WA[:, 12304:12304 + 2 * 1152].rearrange("p (b c) -> p b c", b=2); tSTG = [tWA[4], tWA[5]]
        W1B = WA[:, 14608:14608 + 4096].rearrange("p (q c) -> p q c", q=32); tW1B = tWA[6]
        KCS = WA[:, 18704:18704 + (1 + NS) * 512].rearrange("p (s c) -> p s c", c=512); tKCS = tWA[7]
        VCS = WA[:, 21264:21264 + (1 + NS) * 512].rearrange("p (s k c) -> p s k c", k=4, c=128); tVCS = Tok()
        W2B = WA[:, 23824:23824 + 256].rearrange("p (a c) -> p a c", a=2); tW2B = Tok()
        PWB = WA[:, 24080:24080 + 256].rearrange("p (a c) -> p a c", a=2)
        QT = R[:, 0:4 * TT].rearrange("p (a t) -> p a t", a=4); tQT = Tok()
        o = 4 * TT
        KST = R[:, o:o + TT]; tKST = Tok(); o += TT
        KWT = R[:, o:o + TT]; tKWT = Tok(); o += TT
        VS = R[:, o:o + NTT * 130].rearrange("p (i g c) -> p i g c", g=2, c=65); tVS = Tok(); o += NTT * 130
        VW = R[:, o:o + NTT * 130].rearrange("p (i g c) -> p i g c", g=2, c=65); tVW = Tok(); o += NTT * 130
        OVP = R[:, o:o + 1 + NSELP]; o += 1 + NSELP + (1 + NSELP) % 2
        OVS = R[:, o:o + NKS * (1 + NSELS)].rearrange("p (k c) -> p k c", k=NKS); o += NKS * (1 + NSELS)
        tOV = Tok()
        assert o <= 17472, o
        H2T = R[:, 0:8 * TT].rearrange("p (k t) -> p k t", k=8); tH2T = Tok()
        WADA = R[:, 0:16384].rearrange("p (b k c) -> p b k c", b=4, k=8); tWADA = [Tok() for _ in range(4)]
        WUP = [WA[:, b * 12288:b * 12288 + 8192].rearrange("p (k c) -> p k c", k=8) for b in range(2)]
        WDN = [WA[:, b * 12288 + 8192:b * 12288 + 12288].rearrange("p (j c) -> p j c", j=4) for b in range(2)]
        tWF = [tWA[0], tWA[1]]

        allR = [tQT, tKST, tKWT, tVS, tVW, tOV, tH2T] + tWADA
        allWA = tWA + [tVCS, tW2B]

        def V(e):
            return e

        def ld(q, out, in_, w, r=()):
            S.dma(q, lambda e: e.dma_start(out=out, in_=in_), reads=list(r), writes=list(w))

        ld("sp", IDF[:], k_idf, [tK]); ld("sp", IDB[:], k_idb, [tK]); ld("sp", CAUS[:], k_caus, [tK])
        ld("sp", ANTI[:], k_anti, [tK]); ld("sp", CMPB[:], k_cmpb, [tK]); ld("sp", E64[:], k_e64, [tK])
        ld("sp", FBP[:], k_fbp, [tK]); ld("sp", FBS[:], k_fbs, [tK]); pass
        ld("sp", WINB0[:], k_winb0, [tK]); ld("sp", RC16[:], k_rc, [tK]); ld("sp", BONES[:], k_bones, [tK])
        for i in range(NT):
            ld("sp", X[:, i, :], x_p[i * 128:(i + 1) * 128, :], [tX[i]])
        ld("sp", X[0:TS, NT, :], x_s, [tX[NT]])
        ld("sp", CTF[:], c_all, [tCTF])
        S.dma("pool", lambda e: e.dma_start(out=IDX[:].rearrange("p s j -> p (s j)"),
                                            in_=ptab.rearrange("s j -> (s j)").partition_broadcast(128)), writes=[tIDX])
        PIO = sb("PIO", [128, 2], I32); IDXF = sb("IDXF", [128, NS * NPG])
        S.op("pool", lambda e: e.iota(PIO[:, 0:1], [[0, 1]], base=0, channel_multiplier=1), writes=[tSCR])
        S.op("dve", lambda e: e.tensor_copy(out=SCR[:, 0:1], in_=PIO[:, 0:1]), reads=[tSCR], writes=[tSCR])
        S.op("dve", lambda e: e.tensor_copy(out=IDXF[:], in_=IDX[:].rearrange("p s j -> p (s j)")), reads=[tIDX], writes=[tIDX])
        S.op("dve", lambda e: e.tensor_scalar(out=IDXF[:], in0=IDXF[:], scalar1=128.0, scalar2=SCR[:, 0:1], op0=ALU.mult, op1=ALU.add),
             reads=[tIDX, tSCR], writes=[tIDX])
        S.op("dve", lambda e: e.tensor_copy(out=IDX[:].rearrange("p s j -> p (s j)"), in_=IDXF[:]), reads=[tIDX], writes=[tIDX])
        S.op("pool", lambda e: e.memset(NEWB[:], 0.0), writes=[tK])
        ld("sp", NEWB[0:TS], k_newb, [tK])
        S.op("act", lambda e: e.activation(out=CTF[:], in_=CTF[:], func=AF.Silu), reads=[tCTF], writes=[tCTF])
        SEL = sb("SEL", [1 + NS, 128 + TS]); tSEL = Tok()
        ld("sp", SEL[:], k_cts, [tSEL])
        for k in range(8):
            pb, tp = ps()
            S.op("pe", lambda e, k=k, pb=pb: e.matmul(pb[:, 0:128 + TS], lhsT=CTF[:, k * 128:(k + 1) * 128], rhs=SEL[:],
                                                      start=True, stop=True), reads=[tCTF, tSEL], writes=[tp])
            S.op("dve", lambda e, k=k, pb=pb: e.tensor_copy(out=CTS[:, k, :], in_=pb[:, 0:128 + TS]), reads=[tp], writes=[tK])
        CT5 = sb("CT5", [128, 8, 1 + NS], BF16)
        S.op("dve", lambda e: e.tensor_copy(out=CT5[:, :, 0:1], in_=CTS[:, :, 0:1]), reads=[tK], writes=[tK])
        S.op("dve", lambda e: e.tensor_copy(out=CT5[:, :, 1:1 + NS], in_=CTS[:, :, 128:128 + TS:DS]), reads=[tK], writes=[tK])

        def rstd_from_ss(ss_ap, n, inv, tss):
            S.op("dve", lambda e: e.tensor_scalar(out=ss_ap, in0=ss_ap, scalar1=inv, scalar2=EPS, op0=ALU.mult, op1=ALU.add),
                 reads=[tss], writes=[tss])
            S.op("act", lambda e: e.activation(out=ss_ap, in_=ss_ap, func=AF.Sqrt), reads=[tss], writes=[tss])
            S.op("dve", lambda e: e.reciprocal(out=ss_ap, in_=ss_ap), reads=[tss], writes=[tss])

        def norm_to_HT(tiles, modS, modG, dest, tdest, doff):
            for j, i in enumerate(tiles):
                rows = 128 if i < NT else TS
                S.op("act", lambda e, i=i, rows=rows: e.activation(out=JNK[0:rows, :], in_=X[0:rows, i, :], func=AF.Square,
                                                                   accum_out=ST1[0:rows, 0:1]), reads=[tX[i]], writes=[tJNK, tST1])
                rstd_from_ss(ST1[0:rows, 0:1], 1, 1.0 / D, tST1)
                S.op("dve", lambda e, i=i, rows=rows: e.tensor_scalar(out=SCR[0:rows, 0:1024], in0=X[0:rows, i, :], scalar1=ST1[0:rows, 0:1],
                                                                      scalar2=None, op0=ALU.mult), reads=[tX[i], tST1], writes=[tSCR])
                for hf in range(2):
                    pb, tp = ps()
                    for kk in range(4):
                        k = hf * 4 + kk
                        S.op("pe", lambda e, k=k, kk=kk, pb=pb, rows=rows: e.transpose(pb[:, kk * 128:kk * 128 + rows], SCR[0:rows, k * 128:(k + 1) * 128],
                                                                                       IDF[0:rows, 0:rows]), reads=[tSCR, tK], writes=[tp])
                    for kk in range(4):
                        k = hf * 4 + kk
                        if i < NT:
                            eng = "act" if kk % 2 == 0 else "dve"
                            if eng == "act":
                                S.op("act", lambda e, k=k, kk=kk, pb=pb, j=j: e.activation(out=dest[:, k, doff + j * 128:doff + (j + 1) * 128], in_=pb[:, kk * 128:(kk + 1) * 128],
                                                                                           func=AF.Identity, scale=MODC[:, modG, k, 0:1], bias=MODC[:, modS, k, 0:1]),
                                     reads=[tp, tMODC], writes=[tdest])
                            else:
                                S.op("dve", lambda e, k=k, kk=kk, pb=pb, j=j: e.tensor_scalar(out=dest[:, k, doff + j * 128:doff + (j + 1) * 128], in0=pb[:, kk * 128:(kk + 1) * 128],
                                                                                              scalar1=MODC[:, modG, k, 0:1], scalar2=MODC[:, modS, k, 0:1], op0=ALU.mult, op1=ALU.add),
                                     reads=[tp, tMODC], writes=[tdest])
                        else:
                            for s in range(NS):
                                S.op("dve", lambda e, k=k, kk=kk, pb=pb, j=j, s=s: e.tensor_scalar(
                                    out=dest[:, k, doff + j * 128 + s * DS:doff + j * 128 + (s + 1) * DS], in0=pb[:, kk * 128 + s * DS:kk * 128 + (s + 1) * DS],
                                    scalar1=MODC[:, modG, k, 1 + s:2 + s], scalar2=MODC[:, modS, k, 1 + s:2 + s], op0=ALU.mult, op1=ALU.add),
                                     reads=[tp, tMODC], writes=[tdest])

        def resid_add(i, pb, tp, half, gi):
            rows = 128 if i < NT else TS
            gsel = 0 if i < NT else 1
            S.op("dve", lambda e: e.tensor_tensor(out=SCR[0:rows, 0:512], in0=pb[0:rows, :], in1=GATE[0:rows, gi, gsel, half * 512:(half + 1) * 512], op=ALU.mult),
                 reads=[tp, tGATE[gi][gsel]], writes=[tSCR])
            S.op("dve", lambda e: e.tensor_tensor(out=X[0:rows, i, half * 512:(half + 1) * 512], in0=X[0:rows, i, half * 512:(half + 1) * 512], in1=SCR[0:rows, 0:512], op=ALU.add),
                 reads=[tSCR, tX[i]], writes=[tX[i]])

        def cast_load(out, in_, w, r=()):
            S.dma("pool", lambda e: e.dma_start(out=out, in_=in_), reads=list(r), writes=list(w))

        def LAYER_BODY(l):
            S.barrier()
            MARKS.append(("A", l, dict(S.cnt)))
            for hh in range(2):
                cast_load(WIN[:, :, hh * 1164:(hh + 1) * 1164], w_in[l, :, hh * 1164:(hh + 1) * 1164].rearrange("(k p) c -> p k c", p=128), [tWIN] + allWA)
            cast_load(WOCP, w_out[l, 512:1024, :].rearrange("(k p) c -> p k c", p=128), [tWOCP])
            cast_load(PWB, pwbd[l].rearrange("a p c -> p a c"), [tW2B])
            S.op("pool", lambda e: e.memset(VS[:, :, :, 64:65], 1.0), writes=[tVS] + tWADA)
            S.op("pool", lambda e: e.memset(VW[:, :, :, 64:65], 1.0), writes=[tVW])
            ld("sp", OVP, k_ovp, [tOV]); ld("sp", OVS, k_ovs, [tOV])
            groups = [list(range(g * GTL, min(NT, g * GTL + GTL))) for g in range(NG)] + [[NT]]
            for tiles in groups:
                samp = tiles[0] == NT
                ncol = TS if samp else 128 * len(tiles)
                norm_to_HT(tiles, 0, 1, HT, tHT, 0)
                for j, i in enumerate(tiles):
                    rows = TS if samp else 128
                    c0 = j * 128
                    pq, tq = ps()
                    for k in range(8):
                        S.op("pe", lambda e, k=k, pq=pq, c0=c0, rows=rows: e.matmul(pq[0:rows, :], lhsT=HT[:, k, c0:c0 + rows], rhs=WIN[:, k, 0:512],
                                                                                  start=(k == 0), stop=(k == 7)), reads=[tHT, tWIN], writes=[tq])
                    pk, tk = ps()
                    for k in range(8):
                        S.op("pe", lambda e, k=k, pk=pk, c0=c0, rows=rows: e.matmul(pk[0:rows, :], lhsT=HT[:, k, c0:c0 + rows], rhs=WIN[:, k, 512:1024],
                                                                                  start=(k == 0), stop=(k == 7)), reads=[tHT, tWIN], writes=[tk])
                    pw, tw = ps()
                    for k in range(8):
                        S.op("pe", lambda e, k=k, pw=pw, c0=c0, rows=rows: e.matmul(pw[0:rows, 0:280], lhsT=HT[:, k, c0:c0 + rows], rhs=WIN[:, k, 1024:1304],
                                                                                  start=(k == 0), stop=(k == 7)), reads=[tHT, tWIN], writes=[tw])
                    kb = 0
                    KV = KVO[:, kb, :]
                    S.op("act", lambda e, pk=pk, KV=KV, rows=rows: e.activation(out=KV[0:rows, 0:256], in_=pk[0:rows, 0:256], func=AF.Identity), reads=[tk], writes=[tKVO[kb]])
                    S.op("act", lambda e, pk=pk, KV=KV, rows=rows: e.activation(out=KV[0:rows, 384:512], in_=pk[0:rows, 384:512], func=AF.Identity), reads=[tk], writes=[tKVO[kb]])
                    S.op("act", lambda e, pw=pw, KV=KV, rows=rows: e.activation(out=KV[0:rows, 640:768], in_=pw[0:rows, 128:256], func=AF.Identity), reads=[tw], writes=[tKVO[kb]])
                    S.op("act", lambda e, pq=pq, rows=rows: e.activation(out=SCR[0:rows, 0:512], in_=pq[0:rows, :], func=AF.Square), reads=[tq], writes=[tSCR])
                    S.op("act", lambda e, pk=pk, rows=rows: e.activation(out=SCR[0:rows, 512:640], in_=pk[0:rows, 256:384], func=AF.Square), reads=[tk], writes=[tSCR])
                    S.op("act", lambda e, pw=pw, rows=rows: e.activation(out=SCR[0:rows, 640:768], in_=pw[0:rows, 0:128], func=AF.Square), reads=[tw], writes=[tSCR])
                    S.op("dve", lambda e, rows=rows: e.tensor_reduce(out=ST1[0:rows, 0:12], in_=SCR[0:rows, 0:768].rearrange("p (h d) -> p h d", d=64), axis=AX.X, op=ALU.add),
                         reads=[tSCR], writes=[tST1])
                    rstd_from_ss(ST1[0:rows, 0:12], 12, 1.0 / 64, tST1)
                    S.op("dve", lambda e, pq=pq, rows=rows: e.tensor_tensor(
                        out=QN[0:rows, 0:512].rearrange("t (p a d) -> t a p d", a=2, p=4), in0=pq[0:rows, :].rearrange("t (a p d) -> t a p d", a=2, p=4),
                        in1=ST1[0:rows, 0:8].rearrange("t (a p) -> t a p", a=2).unsqueeze(3).to_broadcast([rows, 2, 4, 64]), op=ALU.mult), reads=[tq, tST1], writes=[tQN])
                    for (src, so, col, do, gi2) in ((pk, 256, 8, 256, 0), (pw, 0, 10, 512, 1)):
                        tsrc = tk if src is pk else tw
                        S.op("dve", lambda e, src=src, so=so, col=col, do=do, KV=KV, rows=rows: e.tensor_tensor(
                            out=KV[0:rows, do:do + 128].rearrange("t (g d) -> t g d", g=2), in0=src[0:rows, so:so + 128].rearrange("t (g d) -> t g d", g=2),
                            in1=ST1[0:rows, col:col + 2].unsqueeze(2).to_broadcast([rows, 2, 64]), op=ALU.mult), reads=[tsrc, tST1], writes=[tKVO[kb]])
                        S.op("dve", lambda e, do=do, gi2=gi2, KV=KV, rows=rows: e.tensor_tensor(out=KV[0:rows, do:do + 128], in0=KV[0:rows, do:do + 128], in1=KGB[0:rows, gi2, :], op=ALU.mult),
                             reads=[tKGB], writes=[tKVO[kb]])
                    if samp:
                        S.dma("sp", lambda e, KV=KV: e.dma_start(out=kv_s[l], in_=KV[0:TS, 0:512]), reads=[tKVO[kb]])
                        for s in range(NS):
                            S.dma("sp", lambda e, KV=KV, s=s: e.dma_start(out=win_s[l, s, 512 - DS:512, :], in_=KV[s * DS:(s + 1) * DS, 512:768]), reads=[tKVO[kb]])
                            for a_ in range(4):
                                S.dma("sp", lambda e, s=s, a_=a_: e.dma_start(out=WST[:, :], in_=st_win[l, s, DS:512, :].rearrange("(p a) c -> p a c", a=4)[:, a_, :]), writes=[tWST])
                                S.dma("sp", lambda e, s=s, a_=a_: e.dma_start(out=win_s[l, s, 0:512 - DS, :].rearrange("(p a) c -> p a c", a=4)[:, a_, :], in_=WST[:, :]), reads=[tWST])
                    else:
                        S.dma("sp", lambda e, KV=KV, i=i: e.dma_start(out=kv_p[l, i * 128:(i + 1) * 128, :], in_=KV[:, 0:512]), reads=[tKVO[kb]], writes=[tKVP[i]])
                        if i >= NT - WINT:
                            S.dma("sp", lambda e, KV=KV, i=i: e.dma_start(out=win_p[l, (i - NT + WINT) * 128:(i - NT + WINT + 1) * 128, :], in_=KV[:, 512:768]), reads=[tKVO[kb]])
                    S.op("act", lambda e, pw=pw, i=i, rows=rows: e.activation(out=GSIG[0:rows, i, :], in_=pw[0:rows, 256:280], func=AF.Exp, scale=-1.0), reads=[tw], writes=[tGS[i]])
                    S.op("dve", lambda e, i=i, rows=rows: e.tensor_scalar(out=GSIG[0:rows, i, :], in0=GSIG[0:rows, i, :], scalar1=1.0, scalar2=None, op0=ALU.add), reads=[tGS[i]], writes=[tGS[i]])
                    S.op("dve", lambda e, i=i, rows=rows: e.reciprocal(out=GSIG[0:rows, i, :], in_=GSIG[0:rows, i, :]), reads=[tGS[i]], writes=[tGS[i]])
                    S.op("act", lambda e, KV=KV, i=i, rows=rows: e.activation(out=VS[0:rows, i, :, 0:64], in_=KV[0:rows, 384:512].rearrange("t (g d) -> t g d", g=2), func=AF.Identity),
                         reads=[tKVO[kb]], writes=[tVS])
                    S.op("act", lambda e, KV=KV, i=i, rows=rows: e.activation(out=VW[0:rows, i, :, 0:64], in_=KV[0:rows, 640:768].rearrange("t (g d) -> t g d", g=2), func=AF.Identity),
                         reads=[tKVO[kb]], writes=[tVW])
                    S.op("act", lambda e, KV=KV, rows=rows: e.activation(out=QN[0:rows, 512:640], in_=KV[0:rows, 256:384], func=AF.Identity), reads=[tKVO[kb]], writes=[tQN])
                    S.op("act", lambda e, KV=KV, rows=rows: e.activation(out=QN[0:rows, 640:768], in_=KV[0:rows, 512:640], func=AF.Identity), reads=[tKVO[kb]], writes=[tQN])
                    pt_, tt_ = ps()
                    ptv = pt_[:].bitcast(BF16)
                    for b6 in range(6):
                        S.op("pe", lambda e, b6=b6, ptv=ptv, rows=rows: e.transpose(ptv[:, b6 * 128:b6 * 128 + rows], QN[0:rows, b6 * 128:(b6 + 1) * 128], IDB[0:rows, 0:rows]),
                             reads=[tQN, tK], writes=[tt_])
                    tc0 = i * 128
                    S.op("dve", lambda e, ptv=ptv, tc0=tc0, rows=rows: e.tensor_scalar(out=QT[:, :, tc0:tc0 + rows], in0=ptv[:, 0:512].rearrange("p (a t) -> p a t", a=4)[:, :, 0:rows],
                                                                                       scalar1=SPT[:, 74:75], scalar2=None, op0=ALU.mult), reads=[tt_, tSPT], writes=[tQT])
                    S.op("act", lambda e, ptv=ptv, tc0=tc0, rows=rows: e.activation(out=KST[:, tc0:tc0 + rows], in_=ptv[:, 512:512 + rows], func=AF.Identity), reads=[tt_], writes=[tKST])
                    S.op("act", lambda e, ptv=ptv, tc0=tc0, rows=rows: e.activation(out=KWT[:, tc0:tc0 + rows], in_=ptv[:, 640:640 + rows], func=AF.Identity), reads=[tt_], writes=[tKWT])


                n = ncol
                if samp:
                    S.dma("sp", lambda e: e.dma_start(out=OUTT[0:NS * 17, :], in_=st_cp[l]), writes=[tOUTT])
                    for c in range(2):
                        ph, th = ps()
                        S.op("pe", lambda e, c=c, ph=ph: e.transpose(ph[:, 0:NS * 17], OUTT[0:NS * 17, c * 128:(c + 1) * 128], IDF[0:NS * 17, 0:NS * 17]), reads=[tOUTT, tK], writes=[th])
                        S.op("dve", lambda e, c=c, ph=ph: e.tensor_copy(out=PGS[:, c, :, 0:15], in_=ph[:, 0:NS * 15].rearrange("p (s x) -> p s x", s=NS)), reads=[th], writes=[tUGS])
                        S.op("dve", lambda e, c=c, ph=ph: e.tensor_copy(out=UGS[:, c, :, 0:2], in_=ph[:, NS * 15:NS * 17].rearrange("p (s x) -> p s x", s=NS)), reads=[th], writes=[tUGS])
                    Uv = lambda c, a, b: UGS[:, c, :, a:b]
                    Pv = lambda c, a, b: PGS[:, c, :, a:b]
                    Sv = lambda sl_, a, b: SA[:, sl_, 0:NS * 23].rearrange("p (s x) -> p s x", s=NS)[:, :, a:b]
                    psv = lambda pb_: pb_[:, 0:TS].rearrange("p (s t) -> p s t", s=NS)
                    Yv = lambda k_: YCP[:, k_, 0:TS].rearrange("p (s t) -> p s t", s=NS)
                    Dv = lambda: DPB[:, 0:TS].rearrange("p (s t) -> p s t", s=NS)
                    nn = DS; tU = tUGS; tP = tUGS
                else:
                    Uv = lambda c, a, b: UG[:, c, a:b]
                    Pv = lambda c, a, b: PG[:, c, a:b]
                    Sv = lambda sl_, a, b: SA[:, sl_, a:b]
                    psv = lambda pb_: pb_[:, 0:n]
                    Yv = lambda k_: YCP[:, k_, 0:n]
                    Dv = lambda: DPB[:, 0:n]
                    nn = n; tU = tUG; tP = tPG
                    if tiles[0] == 0:
                        S.op("pool", lambda e: e.memset(UG[:, :, 0:2], 0.0), writes=[tUG])
                        S.op("pool", lambda e: e.memset(PG[:, :, 0:15], 0.0), writes=[tPG])

                def zT(cc):
                    pz, tz = ps()
                    for k in range(8):
                        S.op("pe", lambda e, k=k, pz=pz, cc=cc: e.matmul(pz[:, 0:n], lhsT=WIN[:, k, 1304 + cc * 128:1304 + (cc + 1) * 128], rhs=HT[:, k, 0:n],
                                                                       start=(k == 0), stop=(k == 7)), reads=[tWIN, tHT], writes=[tz])
                    return pz, tz

                hp = lambda ap, lo, hi: ap[lo:hi]
                for c in range(2):
                    pcg, tcg = zT(2 + c)
                    phn, thn = zT(4 + c)
                    S.op("act", lambda e, pcg=pcg: e.activation(out=Sv(0, 0, nn), in_=psv(pcg), func=AF.Identity), reads=[tcg], writes=[tSA])
                    S.op("dve", lambda e, phn=phn, c=c: e.tensor_tensor(out=Uv(c, 2, 2 + nn), in0=Sv(0, 0, nn), in1=psv(phn), op=ALU.mult), reads=[tSA, thn], writes=[tU])
                    S.op("dve", lambda e, c=c: e.tensor_scalar(out=Sv(1, 0, nn), in0=Uv(c, 2, 2 + nn), scalar1=SPT[:, 68 + c:69 + c], scalar2=SPT[:, 70 + c:71 + c], op0=ALU.mult, op1=ALU.add),
                         reads=[tU, tSPT], writes=[tSA])
                    S.op("dve", lambda e, c=c: e.scalar_tensor_tensor(out=Sv(1, 0, nn), in0=Uv(c, 1, 1 + nn), scalar=SPT[:, 66 + c:67 + c], in1=Sv(1, 0, nn), op0=ALU.mult, op1=ALU.add),
                         reads=[tU, tSPT, tSA], writes=[tSA])
                    S.op("dve", lambda e, c=c: e.scalar_tensor_tensor(out=Sv(1, 0, nn), in0=Uv(c, 0, nn), scalar=SPT[:, 64 + c:65 + c], in1=Sv(1, 0, nn), op0=ALU.mult, op1=ALU.add),
                         reads=[tU, tSPT, tSA], writes=[tSA])
                    pbg, tbg = zT(c)
                    S.op("dve", lambda e, pbg=pbg, c=c: e.tensor_tensor(out=Yv(c), in0=Sv(1, 0, nn), in1=psv(pbg), op=ALU.mult), reads=[tSA, tbg], writes=[tYCP])
                    ppi, tpi = zT(6 + c)
                    S.op("act", lambda e, ppi=ppi, c=c: e.activation(out=Pv(c, 15, 15 + nn), in_=psv(ppi), func=AF.Identity), reads=[tpi], writes=[tP])
                    L = 15 + nn
                    S.op("dve", lambda e, c=c: e.tensor_tensor(out=Sv(0, 1, L), in0=Pv(c, 1, L), in1=Pv(c, 0, L - 1), op=ALU.add), reads=[tP], writes=[tSA])
                    S.op("dve", lambda e, c=c: e.tensor_tensor(out=Sv(1, 3, L), in0=Sv(0, 3, L), in1=Sv(0, 1, L - 2), op=ALU.add), reads=[tSA], writes=[tSA])
                    if c == 0:
                        lo_s, lo_w, hi_s, hi_w = 0, 2.0, 1, 4.0
                    else:
                        S.op("dve", lambda e: e.tensor_tensor(out=Sv(0, 7, L), in0=Sv(1, 7, L), in1=Sv(1, 3, L - 4), op=ALU.add), reads=[tSA], writes=[tSA])
                        S.op("dve", lambda e: e.tensor_tensor(out=Sv(1, 15, L), in0=Sv(0, 15, L), in1=Sv(0, 7, L - 8), op=ALU.add), reads=[tSA], writes=[tSA])
                        lo_s, lo_w, hi_s, hi_w = 0, 8.0, 1, 16.0
                    for (plo, phi, ssl, ww) in ((0, 64, lo_s, lo_w), (64, 128, hi_s, hi_w)):
                        S.op("dve", lambda e, plo=plo, phi=phi, ssl=ssl, ww=ww, c=c: e.scalar_tensor_tensor(out=Dv()[plo:phi], in0=Sv(ssl, 15, L)[plo:phi], scalar=1.0 / ww, in1=Pv(c, 15, L)[plo:phi],
                                                                                                       op0=ALU.mult, op1=ALU.subtract), reads=[tSA, tP], writes=[tDPB])
                        if (not samp) and tiles[0] == 0:
                            S.op("dve", lambda e, plo=plo, phi=phi, ssl=ssl, c=c: e.tensor_tensor(out=ST1[plo:phi, 16:32], in0=SA[plo:phi, ssl, 15:31], in1=RC16[plo:phi, c, :], op=ALU.mult),
                                 reads=[tSA, tK], writes=[tST1])
                            S.op("dve", lambda e, plo=plo, phi=phi, c=c: e.tensor_tensor(out=DPB[plo:phi, 0:16], in0=ST1[plo:phi, 16:32], in1=PG[plo:phi, c, 15:31], op=ALU.subtract),
                                 reads=[tST1, tP], writes=[tDPB])
                    py, ty = ps()
                    S.op("pe", lambda e, py=py, c=c: e.matmul(py[:, 0:n], lhsT=PWB[:, c, :], rhs=DPB[:, 0:n], start=True, stop=True), reads=[tDPB, tW2B], writes=[ty])
                    S.op("dve", lambda e, py=py, c=c: e.tensor_scalar(out=YCP[:, 2 + c, 0:n], in0=py[:, 0:n], scalar1=SPT[:, 72 + c:73 + c], scalar2=None, op0=ALU.mult), reads=[ty, tSPT], writes=[tYCP])
                    last = samp or tiles[-1] == NT - 1
                    if not samp and not last:
                        S.op("pool", lambda e, c=c: e.tensor_copy(out=UG[:, c, 0:2], in_=UG[:, c, nn:nn + 2]), reads=[tUG], writes=[tUG])
                        S.op("pool", lambda e, c=c: e.tensor_copy(out=PG[:, c, 0:15], in_=PG[:, c, nn:nn + 15]), reads=[tPG], writes=[tPG])
                    if last:
                        nq = NS if samp else 1
                        pst, tst = ps()
                        if samp:
                            S.op("dve", lambda e, c=c: e.tensor_copy(out=SA[:, 0, 0:NS * 15].rearrange("p (s x) -> p s x", s=NS), in_=PGS[:, c, :, DS:DS + 15]), reads=[tUGS], writes=[tSA])
                            S.op("dve", lambda e, c=c: e.tensor_copy(out=SA[:, 0, NS * 15:NS * 17].rearrange("p (s x) -> p s x", s=NS), in_=UGS[:, c, :, DS:DS + 2]), reads=[tUGS], writes=[tSA])
                        else:
                            S.op("dve", lambda e, c=c: e.tensor_copy(out=SA[:, 0, 0:15], in_=PG[:, c, nn:nn + 15]), reads=[tPG], writes=[tSA])
                            S.op("dve", lambda e, c=c: e.tensor_copy(out=SA[:, 0, 15:17], in_=UG[:, c, nn:nn + 2]), reads=[tUG], writes=[tSA])
                        S.op("pe", lambda e, pst=pst, nq=nq: e.transpose(pst[0:nq * 17, 0:128], SA[:, 0, 0:nq * 17], IDF[:]), reads=[tSA, tK], writes=[tst])
                        S.op("dve", lambda e, pst=pst, nq=nq, c=c: e.tensor_copy(out=OUTT[0:nq * 17, c * 128:(c + 1) * 128], in_=pst[0:nq * 17, 0:128]), reads=[tst], writes=[tOUTT])
                if samp or tiles[-1] == NT - 1:
                    nq = NS if samp else 1
                    S.dma("sp", lambda e, nq=nq: e.dma_start(out=(pool_s if samp else pool_p)[l], in_=OUTT[0:nq * 15, :]), reads=[tOUTT])
                    S.dma("sp", lambda e, nq=nq: e.dma_start(out=(conv_s if samp else conv_p)[l], in_=OUTT[nq * 15:nq * 17, :]), reads=[tOUTT])
                for j, i in enumerate(tiles):
                    rows = TS if samp else 128
                    for half in range(2):
                        po, to = ps()
                        for k in range(4):
                            S.op("pe", lambda e, k=k, po=po, j=j, rows=rows, half=half: e.matmul(po[0:rows, :], lhsT=YCP[:, k, j * 128:j * 128 + rows], rhs=WOCP[:, k, half * 512:(half + 1) * 512],
                                                                                               start=(k == 0), stop=(k == 3)), reads=[tYCP, tWOCP], writes=[to])
                        resid_add(i, po, to, half, 0)


            ACC = [(PSB[3], tPS[3]), (PSB[4], tPS[4])]
            SACC = [(PSB[5], tPS[5]), (PSB[6], tPS[6]), (PSB[7], tPS[7])]
            accn = [0]
            saccn = [0]
            NPSR[0] = 3

            def acc():
                a = ACC[accn[0] % 2]
                accn[0] += 1
                return a

            def sacc():
                a = SACC[saccn[0] % 3]
                saccn[0] += 1
                return a
            PEND = [None]

            def push(fn):
                old = PEND[0]
                PEND[0] = fn
                if old is not None:
                    old()

            def flush():
                old = PEND[0]
                PEND[0] = None
                if old is not None:
                    old()
            ptn = [0]

            def ptbuf():
                i_ = ptn[0] % 3
                ptn[0] += 1
                return PT[:, i_, :], tPT[i_]

            S.barrier()
            MARKS.append(("B", l, dict(S.cnt)))
            S.op("pool", lambda e: e.memset(PTS[:], 0.0), writes=[t for r_ in tPTS for t in r_])
            S.op("pool", lambda e: e.memset(BTS[:], 0.0), writes=[tBTS])
            S.op("pool", lambda e: e.memset(BT[:], 0.0), writes=[tBT])
            S.op("pool", lambda e: e.memset(VN0[:, :, 0:1], 1.0), writes=[tVN0])
            cast_load(WOA, w_out[l, 0:512, :].rearrange("(k p) c -> p k c", p=128), [tWOA] + allWA)
            cast_load(W2B, w2bd[l].rearrange("a p c -> p a c"), [tW2B])

            def compress(slot, seq, nblk, loader, npages):
                for pg0 in range(0, npages, 8):
                    npc = min(8, npages - pg0)
                    b = (pg0 // 8) % 2
                    stg = STG[:, b, 0:1024].rearrange("p (j c) -> p j c", c=128)
                    loader(stg, tSTG[b], pg0, npc)
                    pt_, tt_ = ps()
                    ptv = pt_[:].bitcast(BF16)
                    for j in range(npc):
                        S.op("pe", lambda e, j=j: e.transpose(ptv[:, j * 128:(j + 1) * 128], stg[:, j, :], IDB[:]), reads=[tSTG[b], tK], writes=[tt_])
                    S.op("dve", lambda e: e.tensor_copy(out=RAWT[:, :, pg0 * 8:(pg0 + npc) * 8], in_=ptv[:, 0:npc * 128].rearrange("p (m r) -> p r m", r=16)),
                         reads=[tt_], writes=[tRAWT])
                    yield
                ph, th = ps()
                for p in range(32):
                    S.op("pe", lambda e, p=p: e.matmul(ph[:, 0:nblk], lhsT=W1B[:, p, :], rhs=RAWT[:, p % 16, p // 16:p // 16 + nblk], start=(p == 0), stop=(p == 31)),
                         reads=[tW1B, tRAWT], writes=[th])
                S.op("act", lambda e: e.activation(out=SH[:, 0:nblk], in_=ph[:, 0:nblk], func=AF.Silu, bias=ST1[:, 32:33]), reads=[th, tST1], writes=[tSH])
                if slot == 0:
                    p2, t2 = ps()
                    S.op("pe", lambda e: e.matmul(p2[:, 0:nblk], lhsT=W2B[:, 0, :], rhs=SH[:, 0:nblk], start=True, stop=True), reads=[tSH, tW2B], writes=[t2])
                    S.op("act", lambda e: e.activation(out=SQ[:, 0:nblk], in_=p2[:, 0:nblk], func=AF.Square), reads=[t2], writes=[tSH])
                    p3, t3 = ps()
                    S.op("pe", lambda e: e.matmul(p3[:, 0:nblk], lhsT=BONES[:], rhs=SQ[:, 0:nblk], start=True, stop=True), reads=[tSH, tK], writes=[t3])
                    S.op("dve", lambda e: e.tensor_scalar(out=SCR[:, 0:nblk], in0=p3[:, 0:nblk], scalar1=1.0 / 64, scalar2=EPS, op0=ALU.mult, op1=ALU.add), reads=[t3], writes=[tSCR])
                    S.op("act", lambda e: e.activation(out=SCR[:, 0:nblk], in_=SCR[:, 0:nblk], func=AF.Sqrt), reads=[tSCR], writes=[tSCR])
                    S.op("dve", lambda e: e.reciprocal(out=SCR[:, 0:nblk], in_=SCR[:, 0:nblk]), reads=[tSCR], writes=[tSCR])
                    S.op("dve", lambda e: e.scalar_tensor_tensor(out=KCS[:, seq, 0:nblk], in0=p2[:, 0:nblk], scalar=SPT[:, 75:76], in1=SCR[:, 0:nblk], op0=ALU.mult, op1=ALU.mult),
                         reads=[t2, tSPT, tSCR], writes=[tKCS])
                else:
                    for kt in range((nblk + 127) // 128):
                        nb_ = min(128, nblk - kt * 128)
                        p2, t2 = ps()
                        S.op("pe", lambda e, kt=kt, nb_=nb_: e.matmul(p2[0:nb_, 0:128], lhsT=SH[:, kt * 128:kt * 128 + nb_], rhs=W2B[:, 1, :], start=True, stop=True),
                             reads=[tSH, tW2B], writes=[t2])
                        S.op("act", lambda e, kt=kt, nb_=nb_: e.activation(out=VCS[0:nb_, seq, kt, :], in_=p2[0:nb_, 0:128], func=AF.Identity), reads=[t2], writes=[tVCS])

            def mk_idxl(which, slot):
                S.op("dve", lambda e: e.tensor_scalar(out=IDXG[:], in0=IDXF[:], scalar1=float(l * NPOOL * 128), scalar2=4.0, op0=ALU.add, op1=ALU.mult), reads=[tIDX], writes=[tIDXG])
                S.op("dve", lambda e: e.tensor_scalar(out=IDXG[:], in0=IDXG[:], scalar1=float(slot), scalar2=None, op0=ALU.add), reads=[tIDXG], writes=[tIDXG])
                S.op("dve", lambda e: e.tensor_copy(out=IDXL[:, which, :], in_=IDXG[:]), reads=[tIDXG], writes=[tIDXL])

            def gather(out_ap, which, sq, pg, wtok, join=False):
                S.dma("pool", lambda e: e.indirect_dma_start(out=out_ap, out_offset=None, in_=cacheR,
                                                             in_offset=bass.IndirectOffsetOnAxis(ap=IDXL[:, which, sq * NPG + pg:sq * NPG + pg + 1], axis=0)),
                      reads=[tIDXL], writes=[wtok], join=join)

            def sample_loader(which, slot, sq):
                def f(stg, tstg, pg0, npc):
                    for j in range(npc):
                        gather(stg[:, j, :], which, sq, pg0 + j, tstg, join=(j > 0))
                return f

            def prompt_loader(slot):
                def f(stg, tstg, pg0, npc):
                    S.dma("pool", lambda e: e.dma_start(out=stg[:, 0:npc, :], in_=kv_p[l, pg0 * 128:(pg0 + npc) * 128, slot * 128:(slot + 1) * 128].rearrange("(j p) c -> p j c", p=128)),
                          reads=tKVP[pg0:pg0 + npc], writes=[tstg])
                return f

            def load_w1(slot):
                for hh in range(2):
                    cast_load(W1B[:, hh * 16:(hh + 1) * 16, :], w1bd[l, slot, :, hh * 16:(hh + 1) * 16, :], [tW1B])
                cast_load(PETB[:], pet[l, slot], [tPET])
                pbias, tbias = ps()
                for p in range(32):
                    S.op("pe", lambda e, p=p: e.matmul(pbias[:, 0:2], lhsT=W1B[:, p, :], rhs=PETB[:, p:p + 1].to_broadcast([128, 2]), start=(p == 0), stop=(p == 31)),
                         reads=[tW1B, tPET], writes=[tbias])
                S.op("dve", lambda e: e.tensor_copy(out=ST1[:, 32:33], in_=pbias[:, 0:1]), reads=[tbias], writes=[tST1])

            for slot in range(2):
                load_w1(slot)
                for _ in compress(slot, 0, NCP, prompt_loader(slot), NT):
                    pass

            def sample_gen():
                for slot in range(2):
                    load_w1(slot)
                    mk_idxl(0, slot)
                    yield
                    for sq in range(NS):
                        yield from compress(slot, 1 + sq, NCS, sample_loader(0, slot, sq), NPG)
                yield from sample_attn()

            def sample_attn():
                MARKS.append(("Bs", l, dict(S.cnt)))
                S.op("pool", lambda e: e.memset(OACS[:], 0.0), writes=[tOACS])
                S.op("dve", lambda e: e.tensor_copy(out=VN0[:, 0, 1:65], in_=VS[0:TS, NT, 0, 0:64]), reads=[tVS], writes=[tVN0])
                S.op("dve", lambda e: e.tensor_copy(out=VN0[:, 1, 1:65], in_=VW[0:TS, NT, 0, 0:64]), reads=[tVW], writes=[tVN0])
                mk_idxl(0, 2); mk_idxl(1, 3)
                for b in range(2):
                    stgv = STG[:, b, :].rearrange("p (j c) -> p j c", c=288)
                    S.op("pool", lambda e, stgv=stgv: e.memset(stgv[:, :, 128:144], 1.0), writes=[tSTG[b]])
                    S.op("pool", lambda e, stgv=stgv: e.memset(stgv[:, :, 272:288], 1.0), writes=[tSTG[b]])
                ptsn = [0]
                NBK = 2 * NPG
                CS = 65 + NSELS
                import os
                for sq in range(NS if not os.environ.get('NOSAMP') else 0):
                    sc = slice(sq * DS, (sq + 1) * DS)
                    qcs = slice(T + sq * DS, T + (sq + 1) * DS)

                    def pts_next():
                        i_ = ptsn[0] % 3
                        ptsn[0] += 1
                        return PTS[:, sq, i_, :].rearrange("p (h t) -> p h t", h=4), tPTS[sq][i_]

                    def score_tile(pS, tS_, nk, lhsT_k, tk_, g, bias=None):
                        gp_ = slice(64 * g, 64 * g + 64)
                        S.op("pe", lambda e: e.matmul(pS[0:nk, 0:4 * DS], lhsT=lhsT_k, rhs=QT[gp_, :, qcs], start=True, stop=(bias is None)), reads=[tk_, tQT], writes=[tS_])
                        if bias is not None:
                            bl, br_, tb_ = bias
                            S.op("pe", lambda e: e.matmul(pS[0:nk, 0:4 * DS], lhsT=bl, rhs=br_, start=False, stop=True), reads=[tK, tb_], writes=[tS_])
                        pts, tpts = pts_next()
                        S.op("act", lambda e: e.activation(out=pts[0:nk, :, sc], in_=pS[0:nk, 0:4 * DS].rearrange("p (h t) -> p h t", h=4), func=AF.Exp, scale=SCALE), reads=[tS_], writes=[tpts])
                        return pts, tpts

                    for g in range(2):
                        gp = slice(64 * g, 64 * g + 64)
                        accs = [sacc(), sacc()]
                        for kt in range(NKS):
                            nk = min(128, NCS - kt * 128)
                            pS, tS_ = ps()
                            pts, tpts = score_tile(pS, tS_, nk, KCS[gp, 1 + sq, kt * 128:kt * 128 + nk], tKCS, g)
                            def fin_scmp(kt=kt, nk=nk, pts=pts, tpts=tpts):
                              for h in range(4):
                                  ac, tac = accs[h // 2]
                                  c0_ = (h % 2) * CS
                                  S.op("pe", lambda e, h=h, ac=ac, c0_=c0_: e.matmul(ac[0:TS, c0_:c0_ + 64], lhsT=pts[0:nk, h, :], rhs=VCS[0:nk, 1 + sq, kt, gp],
                                                                                    start=(kt == 0 and h % 2 == 0), stop=(kt == NKS - 1), skip_group_check=True), reads=[tpts, tVCS], writes=[tac])
                                  S.op("pe", lambda e, h=h, ac=ac, c0_=c0_: e.matmul(ac[0:TS, c0_ + 64:c0_ + CS], lhsT=pts[0:nk, h, :], rhs=OVS[0:nk, kt, :],
                                                                                    start=False, stop=(kt == NKS - 1), skip_group_check=True), reads=[tpts, tOV], writes=[tac])
                            push(fin_scmp)
                        flush()
                        score = SC2[0:TS, 0, 0:NSELS]
                        for hb in range(2):
                            ac, tac = accs[hb]
                            attn_epilogue(ac, tac, g, 0, NT, False, CS, rows=TS, nh=2, h0=2 * hb, dest=OACS, tdest=tOACS)
                            hv = ac[0:TS, 0:2 * CS].rearrange("p (h c) -> p h c", c=CS)
                            for h in range(2):
                                if hb == 0 and h == 0:
                                    S.op("dve", lambda e: e.tensor_scalar(out=score, in0=hv[:, 0, 65:CS], scalar1=ST1[0:TS, 40:41], scalar2=None, op0=ALU.mult), reads=[tac, tST1], writes=[tSC2])
                                else:
                                    S.op("dve", lambda e, h=h: e.scalar_tensor_tensor(out=score, in0=hv[:, h, 65:CS], scalar=ST1[0:TS, 40 + h:41 + h], in1=score, op0=ALU.mult, op1=ALU.add),
                                         reads=[tac, tST1, tSC2], writes=[tSC2])
                        S.op("dve", lambda e: e.tensor_tensor(out=score, in0=score, in1=FBS[:, :], op=ALU.add), reads=[tSC2, tK], writes=[tSC2])
                        select(score, NSELS, TS)
                        yield
                        pB, tB = ps()
                        S.op("pe", lambda e: e.transpose(pB[0:NBK, 0:TS], SC2[0:TS, 0, 0:NBK], IDF[0:TS, 0:TS]), reads=[tSC2, tK], writes=[tB])
                        for hf_ in range((NBK + 63) // 64):
                            r0, r1 = hf_ * 64, min(NBK, hf_ * 64 + 64)
                            S.op("act", lambda e, hf_=hf_, r0=r0, r1=r1: e.activation(out=BTS[r0:r1, hf_, g, :], in_=pB[r0:r1, 0:TS], func=AF.Identity), reads=[tB], writes=[tBTS])

                    def past_pass(npages, loadfn, biasfn, KN, tKN, VNg1, tVNg1, vn0_idx, br):
                        accg = [sacc(), sacc()]
                        first = [True, True]
                        for pg0 in range(0, npages, 4):
                            npc = min(4, npages - pg0)
                            b = (pg0 // 4) % 2
                            stgv = STG[:, b, :].rearrange("p (j c) -> p j c", c=288)
                            loadfn(stgv, tSTG[b], pg0, npc)
                            yield
                            pt_, tt_ = ps()
                            ptv = pt_[:].bitcast(BF16)
                            for j in range(npc):
                                S.op("pe", lambda e, j=j: e.transpose(ptv[:, j * 128:(j + 1) * 128], stgv[:, j, 0:128], IDB[:]), reads=[tSTG[b], tK], writes=[tt_])
                            S.op("dve", lambda e: e.tensor_copy(out=KTC[:, 0:npc * 128], in_=ptv[:, 0:npc * 128]), reads=[tt_], writes=[tRAWT])
                            SLCV = int(os.environ.get('SLCV', '9'))
                            KK_ = int(os.environ.get('KK', '128'))
                            for g in range(2 if SLCV >= 2 else 0):
                                gp = slice(64 * g, 64 * g + 64)
                                ac, tac = accg[g]
                                if os.environ.get('OB') == '1':
                                    ac, tac = ps()
                                for j in range(npc):
                                    pg = pg0 + j
                                    pS, tS_ = ps()
                                    pts, tpts = score_tile(pS, tS_, 128, KTC[gp, j * 128:(j + 1) * 128], tRAWT, g, bias=biasfn(pg, g))
                                    vr = stgv[:, j, 143:208] if g == 0 else stgv[:, j, 208:273]
                                    if os.environ.get('VRT') == '1':
                                        vr = stgv[:, j, 144:209] if g == 0 else stgv[:, j, 208:273]
                                    if os.environ.get('VRT') == '2':
                                        vr = VS[:, 0, g, :]
                                    if os.environ.get('VRT') == '3':
                                        vr = VS[:, 0, g, 0:64]
                                    if os.environ.get('E4') == '1':
                                        for hb in range(2):
                                            S.op("pe", lambda e, hb=hb, vr=vr, ac=ac: e.matmul(ac[0:2 * TS, hb * 65:(hb + 1) * 65], lhsT=pts[:, 2 * hb:2 * hb + 2, :], rhs=vr, start=(first[g] and hb == 0), stop=False, skip_group_check=True),
                                                 reads=[tpts, tSTG[b]], writes=[tac])
                                    def fin_pp(pts=pts, tpts=tpts, vr=vr, ac=ac, tac=tac, fg=first[g], b=b):
                                        for h in range(4 if (SLCV >= 3 and os.environ.get('E4') != '1') else 0):
                                            S.op("pe", lambda e, h=h, vr=vr, ac=ac: e.matmul(ac[0:TS, h * 65:h * 65 + vr.shape[-1]], lhsT=(IDB[0:KK_, 0:32] if os.environ.get('LT') == '1' else pts[0:KK_, h, :]), rhs=vr[0:KK_], start=((fg and h == 0) or os.environ.get('E3') == '1'), stop=False, skip_group_check=True),
                                                 reads=([] if os.environ.get('ND') == '1' else [tpts, tSTG[b]]), writes=[tac])
                                    push(fin_pp)
                                    first[g] = False
                        flush()
                        yield
                        for g in range(2 if SLCV >= 4 else 0):
                            gp = slice(64 * g, 64 * g + 64)
                            ac, tac = accg[g]
                            pS, tS_ = ps()
                            pts, tpts = score_tile(pS, tS_, TS, KN[gp, T:T + TS], tKN, g, bias=(IDB[:, 0:TS], NEWB[:, sq, :], tK))
                            vr = VN0[0:TS, vn0_idx, :] if g == 0 else VNg1[0:TS, NT, 1, :]
                            for h in range(4):
                                S.op("pe", lambda e, h=h, vr=vr, ac=ac: e.matmul(ac[0:TS, h * 65:(h + 1) * 65], lhsT=pts[0:TS, h, :], rhs=vr, start=False, stop=True, skip_group_check=True),
                                     reads=[tpts, tVN0, tVNg1], writes=[tac])
                            attn_epilogue(ac, tac, g, br, NT, False, 65, rows=TS, ocol=(1 if g == 0 else 0), scol=(0 if g == 0 else 64), dest=OACS, tdest=tOACS)

                    def slc_load(stgv, tstg, pg0, npc):
                        for j in range(npc):
                            gather(stgv[:, j, 0:128], 0, sq, pg0 + j, tstg, join=(j > 0))
                            gather(stgv[:, j, 144:272], 1, sq, pg0 + j, tstg, join=True)

                    def slc_bias(pg, g):
                        half = pg // 32
                        return (E64[:, pg % 32, :], BTS[:, half, g, sc].unsqueeze(1).to_broadcast([128, 4, DS]), tBTS)

                    if int(os.environ.get('SST', '9')) >= 2:
                        yield from past_pass(NPG, slc_load, slc_bias, KST, tKST, VS, tVS, 0, 1)

                    def win_load(stgv, tstg, pg0, npc):
                        S.dma("pool", lambda e: e.dma_start(out=stgv[:, 0:npc, 0:128], in_=st_win[l, sq, pg0 * 128:(pg0 + npc) * 128, 0:128].rearrange("(j p) c -> p j c", p=128)), writes=[tstg])
                        S.dma("pool", lambda e: e.dma_start(out=stgv[:, 0:npc, 144:272], in_=st_win[l, sq, pg0 * 128:(pg0 + npc) * 128, 128:256].rearrange("(j p) c -> p j c", p=128)), writes=[tstg], join=True)

                    def win_bias(pg, g):
                        if pg == 0:
                            return (IDB[:], WINB0[:, :], tK)
                        return None

                    if int(os.environ.get('SST', '9')) >= 3:
                        yield from past_pass(4, win_load, win_bias, KWT, tKWT, VW, tVW, 1, 2)

            MARKS.append(("Bp", l, dict(S.cnt)))
            def attn_epilogue(ac, tac, g, br, qt, first, stride, rows=128, nh=4, h0=0, ocol=0, scol=64, dest=None, tdest=None):
                dest = OACC if dest is None else dest
                tdest = tOACC if tdest is None else tdest
                hv = ac[0:rows, 0:nh * stride].rearrange("p (h c) -> p h c", c=stride)
                S.op("dve", lambda e: e.tensor_scalar(out=ST1[0:rows, 40:40 + nh], in0=hv[:, :, scol], scalar1=1e-30, scalar2=None, op0=ALU.add), reads=[tac], writes=[tST1])
                S.op("dve", lambda e: e.reciprocal(out=ST1[0:rows, 40:40 + nh], in_=ST1[0:rows, 40:40 + nh]), reads=[tST1], writes=[tST1])
                S.op("dve", lambda e: e.tensor_tensor(out=ST1[0:rows, 44:44 + nh], in0=ST1[0:rows, 40:40 + nh],
                                                      in1=GSIG[0:rows, qt, :].rearrange("p (h b) -> p h b", b=3)[:, 4 * g + h0:4 * g + h0 + nh, br], op=ALU.mult), reads=[tST1, tGS[qt]], writes=[tST1])
                for h in range(nh):
                    hh_ = 4 * g + h0 + h
                    if first:
                        S.op("dve", lambda e, h=h, hh_=hh_: e.tensor_scalar(out=dest[0:rows, hh_ * 64:(hh_ + 1) * 64], in0=hv[:, h, ocol:ocol + 64], scalar1=ST1[0:rows, 44 + h:45 + h], scalar2=None, op0=ALU.mult),
                             reads=[tac, tST1], writes=[tdest])
                    else:
                        S.op("dve", lambda e, h=h, hh_=hh_: e.scalar_tensor_tensor(out=dest[0:rows, hh_ * 64:(hh_ + 1) * 64], in0=hv[:, h, ocol:ocol + 64], scalar=ST1[0:rows, 44 + h:45 + h],
                                                                                  in1=dest[0:rows, hh_ * 64:(hh_ + 1) * 64], op0=ALU.mult, op1=ALU.add), reads=[tac, tST1, tdest], writes=[tdest])

            def select(score_ap, nsel, rows):
                if nsel > 16:
                    S.op("dve", lambda e: e.max(out=SC2[0:rows, 2, 0:8], in_=score_ap), reads=[tSC2], writes=[tSC2])
                    S.op("dve", lambda e: e.match_replace(out=SC2[0:rows, 1, 0:nsel], in_to_replace=SC2[0:rows, 2, 0:8], in_values=score_ap, imm_value=-3e38), reads=[tSC2], writes=[tSC2])
                    S.op("dve", lambda e: e.max(out=SC2[0:rows, 2, 8:16], in_=SC2[0:rows, 1, 0:nsel]), reads=[tSC2], writes=[tSC2])
                    S.op("dve", lambda e: e.tensor_scalar(out=SC2[0:rows, 1, 0:nsel], in0=score_ap, scalar1=SC2[0:rows, 2, 15:16], scalar2=None, op0=ALU.is_ge), reads=[tSC2], writes=[tSC2])
                    S.op("dve", lambda e: e.tensor_scalar(out=score_ap, in0=score_ap, scalar1=-5e29, scalar2=None, op0=ALU.is_gt), reads=[tSC2], writes=[tSC2])
                    S.op("dve", lambda e: e.tensor_tensor(out=score_ap, in0=score_ap, in1=SC2[0:rows, 1, 0:nsel], op=ALU.mult), reads=[tSC2], writes=[tSC2])
                else:
                    S.op("dve", lambda e: e.tensor_scalar(out=score_ap, in0=score_ap, scalar1=-5e29, scalar2=None, op0=ALU.is_gt), reads=[tSC2], writes=[tSC2])
                S.op("dve", lambda e: e.tensor_scalar(out=score_ap, in0=score_ap, scalar1=-1.0, scalar2=-NEGB, op0=ALU.add, op1=ALU.mult), reads=[tSC2], writes=[tSC2])

            def cmp_branch(qt, g):
                qc = slice(qt * 128, (qt + 1) * 128)
                gp = slice(64 * g, 64 * g + 64)
                pS, tS_ = ps()
                S.op("pe", lambda e: e.matmul(pS[0:NCP, :], lhsT=KCS[gp, 0, 0:NCP], rhs=QT[gp, :, qc], start=True, stop=False), reads=[tKCS, tQT], writes=[tS_])
                S.op("pe", lambda e: e.matmul(pS[0:NCP, :], lhsT=IDB[0:NCP, 0:NCP], rhs=CMPB[0:NCP, qc].unsqueeze(1).to_broadcast([NCP, 4, 128]), start=False, stop=True),
                     reads=[tK], writes=[tS_])
                pt, tpt = ptbuf()
                S.op("act", lambda e: e.activation(out=pt[0:NCP, :], in_=pS[0:NCP, :], func=AF.Exp, scale=SCALE), reads=[tS_], writes=[tpt])
                ac, tac = acc()

                def fin_cmp():
                    for h in range(4):
                        S.op("pe", lambda e, h=h: e.matmul(ac[:, h * 97:h * 97 + 64], lhsT=pt[0:NCP, h * 128:(h + 1) * 128], rhs=VCS[0:NCP, 0, 0, gp], start=True, stop=True),
                             reads=[tpt, tVCS], writes=[tac])
                        S.op("pe", lambda e, h=h: e.matmul(ac[:, h * 97 + 64:h * 97 + 65 + NSELP], lhsT=pt[0:NCP, h * 128:(h + 1) * 128], rhs=OVP[0:NCP, :], start=True, stop=True),
                             reads=[tpt, tOV], writes=[tac])
                    attn_epilogue(ac, tac, g, 0, qt, True, 97)
                    hv = ac[:, 0:388].rearrange("p (h c) -> p h c", c=97)
                    score = SC2[:, 0, 0:NSELP]
                    for h in range(4):
                        if h == 0:
                            S.op("dve", lambda e: e.tensor_scalar(out=score, in0=hv[:, 0, 65:65 + NSELP], scalar1=ST1[:, 40:41], scalar2=None, op0=ALU.mult), reads=[tac, tST1], writes=[tSC2])
                        else:
                            S.op("dve", lambda e, h=h: e.scalar_tensor_tensor(out=score, in0=hv[:, h, 65:65 + NSELP], scalar=ST1[:, 40 + h:41 + h], in1=score, op0=ALU.mult, op1=ALU.add),
                                 reads=[tac, tST1, tSC2], writes=[tSC2])
                    S.op("dve", lambda e: e.tensor_tensor(out=score, in0=score, in1=FBP[:, qt, :], op=ALU.add), reads=[tSC2, tK], writes=[tSC2])
                    select(score, NSELP, 128)
                    pB, tB = ps()
                    S.op("pe", lambda e: e.transpose(pB[0:NSELP, 0:128], score, IDF[:]), reads=[tSC2, tK], writes=[tB])
                    S.op("act", lambda e: e.activation(out=BT[0:NSELP, g, :], in_=pB[0:NSELP, 0:128], func=AF.Identity), reads=[tB], writes=[tBT])
                push(fin_cmp)

            def kv_branch(qt, g, br):
                qc = slice(qt * 128, (qt + 1) * 128)
                gp = slice(64 * g, 64 * g + 64)
                if br == 1:
                    KT_, VV, tKK, tVV, kts = KST, VS, tKST, tVS, list(range(0, qt + 1))
                else:
                    KT_, VV, tKK, tVV, kts = KWT, VW, tKWT, tVW, list(range(max(0, qt - 4), qt + 1))
                if True:
                    ac, tac = acc()
                    for ki, kt in enumerate(kts):
                        pS, tS_ = ps()
                        need_b = (kt == qt) or (br == 1) or (br == 2 and kt == qt - 4)
                        S.op("pe", lambda e, kt=kt: e.matmul(pS[:, :], lhsT=KT_[gp, kt * 128:(kt + 1) * 128], rhs=QT[gp, :, qc], start=True, stop=not need_b), reads=[tKK, tQT], writes=[tS_])
                        if kt == qt:
                            S.op("pe", lambda e: e.matmul(pS[:, :], lhsT=IDB[:], rhs=CAUS[:].unsqueeze(1).to_broadcast([128, 4, 128]), start=False, stop=True), reads=[tK], writes=[tS_])
                        elif br == 1:
                            S.op("pe", lambda e, kt=kt: e.matmul(pS[:, :], lhsT=E64[:, kt, :], rhs=BT[:, g, :].unsqueeze(1).to_broadcast([128, 4, 128]), start=False, stop=True),
                                 reads=[tK, tBT], writes=[tS_])
                        elif kt == qt - 4:
                            S.op("pe", lambda e: e.matmul(pS[:, :], lhsT=IDB[:], rhs=ANTI[:].unsqueeze(1).to_broadcast([128, 4, 128]), start=False, stop=True), reads=[tK], writes=[tS_])
                        pt, tpt = ptbuf()
                        S.op("act", lambda e: e.activation(out=pt, in_=pS[:, :], func=AF.Exp, scale=SCALE), reads=[tS_], writes=[tpt])
                        def fin_kv(ki=ki, kt=kt, pt=pt, tpt=tpt):
                            for h in range(4):
                                S.op("pe", lambda e, h=h: e.matmul(ac[:, h * 65:(h + 1) * 65], lhsT=pt[:, h * 128:(h + 1) * 128], rhs=VV[:, kt, g, :],
                                                                   start=(ki == 0 and h == 0), stop=(ki == len(kts) - 1), skip_group_check=True), reads=[tpt, tVV], writes=[tac])
                            if ki == len(kts) - 1:
                                attn_epilogue(ac, tac, g, br, qt, False, 65)
                        push(fin_kv)

            gen = sample_gen()
            n_steps = 2 * (1 + NS * ((NPG + 7) // 8)) + NS * (3 + (NPG + 3) // 4 + 1 + 3)
            wts = [2 + (qt + 1) + min(qt + 1, 5) for qt in range(NT)]
            done_steps = [0]

            def advance(target):
                while done_steps[0] < target:
                    done_steps[0] += 1
                    try:
                        next(gen)
                    except StopIteration:
                        done_steps[0] = 10 ** 9
                        return

            cum = 0
            for qt in range(NT):
                for g in range(2):
                    cmp_branch(qt, g)
                advance(int(n_steps * (cum + 0.3 * wts[qt]) / sum(wts)))
                for g in range(2):
                    kv_branch(qt, g, 2)
                advance(int(n_steps * (cum + 0.6 * wts[qt]) / sum(wts)))
                for g in range(2):
                    kv_branch(qt, g, 1)
                flush()
                cum += wts[qt]
                advance(int(n_steps * cum / sum(wts)))
                S.op("act", lambda e: e.activation(out=QN[:, 0:512], in_=OACC[:, :], func=AF.Identity), reads=[tOACC], writes=[tQN])
                pt_, tt_ = ps()
                ptv = pt_[:].bitcast(BF16)
                for k in range(4):
                    S.op("pe", lambda e, k=k: e.transpose(ptv[:, k * 128:(k + 1) * 128], QN[:, k * 128:(k + 1) * 128], IDB[:]), reads=[tQN, tK], writes=[tt_])
                S.op("dve", lambda e: e.tensor_copy(out=OT[:], in_=ptv[:, 0:512].rearrange("p (k t) -> p k t", k=4)), reads=[tt_], writes=[tOT])
                for half in range(2):
                    po, to = ps()
                    for k in range(4):
                        S.op("pe", lambda e, k=k, half=half: e.matmul(po[:, :], lhsT=OT[:, k, :], rhs=WOA[:, k, half * 512:(half + 1) * 512], start=(k == 0), stop=(k == 3)),
                             reads=[tOT, tWOA], writes=[to])
                    resid_add(qt, po, to, half, 0)


            for _ in gen:
                pass
            flush()
            S.op("act", lambda e: e.activation(out=QN[0:TS, 0:512], in_=OACS[:, :], func=AF.Identity), reads=[tOACS], writes=[tQN])
            pt_, tt_ = ps()
            ptv = pt_[:].bitcast(BF16)
            for k in range(4):
                S.op("pe", lambda e, k=k: e.transpose(ptv[:, k * 128:k * 128 + TS], QN[0:TS, k * 128:(k + 1) * 128], IDB[0:TS, 0:TS]), reads=[tQN, tK], writes=[tt_])
            S.op("dve", lambda e: e.tensor_copy(out=OT[:, :, 0:TS], in_=ptv[:, 0:512].rearrange("p (k t) -> p k t", k=4)[:, :, 0:TS]), reads=[tt_], writes=[tOT])
            for half in range(2):
                po, to = ps()
                for k in range(4):
                    S.op("pe", lambda e, k=k, half=half: e.matmul(po[0:TS, :], lhsT=OT[:, k, 0:TS], rhs=WOA[:, k, half * 512:(half + 1) * 512], start=(k == 0), stop=(k == 3)),
                         reads=[tOT, tWOA], writes=[to])
                resid_add(NT, po, to, half, 0)

            S.barrier()
            MARKS.append(("C", l, dict(S.cnt)))
            NPSR[0] = 5
            for i in range(NTT):
                norm_to_HT([i], 2, 3, H2T, tH2T, i * 128)
            colgroups = [(c0, min(512, T - c0), list(range(c0 // 128, (c0 + min(512, T - c0)) // 128))) for c0 in range(0, T, 512)] + [(T, TS, [NT])]
            NCH = FFN_H // 128
            blocks = [(j0, min(4, NCH - j0)) for j0 in range(0, NCH, 4)]
            for bi, (j0, nb) in enumerate(blocks):
                sl = bi % 2
                cast_load(WUP[sl][:, :, 0:nb * 128], w_up[l, :, j0 * 128:(j0 + nb) * 128].rearrange("(k p) c -> p k c", p=128), [tWF[sl]] + (allWA if bi < 2 else []))
                cast_load(WUP[sl][:, :, 512:512 + nb * 128], w_up[l, :, FFN_H + j0 * 128:FFN_H + (j0 + nb) * 128].rearrange("(k p) c -> p k c", p=128), [tWF[sl]])
                cast_load(WDN[sl][:, 0:nb, :], w_down[l, j0 * 128:(j0 + nb) * 128, :].rearrange("(j p) c -> p j c", p=128), [tWF[sl]])
                for (c0, ncol, tiles) in colgroups:
                    for j in range(nb):
                        pa, ta = ps()
                        for k in range(8):
                            S.op("pe", lambda e, k=k, j=j, pa=pa, c0=c0, ncol=ncol, sl=sl: e.matmul(pa[:, 0:ncol], lhsT=WUP[sl][:, k, j * 128:(j + 1) * 128], rhs=H2T[:, k, c0:c0 + ncol],
                                                                                               start=(k == 0), stop=(k == 7)), reads=[tWF[sl], tH2T], writes=[ta])
                        pb2, tb2 = ps()
                        for k in range(8):
                            S.op("pe", lambda e, k=k, j=j, pb2=pb2, c0=c0, ncol=ncol, sl=sl: e.matmul(pb2[:, 0:ncol], lhsT=WUP[sl][:, k, 512 + j * 128:512 + (j + 1) * 128], rhs=H2T[:, k, c0:c0 + ncol],
                                                                                                 start=(k == 0), stop=(k == 7)), reads=[tWF[sl], tH2T], writes=[tb2])
                        S.op("act", lambda e, pa=pa, ncol=ncol: e.activation(out=SCR[:, 0:ncol], in_=pa[:, 0:ncol], func=AF.Silu), reads=[ta], writes=[tSCR])
                        S.op("dve", lambda e, pb2=pb2, j=j, ncol=ncol: e.tensor_tensor(out=UT[:, j, 0:ncol], in0=SCR[:, 0:ncol], in1=pb2[:, 0:ncol], op=ALU.mult), reads=[tSCR, tb2], writes=[tUT[0]])
                    for ti, i in enumerate(tiles):
                        rows = 128 if i < NT else TS
                        for half in range(2):
                            po, to = ps()
                            for j in range(nb):
                                S.op("pe", lambda e, j=j, po=po, ti=ti, rows=rows, half=half, sl=sl: e.matmul(po[0:rows, :], lhsT=UT[:, j, ti * 128:ti * 128 + rows], rhs=WDN[sl][:, j, half * 512:(half + 1) * 512],
                                                                                                       start=(j == 0), stop=(j == nb - 1)), reads=[tUT[0], tWF[sl]], writes=[to])
                            resid_add(i, po, to, half, 1)

        for l in range(DEPTH):
            S.barrier()
            MARKS.append(("L", l, dict(S.cnt)))
            ld("sp", SPR[:], spar[l], [tSPR])
            pb, tp = ps()
            S.op("pe", lambda e, pb=pb: e.transpose(pb[:, 0:80], SPR[:], IDF[0:80, 0:80]), reads=[tSPR, tK], writes=[tp])
            S.op("dve", lambda e, pb=pb: e.tensor_copy(out=SPT[:], in_=pb[:, 0:80]), reads=[tp], writes=[tSPT])
            ld("sp", KGB[:].rearrange("p a c -> p (a c)"), kgbc[l].rearrange("a c -> (a c)").partition_broadcast(128), [tKGB])
            for n in range(12):
                kind, half = n // 2, n % 2
                wb = n % 4
                cast_load(WADA[:, wb], w_ada[l, :, n * 512:(n + 1) * 512].rearrange("(k p) c -> p k c", p=128), [tWADA[wb]] + ([tWIN] if False else []),
                          r=[])
                if kind in (2, 5):
                    gi = 0 if kind == 2 else 1
                    ld("sp", BST[:], b_ada[l, n * 512:(n + 1) * 512].partition_broadcast(128), [tBST])
                    for gsel, (c0, cw) in enumerate(((0, 128), (128, TS))):
                        pb, tp = ps()
                        for k in range(8):
                            S.op("pe", lambda e, k=k, pb=pb, c0=c0, cw=cw, wb=wb: e.matmul(pb[0:cw, :], lhsT=CTS[:, k, c0:c0 + cw], rhs=WADA[:, wb, k, :],
                                                                                            start=(k == 0), stop=(k == 7)), reads=[tK, tWADA[wb]], writes=[tp])
                        S.op("dve", lambda e, pb=pb, cw=cw, gi=gi, gsel=gsel, half=half: e.tensor_tensor(
                            out=GATE[0:cw, gi, gsel, half * 512:(half + 1) * 512], in0=pb[0:cw, :], in1=BST[0:cw, :], op=ALU.add),
                             reads=[tp, tBST], writes=[tGATE[gi][gsel]])
                else:
                    mk = {0: 0, 1: 1, 3: 2, 4: 3}[kind]
                    pb, tp = ps()
                    for j in range(4):
                        for k in range(8):
                            S.op("pe", lambda e, k=k, j=j, pb=pb, wb=wb: e.matmul(pb[:, j * 8:j * 8 + 1 + NS], lhsT=WADA[:, wb, k, j * 128:(j + 1) * 128], rhs=CT5[:, k, :],
                                                                                   start=(k == 0), stop=(k == 7)), reads=[tK, tWADA[wb]], writes=[tp])
                    for j in range(4):
                        c = half * 4 + j
                        bcol = SPT[:, n * 4 + j:n * 4 + j + 1]
                        if kind in (0, 3):
                            S.op("dve", lambda e, j=j, pb=pb, c=c, bcol=bcol, mk=mk: e.tensor_scalar(out=MODC[:, mk, c, :], in0=pb[:, j * 8:j * 8 + 1 + NS], scalar1=bcol, scalar2=None, op0=ALU.add),
                                 reads=[tp, tSPT], writes=[tMODC])
                        else:
                            ncol = SPT[:, 48 + c:49 + c] if kind == 1 else SPT[:, 56 + c:57 + c]
                            S.op("dve", lambda e, j=j, pb=pb, c=c, bcol=bcol, mk=mk: e.tensor_scalar(out=MODC[:, mk, c, :], in0=pb[:, j * 8:j * 8 + 1 + NS], scalar1=bcol, scalar2=1.0, op0=ALU.add, op1=ALU.add),
                                 reads=[tp, tSPT], writes=[tMODC])
                            S.op("dve", lambda e, c=c, ncol=ncol, mk=mk: e.tensor_scalar(out=MODC[:, mk, c, :], in0=MODC[:, mk, c, :], scalar1=ncol, scalar2=None, op0=ALU.mult),
                                 reads=[tMODC, tSPT], writes=[tMODC])
            LAYER_BODY(l)
        for i in range(NT):
            S.dma("sp", lambda e, i=i: e.dma_start(out=y_p[i * 128:(i + 1) * 128, :], in_=X[:, i, :]), reads=[tX[i]])
        S.dma("sp", lambda e: e.dma_start(out=y_s, in_=X[0:TS, NT, :]), reads=[tX[NT]])
        S.final_wait("sp")
        print('CNT', S.cnt, {k: v for k, v in S.dval.items() if v > 1000})

        sems = {e: es.enter_context(nc.semaphore("s_" + e)) for e in ENGS}
        for q, n in S.NDS.items():
            for i in range(n):
                sems[("d", q, i)] = es.enter_context(nc.semaphore(f"d_{q}_{i}"))

        def run(e, lst):
            for waits, fn, key, inc in lst:
                for k, v in waits:
                    e.wait_ge(sems[k], v)
                if fn is not None:
                    name, a, k = fn
                    getattr(e, name)(*a, **k).then_inc(sems[key], inc)

        with nc.Block() as block:
            @block.sync
            def _(e):
                run(e, S.ops["sp"])

            @block.scalar
            def _(e):
                run(e, S.ops["act"])

            @block.vector
            def _(e):
                run(e, S.ops["dve"])

            @block.gpsimd
            def _(e):
                run(e, S.ops["pool"])

            @block.tensor
            def _(e):
                run(e, S.ops["pe"])
    return nc


def _consts(cfg):
    T, NPG, NS, DS = cfg["T"], cfg["NPG"], cfg["NS"], cfg["DS"]
    NT = T // 128; TS = NS * DS; PAST = NPG * 128
    NCS = (PAST + DS - 32) // 16 + 1; NSELP = T // 64; NSELS = -(-(PAST + DS) // 64); NKS = (NCS + 127) // 128
    bf = ml_dtypes.bfloat16
    k = {}
    k["k_idf"] = np.eye(128, dtype=np.float32); k["k_idb"] = np.eye(128).astype(bf)
    kk, qq = np.meshgrid(np.arange(128), np.arange(128), indexing="ij")
    k["k_caus"] = np.where(kk > qq, NEGB, 0.0).astype(bf); k["k_anti"] = np.where(kk <= qq, NEGB, 0.0).astype(bf)
    c = np.arange(128)[:, None]; t = np.arange(T)[None, :]
    k["k_cmpb"] = np.where(16 * c + 31 <= t, 0.0, NEGB).astype(bf)
    e64 = np.zeros((128, 32, 128), np.float32)
    for m in range(32):
        for key in range(128):
            j = 2 * m + key // 64
            e64[j % 64, m, key] = 1.0
            e64[64 + j % 64, m, key] = 1.0
    k["k_e64"] = e64.astype(bf)
    fbp = np.zeros((128, NT, NSELP), np.float32)
    for i in range(NT):
        for p in range(128):
            cur = (i * 128 + p) // 64
            for j in range(NSELP):
                if j > cur: fbp[p, i, j] = -1e30
                elif j == 0 or j == cur or j == cur - 1: fbp[p, i, j] = 1e4
    k["k_fbp"] = fbp.astype(bf)
    fbs = np.zeros((TS, NSELS), np.float32); cur = NSELS - 1
    fbs[:, 0] = 1e4; fbs[:, cur] = 1e4; fbs[:, cur - 1] = 1e4
    k["k_fbs"] = fbs
    newb = np.full((TS, NS, 4, DS), NEGB, np.float32)
    for s in range(NS):
        for t2 in range(DS):
            for tq in range(DS):
                if t2 <= tq: newb[s * DS + t2, s, :, tq] = 0.0
    k["k_newb"] = newb.reshape(TS, NS, 4 * DS).astype(bf)
    wb0 = np.zeros((128, 4, DS), np.float32)
    for i in range(128):
        for tq in range(DS):
            if not (i > tq): wb0[i, :, tq] = NEGB
    k["k_winb0"] = wb0.reshape(128, 4 * DS).astype(bf)

    def ov(ncmp, nsel):
        m = np.zeros((ncmp, nsel), np.float32); i = np.arange(ncmp)
        for part in range(2):
            j = np.minimum((i + part) * 16 // 64, nsel - 1); np.add.at(m, (i, j), 1.0)
        return m
    NCP = (T - 32) // 16 + 1
    ovp = np.zeros((128, 1 + NSELP), np.float32); ovp[:, 0] = 1.0; ovp[:NCP, 1:] = ov(NCP, NSELP)
    k["k_ovp"] = ovp.astype(bf)
    ovs = np.zeros((NKS * 128, 1 + NSELS), np.float32); ovs[:, 0] = 1.0; ovs[:NCS, 1:] = ov(NCS, NSELS)
    k["k_ovs"] = ovs.reshape(NKS, 128, 1 + NSELS).transpose(1, 0, 2).astype(bf)
    rc = np.zeros((128, 2, 16), np.float32)
    for ch in range(2):
        for p in range(128):
            w = (2, 4, 8, 16)[ch * 2 + p // 64]
            rc[p, ch, :] = 1.0 / np.minimum(w, np.arange(16) + 1)
    k["k_rc"] = rc
    bo = np.zeros((128, 128), np.float32); bo[:64, :64] = 1; bo[64:, 64:] = 1
    k["k_bones"] = bo.astype(bf)
    sel = np.zeros((1 + NS, 128 + TS), np.float32); sel[0, :128] = 1
    for s in range(NS): sel[1 + s, 128 + s * DS:128 + (s + 1) * DS] = 1
    k["k_cts"] = sel
    return k


_NC_CACHE = {}


def kernel(x_prompt, x_sample, cache_nsa_kv, state_win_kv, state_conv, state_pool, page_table,
           c_prompt, c_sample, norm_mix, norm_ffn, w_ada, b_ada, w_in, w_out, q_norm, k_norm,
           cmp_pe, cmp_w1, cmp_w2, conv_w, conv_bias, pool_w, pool_scale, w_up, w_down, cfg=None):
    cfg = dict(CFG) if cfg is None else cfg
    T, NPG, DEPTH, NS, DS = cfg["T"], cfg["NPG"], cfg["DEPTH"], cfg["NS"], cfg["DS"]
    f = lambda a: np.ascontiguousarray(np.asarray(a))
    NB = x_prompt.shape[0]; TS = NS * DS
    key = tuple(sorted(cfg.items()))
    if key not in _NC_CACHE:
        _NC_CACHE[key] = build(cfg)
    nc = _NC_CACHE[key]
    consts = _consts(cfg)
    spar = np.zeros((DEPTH, 80, 128), np.float32)
    spar[:, 0:48] = f(b_ada).reshape(DEPTH, 48, 128)
    spar[:, 48:56] = f(norm_mix).reshape(DEPTH, 8, 128); spar[:, 56:64] = f(norm_ffn).reshape(DEPTH, 8, 128)
    spar[:, 64:70] = f(conv_w).reshape(DEPTH, 6, 128); spar[:, 70:72] = f(conv_bias).reshape(DEPTH, 2, 128)
    spar[:, 72:74] = f(pool_scale).reshape(DEPTH, 2, 128)
    spar[:, 74, 0:64] = f(q_norm); spar[:, 74, 64:128] = f(q_norm)
    spar[:, 75:78, 0:64] = f(k_norm); spar[:, 75:78, 64:128] = f(k_norm)
    kgbc = np.concatenate([f(k_norm)[:, 1:3], f(k_norm)[:, 1:3]], axis=-1).astype(np.float32)
    w1 = f(cmp_w1).reshape(DEPTH, 2, 32, 64, 64)
    w1bd = np.zeros((DEPTH, 2, 128, 32, 128), np.float32)
    w1bd[:, :, 0:64, :, 0:64] = w1.transpose(0, 1, 3, 2, 4); w1bd[:, :, 64:, :, 64:] = w1.transpose(0, 1, 3, 2, 4)
    w2bd = np.zeros((DEPTH, 2, 128, 128), np.float32)
    w2bd[:, :, :64, :64] = f(cmp_w2); w2bd[:, :, 64:, 64:] = f(cmp_w2)
    pw = f(pool_w); pwbd = np.zeros((DEPTH, 2, 128, 128), np.float32)
    pwbd[:, 0, :64, :64] = pw[:, 0]; pwbd[:, 0, 64:, 64:] = pw[:, 1]; pwbd[:, 1, :64, :64] = pw[:, 2]; pwbd[:, 1, 64:, 64:] = pw[:, 3]
    pe_t = f(cmp_pe).transpose(0, 1, 3, 2)
    pet = np.concatenate([pe_t, pe_t], axis=2).astype(np.float32)
    cache2 = f(cache_nsa_kv).reshape(DEPTH, -1, 512)
    in_maps = []
    for c in range(NB):
        sl = slice(c * NS, (c + 1) * NS)
        st_cp = np.concatenate([f(state_pool)[:, sl].reshape(DEPTH, NS * 15, 256), f(state_conv)[:, sl].reshape(DEPTH, NS * 2, 256)], axis=1)
        m = dict(x_p=f(x_prompt[c]), x_s=f(x_sample[sl]).reshape(TS, D), cache=cache2,
                 st_win=f(state_win_kv)[:, sl].reshape(DEPTH, NS, 512, 256), st_cp=st_cp,
                 ptab=f(page_table[sl]).astype(np.int32), c_all=np.concatenate([f(c_prompt)[c:c + 1], f(c_sample)[sl]], 0),
                 spar=spar, kgbc=kgbc, w_ada=f(w_ada), b_ada=f(b_ada), w_in=f(w_in), w_out=f(w_out), w1bd=w1bd, w2bd=w2bd,
                 pwbd=pwbd, pet=pet, w_up=f(w_up), w_down=f(w_down))
        m.update(consts)
        in_maps.append(m)
    res = run_bass_kernel_spmd(nc, in_maps, core_ids=list(range(NB)))
    R_ = res.results
    cat = lambda k, ax=0: np.stack([r[k] for r in R_], axis=ax)
    yp = cat("y_p"); ys = np.concatenate([r["y_s"].reshape(NS, DS, D) for r in R_], 0)
    kvp = cat("kv_p", 1).reshape(DEPTH, NB, T, 4, 2, 64)
    kvs = np.concatenate([r["kv_s"].reshape(DEPTH, NS, DS, 4, 2, 64) for r in R_], 1)
    wp = cat("win_p", 1).reshape(DEPTH, NB, -1, 2, 2, 64)
    ws = np.concatenate([r["win_s"].reshape(DEPTH, NS, 512, 2, 2, 64) for r in R_], 1)
    cp = cat("conv_p", 1); cs = np.concatenate([r["conv_s"].reshape(DEPTH, NS, 2, 256) for r in R_], 1)
    pp = cat("pool_p", 1); pls = np.concatenate([r["pool_s"].reshape(DEPTH, NS, 15, 256) for r in R_], 1)
    return (yp, ys, kvp, kvs, wp, ws, cp, cs, pp, pls)
```

```python
import numpy as np
import ml_dtypes
import concourse.bass as bass
import concourse.mybir as mybir
from concourse.bass_utils import run_bass_kernel_spmd

F32 = mybir.dt.float32
BF16 = mybir.dt.bfloat16
I32 = mybir.dt.int32
AF = mybir.ActivationFunctionType
ALU = mybir.AluOpType
AX = mybir.AxisListType

D = 1024
HD = 64
IN_W = 2328
FFN_H = 2816
EPS = 1e-6
NEGB = -30000.0
SCALE = 0.125
CFG = dict(T=2048, NPG=64, DEPTH=4, NS=4, DS=8, NPOOL=2560)


class Tok:
    __slots__ = ("w", "r", "excl", "wl")

    def __init__(self, excl=False):
        self.w = None
        self.r = {}
        self.excl = excl
        self.wl = []


ENGS = ("pe", "act", "dve", "pool", "sp")
MARKS = []


class _Rec:
    def __getattr__(self, name):
        return lambda *a, **k: (name, a, k)


_REC = _Rec()


class Sched:
    def __init__(self):
        self.ops = {e: [] for e in ENGS}
        self.cnt = {e: 0 for e in ENGS}
        self.seen = {e: {} for e in ENGS}
        self.dnext = {e: 0 for e in ENGS}
        self.dval = {}
        self.NDS = {"sp": 24, "pool": 24, "act": 4}

    def _need(self, eng, dep, waits):
        if dep is None:
            return
        k, v = dep
        if eng == "pe" and k == "pe":
            return
        if self.seen[eng].get(k, 0) >= v:
            return
        if waits.get(k, 0) < v:
            waits[k] = v

    def _deps(self, eng, reads, writes, join=False):
        waits = {}
        for t in reads:
            self._need(eng, t.w, waits)
            for d_ in t.wl:
                self._need(eng, d_, waits)
        for t in writes:
            if join and t.w is not None and isinstance(t.w[0], tuple):
                continue
            if not (t.w is not None and t.w[0] == eng):
                self._need(eng, t.w, waits)
            for d_ in t.wl:
                self._need(eng, d_, waits)
            for k, v in t.r.items():
                if k != eng:
                    self._need(eng, (k, v), waits)
        for k, v in waits.items():
            self.seen[eng][k] = v
        return list(waits.items())

    def _mark(self, me, reads, writes, join=False):
        k, v = me
        for t in reads:
            if t.r.get(k, 0) < v:
                t.r[k] = v
        for t in writes:
            if join and t.w is not None and isinstance(t.w[0], tuple):
                t.wl.append(t.w)
            else:
                t.wl = []
                t.r = {}
            t.w = me

    def op(self, eng, fn, reads=(), writes=()):
        writes = list(writes) + [t for t in reads if t.excl]
        reads = [t for t in reads if not t.excl]
        waits = self._deps(eng, reads, writes)
        self.cnt[eng] += 1
        me = (eng, self.cnt[eng])
        self.ops[eng].append((waits, fn(_REC), eng, 1))
        self._mark(me, reads, writes)

    def dma(self, q, fn, reads=(), writes=(), join=False):
        i = self.dnext[q] % self.NDS[q]
        self.dnext[q] += 1
        key = ("d", q, i)
        prev = self.dval.get(key, 0)
        waits = self._deps(q, reads, writes, join)
        if prev and self.seen[q].get(key, 0) < prev:
            waits.append((key, prev))
            self.seen[q][key] = prev
        val = prev + 16
        self.dval[key] = val
        self.ops[q].append((waits, fn(_REC), key, 16))
        self._mark((key, val), reads, writes, join)

    def barrier(self):
        snap = dict(self.cnt)
        dsn = dict(self.dval)
        for eng in ENGS:
            waits = [(k, v) for k, v in dsn.items() if self.seen[eng].get(k, 0) < v]
            for e in ENGS:
                if e != eng and snap[e] and self.seen[eng].get(e, 0) < snap[e]:
                    waits.append((e, snap[e]))
            for k, v in waits:
                self.seen[eng][k] = v
            if waits:
                self.ops[eng].append((waits, None, None, 0))

    def final_wait(self, eng):
        waits = [(k, v) for k, v in self.dval.items() if self.seen[eng].get(k, 0) < v]
        for e in ENGS:
            if e != eng and self.cnt[e]:
                waits.append((e, self.cnt[e]))
        self.ops[eng].append((waits, None, None, 0))


def build(cfg):
    T, NPG, DEPTH, NS, DS = cfg["T"], cfg["NPG"], cfg["DEPTH"], cfg["NS"], cfg["DS"]
    NPOOL = cfg["NPOOL"]
    NT = T // 128
    NTT = NT + 1
    TS = NS * DS
    TT = T + TS
    PAST = NPG * 128
    NCP = (T - 32) // 16 + 1
    NCS = (PAST + DS - 32) // 16 + 1
    NSELP = T // 64
    NSELS = -(-(PAST + DS) // 64)
    NKS = (NCS + 127) // 128
    WINT = min(512, T) // 128
    GT = 256
    GTL = GT // 128
    NG = (NT + GTL - 1) // GTL

    nc = bass.Bass("TRN2", target_bir_lowering=False)
    S = Sched()

    def din(name, shape, dt=F32):
        return nc.dram_tensor(name, list(shape), dt, kind="ExternalInput").ap()

    def dout(name, shape, dt=F32):
        return nc.dram_tensor(name, list(shape), dt, kind="ExternalOutput").ap()

    x_p = din("x_p", [T, D]); x_s = din("x_s", [TS, D])
    cache = din("cache", [DEPTH, NPOOL * 128, 512])
    cacheR = cache.rearrange("l r (q c) -> (l r q) c", c=128)
    cacheR2 = cache.rearrange("l r (h c) -> (l r h) c", h=2)
    st_win = din("st_win", [DEPTH, NS, 512, 256])
    st_cp = din("st_cp", [DEPTH, NS * 17, 256])
    ptab = din("ptab", [NS, NPG], I32)
    c_all = din("c_all", [1 + NS, D])
    spar = din("spar", [DEPTH, 80, 128])
    kgbc = din("kgbc", [DEPTH, 2, 128])
    w_ada = din("w_ada", [DEPTH, D, 6 * D]); b_ada = din("b_ada", [DEPTH, 6 * D])
    w_in = din("w_in", [DEPTH, D, IN_W]); w_out = din("w_out", [DEPTH, D, D])
    w1bd = din("w1bd", [DEPTH, 2, 128, 32, 128])
    w2bd = din("w2bd", [DEPTH, 2, 128, 128])
    pwbd = din("pwbd", [DEPTH, 2, 128, 128])
    pet = din("pet", [DEPTH, 2, 128, 32])
    w_up = din("w_up", [DEPTH, D, 2 * FFN_H]); w_down = din("w_down", [DEPTH, FFN_H, D])
    k_idf = din("k_idf", [128, 128]); k_idb = din("k_idb", [128, 128], BF16)
    k_caus = din("k_caus", [128, 128], BF16); k_anti = din("k_anti", [128, 128], BF16)
    k_cmpb = din("k_cmpb", [128, T], BF16)
    k_e64 = din("k_e64", [128, 32, 128], BF16)
    k_fbp = din("k_fbp", [128, NT, NSELP], BF16); k_fbs = din("k_fbs", [TS, NSELS])
    k_newb = din("k_newb", [TS, NS, 4 * DS], BF16); k_winb0 = din("k_winb0", [128, 4 * DS], BF16)
    k_ovp = din("k_ovp", [128, 1 + NSELP], BF16); k_ovs = din("k_ovs", [128, NKS, 1 + NSELS], BF16)
    k_rc = din("k_rc", [128, 2, 16]); k_bones = din("k_bones", [128, 128], BF16)
    k_cts = din("k_cts", [1 + NS, 128 + TS])

    y_p = dout("y_p", [T, D]); y_s = dout("y_s", [TS, D])
    kv_p = dout("kv_p", [DEPTH, T, 512]); kv_s = dout("kv_s", [DEPTH, TS, 512])
    win_p = dout("win_p", [DEPTH, WINT * 128, 256]); win_s = dout("win_s", [DEPTH, NS, 512, 256])
    conv_p = dout("conv_p", [DEPTH, 2, 256]); conv_s = dout("conv_s", [DEPTH, NS * 2, 256])
    pool_p = dout("pool_p", [DEPTH, 15, 256]); pool_s = dout("pool_s", [DEPTH, NS * 15, 256])

    import contextlib
    es = contextlib.ExitStack()
    with es:
        def sb(name, shape, dt=F32):
            return es.enter_context(nc.sbuf_tensor(name, list(shape), dt))

        X = sb("X", [128, NTT, D]); tX = [Tok() for _ in range(NTT)]
        GATE = sb("GATE", [128, 2, 2, D], BF16); tGATE = [[Tok(), Tok()], [Tok(), Tok()]]
        MODC = sb("MODC", [128, 4, 8, 1 + NS]); tMODC = Tok()
        IDF = sb("IDF", [128, 128]); IDB = sb("IDB", [128, 128], BF16)
        CAUS = sb("CAUS", [128, 128], BF16); ANTI = sb("ANTI", [128, 128], BF16)
        CMPB = sb("CMPB", [128, T], BF16); E64 = sb("E64", [128, 32, 128], BF16)
        FBP = sb("FBP", [128, NT, NSELP], BF16); FBS = sb("FBS", [TS, NSELS])
        NEWB = sb("NEWB", [128, NS, 4 * DS], BF16); WINB0 = sb("WINB0", [128, 4 * DS], BF16)
        RC16 = sb("RC16", [128, 2, 16]); BONES = sb("BONES", [128, 128], BF16)
        CTS = sb("CTS", [128, 8, 128 + TS], BF16)
        tK = Tok()
        PETB = sb("PETB", [128, 32], BF16); tPET = Tok()
        tKVP = [Tok() for _ in range(NT)]
        PH = sb("PH", [128, 4768])
        def phf(off, n):
            return PH[:, off:off + n]
        def phb(off, n):
            return PH[:, off:off + n].bitcast(BF16)
        SCR = sb("SCR", [128, 1056]); tSCR = Tok()
        SA = SCR[:].rearrange("p (a c) -> p a c", a=2); tSA = tSCR
        oA = [0]
        def takeA(n):
            o_ = oA[0]; oA[0] += n; return o_
        HT = phb(takeA(4 * GT), 4 * GT).rearrange("p (k c) -> p k c", k=8); tHT = Tok()
        _o = takeA(768); KVO = phf(_o, 768).rearrange("p (a c) -> p a c", a=1); tKVO = [Tok(), Tok()]
        OUTT = phf(_o, 256); tOUTT = tKVO[0]
        UG = phf(takeA(2 * (2 + GT)), 2 * (2 + GT)).rearrange("p (a c) -> p a c", a=2); tUG = Tok()
        PG = phf(takeA(2 * (16 + GT)), 2 * (16 + GT)).rearrange("p (a c) -> p a c", a=2)[:, :, 0:15 + GT]; tPG = Tok()
        YCP = phb(takeA(2 * GT), 2 * GT).rearrange("p (a c) -> p a c", a=4); tYCP = Tok()
        DPB = phb(takeA(GT // 2), GT // 2); tDPB = Tok()
        UGS = phf(takeA(80), 80).rearrange("p (a s c) -> p a s c", a=2, s=NS); PGS = phf(takeA(184), 184).rearrange("p (a s c) -> p a s c", a=2, s=NS); tUGS = Tok()
        JNK = phb(takeA(512), 512); tJNK = Tok()
        WST = phf(takeA(256), 256)[0:126]; tWST = Tok()
        CTF = phf(0, 1024)[0:1 + NS]; tCTF = Tok()
        PT = phb(0, 768).rearrange("p (a c) -> p a c", a=3); tPT = [Tok(), Tok(), Tok()]
        OACC = phf(768, 512); tOACC = Tok()
        OACS = phf(1280, 512)[0:TS]; tOACS = Tok()
        PTS = phb(1792, 768).rearrange("p (s i c) -> p s i c", s=NS, i=3); tPTS = [[Tok() for _ in range(3)] for _ in range(NS)]
        SC2 = phf(2560, 408).rearrange("p (a c) -> p a c", a=3); tSC2 = Tok()
        BT = phb(2968, 128).rearrange("p (a c) -> p a c", a=2); tBT = Tok()
        BTS = phb(3096, 64).rearrange("p (a g c) -> p a g c", a=2, g=2); tBTS = Tok()
        SH = phb(3160, 256); SQ = phb(3416, 256); tSH = Tok()
        OT = phb(3672, 256).rearrange("p (a c) -> p a c", a=4); tOT = Tok()
        VN0 = phb(3928, 66)[0:TS].rearrange("p (a c) -> p a c", a=2)[:, :, 0:65] if False else phb(3928, 65)[0:TS].rearrange("p (a c) -> p a c", a=2); tVN0 = Tok()
        IDXL = phf(3994, 2 * NS * NPG).bitcast(I32).rearrange("p (a c) -> p a c", a=2); tIDXL = Tok()
        IDXG = phf(4506, NS * NPG); tIDXG = Tok()
        UT = phb(0, 1024).rearrange("p (a c) -> p a c", a=4); tUT = [Tok(), Tok()]
        RSC = phf(1024, 1024).rearrange("p (a c) -> p a c", a=2); tRSC = [Tok(), Tok()]
        WA = sb("WA", [128, 24576], BF16); tWA = [Tok() for _ in range(8)]
        R = sb("R", [128, 17472], BF16)
        GSIG = sb("GSIG", [128, NTT, 24]); tGS = [Tok() for _ in range(NTT)]
        SPT = sb("SPT", [128, 80]); tSPT = Tok()
        KGB = sb("KGB", [128, 2, 128]); tKGB = Tok()
        ST1 = sb("ST1", [128, 64]); tST1 = Tok()
        QN = sb("QN", [128, 768], BF16); tQN = Tok()
        SPR = sb("SPR", [80, 128]); tSPR = Tok()
        BST = SCR[:, 0:512]; tBST = tSCR
        IDX = sb("IDX", [128, NS, NPG], I32); tIDX = Tok()
        PSB = [es.enter_context(nc.psum_tensor(f"ps{i}", [128, 512], F32)) for i in range(8)]
        tPS = [Tok(True) for _ in range(8)]
        psn = [0]
        NPSR = [5]

        def ps():
            i = psn[0] % NPSR[0]
            psn[0] += 1
            return PSB[i], tPS[i]

        WIN = WA[:, 0:8 * IN_W].rearrange("p (k c) -> p k c", k=8)
        WOCP = WA[:, 18688:18688 + 4096].rearrange("p (k c) -> p k c", k=4)
        tWIN, tWOCP = tWA[0], tWA[1]
        WOA = WA[:, 0:4096].rearrange("p (k c) -> p k c", k=4); tWOA = tWA[2]
        RAWT = WA[:, 4096:4096 + 16 * 513].rearrange("p (r m) -> p r m", r=16); tRAWT = tWA[3]
        KTC = WA[:, 4096:4096 + 512]
        STG = WA[:, 12304:12304 + 2 * 1152].rearrange("p (b c) -> p b c", b=2); tSTG = [tWA[4], tWA[5]]
        W1B = WA[:, 14608:14608 + 4096].rearrange("p (q c) -> p q c", q=32); tW1B = tWA[6]
        KCS = WA[:, 18704:18704 + (1 + NS) * 512].rearrange("p (s c) -> p s c", c=512); tKCS = tWA[7]
        VCS = WA[:, 21264:21264 + (1 + NS) * 512].rearrange("p (s k c) -> p s k c", k=4, c=128); tVCS = Tok()
        W2B = WA[:, 23824:23824 + 256].rearrange("p (a c) -> p a c", a=2); tW2B = Tok()
        PWB = WA[:, 24080:24080 + 256].rearrange("p (a c) -> p a c", a=2)
        QT = R[:, 0:4 * TT].rearrange("p (a t) -> p a t", a=4); tQT = Tok()
        o = 4 * TT
        KST = R[:, o:o + TT]; tKST = Tok(); o += TT
        KWT = R[:, o:o + TT]; tKWT = Tok(); o += TT
        VS = R[:, o:o + NTT * 130].rearrange("p (i g c) -> p i g c", g=2, c=65); tVS = Tok(); o += NTT * 130
        VW = R[:, o:o + NTT * 130].rearrange("p (i g c) -> p i g c", g=2, c=65); tVW = Tok(); o += NTT * 130
        OVP = R[:, o:o + 1 + NSELP]; o += 1 + NSELP + (1 + NSELP) % 2
        OVS = R[:, o:o + NKS * (1 + NSELS)].rearrange("p (k c) -> p k c", k=NKS); o += NKS * (1 + NSELS)
        tOV = Tok()
        assert o <= 17472, o
        H2T = R[:, 0:8 * TT].rearrange("p (k t) -> p k t", k=8); tH2T = Tok()
        WADA = R[:, 0:16384].rearrange("p (b k c) -> p b k c", b=4, k=8); tWADA = [Tok() for _ in range(4)]
        WUP = [WA[:, b * 12288:b * 12288 + 8192].rearrange("p (k c) -> p k c", k=8) for b in range(2)]
        WDN = [WA[:, b * 12288 + 8192:b * 12288 + 12288].rearrange("p (j c) -> p j c", j=4) for b in range(2)]
        tWF = [tWA[0], tWA[1]]

        allR = [tQT, tKST, tKWT, tVS, tVW, tOV, tH2T] + tWADA
        allWA = tWA + [tVCS, tW2B]

        def V(e):
            return e

        def ld(q, out, in_, w, r=()):
            S.dma(q, lambda e: e.dma_start(out=out, in_=in_), reads=list(r), writes=list(w))

        ld("sp", IDF[:], k_idf, [tK]); ld("sp", IDB[:], k_idb, [tK]); ld("sp", CAUS[:], k_caus, [tK])
        ld("sp", ANTI[:], k_anti, [tK]); ld("sp", CMPB[:], k_cmpb, [tK]); ld("sp", E64[:], k_e64, [tK])
        ld("sp", FBP[:], k_fbp, [tK]); ld("sp", FBS[:], k_fbs, [tK]); pass
        ld("sp", WINB0[:], k_winb0, [tK]); ld("sp", RC16[:], k_rc, [tK]); ld("sp", BONES[:], k_bones, [tK])
        for i in range(NT):
            ld("sp", X[:, i, :], x_p[i * 128:(i + 1) * 128, :], [tX[i]])
        ld("sp", X[0:TS, NT, :], x_s, [tX[NT]])
        ld("sp", CTF[:], c_all, [tCTF])
        S.dma("pool", lambda e: e.dma_start(out=IDX[:].rearrange("p s j -> p (s j)"),
                                            in_=ptab.rearrange("s j -> (s j)").partition_broadcast(128)), writes=[tIDX])
        PIO = sb("PIO", [128, 2], I32); IDXF = sb("IDXF", [128, NS * NPG])
        S.op("pool", lambda e: e.iota(PIO[:, 0:1], [[0, 1]], base=0, channel_multiplier=1), writes=[tSCR])
        S.op("dve", lambda e: e.tensor_copy(out=SCR[:, 0:1], in_=PIO[:, 0:1]), reads=[tSCR], writes=[tSCR])
        S.op("dve", lambda e: e.tensor_copy(out=IDXF[:], in_=IDX[:].rearrange("p s j -> p (s j)")), reads=[tIDX], writes=[tIDX])
        S.op("dve", lambda e: e.tensor_scalar(out=IDXF[:], in0=IDXF[:], scalar1=128.0, scalar2=SCR[:, 0:1], op0=ALU.mult, op1=ALU.add),
             reads=[tIDX, tSCR], writes=[tIDX])
        S.op("dve", lambda e: e.tensor_copy(out=IDX[:].rearrange("p s j -> p (s j)"), in_=IDXF[:]), reads=[tIDX], writes=[tIDX])
        S.op("pool", lambda e: e.memset(NEWB[:], 0.0), writes=[tK])
        ld("sp", NEWB[0:TS], k_newb, [tK])
        S.op("act", lambda e: e.activation(out=CTF[:], in_=CTF[:], func=AF.Silu), reads=[tCTF], writes=[tCTF])
        SEL = sb("SEL", [1 + NS, 128 + TS]); tSEL = Tok()
        ld("sp", SEL[:], k_cts, [tSEL])
        for k in range(8):
            pb, tp = ps()
            S.op("pe", lambda e, k=k, pb=pb: e.matmul(pb[:, 0:128 + TS], lhsT=CTF[:, k * 128:(k + 1) * 128], rhs=SEL[:],
                                                      start=True, stop=True), reads=[tCTF, tSEL], writes=[tp])
            S.op("dve", lambda e, k=k, pb=pb: e.tensor_copy(out=CTS[:, k, :], in_=pb[:, 0:128 + TS]), reads=[tp], writes=[tK])
        CT5 = sb("CT5", [128, 8, 1 + NS], BF16)
        S.op("dve", lambda e: e.tensor_copy(out=CT5[:, :, 0:1], in_=CTS[:, :, 0:1]), reads=[tK], writes=[tK])
        S.op("dve", lambda e: e.tensor_copy(out=CT5[:, :, 1:1 + NS], in_=CTS[:, :, 128:128 + TS:DS]), reads=[tK], writes=[tK])

        def rstd_from_ss(ss_ap, n, inv, tss):
            S.op("dve", lambda e: e.tensor_scalar(out=ss_ap, in0=ss_ap, scalar1=inv, scalar2=EPS, op0=ALU.mult, op1=ALU.add),
                 reads=[tss], writes=[tss])
            S.op("act", lambda e: e.activation(out=ss_ap, in_=ss_ap, func=AF.Sqrt), reads=[tss], writes=[tss])
            S.op("dve", lambda e: e.reciprocal(out=ss_ap, in_=ss_ap), reads=[tss], writes=[tss])

        def norm_to_HT(tiles, modS, modG, dest, tdest, doff):
            for j, i in enumerate(tiles):
                rows = 128 if i < NT else TS
                S.op("act", lambda e, i=i, rows=rows: e.activation(out=JNK[0:rows, :], in_=X[0:rows, i, :], func=AF.Square,
                                                                   accum_out=ST1[0:rows, 0:1]), reads=[tX[i]], writes=[tJNK, tST1])
                rstd_from_ss(ST1[0:rows, 0:1], 1, 1.0 / D, tST1)
                S.op("dve", lambda e, i=i, rows=rows: e.tensor_scalar(out=SCR[0:rows, 0:1024], in0=X[0:rows, i, :], scalar1=ST1[0:rows, 0:1],
                                                                      scalar2=None, op0=ALU.mult), reads=[tX[i], tST1], writes=[tSCR])
                for hf in range(2):
                    pb, tp = ps()
                    for kk in range(4):
                        k = hf * 4 + kk
                        S.op("pe", lambda e, k=k, kk=kk, pb=pb, rows=rows: e.transpose(pb[:, kk * 128:kk * 128 + rows], SCR[0:rows, k * 128:(k + 1) * 128],
                                                                                       IDF[0:rows, 0:rows]), reads=[tSCR, tK], writes=[tp])
                    for kk in range(4):
                        k = hf * 4 + kk
                        if i < NT:
                            eng = "act" if kk % 2 == 0 else "dve"
                            if eng == "act":
                                S.op("act", lambda e, k=k, kk=kk, pb=pb, j=j: e.activation(out=dest[:, k, doff + j * 128:doff + (j + 1) * 128], in_=pb[:, kk * 128:(kk + 1) * 128],
                                                                                           func=AF.Identity, scale=MODC[:, modG, k, 0:1], bias=MODC[:, modS, k, 0:1]),
                                     reads=[tp, tMODC], writes=[tdest])
                            else:
                                S.op("dve", lambda e, k=k, kk=kk, pb=pb, j=j: e.tensor_scalar(out=dest[:, k, doff + j * 128:doff + (j + 1) * 128], in0=pb[:, kk * 128:(kk + 1) * 128],
                                                                                              scalar1=MODC[:, modG, k, 0:1], scalar2=MODC[:, modS, k, 0:1], op0=ALU.mult, op1=ALU.add),
                                     reads=[tp, tMODC], writes=[tdest])
                        else:
                            for s in range(NS):
                                S.op("dve", lambda e, k=k, kk=kk, pb=pb, j=j, s=s: e.tensor_scalar(
                                    out=dest[:, k, doff + j * 128 + s * DS:doff + j * 128 + (s + 1) * DS], in0=pb[:, kk * 128 + s * DS:kk * 128 + (s + 1) * DS],
                                    scalar1=MODC[:, modG, k, 1 + s:2 + s], scalar2=MODC[:, modS, k, 1 + s:2 + s], op0=ALU.mult, op1=ALU.add),
                                     reads=[tp, tMODC], writes=[tdest])

        def resid_add(i, pb, tp, half, gi):
            rows = 128 if i < NT else TS
            gsel = 0 if i < NT else 1
            S.op("dve", lambda e: e.tensor_tensor(out=SCR[0:rows, 0:512], in0=pb[0:rows, :], in1=GATE[0:rows, gi, gsel, half * 512:(half + 1) * 512], op=ALU.mult),
                 reads=[tp, tGATE[gi][gsel]], writes=[tSCR])
            S.op("dve", lambda e: e.tensor_tensor(out=X[0:rows, i, half * 512:(half + 1) * 512], in0=X[0:rows, i, half * 512:(half + 1) * 512], in1=SCR[0:rows, 0:512], op=ALU.add),
                 reads=[tSCR, tX[i]], writes=[tX[i]])

        def cast_load(out, in_, w, r=()):
            S.dma("pool", lambda e: e.dma_start(out=out, in_=in_), reads=list(r), writes=list(w))

        def LAYER_BODY(l):
            S.barrier()
            MARKS.append(("A", l, dict(S.cnt)))
            for hh in range(2):
                cast_load(WIN[:, :, hh * 1164:(hh + 1) * 1164], w_in[l, :, hh * 1164:(hh + 1) * 1164].rearrange("(k p) c -> p k c", p=128), [tWIN] + allWA)
            cast_load(WOCP, w_out[l, 512:1024, :].rearrange("(k p) c -> p k c", p=128), [tWOCP])
            cast_load(PWB, pwbd[l].rearrange("a p c -> p a c"), [tW2B])
            S.op("pool", lambda e: e.memset(VS[:, :, :, 64:65], 1.0), writes=[tVS] + tWADA)
            S.op("pool", lambda e: e.memset(VW[:, :, :, 64:65], 1.0), writes=[tVW])
            ld("sp", OVP, k_ovp, [tOV]); ld("sp", OVS, k_ovs, [tOV])
            groups = [list(range(g * GTL, min(NT, g * GTL + GTL))) for g in range(NG)] + [[NT]]
            for tiles in groups:
                samp = tiles[0] == NT
                ncol = TS if samp else 128 * len(tiles)
                norm_to_HT(tiles, 0, 1, HT, tHT, 0)
                for j, i in enumerate(tiles):
                    rows = TS if samp else 128
                    c0 = j * 128
                    pq, tq = ps()
                    for k in range(8):
                        S.op("pe", lambda e, k=k, pq=pq, c0=c0, rows=rows: e.matmul(pq[0:rows, :], lhsT=HT[:, k, c0:c0 + rows], rhs=WIN[:, k, 0:512],
                                                                                  start=(k == 0), stop=(k == 7)), reads=[tHT, tWIN], writes=[tq])
                    pk, tk = ps()
                    for k in range(8):
                        S.op("pe", lambda e, k=k, pk=pk, c0=c0, rows=rows: e.matmul(pk[0:rows, :], lhsT=HT[:, k, c0:c0 + rows], rhs=WIN[:, k, 512:1024],
                                                                                  start=(k == 0), stop=(k == 7)), reads=[tHT, tWIN], writes=[tk])
                    pw, tw = ps()
                    for k in range(8):
                        S.op("pe", lambda e, k=k, pw=pw, c0=c0, rows=rows: e.matmul(pw[0:rows, 0:280], lhsT=HT[:, k, c0:c0 + rows], rhs=WIN[:, k, 1024:1304],
                                                                                  start=(k == 0), stop=(k == 7)), reads=[tHT, tWIN], writes=[tw])
                    kb = 0
                    KV = KVO[:, kb, :]
                    S.op("act", lambda e, pk=pk, KV=KV, rows=rows: e.activation(out=KV[0:rows, 0:256], in_=pk[0:rows, 0:256], func=AF.Identity), reads=[tk], writes=[tKVO[kb]])
                    S.op("act", lambda e, pk=pk, KV=KV, rows=rows: e.activation(out=KV[0:rows, 384:512], in_=pk[0:rows, 384:512], func=AF.Identity), reads=[tk], writes=[tKVO[kb]])
                    S.op("act", lambda e, pw=pw, KV=KV, rows=rows: e.activation(out=KV[0:rows, 640:768], in_=pw[0:rows, 128:256], func=AF.Identity), reads=[tw], writes=[tKVO[kb]])
                    S.op("act", lambda e, pq=pq, rows=rows: e.activation(out=SCR[0:rows, 0:512], in_=pq[0:rows, :], func=AF.Square), reads=[tq], writes=[tSCR])
                    S.op("act", lambda e, pk=pk, rows=rows: e.activation(out=SCR[0:rows, 512:640], in_=pk[0:rows, 256:384], func=AF.Square), reads=[tk], writes=[tSCR])
                    S.op("act", lambda e, pw=pw, rows=rows: e.activation(out=SCR[0:rows, 640:768], in_=pw[0:rows, 0:128], func=AF.Square), reads=[tw], writes=[tSCR])
                    S.op("dve", lambda e, rows=rows: e.tensor_reduce(out=ST1[0:rows, 0:12], in_=SCR[0:rows, 0:768].rearrange("p (h d) -> p h d", d=64), axis=AX.X, op=ALU.add),
                         reads=[tSCR], writes=[tST1])
                    rstd_from_ss(ST1[0:rows, 0:12], 12, 1.0 / 64, tST1)
                    S.op("dve", lambda e, pq=pq, rows=rows: e.tensor_tensor(
                        out=QN[0:rows, 0:512].rearrange("t (p a d) -> t a p d", a=2, p=4), in0=pq[0:rows, :].rearrange("t (a p d) -> t a p d", a=2, p=4),
                        in1=ST1[0:rows, 0:8].rearrange("t (a p) -> t a p", a=2).unsqueeze(3).to_broadcast([rows, 2, 4, 64]), op=ALU.mult), reads=[tq, tST1], writes=[tQN])
                    for (src, so, col, do, gi2) in ((pk, 256, 8, 256, 0), (pw, 0, 10, 512, 1)):
                        tsrc = tk if src is pk else tw
                        S.op("dve", lambda e, src=src, so=so, col=col, do=do, KV=KV, rows=rows: e.tensor_tensor(
                            out=KV[0:rows, do:do + 128].rearrange("t (g d) -> t g d", g=2), in0=src[0:rows, so:so + 128].rearrange("t (g d) -> t g d", g=2),
                            in1=ST1[0:rows, col:col + 2].unsqueeze(2).to_broadcast([rows, 2, 64]), op=ALU.mult), reads=[tsrc, tST1], writes=[tKVO[kb]])
                        S.op("dve", lambda e, do=do, gi2=gi2, KV=KV, rows=rows: e.tensor_tensor(out=KV[0:rows, do:do + 128], in0=KV[0:rows, do:do + 128], in1=KGB[0:rows, gi2, :], op=ALU.mult),
                             reads=[tKGB], writes=[tKVO[kb]])
                    if samp:
                        S.dma("sp", lambda e, KV=KV: e.dma_start(out=kv_s[l], in_=KV[0:TS, 0:512]), reads=[tKVO[kb]])
                        for s in range(NS):
                            S.dma("sp", lambda e, KV=KV, s=s: e.dma_start(out=win_s[l, s, 512 - DS:512, :], in_=KV[s * DS:(s + 1) * DS, 512:768]), reads=[tKVO[kb]])
                            for a_ in range(4):
                                S.dma("sp", lambda e, s=s, a_=a_: e.dma_start(out=WST[:, :], in_=st_win[l, s, DS:512, :].rearrange("(p a) c -> p a c", a=4)[:, a_, :]), writes=[tWST])
                                S.dma("sp", lambda e, s=s, a_=a_: e.dma_start(out=win_s[l, s, 0:512 - DS, :].rearrange("(p a) c -> p a c", a=4)[:, a_, :], in_=WST[:, :]), reads=[tWST])
                    else:
                        S.dma("sp", lambda e, KV=KV, i=i: e.dma_start(out=kv_p[l, i * 128:(i + 1) * 128, :], in_=KV[:, 0:512]), reads=[tKVO[kb]], writes=[tKVP[i]])
                        if i >= NT - WINT:
                            S.dma("sp", lambda e, KV=KV, i=i: e.dma_start(out=win_p[l, (i - NT + WINT) * 128:(i - NT + WINT + 1) * 128, :], in_=KV[:, 512:768]), reads=[tKVO[kb]])
                    S.op("act", lambda e, pw=pw, i=i, rows=rows: e.activation(out=GSIG[0:rows, i, :], in_=pw[0:rows, 256:280], func=AF.Exp, scale=-1.0), reads=[tw], writes=[tGS[i]])
                    S.op("dve", lambda e, i=i, rows=rows: e.tensor_scalar(out=GSIG[0:rows, i, :], in0=GSIG[0:rows, i, :], scalar1=1.0, scalar2=None, op0=ALU.add), reads=[tGS[i]], writes=[tGS[i]])
                    S.op("dve", lambda e, i=i, rows=rows: e.reciprocal(out=GSIG[0:rows, i, :], in_=GSIG[0:rows, i, :]), reads=[tGS[i]], writes=[tGS[i]])
                    S.op("act", lambda e, KV=KV, i=i, rows=rows: e.activation(out=VS[0:rows, i, :, 0:64], in_=KV[0:rows, 384:512].rearrange("t (g d) -> t g d", g=2), func=AF.Identity),
                         reads=[tKVO[kb]], writes=[tVS])
                    S.op("act", lambda e, KV=KV, i=i, rows=rows: e.activation(out=VW[0:rows, i, :, 0:64], in_=KV[0:rows, 640:768].rearrange("t (g d) -> t g d", g=2), func=AF.Identity),
                         reads=[tKVO[kb]], writes=[tVW])
                    S.op("act", lambda e, KV=KV, rows=rows: e.activation(out=QN[0:rows, 512:640], in_=KV[0:rows, 256:384], func=AF.Identity), reads=[tKVO[kb]], writes=[tQN])
                    S.op("act", lambda e, KV=KV, rows=rows: e.activation(out=QN[0:rows, 640:768], in_=KV[0:rows, 512:640], func=AF.Identity), reads=[tKVO[kb]], writes=[tQN])
                    pt_, tt_ = ps()
                    ptv = pt_[:].bitcast(BF16)
                    for b6 in range(6):
                        S.op("pe", lambda e, b6=b6, ptv=ptv, rows=rows: e.transpose(ptv[:, b6 * 128:b6 * 128 + rows], QN[0:rows, b6 * 128:(b6 + 1) * 128], IDB[0:rows, 0:rows]),
                             reads=[tQN, tK], writes=[tt_])
                    tc0 = i * 128
                    S.op("dve", lambda e, ptv=ptv, tc0=tc0, rows=rows: e.tensor_scalar(out=QT[:, :, tc0:tc0 + rows], in0=ptv[:, 0:512].rearrange("p (a t) -> p a t", a=4)[:, :, 0:rows],
                                                                                       scalar1=SPT[:, 74:75], scalar2=None, op0=ALU.mult), reads=[tt_, tSPT], writes=[tQT])
                    S.op("act", lambda e, ptv=ptv, tc0=tc0, rows=rows: e.activation(out=KST[:, tc0:tc0 + rows], in_=ptv[:, 512:512 + rows], func=AF.Identity), reads=[tt_], writes=[tKST])
                    S.op("act", lambda e, ptv=ptv, tc0=tc0, rows=rows: e.activation(out=KWT[:, tc0:tc0 + rows], in_=ptv[:, 640:640 + rows], func=AF.Identity), reads=[tt_], writes=[tKWT])


                n = ncol
                if samp:
                    S.dma("sp", lambda e: e.dma_start(out=OUTT[0:NS * 17, :], in_=st_cp[l]), writes=[tOUTT])
                    for c in range(2):
                        ph, th = ps()
                        S.op("pe", lambda e, c=c, ph=ph: e.transpose(ph[:, 0:NS * 17], OUTT[0:NS * 17, c * 128:(c + 1) * 128], IDF[0:NS * 17, 0:NS * 17]), reads=[tOUTT, tK], writes=[th])
                        S.op("dve", lambda e, c=c, ph=ph: e.tensor_copy(out=PGS[:, c, :, 0:15], in_=ph[:, 0:NS * 15].rearrange("p (s x) -> p s x", s=NS)), reads=[th], writes=[tUGS])
                        S.op("dve", lambda e, c=c, ph=ph: e.tensor_copy(out=UGS[:, c, :, 0:2], in_=ph[:, NS * 15:NS * 17].rearrange("p (s x) -> p s x", s=NS)), reads=[th], writes=[tUGS])
                    Uv = lambda c, a, b: UGS[:, c, :, a:b]
                    Pv = lambda c, a, b: PGS[:, c, :, a:b]
                    Sv = lambda sl_, a, b: SA[:, sl_, 0:NS * 23].rearrange("p (s x) -> p s x", s=NS)[:, :, a:b]
                    psv = lambda pb_: pb_[:, 0:TS].rearrange("p (s t) -> p s t", s=NS)
                    Yv = lambda k_: YCP[:, k_, 0:TS].rearrange("p (s t) -> p s t", s=NS)
                    Dv = lambda: DPB[:, 0:TS].rearrange("p (s t) -> p s t", s=NS)
                    nn = DS; tU = tUGS; tP = tUGS
                else:
                    Uv = lambda c, a, b: UG[:, c, a:b]
                    Pv = lambda c, a, b: PG[:, c, a:b]
                    Sv = lambda sl_, a, b: SA[:, sl_, a:b]
                    psv = lambda pb_: pb_[:, 0:n]
                    Yv = lambda k_: YCP[:, k_, 0:n]
                    Dv = lambda: DPB[:, 0:n]
                    nn = n; tU = tUG; tP = tPG
                    if tiles[0] == 0:
                        S.op("pool", lambda e: e.memset(UG[:, :, 0:2], 0.0), writes=[tUG])
                        S.op("pool", lambda e: e.memset(PG[:, :, 0:15], 0.0), writes=[tPG])

                def zT(cc):
                    pz, tz = ps()
                    for k in range(8):
                        S.op("pe", lambda e, k=k, pz=pz, cc=cc: e.matmul(pz[:, 0:n], lhsT=WIN[:, k, 1304 + cc * 128:1304 + (cc + 1) * 128], rhs=HT[:, k, 0:n],
                                                                       start=(k == 0), stop=(k == 7)), reads=[tWIN, tHT], writes=[tz])
                    return pz, tz

                hp = lambda ap, lo, hi: ap[lo:hi]
                for c in range(2):
                    pcg, tcg = zT(2 + c)
                    phn, thn = zT(4 + c)
                    S.op("act", lambda e, pcg=pcg: e.activation(out=Sv(0, 0, nn), in_=psv(pcg), func=AF.Identity), reads=[tcg], writes=[tSA])
                    S.op("dve", lambda e, phn=phn, c=c: e.tensor_tensor(out=Uv(c, 2, 2 + nn), in0=Sv(0, 0, nn), in1=psv(phn), op=ALU.mult), reads=[tSA, thn], writes=[tU])
                    S.op("dve", lambda e, c=c: e.tensor_scalar(out=Sv(1, 0, nn), in0=Uv(c, 2, 2 + nn), scalar1=SPT[:, 68 + c:69 + c], scalar2=SPT[:, 70 + c:71 + c], op0=ALU.mult, op1=ALU.add),
                         reads=[tU, tSPT], writes=[tSA])
                    S.op("dve", lambda e, c=c: e.scalar_tensor_tensor(out=Sv(1, 0, nn), in0=Uv(c, 1, 1 + nn), scalar=SPT[:, 66 + c:67 + c], in1=Sv(1, 0, nn), op0=ALU.mult, op1=ALU.add),
                         reads=[tU, tSPT, tSA], writes=[tSA])
                    S.op("dve", lambda e, c=c: e.scalar_tensor_tensor(out=Sv(1, 0, nn), in0=Uv(c, 0, nn), scalar=SPT[:, 64 + c:65 + c], in1=Sv(1, 0, nn), op0=ALU.mult, op1=ALU.add),
                         reads=[tU, tSPT, tSA], writes=[tSA])
                    pbg, tbg = zT(c)
                    S.op("dve", lambda e, pbg=pbg, c=c: e.tensor_tensor(out=Yv(c), in0=Sv(1, 0, nn), in1=psv(pbg), op=ALU.mult), reads=[tSA, tbg], writes=[tYCP])
                    ppi, tpi = zT(6 + c)
                    S.op("act", lambda e, ppi=ppi, c=c: e.activation(out=Pv(c, 15, 15 + nn), in_=psv(ppi), func=AF.Identity), reads=[tpi], writes=[tP])
                    L = 15 + nn
                    S.op("dve", lambda e, c=c: e.tensor_tensor(out=Sv(0, 1, L), in0=Pv(c, 1, L), in1=Pv(c, 0, L - 1), op=ALU.add), reads=[tP], writes=[tSA])
                    S.op("dve", lambda e, c=c: e.tensor_tensor(out=Sv(1, 3, L), in0=Sv(0, 3, L), in1=Sv(0, 1, L - 2), op=ALU.add), reads=[tSA], writes=[tSA])
                    if c == 0:
                        lo_s, lo_w, hi_s, hi_w = 0, 2.0, 1, 4.0
                    else:
                        S.op("dve", lambda e: e.tensor_tensor(out=Sv(0, 7, L), in0=Sv(1, 7, L), in1=Sv(1, 3, L - 4), op=ALU.add), reads=[tSA], writes=[tSA])
                        S.op("dve", lambda e: e.tensor_tensor(out=Sv(1, 15, L), in0=Sv(0, 15, L), in1=Sv(0, 7, L - 8), op=ALU.add), reads=[tSA], writes=[tSA])
                        lo_s, lo_w, hi_s, hi_w = 0, 8.0, 1, 16.0
                    for (plo, phi, ssl, ww) in ((0, 64, lo_s, lo_w), (64, 128, hi_s, hi_w)):
                        S.op("dve", lambda e, plo=plo, phi=phi, ssl=ssl, ww=ww, c=c: e.scalar_tensor_tensor(out=Dv()[plo:phi], in0=Sv(ssl, 15, L)[plo:phi], scalar=1.0 / ww, in1=Pv(c, 15, L)[plo:phi],
                                                                                                       op0=ALU.mult, op1=ALU.subtract), reads=[tSA, tP], writes=[tDPB])
                        if (not samp) and tiles[0] == 0:
                            S.op("dve", lambda e, plo=plo, phi=phi, ssl=ssl, c=c: e.tensor_tensor(out=ST1[plo:phi, 16:32], in0=SA[plo:phi, ssl, 15:31], in1=RC16[plo:phi, c, :], op=ALU.mult),
                                 reads=[tSA, tK], writes=[tST1])
                            S.op("dve", lambda e, plo=plo, phi=phi, c=c: e.tensor_tensor(out=DPB[plo:phi, 0:16], in0=ST1[plo:phi, 16:32], in1=PG[plo:phi, c, 15:31], op=ALU.subtract),
                                 reads=[tST1, tP], writes=[tDPB])
                    py, ty = ps()
                    S.op("pe", lambda e, py=py, c=c: e.matmul(py[:, 0:n], lhsT=PWB[:, c, :], rhs=DPB[:, 0:n], start=True, stop=True), reads=[tDPB, tW2B], writes=[ty])
                    S.op("dve", lambda e, py=py, c=c: e.tensor_scalar(out=YCP[:, 2 + c, 0:n], in0=py[:, 0:n], scalar1=SPT[:, 72 + c:73 + c], scalar2=None, op0=ALU.mult), reads=[ty, tSPT], writes=[tYCP])
                    last = samp or tiles[-1] == NT - 1
                    if not samp and not last:
                        S.op("pool", lambda e, c=c: e.tensor_copy(out=UG[:, c, 0:2], in_=UG[:, c, nn:nn + 2]), reads=[tUG], writes=[tUG])
                        S.op("pool", lambda e, c=c: e.tensor_copy(out=PG[:, c, 0:15], in_=PG[:, c, nn:nn + 15]), reads=[tPG], writes=[tPG])
                    if last:
                        nq = NS if samp else 1
                        pst, tst = ps()
                        if samp:
                            S.op("dve", lambda e, c=c: e.tensor_copy(out=SA[:, 0, 0:NS * 15].rearrange("p (s x) -> p s x", s=NS), in_=PGS[:, c, :, DS:DS + 15]), reads=[tUGS], writes=[tSA])
                            S.op("dve", lambda e, c=c: e.tensor_copy(out=SA[:, 0, NS * 15:NS * 17].rearrange("p (s x) -> p s x", s=NS), in_=UGS[:, c, :, DS:DS + 2]), reads=[tUGS], writes=[tSA])
                        else:
                            S.op("dve", lambda e, c=c: e.tensor_copy(out=SA[:, 0, 0:15], in_=PG[:, c, nn:nn + 15]), reads=[tPG], writes=[tSA])
                            S.op("dve", lambda e, c=c: e.tensor_copy(out=SA[:, 0, 15:17], in_=UG[:, c, nn:nn + 2]), reads=[tUG], writes=[tSA])
                        S.op("pe", lambda e, pst=pst, nq=nq: e.transpose(pst[0:nq * 17, 0:128], SA[:, 0, 0:nq * 17], IDF[:]), reads=[tSA, tK], writes=[tst])
                        S.op("dve", lambda e, pst=pst, nq=nq, c=c: e.tensor_copy(out=OUTT[0:nq * 17, c * 128:(c + 1) * 128], in_=pst[0:nq * 17, 0:128]), reads=[tst], writes=[tOUTT])
                if samp or tiles[-1] == NT - 1:
                    nq = NS if samp else 1
                    S.dma("sp", lambda e, nq=nq: e.dma_start(out=(pool_s if samp else pool_p)[l], in_=OUTT[0:nq * 15, :]), reads=[tOUTT])
                    S.dma("sp", lambda e, nq=nq: e.dma_start(out=(conv_s if samp else conv_p)[l], in_=OUTT[nq * 15:nq * 17, :]), reads=[tOUTT])
                for j, i in enumerate(tiles):
                    rows = TS if samp else 128
                    for half in range(2):
                        po, to = ps()
                        for k in range(4):
                            S.op("pe", lambda e, k=k, po=po, j=j, rows=rows, half=half: e.matmul(po[0:rows, :], lhsT=YCP[:, k, j * 128:j * 128 + rows], rhs=WOCP[:, k, half * 512:(half + 1) * 512],
                                                                                               start=(k == 0), stop=(k == 3)), reads=[tYCP, tWOCP], writes=[to])
                        resid_add(i, po, to, half, 0)


            ACC = [(PSB[3], tPS[3]), (PSB[4], tPS[4])]
            SACC = [(PSB[5], tPS[5]), (PSB[6], tPS[6]), (PSB[7], tPS[7])]
            accn = [0]
            saccn = [0]
            NPSR[0] = 3

            def acc():
                a = ACC[accn[0] % 2]
                accn[0] += 1
                return a

            def sacc():
                a = SACC[saccn[0] % 3]
                saccn[0] += 1
                return a
            PEND = [None]

            def push(fn):
                old = PEND[0]
                PEND[0] = fn
                if old is not None:
                    old()

            def flush():
                old = PEND[0]
                PEND[0] = None
                if old is not None:
                    old()
            ptn = [0]

            def ptbuf():
                i_ = ptn[0] % 3
                ptn[0] += 1
                return PT[:, i_, :], tPT[i_]

            S.barrier()
            MARKS.append(("B", l, dict(S.cnt)))
            S.op("pool", lambda e: e.memset(PTS[:], 0.0), writes=[t for r_ in tPTS for t in r_])
            S.op("pool", lambda e: e.memset(BTS[:], 0.0), writes=[tBTS])
            S.op("pool", lambda e: e.memset(BT[:], 0.0), writes=[tBT])
            S.op("pool", lambda e: e.memset(VN0[:, :, 0:1], 1.0), writes=[tVN0])
            cast_load(WOA, w_out[l, 0:512, :].rearrange("(k p) c -> p k c", p=128), [tWOA] + allWA)
            cast_load(W2B, w2bd[l].rearrange("a p c -> p a c"), [tW2B])

            def compress(slot, seq, nblk, loader, npages):
                for pg0 in range(0, npages, 8):
                    npc = min(8, npages - pg0)
                    b = (pg0 // 8) % 2
                    stg = STG[:, b, 0:1024].rearrange("p (j c) -> p j c", c=128)
                    loader(stg, tSTG[b], pg0, npc)
                    pt_, tt_ = ps()
                    ptv = pt_[:].bitcast(BF16)
                    for j in range(npc):
                        S.op("pe", lambda e, j=j: e.transpose(ptv[:, j * 128:(j + 1) * 128], stg[:, j, :], IDB[:]), reads=[tSTG[b], tK], writes=[tt_])
                    S.op("dve", lambda e: e.tensor_copy(out=RAWT[:, :, pg0 * 8:(pg0 + npc) * 8], in_=ptv[:, 0:npc * 128].rearrange("p (m r) -> p r m", r=16)),
                         reads=[tt_], writes=[tRAWT])
                    yield
                ph, th = ps()
                for p in range(32):
                    S.op("pe", lambda e, p=p: e.matmul(ph[:, 0:nblk], lhsT=W1B[:, p, :], rhs=RAWT[:, p % 16, p // 16:p // 16 + nblk], start=(p == 0), stop=(p == 31)),
                         reads=[tW1B, tRAWT], writes=[th])
                S.op("act", lambda e: e.activation(out=SH[:, 0:nblk], in_=ph[:, 0:nblk], func=AF.Silu, bias=ST1[:, 32:33]), reads=[th, tST1], writes=[tSH])
                if slot == 0:
                    p2, t2 = ps()
                    S.op("pe", lambda e: e.matmul(p2[:, 0:nblk], lhsT=W2B[:, 0, :], rhs=SH[:, 0:nblk], start=True, stop=True), reads=[tSH, tW2B], writes=[t2])
                    S.op("act", lambda e: e.activation(out=SQ[:, 0:nblk], in_=p2[:, 0:nblk], func=AF.Square), reads=[t2], writes=[tSH])
                    p3, t3 = ps()
                    S.op("pe", lambda e: e.matmul(p3[:, 0:nblk], lhsT=BONES[:], rhs=SQ[:, 0:nblk], start=True, stop=True), reads=[tSH, tK], writes=[t3])
                    S.op("dve", lambda e: e.tensor_scalar(out=SCR[:, 0:nblk], in0=p3[:, 0:nblk], scalar1=1.0 / 64, scalar2=EPS, op0=ALU.mult, op1=ALU.add), reads=[t3], writes=[tSCR])
                    S.op("act", lambda e: e.activation(out=SCR[:, 0:nblk], in_=SCR[:, 0:nblk], func=AF.Sqrt), reads=[tSCR], writes=[tSCR])
                    S.op("dve", lambda e: e.reciprocal(out=SCR[:, 0:nblk], in_=SCR[:, 0:nblk]), reads=[tSCR], writes=[tSCR])
                    S.op("dve", lambda e: e.scalar_tensor_tensor(out=KCS[:, seq, 0:nblk], in0=p2[:, 0:nblk], scalar=SPT[:, 75:76], in1=SCR[:, 0:nblk], op0=ALU.mult, op1=ALU.mult),
                         reads=[t2, tSPT, tSCR], writes=[tKCS])
                else:
                    for kt in range((nblk + 127) // 128):
                        nb_ = min(128, nblk - kt * 128)
                        p2, t2 = ps()
                        S.op("pe", lambda e, kt=kt, nb_=nb_: e.matmul(p2[0:nb_, 0:128], lhsT=SH[:, kt * 128:kt * 128 + nb_], rhs=W2B[:, 1, :], start=True, stop=True),
                             reads=[tSH, tW2B], writes=[t2])
                        S.op("act", lambda e, kt=kt, nb_=nb_: e.activation(out=VCS[0:nb_, seq, kt, :], in_=p2[0:nb_, 0:128], func=AF.Identity), reads=[t2], writes=[tVCS])

            def mk_idxl(which, slot):
                S.op("dve", lambda e: e.tensor_scalar(out=IDXG[:], in0=IDXF[:], scalar1=float(l * NPOOL * 128), scalar2=4.0, op0=ALU.add, op1=ALU.mult), reads=[tIDX], writes=[tIDXG])
                S.op("dve", lambda e: e.tensor_scalar(out=IDXG[:], in0=IDXG[:], scalar1=float(slot), scalar2=None, op0=ALU.add), reads=[tIDXG], writes=[tIDXG])
                S.op("dve", lambda e: e.tensor_copy(out=IDXL[:, which, :], in_=IDXG[:]), reads=[tIDXG], writes=[tIDXL])

            def mk_idxl2():
                S.op("dve", lambda e: e.tensor_scalar(out=IDXG[:], in0=IDXF[:], scalar1=float(l * NPOOL * 128), scalar2=2.0, op0=ALU.add, op1=ALU.mult), reads=[tIDX], writes=[tIDXG])
                S.op("dve", lambda e: e.tensor_scalar(out=IDXG[:], in0=IDXG[:], scalar1=1.0, scalar2=None, op0=ALU.add), reads=[tIDXG], writes=[tIDXG])
                S.op("dve", lambda e: e.tensor_copy(out=IDXL[:, 1, :], in_=IDXG[:]), reads=[tIDXG], writes=[tIDXL])

            def gather2(out_ap, sq, pg, wtok, join=False):
                S.dma("pool", lambda e: e.indirect_dma_start(out=out_ap, out_offset=None, in_=cacheR2,
                                                             in_offset=bass.IndirectOffsetOnAxis(ap=IDXL[:, 1, sq * NPG + pg:sq * NPG + pg + 1], axis=0)),
                      reads=[tIDXL], writes=[wtok], join=join)

            def gather(out_ap, which, sq, pg, wtok, join=False):
                S.dma("pool", lambda e: e.indirect_dma_start(out=out_ap, out_offset=None, in_=cacheR,
                                                             in_offset=bass.IndirectOffsetOnAxis(ap=IDXL[:, which, sq * NPG + pg:sq * NPG + pg + 1], axis=0)),
                      reads=[tIDXL], writes=[wtok], join=join)

            def sample_loader(which, slot, sq):
                def f(stg, tstg, pg0, npc):
                    for j in range(npc):
                        gather(stg[:, j, :], which, sq, pg0 + j, tstg, join=(j > 0))
                return f

            def prompt_loader(slot):
                def f(stg, tstg, pg0, npc):
                    S.dma("pool", lambda e: e.dma_start(out=stg[:, 0:npc, :], in_=kv_p[l, pg0 * 128:(pg0 + npc) * 128, slot * 128:(slot + 1) * 128].rearrange("(j p) c -> p j c", p=128)),
                          reads=tKVP[pg0:pg0 + npc], writes=[tstg])
                return f

            def load_w1(slot):
                for hh in range(2):
                    cast_load(W1B[:, hh * 16:(hh + 1) * 16, :], w1bd[l, slot, :, hh * 16:(hh + 1) * 16, :], [tW1B])
                cast_load(PETB[:], pet[l, slot], [tPET])
                pbias, tbias = ps()
                for p in range(32):
                    S.op("pe", lambda e, p=p: e.matmul(pbias[:, 0:2], lhsT=W1B[:, p, :], rhs=PETB[:, p:p + 1].to_broadcast([128, 2]), start=(p == 0), stop=(p == 31)),
                         reads=[tW1B, tPET], writes=[tbias])
                S.op("dve", lambda e: e.tensor_copy(out=ST1[:, 32:33], in_=pbias[:, 0:1]), reads=[tbias], writes=[tST1])

            for slot in range(2):
                load_w1(slot)
                for _ in compress(slot, 0, NCP, prompt_loader(slot), NT):
                    pass

            def sample_gen():
                for slot in range(2):
                    load_w1(slot)
                    mk_idxl(0, slot)
                    yield
                    for sq in range(NS):
                        yield from compress(slot, 1 + sq, NCS, sample_loader(0, slot, sq), NPG)
                yield from sample_attn()

            def sample_attn():
                MARKS.append(("Bs", l, dict(S.cnt)))
                S.op("pool", lambda e: e.memset(OACS[:], 0.0), writes=[tOACS])
                S.op("dve", lambda e: e.tensor_copy(out=VN0[:, 0, 1:65], in_=VS[0:TS, NT, 0, 0:64]), reads=[tVS], writes=[tVN0])
                S.op("dve", lambda e: e.tensor_copy(out=VN0[:, 1, 1:65], in_=VW[0:TS, NT, 0, 0:64]), reads=[tVW], writes=[tVN0])
                mk_idxl2()
                for b in range(2):
                    stgv = STG[:, b, :].rearrange("p (j c) -> p j c", c=288)
                    S.op("pool", lambda e, stgv=stgv: e.memset(stgv[:, :, 256:288], 1.0), writes=[tSTG[b]])
                ptsn = [0]
                NBK = 2 * NPG
                CS = 65 + NSELS
                import os
                for sq in range(NS if not os.environ.get('NOSAMP') else 0):
                    sc = slice(sq * DS, (sq + 1) * DS)
                    qcs = slice(T + sq * DS, T + (sq + 1) * DS)

                    def pts_next():
                        i_ = ptsn[0] % 3
                        ptsn[0] += 1
                        return PTS[:, sq, i_, :].rearrange("p (h t) -> p h t", h=4), tPTS[sq][i_]

                    def score_tile(pS, tS_, nk, lhsT_k, tk_, g, bias=None):
                        gp_ = slice(64 * g, 64 * g + 64)
                        S.op("pe", lambda e: e.matmul(pS[0:nk, 0:4 * DS], lhsT=lhsT_k, rhs=QT[gp_, :, qcs], start=True, stop=(bias is None)), reads=[tk_, tQT], writes=[tS_])
                        if bias is not None:
                            bl, br_, tb_ = bias
                            S.op("pe", lambda e: e.matmul(pS[0:nk, 0:4 * DS], lhsT=bl, rhs=br_, start=False, stop=True), reads=[tK, tb_], writes=[tS_])
                        pts, tpts = pts_next()
                        S.op("act", lambda e: e.activation(out=pts[0:nk, :, sc], in_=pS[0:nk, 0:4 * DS].rearrange("p (h t) -> p h t", h=4), func=AF.Exp, scale=SCALE), reads=[tS_], writes=[tpts])
                        return pts, tpts

                    for g in range(2):
                        gp = slice(64 * g, 64 * g + 64)
                        accs = [sacc(), sacc()]
                        for kt in range(NKS):
                            nk = min(128, NCS - kt * 128)
                            pS, tS_ = ps()
                            pts, tpts = score_tile(pS, tS_, nk, KCS[gp, 1 + sq, kt * 128:kt * 128 + nk], tKCS, g)
                            def fin_scmp(kt=kt, nk=nk, pts=pts, tpts=tpts):
                              for h in range(4):
                                  ac, tac = accs[h // 2]
                                  c0_ = (h % 2) * CS
                                  S.op("pe", lambda e, h=h, ac=ac, c0_=c0_: e.matmul(ac[0:TS, c0_:c0_ + 64], lhsT=pts[0:nk, h, :], rhs=VCS[0:nk, 1 + sq, kt, gp],
                                                                                    start=(kt == 0 and h % 2 == 0), stop=(kt == NKS - 1), skip_group_check=True), reads=[tpts, tVCS], writes=[tac])
                                  S.op("pe", lambda e, h=h, ac=ac, c0_=c0_: e.matmul(ac[0:TS, c0_ + 64:c0_ + CS], lhsT=pts[0:nk, h, :], rhs=OVS[0:nk, kt, :],
                                                                                    start=False, stop=(kt == NKS - 1), skip_group_check=True), reads=[tpts, tOV], writes=[tac])
                            push(fin_scmp)
                        flush()
                        score = SC2[0:TS, 0, 0:NSELS]
                        for hb in range(2):
                            ac, tac = accs[hb]
                            attn_epilogue(ac, tac, g, 0, NT, False, CS, rows=TS, nh=2, h0=2 * hb, dest=OACS, tdest=tOACS)
                            hv = ac[0:TS, 0:2 * CS].rearrange("p (h c) -> p h c", c=CS)
                            for h in range(2):
                                if hb == 0 and h == 0:
                                    S.op("dve", lambda e: e.tensor_scalar(out=score, in0=hv[:, 0, 65:CS], scalar1=ST1[0:TS, 40:41], scalar2=None, op0=ALU.mult), reads=[tac, tST1], writes=[tSC2])
                                else:
                                    S.op("dve", lambda e, h=h: e.scalar_tensor_tensor(out=score, in0=hv[:, h, 65:CS], scalar=ST1[0:TS, 40 + h:41 + h], in1=score, op0=ALU.mult, op1=ALU.add),
                                         reads=[tac, tST1, tSC2], writes=[tSC2])
                        S.op("dve", lambda e: e.tensor_tensor(out=score, in0=score, in1=FBS[:, :], op=ALU.add), reads=[tSC2, tK], writes=[tSC2])
                        select(score, NSELS, TS)
                        yield
                        pB, tB = ps()
                        S.op("pe", lambda e: e.transpose(pB[0:NBK, 0:TS], SC2[0:TS, 0, 0:NBK], IDF[0:TS, 0:TS]), reads=[tSC2, tK], writes=[tB])
                        for hf_ in range((NBK + 63) // 64):
                            r0, r1 = hf_ * 64, min(NBK, hf_ * 64 + 64)
                            S.op("act", lambda e, hf_=hf_, r0=r0, r1=r1: e.activation(out=BTS[r0:r1, hf_, g, :], in_=pB[r0:r1, 0:TS], func=AF.Identity), reads=[tB], writes=[tBTS])

                    def past_pass(npages, loadfn, biasfn, KN, tKN, VNg1, tVNg1, vn0_idx, br):
                        accg = [sacc(), sacc()]
                        first = [True, True]
                        for pg0 in range(0, npages, 4):
                            npc = min(4, npages - pg0)
                            b = (pg0 // 4) % 2
                            stgv = STG[:, b, :].rearrange("p (j c) -> p j c", c=288)
                            loadfn(stgv, tSTG[b], pg0, npc)
                            yield
                            pt_, tt_ = ps()
                            ptv = pt_[:].bitcast(BF16)
                            for j in range(npc):
                                S.op("pe", lambda e, j=j: e.transpose(ptv[:, j * 128:(j + 1) * 128], stgv[:, j, 0:128], IDB[:]), reads=[tSTG[b], tK], writes=[tt_])
                            S.op("dve", lambda e: e.tensor_copy(out=KTC[:, 0:npc * 128], in_=ptv[:, 0:npc * 128]), reads=[tt_], writes=[tRAWT])
                            SLCV = int(os.environ.get('SLCV', '9'))
                            KK_ = int(os.environ.get('KK', '128'))
                            for g in range(2 if SLCV >= 2 else 0):
                                gp = slice(64 * g, 64 * g + 64)
                                ac, tac = accg[g]
                                if os.environ.get('OB') == '1':
                                    ac, tac = ps()
                                for j in range(npc):
                                    pg = pg0 + j
                                    pS, tS_ = ps()
                                    pts, tpts = score_tile(pS, tS_, 128, KTC[gp, j * 128:(j + 1) * 128], tRAWT, g, bias=biasfn(pg, g))
                                    vr = stgv[:, j, 128:192] if g == 0 else stgv[:, j, 192:257]
                                    onec = stgv[:, j, 256:257]
                                    if os.environ.get('VRT') == '1':
                                        vr = stgv[:, j, 144:209] if g == 0 else stgv[:, j, 208:273]
                                    if os.environ.get('VRT') == '2':
                                        vr = VS[:, 0, g, :]
                                    if os.environ.get('VRT') == '3':
                                        vr = VS[:, 0, g, 0:64]
                                    if os.environ.get('E4') == '1':
                                        for hb in range(2):
                                            S.op("pe", lambda e, hb=hb, vr=vr, ac=ac: e.matmul(ac[0:2 * TS, hb * 65:(hb + 1) * 65], lhsT=pts[:, 2 * hb:2 * hb + 2, :], rhs=vr, start=(first[g] and hb == 0), stop=False, skip_group_check=True),
                                                 reads=[tpts, tSTG[b]], writes=[tac])
                                    def fin_pp(pts=pts, tpts=tpts, vr=vr, onec=onec, ac=ac, tac=tac, fg=first[g], b=b, g=g):
                                        for h in range(4):
                                            oc = h * 65 + (1 if g == 0 else 0)
                                            S.op("pe", lambda e, h=h, oc=oc: e.matmul(ac[0:TS, oc:oc + vr.shape[-1]], lhsT=pts[:, h, :], rhs=vr, start=(fg and h == 0), stop=False, skip_group_check=True),
                                                 reads=[tpts, tSTG[b]], writes=[tac])
                                            if g == 0:
                                                S.op("pe", lambda e, h=h: e.matmul(ac[0:TS, h * 65:h * 65 + 1], lhsT=pts[:, h, :], rhs=onec, start=False, stop=False, skip_group_check=True),
                                                     reads=[tpts, tSTG[b]], writes=[tac])
                                    push(fin_pp)
                                    first[g] = False
                        flush()
                        yield
                        for g in range(2 if SLCV >= 4 else 0):
                            gp = slice(64 * g, 64 * g + 64)
                            ac, tac = accg[g]
                            pS, tS_ = ps()
                            pts, tpts = score_tile(pS, tS_, TS, KN[gp, T:T + TS], tKN, g, bias=(IDB[:, 0:TS], NEWB[:, sq, :], tK))
                            vr = VN0[0:TS, vn0_idx, :] if g == 0 else VNg1[0:TS, NT, 1, :]
                            for h in range(4):
                                S.op("pe", lambda e, h=h, vr=vr, ac=ac: e.matmul(ac[0:TS, h * 65:(h + 1) * 65], lhsT=pts[0:TS, h, :], rhs=vr, start=False, stop=True, skip_group_check=True),
                                     reads=[tpts, tVN0, tVNg1], writes=[tac])
                            attn_epilogue(ac, tac, g, br, NT, False, 65, rows=TS, ocol=(1 if g == 0 else 0), scol=(0 if g == 0 else 64), dest=OACS, tdest=tOACS)

                    def slc_load(stgv, tstg, pg0, npc):
                        for j in range(npc):
                            gather2(stgv[:, j, 0:256], sq, pg0 + j, tstg, join=(j > 0))

                    def slc_bias(pg, g):
                        half = pg // 32
                        return (E64[:, pg % 32, :], BTS[:, half, g, sc].unsqueeze(1).to_broadcast([128, 4, DS]), tBTS)

                    if int(os.environ.get('SST', '9')) >= 2:
                        yield from past_pass(NPG, slc_load, slc_bias, KST, tKST, VS, tVS, 0, 1)

                    def win_load(stgv, tstg, pg0, npc):
                        S.dma("pool", lambda e: e.dma_start(out=stgv[:, 0:npc, 0:128], in_=st_win[l, sq, pg0 * 128:(pg0 + npc) * 128, 0:128].rearrange("(j p) c -> p j c", p=128)), writes=[tstg])
                        S.dma("pool", lambda e: e.dma_start(out=stgv[:, 0:npc, 128:256], in_=st_win[l, sq, pg0 * 128:(pg0 + npc) * 128, 128:256].rearrange("(j p) c -> p j c", p=128)), writes=[tstg], join=True)

                    def win_bias(pg, g):
                        if pg == 0:
                            return (IDB[:], WINB0[:, :], tK)
                        return None

                    if int(os.environ.get('SST', '9')) >= 3:
                        yield from past_pass(4, win_load, win_bias, KWT, tKWT, VW, tVW, 1, 2)

            MARKS.append(("Bp", l, dict(S.cnt)))
            def attn_epilogue(ac, tac, g, br, qt, first, stride, rows=128, nh=4, h0=0, ocol=0, scol=64, dest=None, tdest=None):
                dest = OACC if dest is None else dest
                tdest = tOACC if tdest is None else tdest
                hv = ac[0:rows, 0:nh * stride].rearrange("p (h c) -> p h c", c=stride)
                S.op("dve", lambda e: e.tensor_scalar(out=ST1[0:rows, 40:40 + nh], in0=hv[:, :, scol], scalar1=1e-30, scalar2=None, op0=ALU.add), reads=[tac], writes=[tST1])
                S.op("dve", lambda e: e.reciprocal(out=ST1[0:rows, 40:40 + nh], in_=ST1[0:rows, 40:40 + nh]), reads=[tST1], writes=[tST1])
                S.op("dve", lambda e: e.tensor_tensor(out=ST1[0:rows, 44:44 + nh], in0=ST1[0:rows, 40:40 + nh],
                                                      in1=GSIG[0:rows, qt, :].rearrange("p (h b) -> p h b", b=3)[:, 4 * g + h0:4 * g + h0 + nh, br], op=ALU.mult), reads=[tST1, tGS[qt]], writes=[tST1])
                for h in range(nh):
                    hh_ = 4 * g + h0 + h
                    if first:
                        S.op("dve", lambda e, h=h, hh_=hh_: e.tensor_scalar(out=dest[0:rows, hh_ * 64:(hh_ + 1) * 64], in0=hv[:, h, ocol:ocol + 64], scalar1=ST1[0:rows, 44 + h:45 + h], scalar2=None, op0=ALU.mult),
                             reads=[tac, tST1], writes=[tdest])
                    else:
                        S.op("dve", lambda e, h=h, hh_=hh_: e.scalar_tensor_tensor(out=dest[0:rows, hh_ * 64:(hh_ + 1) * 64], in0=hv[:, h, ocol:ocol + 64], scalar=ST1[0:rows, 44 + h:45 + h],
                                                                                  in1=dest[0:rows, hh_ * 64:(hh_ + 1) * 64], op0=ALU.mult, op1=ALU.add), reads=[tac, tST1, tdest], writes=[tdest])

            def select(score_ap, nsel, rows):
                if nsel > 16:
                    S.op("dve", lambda e: e.max(out=SC2[0:rows, 2, 0:8], in_=score_ap), reads=[tSC2], writes=[tSC2])
                    S.op("dve", lambda e: e.match_replace(out=SC2[0:rows, 1, 0:nsel], in_to_replace=SC2[0:rows, 2, 0:8], in_values=score_ap, imm_value=-3e38), reads=[tSC2], writes=[tSC2])
                    S.op("dve", lambda e: e.max(out=SC2[0:rows, 2, 8:16], in_=SC2[0:rows, 1, 0:nsel]), reads=[tSC2], writes=[tSC2])
                    S.op("dve", lambda e: e.tensor_scalar(out=SC2[0:rows, 1, 0:nsel], in0=score_ap, scalar1=SC2[0:rows, 2, 15:16], scalar2=None, op0=ALU.is_ge), reads=[tSC2], writes=[tSC2])
                    S.op("dve", lambda e: e.tensor_scalar(out=score_ap, in0=score_ap, scalar1=-5e29, scalar2=None, op0=ALU.is_gt), reads=[tSC2], writes=[tSC2])
                    S.op("dve", lambda e: e.tensor_tensor(out=score_ap, in0=score_ap, in1=SC2[0:rows, 1, 0:nsel], op=ALU.mult), reads=[tSC2], writes=[tSC2])
                else:
                    S.op("dve", lambda e: e.tensor_scalar(out=score_ap, in0=score_ap, scalar1=-5e29, scalar2=None, op0=ALU.is_gt), reads=[tSC2], writes=[tSC2])
                S.op("dve", lambda e: e.tensor_scalar(out=score_ap, in0=score_ap, scalar1=-1.0, scalar2=-NEGB, op0=ALU.add, op1=ALU.mult), reads=[tSC2], writes=[tSC2])

            def cmp_branch(qt, g):
                qc = slice(qt * 128, (qt + 1) * 128)
                gp = slice(64 * g, 64 * g + 64)
                pS, tS_ = ps()
                S.op("pe", lambda e: e.matmul(pS[0:NCP, :], lhsT=KCS[gp, 0, 0:NCP], rhs=QT[gp, :, qc], start=True, stop=False), reads=[tKCS, tQT], writes=[tS_])
                S.op("pe", lambda e: e.matmul(pS[0:NCP, :], lhsT=IDB[0:NCP, 0:NCP], rhs=CMPB[0:NCP, qc].unsqueeze(1).to_broadcast([NCP, 4, 128]), start=False, stop=True),
                     reads=[tK], writes=[tS_])
                pt, tpt = ptbuf()
                S.op("act", lambda e: e.activation(out=pt[0:NCP, :], in_=pS[0:NCP, :], func=AF.Exp, scale=SCALE), reads=[tS_], writes=[tpt])
                ac, tac = acc()

                def fin_cmp():
                    for h in range(4):
                        S.op("pe", lambda e, h=h: e.matmul(ac[:, h * 97:h * 97 + 64], lhsT=pt[0:NCP, h * 128:(h + 1) * 128], rhs=VCS[0:NCP, 0, 0, gp], start=True, stop=True),
                             reads=[tpt, tVCS], writes=[tac])
                        S.op("pe", lambda e, h=h: e.matmul(ac[:, h * 97 + 64:h * 97 + 65 + NSELP], lhsT=pt[0:NCP, h * 128:(h + 1) * 128], rhs=OVP[0:NCP, :], start=True, stop=True),
                             reads=[tpt, tOV], writes=[tac])
                    attn_epilogue(ac, tac, g, 0, qt, True, 97)
                    hv = ac[:, 0:388].rearrange("p (h c) -> p h c", c=97)
                    score = SC2[:, 0, 0:NSELP]
                    for h in range(4):
                        if h == 0:
                            S.op("dve", lambda e: e.tensor_scalar(out=score, in0=hv[:, 0, 65:65 + NSELP], scalar1=ST1[:, 40:41], scalar2=None, op0=ALU.mult), reads=[tac, tST1], writes=[tSC2])
                        else:
                            S.op("dve", lambda e, h=h: e.scalar_tensor_tensor(out=score, in0=hv[:, h, 65:65 + NSELP], scalar=ST1[:, 40 + h:41 + h], in1=score, op0=ALU.mult, op1=ALU.add),
                                 reads=[tac, tST1, tSC2], writes=[tSC2])
                    S.op("dve", lambda e: e.tensor_tensor(out=score, in0=score, in1=FBP[:, qt, :], op=ALU.add), reads=[tSC2, tK], writes=[tSC2])
                    select(score, NSELP, 128)
                    pB, tB = ps()
                    S.op("pe", lambda e: e.transpose(pB[0:NSELP, 0:128], score, IDF[:]), reads=[tSC2, tK], writes=[tB])
                    S.op("act", lambda e: e.activation(out=BT[0:NSELP, g, :], in_=pB[0:NSELP, 0:128], func=AF.Identity), reads=[tB], writes=[tBT])
                push(fin_cmp)

            def kv_branch(qt, g, br):
                qc = slice(qt * 128, (qt + 1) * 128)
                gp = slice(64 * g, 64 * g + 64)
                if br == 1:
                    KT_, VV, tKK, tVV, kts = KST, VS, tKST, tVS, list(range(0, qt + 1))
                else:
                    KT_, VV, tKK, tVV, kts = KWT, VW, tKWT, tVW, list(range(max(0, qt - 4), qt + 1))
                if True:
                    ac, tac = acc()
                    for ki, kt in enumerate(kts):
                        pS, tS_ = ps()
                        need_b = (kt == qt) or (br == 1) or (br == 2 and kt == qt - 4)
                        S.op("pe", lambda e, kt=kt: e.matmul(pS[:, :], lhsT=KT_[gp, kt * 128:(kt + 1) * 128], rhs=QT[gp, :, qc], start=True, stop=not need_b), reads=[tKK, tQT], writes=[tS_])
                        if kt == qt:
                            S.op("pe", lambda e: e.matmul(pS[:, :], lhsT=IDB[:], rhs=CAUS[:].unsqueeze(1).to_broadcast([128, 4, 128]), start=False, stop=True), reads=[tK], writes=[tS_])
                        elif br == 1:
                            S.op("pe", lambda e, kt=kt: e.matmul(pS[:, :], lhsT=E64[:, kt, :], rhs=BT[:, g, :].unsqueeze(1).to_broadcast([128, 4, 128]), start=False, stop=True),
                                 reads=[tK, tBT], writes=[tS_])
                        elif kt == qt - 4:
                            S.op("pe", lambda e: e.matmul(pS[:, :], lhsT=IDB[:], rhs=ANTI[:].unsqueeze(1).to_broadcast([128, 4, 128]), start=False, stop=True), reads=[tK], writes=[tS_])
                        pt, tpt = ptbuf()
                        S.op("act", lambda e: e.activation(out=pt, in_=pS[:, :], func=AF.Exp, scale=SCALE), reads=[tS_], writes=[tpt])
                        def fin_kv(ki=ki, kt=kt, pt=pt, tpt=tpt):
                            for h in range(4):
                                S.op("pe", lambda e, h=h: e.matmul(ac[:, h * 65:(h + 1) * 65], lhsT=pt[:, h * 128:(h + 1) * 128], rhs=VV[:, kt, g, :],
                                                                   start=(ki == 0 and h == 0), stop=(ki == len(kts) - 1), skip_group_check=True), reads=[tpt, tVV], writes=[tac])
                            if ki == len(kts) - 1:
                                attn_epilogue(ac, tac, g, br, qt, False, 65)
                        push(fin_kv)

            gen = sample_gen()
            n_steps = 2 * (1 + NS * ((NPG + 7) // 8)) + NS * (3 + (NPG + 3) // 4 + 1 + 3)
            wts = [2 + (qt + 1) + min(qt + 1, 5) for qt in range(NT)]
            done_steps = [0]

            def advance(target):
                while done_steps[0] < target:
                    done_steps[0] += 1
                    try:
                        next(gen)
                    except StopIteration:
                        done_steps[0] = 10 ** 9
                        return

            cum = 0
            for qt in range(NT):
                for g in range(2):
                    cmp_branch(qt, g)
                advance(int(n_steps * (cum + 0.3 * wts[qt]) / sum(wts)))
                for g in range(2):
                    kv_branch(qt, g, 2)
                advance(int(n_steps * (cum + 0.6 * wts[qt]) / sum(wts)))
                for g in range(2):
                    kv_branch(qt, g, 1)
                flush()
                cum += wts[qt]
                advance(int(n_steps * cum / sum(wts)))
                S.op("act", lambda e: e.activation(out=QN[:, 0:512], in_=OACC[:, :], func=AF.Identity), reads=[tOACC], writes=[tQN])
                pt_, tt_ = ps()
                ptv = pt_[:].bitcast(BF16)
                for k in range(4):
                    S.op("pe", lambda e, k=k: e.transpose(ptv[:, k * 128:(k + 1) * 128], QN[:, k * 128:(k + 1) * 128], IDB[:]), reads=[tQN, tK], writes=[tt_])
                S.op("dve", lambda e: e.tensor_copy(out=OT[:], in_=ptv[:, 0:512].rearrange("p (k t) -> p k t", k=4)), reads=[tt_], writes=[tOT])
                for half in range(2):
                    po, to = ps()
                    for k in range(4):
                        S.op("pe", lambda e, k=k, half=half: e.matmul(po[:, :], lhsT=OT[:, k, :], rhs=WOA[:, k, half * 512:(half + 1) * 512], start=(k == 0), stop=(k == 3)),
                             reads=[tOT, tWOA], writes=[to])
                    resid_add(qt, po, to, half, 0)


            for _ in gen:
                pass
            flush()
            S.op("act", lambda e: e.activation(out=QN[0:TS, 0:512], in_=OACS[:, :], func=AF.Identity), reads=[tOACS], writes=[tQN])
            pt_, tt_ = ps()
            ptv = pt_[:].bitcast(BF16)
            for k in range(4):
                S.op("pe", lambda e, k=k: e.transpose(ptv[:, k * 128:k * 128 + TS], QN[0:TS, k * 128:(k + 1) * 128], IDB[0:TS, 0:TS]), reads=[tQN, tK], writes=[tt_])
            S.op("dve", lambda e: e.tensor_copy(out=OT[:, :, 0:TS], in_=ptv[:, 0:512].rearrange("p (k t) -> p k t", k=4)[:, :, 0:TS]), reads=[tt_], writes=[tOT])
            for half in range(2):
                po, to = ps()
                for k in range(4):
                    S.op("pe", lambda e, k=k, half=half: e.matmul(po[0:TS, :], lhsT=OT[:, k, 0:TS], rhs=WOA[:, k, half * 512:(half + 1) * 512], start=(k == 0), stop=(k == 3)),
                         reads=[tOT, tWOA], writes=[to])
                resid_add(NT, po, to, half, 0)

            S.barrier()
            MARKS.append(("C", l, dict(S.cnt)))
            NPSR[0] = 5
            for i in range(NTT):
                norm_to_HT([i], 2, 3, H2T, tH2T, i * 128)
            colgroups = [(c0, min(512, T - c0), list(range(c0 // 128, (c0 + min(512, T - c0)) // 128))) for c0 in range(0, T, 512)] + [(T, TS, [NT])]
            NCH = FFN_H // 128
            blocks = [(j0, min(4, NCH - j0)) for j0 in range(0, NCH, 4)]
            for bi, (j0, nb) in enumerate(blocks):
                sl = bi % 2
                cast_load(WUP[sl][:, :, 0:nb * 128], w_up[l, :, j0 * 128:(j0 + nb) * 128].rearrange("(k p) c -> p k c", p=128), [tWF[sl]] + (allWA if bi < 2 else []))
                cast_load(WUP[sl][:, :, 512:512 + nb * 128], w_up[l, :, FFN_H + j0 * 128:FFN_H + (j0 + nb) * 128].rearrange("(k p) c -> p k c", p=128), [tWF[sl]])
                cast_load(WDN[sl][:, 0:nb, :], w_down[l, j0 * 128:(j0 + nb) * 128, :].rearrange("(j p) c -> p j c", p=128), [tWF[sl]])
                for (c0, ncol, tiles) in colgroups:
                    for j in range(nb):
                        pa, ta = ps()
                        for k in range(8):
                            S.op("pe", lambda e, k=k, j=j, pa=pa, c0=c0, ncol=ncol, sl=sl: e.matmul(pa[:, 0:ncol], lhsT=WUP[sl][:, k, j * 128:(j + 1) * 128], rhs=H2T[:, k, c0:c0 + ncol],
                                                                                               start=(k == 0), stop=(k == 7)), reads=[tWF[sl], tH2T], writes=[ta])
                        pb2, tb2 = ps()
                        for k in range(8):
                            S.op("pe", lambda e, k=k, j=j, pb2=pb2, c0=c0, ncol=ncol, sl=sl: e.matmul(pb2[:, 0:ncol], lhsT=WUP[sl][:, k, 512 + j * 128:512 + (j + 1) * 128], rhs=H2T[:, k, c0:c0 + ncol],
                                                                                                 start=(k == 0), stop=(k == 7)), reads=[tWF[sl], tH2T], writes=[tb2])
                        S.op("act", lambda e, pa=pa, ncol=ncol: e.activation(out=SCR[:, 0:ncol], in_=pa[:, 0:ncol], func=AF.Silu), reads=[ta], writes=[tSCR])
                        S.op("dve", lambda e, pb2=pb2, j=j, ncol=ncol: e.tensor_tensor(out=UT[:, j, 0:ncol], in0=SCR[:, 0:ncol], in1=pb2[:, 0:ncol], op=ALU.mult), reads=[tSCR, tb2], writes=[tUT[0]])
                    for ti, i in enumerate(tiles):
                        rows = 128 if i < NT else TS
                        for half in range(2):
                            po, to = ps()
                            for j in range(nb):
                                S.op("pe", lambda e, j=j, po=po, ti=ti, rows=rows, half=half, sl=sl: e.matmul(po[0:rows, :], lhsT=UT[:, j, ti * 128:ti * 128 + rows], rhs=WDN[sl][:, j, half * 512:(half + 1) * 512],
                                                                                                       start=(j == 0), stop=(j == nb - 1)), reads=[tUT[0], tWF[sl]], writes=[to])
                            resid_add(i, po, to, half, 1)

        for l in range(DEPTH):
            S.barrier()
            MARKS.append(("L", l, dict(S.cnt)))
            ld("sp", SPR[:], spar[l], [tSPR])
            pb, tp = ps()
            S.op("pe", lambda e, pb=pb: e.transpose(pb[:, 0:80], SPR[:], IDF[0:80, 0:80]), reads=[tSPR, tK], writes=[tp])
            S.op("dve", lambda e, pb=pb: e.tensor_copy(out=SPT[:], in_=pb[:, 0:80]), reads=[tp], writes=[tSPT])
            ld("sp", KGB[:].rearrange("p a c -> p (a c)"), kgbc[l].rearrange("a c -> (a c)").partition_broadcast(128), [tKGB])
            for n in range(12):
                kind, half = n // 2, n % 2
                wb = n % 4
                cast_load(WADA[:, wb], w_ada[l, :, n * 512:(n + 1) * 512].rearrange("(k p) c -> p k c", p=128), [tWADA[wb]] + ([tWIN] if False else []),
                          r=[])
                if kind in (2, 5):
                    gi = 0 if kind == 2 else 1
                    ld("sp", BST[:], b_ada[l, n * 512:(n + 1) * 512].partition_broadcast(128), [tBST])
                    for gsel, (c0, cw) in enumerate(((0, 128), (128, TS))):
                        pb, tp = ps()
                        for k in range(8):
                            S.op("pe", lambda e, k=k, pb=pb, c0=c0, cw=cw, wb=wb: e.matmul(pb[0:cw, :], lhsT=CTS[:, k, c0:c0 + cw], rhs=WADA[:, wb, k, :],
                                                                                            start=(k == 0), stop=(k == 7)), reads=[tK, tWADA[wb]], writes=[tp])
                        S.op("dve", lambda e, pb=pb, cw=cw, gi=gi, gsel=gsel, half=half: e.tensor_tensor(
                            out=GATE[0:cw, gi, gsel, half * 512:(half + 1) * 512], in0=pb[0:cw, :], in1=BST[0:cw, :], op=ALU.add),
                             reads=[tp, tBST], writes=[tGATE[gi][gsel]])
                else:
                    mk = {0: 0, 1: 1, 3: 2, 4: 3}[kind]
                    pb, tp = ps()
                    for j in range(4):
                        for k in range(8):
                            S.op("pe", lambda e, k=k, j=j, pb=pb, wb=wb: e.matmul(pb[:, j * 8:j * 8 + 1 + NS], lhsT=WADA[:, wb, k, j * 128:(j + 1) * 128], rhs=CT5[:, k, :],
                                                                                   start=(k == 0), stop=(k == 7)), reads=[tK, tWADA[wb]], writes=[tp])
                    for j in range(4):
                        c = half * 4 + j
                        bcol = SPT[:, n * 4 + j:n * 4 + j + 1]
                        if kind in (0, 3):
                            S.op("dve", lambda e, j=j, pb=pb, c=c, bcol=bcol, mk=mk: e.tensor_scalar(out=MODC[:, mk, c, :], in0=pb[:, j * 8:j * 8 + 1 + NS], scalar1=bcol, scalar2=None, op0=ALU.add),
                                 reads=[tp, tSPT], writes=[tMODC])
                        else:
                            ncol = SPT[:, 48 + c:49 + c] if kind == 1 else SPT[:, 56 + c:57 + c]
                            S.op("dve", lambda e, j=j, pb=pb, c=c, bcol=bcol, mk=mk: e.tensor_scalar(out=MODC[:, mk, c, :], in0=pb[:, j * 8:j * 8 + 1 + NS], scalar1=bcol, scalar2=1.0, op0=ALU.add, op1=ALU.add),
                                 reads=[tp, tSPT], writes=[tMODC])
                            S.op("dve", lambda e, c=c, ncol=ncol, mk=mk: e.tensor_scalar(out=MODC[:, mk, c, :], in0=MODC[:, mk, c, :], scalar1=ncol, scalar2=None, op0=ALU.mult),
                                 reads=[tMODC, tSPT], writes=[tMODC])
            LAYER_BODY(l)
        for i in range(NT):
            S.dma("sp", lambda e, i=i: e.dma_start(out=y_p[i * 128:(i + 1) * 128, :], in_=X[:, i, :]), reads=[tX[i]])
        S.dma("sp", lambda e: e.dma_start(out=y_s, in_=X[0:TS, NT, :]), reads=[tX[NT]])
        S.final_wait("sp")
        print('CNT', S.cnt, {k: v for k, v in S.dval.items() if v > 1000})

        sems = {e: es.enter_context(nc.semaphore("s_" + e)) for e in ENGS}
        for q, n in S.NDS.items():
            for i in range(n):
                sems[("d", q, i)] = es.enter_context(nc.semaphore(f"d_{q}_{i}"))

        def run(e, lst):
            for waits, fn, key, inc in lst:
                for k, v in waits:
                    e.wait_ge(sems[k], v)
                if fn is not None:
                    name, a, k = fn
                    getattr(e, name)(*a, **k).then_inc(sems[key], inc)

        with nc.Block() as block:
            @block.sync
            def _(e):
                run(e, S.ops["sp"])

            @block.scalar
            def _(e):
                run(e, S.ops["act"])

            @block.vector
            def _(e):
                run(e, S.ops["dve"])

            @block.gpsimd
            def _(e):
                run(e, S.ops["pool"])

            @block.tensor
            def _(e):
                run(e, S.ops["pe"])
    return nc


def _consts(cfg):
    T, NPG, NS, DS = cfg["T"], cfg["NPG"], cfg["NS"], cfg["DS"]
    NT = T // 128; TS = NS * DS; PAST = NPG * 128
    NCS = (PAST + DS - 32) // 16 + 1; NSELP = T // 64; NSELS = -(-(PAST + DS) // 64); NKS = (NCS + 127) // 128
    bf = ml_dtypes.bfloat16
    k = {}
    k["k_idf"] = np.eye(128, dtype=np.float32); k["k_idb"] = np.eye(128).astype(bf)
    kk, qq = np.meshgrid(np.arange(128), np.arange(128), indexing="ij")
    k["k_caus"] = np.where(kk > qq, NEGB, 0.0).astype(bf); k["k_anti"] = np.where(kk <= qq, NEGB, 0.0).astype(bf)
    c = np.arange(128)[:, None]; t = np.arange(T)[None, :]
    k["k_cmpb"] = np.where(16 * c + 31 <= t, 0.0, NEGB).astype(bf)
    e64 = np.zeros((128, 32, 128), np.float32)
    for m in range(32):
        for key in range(128):
            j = 2 * m + key // 64
            e64[j % 64, m, key] = 1.0
            e64[64 + j % 64, m, key] = 1.0
    k["k_e64"] = e64.astype(bf)
    fbp = np.zeros((128, NT, NSELP), np.float32)
    for i in range(NT):
        for p in range(128):
            cur = (i * 128 + p) // 64
            for j in range(NSELP):
                if j > cur: fbp[p, i, j] = -1e30
                elif j == 0 or j == cur or j == cur - 1: fbp[p, i, j] = 1e4
    k["k_fbp"] = fbp.astype(bf)
    fbs = np.zeros((TS, NSELS), np.float32); cur = NSELS - 1
    fbs[:, 0] = 1e4; fbs[:, cur] = 1e4; fbs[:, cur - 1] = 1e4
    k["k_fbs"] = fbs
    newb = np.full((TS, NS, 4, DS), NEGB, np.float32)
    for s in range(NS):
        for t2 in range(DS):
            for tq in range(DS):
                if t2 <= tq: newb[s * DS + t2, s, :, tq] = 0.0
    k["k_newb"] = newb.reshape(TS, NS, 4 * DS).astype(bf)
    wb0 = np.zeros((128, 4, DS), np.float32)
    for i in range(128):
        for tq in range(DS):
            if not (i > tq): wb0[i, :, tq] = NEGB
    k["k_winb0"] = wb0.reshape(128, 4 * DS).astype(bf)

    def ov(ncmp, nsel):
        m = np.zeros((ncmp, nsel), np.float32); i = np.arange(ncmp)
        for part in range(2):
            j = np.minimum((i + part) * 16 // 64, nsel - 1); np.add.at(m, (i, j), 1.0)
        return m
    NCP = (T - 32) // 16 + 1
    ovp = np.zeros((128, 1 + NSELP), np.float32); ovp[:, 0] = 1.0; ovp[:NCP, 1:] = ov(NCP, NSELP)
    k["k_ovp"] = ovp.astype(bf)
    ovs = np.zeros((NKS * 128, 1 + NSELS), np.float32); ovs[:, 0] = 1.0; ovs[:NCS, 1:] = ov(NCS, NSELS)
    k["k_ovs"] = ovs.reshape(NKS, 128, 1 + NSELS).transpose(1, 0, 2).astype(bf)
    rc = np.zeros((128, 2, 16), np.float32)
    for ch in range(2):
        for p in range(128):
            w = (2, 4, 8, 16)[ch * 2 + p // 64]
            rc[p, ch, :] = 1.0 / np.minimum(w, np.arange(16) + 1)
    k["k_rc"] = rc
    bo = np.zeros((128, 128), np.float32); bo[:64, :64] = 1; bo[64:, 64:] = 1
    k["k_bones"] = bo.astype(bf)
    sel = np.zeros((1 + NS, 128 + TS), np.float32); sel[0, :128] = 1
    for s in range(NS): sel[1 + s, 128 + s * DS:128 + (s + 1) * DS] = 1
    k["k_cts"] = sel
    return k


_NC_CACHE = {}


def kernel(x_prompt, x_sample, cache_nsa_kv, state_win_kv, state_conv, state_pool, page_table,
           c_prompt, c_sample, norm_mix, norm_ffn, w_ada, b_ada, w_in, w_out, q_norm, k_norm,
           cmp_pe, cmp_w1, cmp_w2, conv_w, conv_bias, pool_w, pool_scale, w_up, w_down, cfg=None):
    cfg = dict(CFG) if cfg is None else cfg
    T, NPG, DEPTH, NS, DS = cfg["T"], cfg["NPG"], cfg["DEPTH"], cfg["NS"], cfg["DS"]
    f = lambda a: np.ascontiguousarray(np.asarray(a))
    NB = x_prompt.shape[0]; TS = NS * DS
    key = tuple(sorted(cfg.items()))
    if key not in _NC_CACHE:
        _NC_CACHE[key] = build(cfg)
    nc = _NC_CACHE[key]
    consts = _consts(cfg)
    spar = np.zeros((DEPTH, 80, 128), np.float32)
    spar[:, 0:48] = f(b_ada).reshape(DEPTH, 48, 128)
    spar[:, 48:56] = f(norm_mix).reshape(DEPTH, 8, 128); spar[:, 56:64] = f(norm_ffn).reshape(DEPTH, 8, 128)
    spar[:, 64:70] = f(conv_w).reshape(DEPTH, 6, 128); spar[:, 70:72] = f(conv_bias).reshape(DEPTH, 2, 128)
    spar[:, 72:74] = f(pool_scale).reshape(DEPTH, 2, 128)
    spar[:, 74, 0:64] = f(q_norm); spar[:, 74, 64:128] = f(q_norm)
    spar[:, 75:78, 0:64] = f(k_norm); spar[:, 75:78, 64:128] = f(k_norm)
    kgbc = np.concatenate([f(k_norm)[:, 1:3], f(k_norm)[:, 1:3]], axis=-1).astype(np.float32)
    w1 = f(cmp_w1).reshape(DEPTH, 2, 32, 64, 64)
    w1bd = np.zeros((DEPTH, 2, 128, 32, 128), np.float32)
    w1bd[:, :, 0:64, :, 0:64] = w1.transpose(0, 1, 3, 2, 4); w1bd[:, :, 64:, :, 64:] = w1.transpose(0, 1, 3, 2, 4)
    w2bd = np.zeros((DEPTH, 2, 128, 128), np.float32)
    w2bd[:, :, :64, :64] = f(cmp_w2); w2bd[:, :, 64:, 64:] = f(cmp_w2)
    pw = f(pool_w); pwbd = np.zeros((DEPTH, 2, 128, 128), np.float32)
    pwbd[:, 0, :64, :64] = pw[:, 0]; pwbd[:, 0, 64:, 64:] = pw[:, 1]; pwbd[:, 1, :64, :64] = pw[:, 2]; pwbd[:, 1, 64:, 64:] = pw[:, 3]
    pe_t = f(cmp_pe).transpose(0, 1, 3, 2)
    pet = np.concatenate([pe_t, pe_t], axis=2).astype(np.float32)
    cache2 = f(cache_nsa_kv).reshape(DEPTH, -1, 512)
    in_maps = []
    for c in range(NB):
        sl = slice(c * NS, (c + 1) * NS)
        st_cp = np.concatenate([f(state_pool)[:, sl].reshape(DEPTH, NS * 15, 256), f(state_conv)[:, sl].reshape(DEPTH, NS * 2, 256)], axis=1)
        m = dict(x_p=f(x_prompt[c]), x_s=f(x_sample[sl]).reshape(TS, D), cache=cache2,
                 st_win=f(state_win_kv)[:, sl].reshape(DEPTH, NS, 512, 256), st_cp=st_cp,
                 ptab=f(page_table[sl]).astype(np.int32), c_all=np.concatenate([f(c_prompt)[c:c + 1], f(c_sample)[sl]], 0),
                 spar=spar, kgbc=kgbc, w_ada=f(w_ada), b_ada=f(b_ada), w_in=f(w_in), w_out=f(w_out), w1bd=w1bd, w2bd=w2bd,
                 pwbd=pwbd, pet=pet, w_up=f(w_up), w_down=f(w_down))
        m.update(consts)
        in_maps.append(m)
    res = run_bass_kernel_spmd(nc, in_maps, core_ids=list(range(NB)))
    R_ = res.results
    cat = lambda k, ax=0: np.stack([r[k] for r in R_], axis=ax)
    yp = cat("y_p"); ys = np.concatenate([r["y_s"].reshape(NS, DS, D) for r in R_], 0)
    kvp = cat("kv_p", 1).reshape(DEPTH, NB, T, 4, 2, 64)
    kvs = np.concatenate([r["kv_s"].reshape(DEPTH, NS, DS, 4, 2, 64) for r in R_], 1)
    wp = cat("win_p", 1).reshape(DEPTH, NB, -1, 2, 2, 64)
    ws = np.concatenate([r["win_s"].reshape(DEPTH, NS, 512, 2, 2, 64) for r in R_], 1)
    cp = cat("conv_p", 1); cs = np.concatenate([r["conv_s"].reshape(DEPTH, NS, 2, 256) for r in R_], 1)
    pp = cat("pool_p", 1); pls = np.concatenate([r["pool_s"].reshape(DEPTH, NS, 15, 256) for r in R_], 1)
    return (yp, ys, kvp, kvs, wp, ws, cp, cs, pp, pls)
```

```python
import numpy as np
import ml_dtypes
import concourse.bass as bass
import concourse.mybir as mybir
from concourse.bass_utils import run_bass_kernel_spmd

F32 = mybir.dt.float32
BF16 = mybir.dt.bfloat16
I32 = mybir.dt.int32
AF = mybir.ActivationFunctionType
ALU = mybir.AluOpType
AX = mybir.AxisListType

D = 1024
HD = 64
IN_W = 2328
FFN_H = 2816
EPS = 1e-6
NEGB = -30000.0
SCALE = 0.125
CFG = dict(T=2048, NPG=64, DEPTH=4, NS=4, DS=8, NPOOL=2560)


class Tok:
    __slots__ = ("w", "r", "excl", "wl")

    def __init__(self, excl=False):
        self.w = None
        self.r = {}
        self.excl = excl
        self.wl = []


ENGS = ("pe", "act", "dve", "pool", "sp")
MARKS = []


class _Rec:
    def __getattr__(self, name):
        return lambda *a, **k: (name, a, k)


_REC = _Rec()


class Sched:
    def __init__(self):
        self.ops = {e: [] for e in ENGS}
        self.cnt = {e: 0 for e in ENGS}
        self.seen = {e: {} for e in ENGS}
        self.dnext = {e: 0 for e in ENGS}
        self.dval = {}
        self.NDS = {"sp": 24, "pool": 24, "act": 4}

    def _need(self, eng, dep, waits):
        if dep is None:
            return
        k, v = dep
        if eng == "pe" and k == "pe":
            return
        if self.seen[eng].get(k, 0) >= v:
            return
        if waits.get(k, 0) < v:
            waits[k] = v

    def _deps(self, eng, reads, writes, join=False):
        waits = {}
        for t in reads:
            self._need(eng, t.w, waits)
            for d_ in t.wl:
                self._need(eng, d_, waits)
        for t in writes:
            if join and t.w is not None and isinstance(t.w[0], tuple):
                continue
            if not (t.w is not None and t.w[0] == eng):
                self._need(eng, t.w, waits)
            for d_ in t.wl:
                self._need(eng, d_, waits)
            for k, v in t.r.items():
                if k != eng:
                    self._need(eng, (k, v), waits)
        for k, v in waits.items():
            self.seen[eng][k] = v
        return list(waits.items())

    def _mark(self, me, reads, writes, join=False):
        k, v = me
        for t in reads:
            if t.r.get(k, 0) < v:
                t.r[k] = v
        for t in writes:
            if join and t.w is not None and isinstance(t.w[0], tuple):
                t.wl.append(t.w)
            else:
                t.wl = []
                t.r = {}
            t.w = me

    def op(self, eng, fn, reads=(), writes=()):
        writes = list(writes) + [t for t in reads if t.excl]
        reads = [t for t in reads if not t.excl]
        waits = self._deps(eng, reads, writes)
        self.cnt[eng] += 1
        me = (eng, self.cnt[eng])
        self.ops[eng].append((waits, fn(_REC), eng, 1))
        self._mark(me, reads, writes)

    def dma(self, q, fn, reads=(), writes=(), join=False):
        i = self.dnext[q] % self.NDS[q]
        self.dnext[q] += 1
        key = ("d", q, i)
        prev = self.dval.get(key, 0)
        waits = self._deps(q, reads, writes, join)
        if prev and self.seen[q].get(key, 0) < prev:
            waits.append((key, prev))
            self.seen[q][key] = prev
        val = prev + 16
        self.dval[key] = val
        self.ops[q].append((waits, fn(_REC), key, 16))
        self._mark((key, val), reads, writes, join)

    def barrier(self):
        snap = dict(self.cnt)
        dsn = dict(self.dval)
        for eng in ENGS:
            waits = [(k, v) for k, v in dsn.items() if self.seen[eng].get(k, 0) < v]
            for e in ENGS:
                if e != eng and snap[e] and self.seen[eng].get(e, 0) < snap[e]:
                    waits.append((e, snap[e]))
            for k, v in waits:
                self.seen[eng][k] = v
            if waits:
                self.ops[eng].append((waits, None, None, 0))

    def final_wait(self, eng):
        waits = [(k, v) for k, v in self.dval.items() if self.seen[eng].get(k, 0) < v]
        for e in ENGS:
            if e != eng and self.cnt[e]:
                waits.append((e, self.cnt[e]))
        self.ops[eng].append((waits, None, None, 0))


def build(cfg):
    T, NPG, DEPTH, NS, DS = cfg["T"], cfg["NPG"], cfg["DEPTH"], cfg["NS"], cfg["DS"]
    NPOOL = cfg["NPOOL"]
    NT = T // 128
    NTT = NT + 1
    TS = NS * DS
    TT = T + TS
    PAST = NPG * 128
    NCP = (T - 32) // 16 + 1
    NCS = (PAST + DS - 32) // 16 + 1
    NSELP = T // 64
    NSELS = -(-(PAST + DS) // 64)
    NKS = (NCS + 127) // 128
    WINT = min(512, T) // 128
    GT = 256
    GTL = GT // 128
    NG = (NT + GTL - 1) // GTL

    nc = bass.Bass("TRN2", target_bir_lowering=False)
    S = Sched()

    def din(name, shape, dt=F32):
        return nc.dram_tensor(name, list(shape), dt, kind="ExternalInput").ap()

    def dout(name, shape, dt=F32):
        return nc.dram_tensor(name, list(shape), dt, kind="ExternalOutput").ap()

    x_p = din("x_p", [T, D]); x_s = din("x_s", [TS, D])
    cache = din("cache", [DEPTH, NPOOL * 128, 512])
    cacheR = cache.rearrange("l r (q c) -> (l r q) c", c=128)
    cacheR2 = cache.rearrange("l r (h c) -> (l r h) c", h=2)
    st_win = din("st_win", [DEPTH, NS, 512, 256])
    st_cp = din("st_cp", [DEPTH, NS * 17, 256])
    ptab = din("ptab", [NS, NPG], I32)
    c_all = din("c_all", [1 + NS, D])
    spar = din("spar", [DEPTH, 80, 128])
    kgbc = din("kgbc", [DEPTH, 2, 128])
    w_ada = din("w_ada", [DEPTH, D, 6 * D]); b_ada = din("b_ada", [DEPTH, 6 * D])
    w_in = din("w_in", [DEPTH, D, IN_W]); w_out = din("w_out", [DEPTH, D, D])
    w1bd = din("w1bd", [DEPTH, 2, 128, 32, 128])
    w2bd = din("w2bd", [DEPTH, 2, 128, 128])
    pwbd = din("pwbd", [DEPTH, 2, 128, 128])
    pet = din("pet", [DEPTH, 2, 128, 32])
    w_up = din("w_up", [DEPTH, D, 2 * FFN_H]); w_down = din("w_down", [DEPTH, FFN_H, D])
    k_idf = din("k_idf", [128, 128]); k_idb = din("k_idb", [128, 128], BF16)
    k_caus = din("k_caus", [128, 128], BF16); k_anti = din("k_anti", [128, 128], BF16)
    k_cmpb = din("k_cmpb", [128, T], BF16)
    k_e64 = din("k_e64", [128, 32, 128], BF16)
    k_fbp = din("k_fbp", [128, NT, NSELP], BF16); k_fbs = din("k_fbs", [TS, NSELS])
    k_newb = din("k_newb", [TS, NS, 4 * DS], BF16); k_winb0 = din("k_winb0", [128, 4 * DS], BF16)
    k_ovp = din("k_ovp", [128, 1 + NSELP], BF16); k_ovs = din("k_ovs", [128, NKS, 1 + NSELS], BF16)
    k_rc = din("k_rc", [128, 2, 16]); k_bones = din("k_bones", [128, 128], BF16)
    k_cts = din("k_cts", [1 + NS, 128 + TS])

    y_p = dout("y_p", [T, D]); y_s = dout("y_s", [TS, D])
    kv_p = dout("kv_p", [DEPTH, T, 512]); kv_s = dout("kv_s", [DEPTH, TS, 512])
    win_p = dout("win_p", [DEPTH, WINT * 128, 256]); win_s = dout("win_s", [DEPTH, NS, 512, 256])
    conv_p = dout("conv_p", [DEPTH, 2, 256]); conv_s = dout("conv_s", [DEPTH, NS * 2, 256])
    pool_p = dout("pool_p", [DEPTH, 15, 256]); pool_s = dout("pool_s", [DEPTH, NS * 15, 256])

    import contextlib
    es = contextlib.ExitStack()
    with es:
        def sb(name, shape, dt=F32):
            return es.enter_context(nc.sbuf_tensor(name, list(shape), dt))

        X = sb("X", [128, NTT, D]); tX = [Tok() for _ in range(NTT)]
        GATE = sb("GATE", [128, 2, 2, D], BF16); tGATE = [[Tok(), Tok()], [Tok(), Tok()]]
        MODC = sb("MODC", [128, 4, 8, 1 + NS]); tMODC = Tok()
        IDF = sb("IDF", [128, 128]); IDB = sb("IDB", [128, 128], BF16)
        CAUS = sb("CAUS", [128, 128], BF16); ANTI = sb("ANTI", [128, 128], BF16)
        CMPB = sb("CMPB", [128, T], BF16); E64 = sb("E64", [128, 32, 128], BF16)
        FBP = sb("FBP", [128, NT, NSELP], BF16); FBS = sb("FBS", [TS, NSELS])
        NEWB = sb("NEWB", [128, NS, 4 * DS], BF16); WINB0 = sb("WINB0", [128, 4 * DS], BF16)
        RC16 = sb("RC16", [128, 2, 16]); BONES = sb("BONES", [128, 128], BF16)
        CTS = sb("CTS", [128, 8, 128 + TS], BF16)
        tK = Tok()
        PETB = sb("PETB", [128, 32], BF16); tPET = Tok()
        tKVP = [Tok() for _ in range(NT)]
        PH = sb("PH", [128, 4768])
        def phf(off, n):
            return PH[:, off:off + n]
        def phb(off, n):
            return PH[:, off:off + n].bitcast(BF16)
        SCR = sb("SCR", [128, 1056]); tSCR = Tok()
        SA = SCR[:].rearrange("p (a c) -> p a c", a=2); tSA = tSCR
        oA = [0]
        def takeA(n):
            o_ = oA[0]; oA[0] += n; return o_
        HT = phb(takeA(4 * GT), 4 * GT).rearrange("p (k c) -> p k c", k=8); tHT = Tok()
        _o = takeA(768); KVO = phf(_o, 768).rearrange("p (a c) -> p a c", a=1); tKVO = [Tok(), Tok()]
        OUTT = phf(_o, 256); tOUTT = tKVO[0]
        UG = phf(takeA(2 * (2 + GT)), 2 * (2 + GT)).rearrange("p (a c) -> p a c", a=2); tUG = Tok()
        PG = phf(takeA(2 * (16 + GT)), 2 * (16 + GT)).rearrange("p (a c) -> p a c", a=2)[:, :, 0:15 + GT]; tPG = Tok()
        YCP = phb(takeA(2 * GT), 2 * GT).rearrange("p (a c) -> p a c", a=4); tYCP = Tok()
        DPB = phb(takeA(GT // 2), GT // 2); tDPB = Tok()
        UGS = phf(takeA(80), 80).rearrange("p (a s c) -> p a s c", a=2, s=NS); PGS = phf(takeA(184), 184).rearrange("p (a s c) -> p a s c", a=2, s=NS); tUGS = Tok()
        JNK = phb(takeA(512), 512); tJNK = Tok()
        WST = phf(takeA(256), 256)[0:126]; tWST = Tok()
        CTF = phf(0, 1024)[0:1 + NS]; tCTF = Tok()
        PT = phb(0, 768).rearrange("p (a c) -> p a c", a=3); tPT = [Tok(), Tok(), Tok()]
        OACC = phf(768, 512); tOACC = Tok()
        OACS = phf(1280, 512)[0:TS]; tOACS = Tok()
        PTS = phb(1792, 768).rearrange("p (s i c) -> p s i c", s=NS, i=3); tPTS = [[Tok() for _ in range(3)] for _ in range(NS)]
        SC2 = phf(2560, 408).rearrange("p (a c) -> p a c", a=3); tSC2 = Tok()
        BT = phb(2968, 128).rearrange("p (a c) -> p a c", a=2); tBT = Tok()
        BTS = phb(3096, 64).rearrange("p (a g c) -> p a g c", a=2, g=2); tBTS = Tok()
        SH = phb(3160, 256); SQ = phb(3416, 256); tSH = Tok()
        OT = phb(3672, 256).rearrange("p (a c) -> p a c", a=4); tOT = Tok()
        VN0 = phb(3928, 66)[0:TS].rearrange("p (a c) -> p a c", a=2)[:, :, 0:65] if False else phb(3928, 65)[0:TS].rearrange("p (a c) -> p a c", a=2); tVN0 = Tok()
        IDXL = phf(3994, 2 * NS * NPG).bitcast(I32).rearrange("p (a c) -> p a c", a=2); tIDXL = Tok()
        IDXG = phf(4506, NS * NPG); tIDXG = Tok()
        UT = phb(0, 1024).rearrange("p (a c) -> p a c", a=4); tUT = [Tok(), Tok()]
        RSC = phf(1024, 1024).rearrange("p (a c) -> p a c", a=2); tRSC = [Tok(), Tok()]
        WA = sb("WA", [128, 24576], BF16); tWA = [Tok() for _ in range(8)]
        R = sb("R", [128, 17472], BF16)
        GSIG = sb("GSIG", [128, NTT, 24]); tGS = [Tok() for _ in range(NTT)]
        SPT = sb("SPT", [128, 80]); tSPT = Tok()
        KGB = sb("KGB", [128, 2, 128]); tKGB = Tok()
        ST1 = sb("ST1", [128, 64]); tST1 = Tok()
        QN = sb("QN", [128, 768], BF16); tQN = Tok()
        SPR = sb("SPR", [80, 128]); tSPR = Tok()
        BST = SCR[:, 0:512]; tBST = tSCR
        IDX = sb("IDX", [128, NS, NPG], I32); tIDX = Tok()
        PSB = [es.enter_context(nc.psum_tensor(f"ps{i}", [128, 512], F32)) for i in range(8)]
        tPS = [Tok(True) for _ in range(8)]
        psn = [0]
        NPSR = [5]

        def ps():
            i = psn[0] % NPSR[0]
            psn[0] += 1
            return PSB[i], tPS[i]

        WIN = WA[:, 0:8 * IN_W].rearrange("p (k c) -> p k c", k=8)
        WOCP = WA[:, 18688:18688 + 4096].rearrange("p (k c) -> p k c", k=4)
        tWIN, tWOCP = tWA[0], tWA[1]
        WOA = WA[:, 0:4096].rearrange("p (k c) -> p k c", k=4); tWOA = tWA[2]
        RAWT = WA[:, 4096:4096 + 16 * 513].rearrange("p (r m) -> p r m", r=16); tRAWT = tWA[3]
        KTC = WA[:, 4096:4096 + 512]
        STG = WA[:, 12304:12304 + 2 * 1152].rearrange("p (b c) -> p b c", b=2); tSTG = [tWA[4], tWA[5]]
        W1B = WA[:, 14608:14608 + 4096].rearrange("p (q c) -> p q c", q=32); tW1B = tWA[6]
        KCS = WA[:, 18704:18704 + (1 + NS) * 512].rearrange("p (s c) -> p s c", c=512); tKCS = tWA[7]
        VCS = WA[:, 21264:21264 + (1 + NS) * 512].rearrange("p (s k c) -> p s k c", k=4, c=128); tVCS = Tok()
        W2B = WA[:, 23824:23824 + 256].rearrange("p (a c) -> p a c", a=2); tW2B = Tok()
        PWB = WA[:, 24080:24080 + 256].rearrange("p (a c) -> p a c", a=2)
        QT = R[:, 0:4 * TT].rearrange("p (a t) -> p a t", a=4); tQT = Tok()
        o = 4 * TT
        KST = R[:, o:o + TT]; tKST = Tok(); o += TT
        KWT = R[:, o:o + TT]; tKWT = Tok(); o += TT
        VS = R[:, o:o + NTT * 130].rearrange("p (i g c) -> p i g c", g=2, c=65); tVS = Tok(); o += NTT * 130
        VW = R[:, o:o + NTT * 130].rearrange("p (i g c) -> p i g c", g=2, c=65); tVW = Tok(); o += NTT * 130
        OVP = R[:, o:o + 1 + NSELP]; o += 1 + NSELP + (1 + NSELP) % 2
        OVS = R[:, o:o + NKS * (1 + NSELS)].rearrange("p (k c) -> p k c", k=NKS); o += NKS * (1 + NSELS)
        tOV = Tok()
        assert o <= 17472, o
        H2T = R[:, 0:8 * TT].rearrange("p (k t) -> p k t", k=8); tH2T = Tok()
        WADA = R[:, 0:16384].rearrange("p (b k c) -> p b k c", b=4, k=8); tWADA = [Tok() for _ in range(4)]
        WUP = [WA[:, b * 12288:b * 12288 + 8192].rearrange("p (k c) -> p k c", k=8) for b in range(2)]
        WDN = [WA[:, b * 12288 + 8192:b * 12288 + 12288].rearrange("p (j c) -> p j c", j=4) for b in range(2)]
        tWF = [tWA[0], tWA[1]]

        allR = [tQT, tKST, tKWT, tVS, tVW, tOV, tH2T] + tWADA
        allWA = tWA + [tVCS, tW2B]

        def V(e):
            return e

        def ld(q, out, in_, w, r=()):
            S.dma(q, lambda e: e.dma_start(out=out, in_=in_), reads=list(r), writes=list(w))

        ld("sp", IDF[:], k_idf, [tK]); ld("sp", IDB[:], k_idb, [tK]); ld("sp", CAUS[:], k_caus, [tK])
        ld("sp", ANTI[:], k_anti, [tK]); ld("sp", CMPB[:], k_cmpb, [tK]); ld("sp", E64[:], k_e64, [tK])
        ld("sp", FBP[:], k_fbp, [tK]); ld("sp", FBS[:], k_fbs, [tK]); pass
        ld("sp", WINB0[:], k_winb0, [tK]); ld("sp", RC16[:], k_rc, [tK]); ld("sp", BONES[:], k_bones, [tK])
        for i in range(NT):
            ld("sp", X[:, i, :], x_p[i * 128:(i + 1) * 128, :], [tX[i]])
        ld("sp", X[0:TS, NT, :], x_s, [tX[NT]])
        ld("sp", CTF[:], c_all, [tCTF])
        S.dma("pool", lambda e: e.dma_start(out=IDX[:].rearrange("p s j -> p (s j)"),
                                            in_=ptab.rearrange("s j -> (s j)").partition_broadcast(128)), writes=[tIDX])
        PIO = sb("PIO", [128, 2], I32); IDXF = sb("IDXF", [128, NS * NPG])
        S.op("pool", lambda e: e.iota(PIO[:, 0:1], [[0, 1]], base=0, channel_multiplier=1), writes=[tSCR])
        S.op("dve", lambda e: e.tensor_copy(out=SCR[:, 0:1], in_=PIO[:, 0:1]), reads=[tSCR], writes=[tSCR])
        S.op("dve", lambda e: e.tensor_copy(out=IDXF[:], in_=IDX[:].rearrange("p s j -> p (s j)")), reads=[tIDX], writes=[tIDX])
        S.op("dve", lambda e: e.tensor_scalar(out=IDXF[:], in0=IDXF[:], scalar1=128.0, scalar2=SCR[:, 0:1], op0=ALU.mult, op1=ALU.add),
             reads=[tIDX, tSCR], writes=[tIDX])
        S.op("dve", lambda e: e.tensor_copy(out=IDX[:].rearrange("p s j -> p (s j)"), in_=IDXF[:]), reads=[tIDX], writes=[tIDX])
        S.op("pool", lambda e: e.memset(NEWB[:], 0.0), writes=[tK])
        ld("sp", NEWB[0:TS], k_newb, [tK])
        S.op("act", lambda e: e.activation(out=CTF[:], in_=CTF[:], func=AF.Silu), reads=[tCTF], writes=[tCTF])
        SEL = sb("SEL", [1 + NS, 128 + TS]); tSEL = Tok()
        ld("sp", SEL[:], k_cts, [tSEL])
        for k in range(8):
            pb, tp = ps()
            S.op("pe", lambda e, k=k, pb=pb: e.matmul(pb[:, 0:128 + TS], lhsT=CTF[:, k * 128:(k + 1) * 128], rhs=SEL[:],
                                                      start=True, stop=True), reads=[tCTF, tSEL], writes=[tp])
            S.op("dve", lambda e, k=k, pb=pb: e.tensor_copy(out=CTS[:, k, :], in_=pb[:, 0:128 + TS]), reads=[tp], writes=[tK])
        CT5 = sb("CT5", [128, 8, 1 + NS], BF16)
        S.op("dve", lambda e: e.tensor_copy(out=CT5[:, :, 0:1], in_=CTS[:, :, 0:1]), reads=[tK], writes=[tK])
        S.op("dve", lambda e: e.tensor_copy(out=CT5[:, :, 1:1 + NS], in_=CTS[:, :, 128:128 + TS:DS]), reads=[tK], writes=[tK])

        def rstd_from_ss(ss_ap, n, inv, tss):
            S.op("dve", lambda e: e.tensor_scalar(out=ss_ap, in0=ss_ap, scalar1=inv, scalar2=EPS, op0=ALU.mult, op1=ALU.add),
                 reads=[tss], writes=[tss])
            S.op("act", lambda e: e.activation(out=ss_ap, in_=ss_ap, func=AF.Sqrt), reads=[tss], writes=[tss])
            S.op("dve", lambda e: e.reciprocal(out=ss_ap, in_=ss_ap), reads=[tss], writes=[tss])

        def norm_to_HT(tiles, modS, modG, dest, tdest, doff):
            for j, i in enumerate(tiles):
                rows = 128 if i < NT else TS
                S.op("act", lambda e, i=i, rows=rows: e.activation(out=JNK[0:rows, :], in_=X[0:rows, i, :], func=AF.Square,
                                                                   accum_out=ST1[0:rows, 0:1]), reads=[tX[i]], writes=[tJNK, tST1])
                rstd_from_ss(ST1[0:rows, 0:1], 1, 1.0 / D, tST1)
                S.op("dve", lambda e, i=i, rows=rows: e.tensor_scalar(out=SCR[0:rows, 0:1024], in0=X[0:rows, i, :], scalar1=ST1[0:rows, 0:1],
                                                                      scalar2=None, op0=ALU.mult), reads=[tX[i], tST1], writes=[tSCR])
                for hf in range(2):
                    pb, tp = ps()
                    for kk in range(4):
                        k = hf * 4 + kk
                        S.op("pe", lambda e, k=k, kk=kk, pb=pb, rows=rows: e.transpose(pb[:, kk * 128:kk * 128 + rows], SCR[0:rows, k * 128:(k + 1) * 128],
                                                                                       IDF[0:rows, 0:rows]), reads=[tSCR, tK], writes=[tp])
                    for kk in range(4):
                        k = hf * 4 + kk
                        if i < NT:
                            eng = "act" if kk % 2 == 0 else "dve"
                            if eng == "act":
                                S.op("act", lambda e, k=k, kk=kk, pb=pb, j=j: e.activation(out=dest[:, k, doff + j * 128:doff + (j + 1) * 128], in_=pb[:, kk * 128:(kk + 1) * 128],
                                                                                           func=AF.Identity, scale=MODC[:, modG, k, 0:1], bias=MODC[:, modS, k, 0:1]),
                                     reads=[tp, tMODC], writes=[tdest])
                            else:
                                S.op("dve", lambda e, k=k, kk=kk, pb=pb, j=j: e.tensor_scalar(out=dest[:, k, doff + j * 128:doff + (j + 1) * 128], in0=pb[:, kk * 128:(kk + 1) * 128],
                                                                                              scalar1=MODC[:, modG, k, 0:1], scalar2=MODC[:, modS, k, 0:1], op0=ALU.mult, op1=ALU.add),
                                     reads=[tp, tMODC], writes=[tdest])
                        else:
                            for s in range(NS):
                                S.op("dve", lambda e, k=k, kk=kk, pb=pb, j=j, s=s: e.tensor_scalar(
                                    out=dest[:, k, doff + j * 128 + s * DS:doff + j * 128 + (s + 1) * DS], in0=pb[:, kk * 128 + s * DS:kk * 128 + (s + 1) * DS],
                                    scalar1=MODC[:, modG, k, 1 + s:2 + s], scalar2=MODC[:, modS, k, 1 + s:2 + s], op0=ALU.mult, op1=ALU.add),
                                     reads=[tp, tMODC], writes=[tdest])

        def resid_add(i, pb, tp, half, gi):
            rows = 128 if i < NT else TS
            gsel = 0 if i < NT else 1
            S.op("dve", lambda e: e.tensor_tensor(out=SCR[0:rows, 0:512], in0=pb[0:rows, :], in1=GATE[0:rows, gi, gsel, half * 512:(half + 1) * 512], op=ALU.mult),
                 reads=[tp, tGATE[gi][gsel]], writes=[tSCR])
            S.op("dve", lambda e: e.tensor_tensor(out=X[0:rows, i, half * 512:(half + 1) * 512], in0=X[0:rows, i, half * 512:(half + 1) * 512], in1=SCR[0:rows, 0:512], op=ALU.add),
                 reads=[tSCR, tX[i]], writes=[tX[i]])

        def cast_load(out, in_, w, r=()):
            S.dma("pool", lambda e: e.dma_start(out=out, in_=in_), reads=list(r), writes=list(w))

        def LAYER_BODY(l):
            S.barrier()
            MARKS.append(("A", l, dict(S.cnt)))
            for hh in range(2):
                cast_load(WIN[:, :, hh * 1164:(hh + 1) * 1164], w_in[l, :, hh * 1164:(hh + 1) * 1164].rearrange("(k p) c -> p k c", p=128), [tWIN] + allWA)
            cast_load(WOCP, w_out[l, 512:1024, :].rearrange("(k p) c -> p k c", p=128), [tWOCP])
            cast_load(PWB, pwbd[l].rearrange("a p c -> p a c"), [tW2B])
            S.op("pool", lambda e: e.memset(VS[:, :, :, 64:65], 1.0), writes=[tVS] + tWADA)
            S.op("pool", lambda e: e.memset(VW[:, :, :, 64:65], 1.0), writes=[tVW])
            ld("sp", OVP, k_ovp, [tOV]); ld("sp", OVS, k_ovs, [tOV])
            groups = [list(range(g * GTL, min(NT, g * GTL + GTL))) for g in range(NG)] + [[NT]]
            for tiles in groups:
                samp = tiles[0] == NT
                ncol = TS if samp else 128 * len(tiles)
                norm_to_HT(tiles, 0, 1, HT, tHT, 0)
                for j, i in enumerate(tiles):
                    rows = TS if samp else 128
                    c0 = j * 128
                    pq, tq = ps()
                    for k in range(8):
                        S.op("pe", lambda e, k=k, pq=pq, c0=c0, rows=rows: e.matmul(pq[0:rows, :], lhsT=HT[:, k, c0:c0 + rows], rhs=WIN[:, k, 0:512],
                                                                                  start=(k == 0), stop=(k == 7)), reads=[tHT, tWIN], writes=[tq])
                    pk, tk = ps()
                    for k in range(8):
                        S.op("pe", lambda e, k=k, pk=pk, c0=c0, rows=rows: e.matmul(pk[0:rows, :], lhsT=HT[:, k, c0:c0 + rows], rhs=WIN[:, k, 512:1024],
                                                                                  start=(k == 0), stop=(k == 7)), reads=[tHT, tWIN], writes=[tk])
                    pw, tw = ps()
                    for k in range(8):
                        S.op("pe", lambda e, k=k, pw=pw, c0=c0, rows=rows: e.matmul(pw[0:rows, 0:280], lhsT=HT[:, k, c0:c0 + rows], rhs=WIN[:, k, 1024:1304],
                                                                                  start=(k == 0), stop=(k == 7)), reads=[tHT, tWIN], writes=[tw])
                    kb = 0
                    KV = KVO[:, kb, :]
                    S.op("act", lambda e, pk=pk, KV=KV, rows=rows: e.activation(out=KV[0:rows, 0:256], in_=pk[0:rows, 0:256], func=AF.Identity), reads=[tk], writes=[tKVO[kb]])
                    S.op("act", lambda e, pk=pk, KV=KV, rows=rows: e.activation(out=KV[0:rows, 384:512], in_=pk[0:rows, 384:512], func=AF.Identity), reads=[tk], writes=[tKVO[kb]])
                    S.op("act", lambda e, pw=pw, KV=KV, rows=rows: e.activation(out=KV[0:rows, 640:768], in_=pw[0:rows, 128:256], func=AF.Identity), reads=[tw], writes=[tKVO[kb]])
                    S.op("act", lambda e, pq=pq, rows=rows: e.activation(out=SCR[0:rows, 0:512], in_=pq[0:rows, :], func=AF.Square), reads=[tq], writes=[tSCR])
                    S.op("act", lambda e, pk=pk, rows=rows: e.activation(out=SCR[0:rows, 512:640], in_=pk[0:rows, 256:384], func=AF.Square), reads=[tk], writes=[tSCR])
                    S.op("act", lambda e, pw=pw, rows=rows: e.activation(out=SCR[0:rows, 640:768], in_=pw[0:rows, 0:128], func=AF.Square), reads=[tw], writes=[tSCR])
                    S.op("dve", lambda e, rows=rows: e.tensor_reduce(out=ST1[0:rows, 0:12], in_=SCR[0:rows, 0:768].rearrange("p (h d) -> p h d", d=64), axis=AX.X, op=ALU.add),
                         reads=[tSCR], writes=[tST1])
                    rstd_from_ss(ST1[0:rows, 0:12], 12, 1.0 / 64, tST1)
                    S.op("dve", lambda e, pq=pq, rows=rows: e.tensor_tensor(
                        out=QN[0:rows, 0:512].rearrange("t (p a d) -> t a p d", a=2, p=4), in0=pq[0:rows, :].rearrange("t (a p d) -> t a p d", a=2, p=4),
                        in1=ST1[0:rows, 0:8].rearrange("t (a p) -> t a p", a=2).unsqueeze(3).to_broadcast([rows, 2, 4, 64]), op=ALU.mult), reads=[tq, tST1], writes=[tQN])
                    for (src, so, col, do, gi2) in ((pk, 256, 8, 256, 0), (pw, 0, 10, 512, 1)):
                        tsrc = tk if src is pk else tw
                        S.op("dve", lambda e, src=src, so=so, col=col, do=do, KV=KV, rows=rows: e.tensor_tensor(
                            out=KV[0:rows, do:do + 128].rearrange("t (g d) -> t g d", g=2), in0=src[0:rows, so:so + 128].rearrange("t (g d) -> t g d", g=2),
                            in1=ST1[0:rows, col:col + 2].unsqueeze(2).to_broadcast([rows, 2, 64]), op=ALU.mult), reads=[tsrc, tST1], writes=[tKVO[kb]])
                        S.op("dve", lambda e, do=do, gi2=gi2, KV=KV, rows=rows: e.tensor_tensor(out=KV[0:rows, do:do + 128], in0=KV[0:rows, do:do + 128], in1=KGB[0:rows, gi2, :], op=ALU.mult),
                             reads=[tKGB], writes=[tKVO[kb]])
                    if samp:
                        S.dma("sp", lambda e, KV=KV: e.dma_start(out=kv_s[l], in_=KV[0:TS, 0:512]), reads=[tKVO[kb]])
                        for s in range(NS):
                            S.dma("sp", lambda e, KV=KV, s=s: e.dma_start(out=win_s[l, s, 512 - DS:512, :], in_=KV[s * DS:(s + 1) * DS, 512:768]), reads=[tKVO[kb]])
                            for a_ in range(4):
                                S.dma("sp", lambda e, s=s, a_=a_: e.dma_start(out=WST[:, :], in_=st_win[l, s, DS:512, :].rearrange("(p a) c -> p a c", a=4)[:, a_, :]), writes=[tWST])
                                S.dma("sp", lambda e, s=s, a_=a_: e.dma_start(out=win_s[l, s, 0:512 - DS, :].rearrange("(p a) c -> p a c", a=4)[:, a_, :], in_=WST[:, :]), reads=[tWST])
                    else:
                        S.dma("sp", lambda e, KV=KV, i=i: e.dma_start(out=kv_p[l, i * 128:(i + 1) * 128, :], in_=KV[:, 0:512]), reads=[tKVO[kb]], writes=[tKVP[i]])
                        if i >= NT - WINT:
                            S.dma("sp", lambda e, KV=KV, i=i: e.dma_start(out=win_p[l, (i - NT + WINT) * 128:(i - NT + WINT + 1) * 128, :], in_=KV[:, 512:768]), reads=[tKVO[kb]])
                    S.op("act", lambda e, pw=pw, i=i, rows=rows: e.activation(out=GSIG[0:rows, i, :], in_=pw[0:rows, 256:280], func=AF.Exp, scale=-1.0), reads=[tw], writes=[tGS[i]])
                    S.op("dve", lambda e, i=i, rows=rows: e.tensor_scalar(out=GSIG[0:rows, i, :], in0=GSIG[0:rows, i, :], scalar1=1.0, scalar2=None, op0=ALU.add), reads=[tGS[i]], writes=[tGS[i]])
                    S.op("dve", lambda e, i=i, rows=rows: e.reciprocal(out=GSIG[0:rows, i, :], in_=GSIG[0:rows, i, :]), reads=[tGS[i]], writes=[tGS[i]])
                    S.op("act", lambda e, KV=KV, i=i, rows=rows: e.activation(out=VS[0:rows, i, :, 0:64], in_=KV[0:rows, 384:512].rearrange("t (g d) -> t g d", g=2), func=AF.Identity),
                         reads=[tKVO[kb]], writes=[tVS])
                    S.op("act", lambda e, KV=KV, i=i, rows=rows: e.activation(out=VW[0:rows, i, :, 0:64], in_=KV[0:rows, 640:768].rearrange("t (g d) -> t g d", g=2), func=AF.Identity),
                         reads=[tKVO[kb]], writes=[tVW])
                    S.op("act", lambda e, KV=KV, rows=rows: e.activation(out=QN[0:rows, 512:640], in_=KV[0:rows, 256:384], func=AF.Identity), reads=[tKVO[kb]], writes=[tQN])
                    S.op("act", lambda e, KV=KV, rows=rows: e.activation(out=QN[0:rows, 640:768], in_=KV[0:rows, 512:640], func=AF.Identity), reads=[tKVO[kb]], writes=[tQN])
                    pt_, tt_ = ps()
                    ptv = pt_[:].bitcast(BF16)
                    for b6 in range(6):
                        S.op("pe", lambda e, b6=b6, ptv=ptv, rows=rows: e.transpose(ptv[:, b6 * 128:b6 * 128 + rows], QN[0:rows, b6 * 128:(b6 + 1) * 128], IDB[0:rows, 0:rows]),
                             reads=[tQN, tK], writes=[tt_])
                    tc0 = i * 128
                    S.op("dve", lambda e, ptv=ptv, tc0=tc0, rows=rows: e.tensor_scalar(out=QT[:, :, tc0:tc0 + rows], in0=ptv[:, 0:512].rearrange("p (a t) -> p a t", a=4)[:, :, 0:rows],
                                                                                       scalar1=SPT[:, 74:75], scalar2=None, op0=ALU.mult), reads=[tt_, tSPT], writes=[tQT])
                    S.op("act", lambda e, ptv=ptv, tc0=tc0, rows=rows: e.activation(out=KST[:, tc0:tc0 + rows], in_=ptv[:, 512:512 + rows], func=AF.Identity), reads=[tt_], writes=[tKST])
                    S.op("act", lambda e, ptv=ptv, tc0=tc0, rows=rows: e.activation(out=KWT[:, tc0:tc0 + rows], in_=ptv[:, 640:640 + rows], func=AF.Identity), reads=[tt_], writes=[tKWT])


                n = ncol
                if samp:
                    S.dma("sp", lambda e: e.dma_start(out=OUTT[0:NS * 17, :], in_=st_cp[l]), writes=[tOUTT])
                    for c in range(2):
                        ph, th = ps()
                        S.op("pe", lambda e, c=c, ph=ph: e.transpose(ph[:, 0:NS * 17], OUTT[0:NS * 17, c * 128:(c + 1) * 128], IDF[0:NS * 17, 0:NS * 17]), reads=[tOUTT, tK], writes=[th])
                        S.op("dve", lambda e, c=c, ph=ph: e.tensor_copy(out=PGS[:, c, :, 0:15], in_=ph[:, 0:NS * 15].rearrange("p (s x) -> p s x", s=NS)), reads=[th], writes=[tUGS])
                        S.op("dve", lambda e, c=c, ph=ph: e.tensor_copy(out=UGS[:, c, :, 0:2], in_=ph[:, NS * 15:NS * 17].rearrange("p (s x) -> p s x", s=NS)), reads=[th], writes=[tUGS])
                    Uv = lambda c, a, b: UGS[:, c, :, a:b]
                    Pv = lambda c, a, b: PGS[:, c, :, a:b]
                    Sv = lambda sl_, a, b: SA[:, sl_, 0:NS * 23].rearrange("p (s x) -> p s x", s=NS)[:, :, a:b]
                    psv = lambda pb_: pb_[:, 0:TS].rearrange("p (s t) -> p s t", s=NS)
                    Yv = lambda k_: YCP[:, k_, 0:TS].rearrange("p (s t) -> p s t", s=NS)
                    Dv = lambda: DPB[:, 0:TS].rearrange("p (s t) -> p s t", s=NS)
                    nn = DS; tU = tUGS; tP = tUGS
                else:
                    Uv = lambda c, a, b: UG[:, c, a:b]
                    Pv = lambda c, a, b: PG[:, c, a:b]
                    Sv = lambda sl_, a, b: SA[:, sl_, a:b]
                    psv = lambda pb_: pb_[:, 0:n]
                    Yv = lambda k_: YCP[:, k_, 0:n]
                    Dv = lambda: DPB[:, 0:n]
                    nn = n; tU = tUG; tP = tPG
                    if tiles[0] == 0:
                        S.op("pool", lambda e: e.memset(UG[:, :, 0:2], 0.0), writes=[tUG])
                        S.op("pool", lambda e: e.memset(PG[:, :, 0:15], 0.0), writes=[tPG])

                def zT(cc):
                    pz, tz = ps()
                    for k in range(8):
                        S.op("pe", lambda e, k=k, pz=pz, cc=cc: e.matmul(pz[:, 0:n], lhsT=WIN[:, k, 1304 + cc * 128:1304 + (cc + 1) * 128], rhs=HT[:, k, 0:n],
                                                                       start=(k == 0), stop=(k == 7)), reads=[tWIN, tHT], writes=[tz])
                    return pz, tz

                hp = lambda ap, lo, hi: ap[lo:hi]
                for c in range(2):
                    pcg, tcg = zT(2 + c)
                    phn, thn = zT(4 + c)
                    S.op("act", lambda e, pcg=pcg: e.activation(out=Sv(0, 0, nn), in_=psv(pcg), func=AF.Identity), reads=[tcg], writes=[tSA])
                    S.op("dve", lambda e, phn=phn, c=c: e.tensor_tensor(out=Uv(c, 2, 2 + nn), in0=Sv(0, 0, nn), in1=psv(phn), op=ALU.mult), reads=[tSA, thn], writes=[tU])
                    S.op("dve", lambda e, c=c: e.tensor_scalar(out=Sv(1, 0, nn), in0=Uv(c, 2, 2 + nn), scalar1=SPT[:, 68 + c:69 + c], scalar2=SPT[:, 70 + c:71 + c], op0=ALU.mult, op1=ALU.add),
                         reads=[tU, tSPT], writes=[tSA])
                    S.op("dve", lambda e, c=c: e.scalar_tensor_tensor(out=Sv(1, 0, nn), in0=Uv(c, 1, 1 + nn), scalar=SPT[:, 66 + c:67 + c], in1=Sv(1, 0, nn), op0=ALU.mult, op1=ALU.add),
                         reads=[tU, tSPT, tSA], writes=[tSA])
                    S.op("dve", lambda e, c=c: e.scalar_tensor_tensor(out=Sv(1, 0, nn), in0=Uv(c, 0, nn), scalar=SPT[:, 64 + c:65 + c], in1=Sv(1, 0, nn), op0=ALU.mult, op1=ALU.add),
                         reads=[tU, tSPT, tSA], writes=[tSA])
                    pbg, tbg = zT(c)
                    S.op("dve", lambda e, pbg=pbg, c=c: e.tensor_tensor(out=Yv(c), in0=Sv(1, 0, nn), in1=psv(pbg), op=ALU.mult), reads=[tSA, tbg], writes=[tYCP])
                    ppi, tpi = zT(6 + c)
                    S.op("act", lambda e, ppi=ppi, c=c: e.activation(out=Pv(c, 15, 15 + nn), in_=psv(ppi), func=AF.Identity), reads=[tpi], writes=[tP])
                    L = 15 + nn
                    S.op("dve", lambda e, c=c: e.tensor_tensor(out=Sv(0, 1, L), in0=Pv(c, 1, L), in1=Pv(c, 0, L - 1), op=ALU.add), reads=[tP], writes=[tSA])
                    S.op("dve", lambda e, c=c: e.tensor_tensor(out=Sv(1, 3, L), in0=Sv(0, 3, L), in1=Sv(0, 1, L - 2), op=ALU.add), reads=[tSA], writes=[tSA])
                    if c == 0:
                        lo_s, lo_w, hi_s, hi_w = 0, 2.0, 1, 4.0
                    else:
                        S.op("dve", lambda e: e.tensor_tensor(out=Sv(0, 7, L), in0=Sv(1, 7, L), in1=Sv(1, 3, L - 4), op=ALU.add), reads=[tSA], writes=[tSA])
                        S.op("dve", lambda e: e.tensor_tensor(out=Sv(1, 15, L), in0=Sv(0, 15, L), in1=Sv(0, 7, L - 8), op=ALU.add), reads=[tSA], writes=[tSA])
                        lo_s, lo_w, hi_s, hi_w = 0, 8.0, 1, 16.0
                    for (plo, phi, ssl, ww) in ((0, 64, lo_s, lo_w), (64, 128, hi_s, hi_w)):
                        S.op("dve", lambda e, plo=plo, phi=phi, ssl=ssl, ww=ww, c=c: e.scalar_tensor_tensor(out=Dv()[plo:phi], in0=Sv(ssl, 15, L)[plo:phi], scalar=1.0 / ww, in1=Pv(c, 15, L)[plo:phi],
                                                                                                       op0=ALU.mult, op1=ALU.subtract), reads=[tSA, tP], writes=[tDPB])
                        if (not samp) and tiles[0] == 0:
                            S.op("dve", lambda e, plo=plo, phi=phi, ssl=ssl, c=c: e.tensor_tensor(out=ST1[plo:phi, 16:32], in0=SA[plo:phi, ssl, 15:31], in1=RC16[plo:phi, c, :], op=ALU.mult),
                                 reads=[tSA, tK], writes=[tST1])
                            S.op("dve", lambda e, plo=plo, phi=phi, c=c: e.tensor_tensor(out=DPB[plo:phi, 0:16], in0=ST1[plo:phi, 16:32], in1=PG[plo:phi, c, 15:31], op=ALU.subtract),
                                 reads=[tST1, tP], writes=[tDPB])
                    py, ty = ps()
                    S.op("pe", lambda e, py=py, c=c: e.matmul(py[:, 0:n], lhsT=PWB[:, c, :], rhs=DPB[:, 0:n], start=True, stop=True), reads=[tDPB, tW2B], writes=[ty])
                    S.op("dve", lambda e, py=py, c=c: e.tensor_scalar(out=YCP[:, 2 + c, 0:n], in0=py[:, 0:n], scalar1=SPT[:, 72 + c:73 + c], scalar2=None, op0=ALU.mult), reads=[ty, tSPT], writes=[tYCP])
                    last = samp or tiles[-1] == NT - 1
                    if not samp and not last:
                        S.op("pool", lambda e, c=c: e.tensor_copy(out=UG[:, c, 0:2], in_=UG[:, c, nn:nn + 2]), reads=[tUG], writes=[tUG])
                        S.op("pool", lambda e, c=c: e.tensor_copy(out=PG[:, c, 0:15], in_=PG[:, c, nn:nn + 15]), reads=[tPG], writes=[tPG])
                    if last:
                        nq = NS if samp else 1
                        pst, tst = ps()
                        if samp:
                            S.op("dve", lambda e, c=c: e.tensor_copy(out=SA[:, 0, 0:NS * 15].rearrange("p (s x) -> p s x", s=NS), in_=PGS[:, c, :, DS:DS + 15]), reads=[tUGS], writes=[tSA])
                            S.op("dve", lambda e, c=c: e.tensor_copy(out=SA[:, 0, NS * 15:NS * 17].rearrange("p (s x) -> p s x", s=NS), in_=UGS[:, c, :, DS:DS + 2]), reads=[tUGS], writes=[tSA])
                        else:
                            S.op("dve", lambda e, c=c: e.tensor_copy(out=SA[:, 0, 0:15], in_=PG[:, c, nn:nn + 15]), reads=[tPG], writes=[tSA])
                            S.op("dve", lambda e, c=c: e.tensor_copy(out=SA[:, 0, 15:17], in_=UG[:, c, nn:nn + 2]), reads=[tUG], writes=[tSA])
                        S.op("pe", lambda e, pst=pst, nq=nq: e.transpose(pst[0:nq * 17, 0:128], SA[:, 0, 0:nq * 17], IDF[:]), reads=[tSA, tK], writes=[tst])
                        S.op("dve", lambda e, pst=pst, nq=nq, c=c: e.tensor_copy(out=OUTT[0:nq * 17, c * 128:(c + 1) * 128], in_=pst[0:nq * 17, 0:128]), reads=[tst], writes=[tOUTT])
                if samp or tiles[-1] == NT - 1:
                    nq = NS if samp else 1
                    S.dma("sp", lambda e, nq=nq: e.dma_start(out=(pool_s if samp else pool_p)[l], in_=OUTT[0:nq * 15, :]), reads=[tOUTT])
                    S.dma("sp", lambda e, nq=nq: e.dma_start(out=(conv_s if samp else conv_p)[l], in_=OUTT[nq * 15:nq * 17, :]), reads=[tOUTT])
                for j, i in enumerate(tiles):
                    rows = TS if samp else 128
                    for half in range(2):
                        po, to = ps()
                        for k in range(4):
                            S.op("pe", lambda e, k=k, po=po, j=j, rows=rows, half=half: e.matmul(po[0:rows, :], lhsT=YCP[:, k, j * 128:j * 128 + rows], rhs=WOCP[:, k, half * 512:(half + 1) * 512],
                                                                                               start=(k == 0), stop=(k == 3)), reads=[tYCP, tWOCP], writes=[to])
                        resid_add(i, po, to, half, 0)


            ACC = [(PSB[3], tPS[3]), (PSB[4], tPS[4])]
            SACC = [(PSB[5], tPS[5]), (PSB[6], tPS[6]), (PSB[7], tPS[7])]
            accn = [0]
            saccn = [0]
            NPSR[0] = 3

            def acc():
                a = ACC[accn[0] % 2]
                accn[0] += 1
                return a

            def sacc():
                a = SACC[saccn[0] % 3]
                saccn[0] += 1
                return a
            PEND = [None]

            def push(fn):
                old = PEND[0]
                PEND[0] = fn
                if old is not None:
                    old()

            def flush():
                old = PEND[0]
                PEND[0] = None
                if old is not None:
                    old()
            ptn = [0]

            def ptbuf():
                i_ = ptn[0] % 3
                ptn[0] += 1
                return PT[:, i_, :], tPT[i_]

            S.barrier()
            MARKS.append(("B", l, dict(S.cnt)))
            S.op("pool", lambda e: e.memset(PTS[:], 0.0), writes=[t for r_ in tPTS for t in r_])
            S.op("pool", lambda e: e.memset(BTS[:], 0.0), writes=[tBTS])
            S.op("pool", lambda e: e.memset(BT[:], 0.0), writes=[tBT])
            S.op("pool", lambda e: e.memset(VN0[:, :, 0:1], 1.0), writes=[tVN0])
            cast_load(WOA, w_out[l, 0:512, :].rearrange("(k p) c -> p k c", p=128), [tWOA] + allWA)
            cast_load(W2B, w2bd[l].rearrange("a p c -> p a c"), [tW2B])

            def compress(slot, seq, nblk, loader, npages):
                for pg0 in range(0, npages, 8):
                    npc = min(8, npages - pg0)
                    b = (pg0 // 8) % 2
                    stg = STG[:, b, 0:1024].rearrange("p (j c) -> p j c", c=128)
                    loader(stg, tSTG[b], pg0, npc)
                    pt_, tt_ = ps()
                    ptv = pt_[:].bitcast(BF16)
                    for j in range(npc):
                        S.op("pe", lambda e, j=j: e.transpose(ptv[:, j * 128:(j + 1) * 128], stg[:, j, :], IDB[:]), reads=[tSTG[b], tK], writes=[tt_])
                    S.op("dve", lambda e: e.tensor_copy(out=RAWT[:, :, pg0 * 8:(pg0 + npc) * 8], in_=ptv[:, 0:npc * 128].rearrange("p (m r) -> p r m", r=16)),
                         reads=[tt_], writes=[tRAWT])
                    yield 8
                ph, th = ps()
                for p in range(32):
                    S.op("pe", lambda e, p=p: e.matmul(ph[:, 0:nblk], lhsT=W1B[:, p, :], rhs=RAWT[:, p % 16, p // 16:p // 16 + nblk], start=(p == 0), stop=(p == 31)),
                         reads=[tW1B, tRAWT], writes=[th])
                S.op("act", lambda e: e.activation(out=SH[:, 0:nblk], in_=ph[:, 0:nblk], func=AF.Silu, bias=ST1[:, 32:33]), reads=[th, tST1], writes=[tSH])
                if slot == 0:
                    p2, t2 = ps()
                    S.op("pe", lambda e: e.matmul(p2[:, 0:nblk], lhsT=W2B[:, 0, :], rhs=SH[:, 0:nblk], start=True, stop=True), reads=[tSH, tW2B], writes=[t2])
                    S.op("act", lambda e: e.activation(out=SQ[:, 0:nblk], in_=p2[:, 0:nblk], func=AF.Square), reads=[t2], writes=[tSH])
                    p3, t3 = ps()
                    S.op("pe", lambda e: e.matmul(p3[:, 0:nblk], lhsT=BONES[:], rhs=SQ[:, 0:nblk], start=True, stop=True), reads=[tSH, tK], writes=[t3])
                    S.op("dve", lambda e: e.tensor_scalar(out=SCR[:, 0:nblk], in0=p3[:, 0:nblk], scalar1=1.0 / 64, scalar2=EPS, op0=ALU.mult, op1=ALU.add), reads=[t3], writes=[tSCR])
                    S.op("act", lambda e: e.activation(out=SCR[:, 0:nblk], in_=SCR[:, 0:nblk], func=AF.Sqrt), reads=[tSCR], writes=[tSCR])
                    S.op("dve", lambda e: e.reciprocal(out=SCR[:, 0:nblk], in_=SCR[:, 0:nblk]), reads=[tSCR], writes=[tSCR])
                    S.op("dve", lambda e: e.scalar_tensor_tensor(out=KCS[:, seq, 0:nblk], in0=p2[:, 0:nblk], scalar=SPT[:, 75:76], in1=SCR[:, 0:nblk], op0=ALU.mult, op1=ALU.mult),
                         reads=[t2, tSPT, tSCR], writes=[tKCS])
                else:
                    for kt in range((nblk + 127) // 128):
                        nb_ = min(128, nblk - kt * 128)
                        p2, t2 = ps()
                        S.op("pe", lambda e, kt=kt, nb_=nb_: e.matmul(p2[0:nb_, 0:128], lhsT=SH[:, kt * 128:kt * 128 + nb_], rhs=W2B[:, 1, :], start=True, stop=True),
                             reads=[tSH, tW2B], writes=[t2])
                        S.op("act", lambda e, kt=kt, nb_=nb_: e.activation(out=VCS[0:nb_, seq, kt, :], in_=p2[0:nb_, 0:128], func=AF.Identity), reads=[t2], writes=[tVCS])

            def mk_idxl(which, slot):
                S.op("dve", lambda e: e.tensor_scalar(out=IDXG[:], in0=IDXF[:], scalar1=float(l * NPOOL * 128), scalar2=4.0, op0=ALU.add, op1=ALU.mult), reads=[tIDX], writes=[tIDXG])
                S.op("dve", lambda e: e.tensor_scalar(out=IDXG[:], in0=IDXG[:], scalar1=float(slot), scalar2=None, op0=ALU.add), reads=[tIDXG], writes=[tIDXG])
                S.op("dve", lambda e: e.tensor_copy(out=IDXL[:, which, :], in_=IDXG[:]), reads=[tIDXG], writes=[tIDXL])

            def mk_idxl2():
                S.op("dve", lambda e: e.tensor_scalar(out=IDXG[:], in0=IDXF[:], scalar1=float(l * NPOOL * 128), scalar2=2.0, op0=ALU.add, op1=ALU.mult), reads=[tIDX], writes=[tIDXG])
                S.op("dve", lambda e: e.tensor_scalar(out=IDXG[:], in0=IDXG[:], scalar1=1.0, scalar2=None, op0=ALU.add), reads=[tIDXG], writes=[tIDXG])
                S.op("dve", lambda e: e.tensor_copy(out=IDXL[:, 1, :], in_=IDXG[:]), reads=[tIDXG], writes=[tIDXL])

            def gather2(out_ap, sq, pg, wtok, join=False):
                S.dma("pool", lambda e: e.indirect_dma_start(out=out_ap, out_offset=None, in_=cacheR2,
                                                             in_offset=bass.IndirectOffsetOnAxis(ap=IDXL[:, 1, sq * NPG + pg:sq * NPG + pg + 1], axis=0)),
                      reads=[tIDXL], writes=[wtok], join=join)

            def gather(out_ap, which, sq, pg, wtok, join=False):
                S.dma("pool", lambda e: e.indirect_dma_start(out=out_ap, out_offset=None, in_=cacheR,
                                                             in_offset=bass.IndirectOffsetOnAxis(ap=IDXL[:, which, sq * NPG + pg:sq * NPG + pg + 1], axis=0)),
                      reads=[tIDXL], writes=[wtok], join=join)

            def sample_loader(which, slot, sq):
                def f(stg, tstg, pg0, npc):
                    for j in range(npc):
                        gather(stg[:, j, :], which, sq, pg0 + j, tstg, join=(j > 0))
                return f

            def prompt_loader(slot):
                def f(stg, tstg, pg0, npc):
                    S.dma("pool", lambda e: e.dma_start(out=stg[:, 0:npc, :], in_=kv_p[l, pg0 * 128:(pg0 + npc) * 128, slot * 128:(slot + 1) * 128].rearrange("(j p) c -> p j c", p=128)),
                          reads=tKVP[pg0:pg0 + npc], writes=[tstg])
                return f

            def load_w1(slot):
                for hh in range(2):
                    cast_load(W1B[:, hh * 16:(hh + 1) * 16, :], w1bd[l, slot, :, hh * 16:(hh + 1) * 16, :], [tW1B])
                cast_load(PETB[:], pet[l, slot], [tPET])
                pbias, tbias = ps()
                for p in range(32):
                    S.op("pe", lambda e, p=p: e.matmul(pbias[:, 0:2], lhsT=W1B[:, p, :], rhs=PETB[:, p:p + 1].to_broadcast([128, 2]), start=(p == 0), stop=(p == 31)),
                         reads=[tW1B, tPET], writes=[tbias])
                S.op("dve", lambda e: e.tensor_copy(out=ST1[:, 32:33], in_=pbias[:, 0:1]), reads=[tbias], writes=[tST1])

            for slot in range(2):
                load_w1(slot)
                for _ in compress(slot, 0, NCP, prompt_loader(slot), NT):
                    pass

            def sample_gen():
                for slot in range(2):
                    load_w1(slot)
                    mk_idxl(0, slot)
                    yield 2
                    for sq in range(NS):
                        yield from compress(slot, 1 + sq, NCS, sample_loader(0, slot, sq), NPG)
                yield from sample_attn()

            def sample_attn():
                MARKS.append(("Bs", l, dict(S.cnt)))
                S.op("pool", lambda e: e.memset(OACS[:], 0.0), writes=[tOACS])
                S.op("dve", lambda e: e.tensor_copy(out=VN0[:, 0, 1:65], in_=VS[0:TS, NT, 0, 0:64]), reads=[tVS], writes=[tVN0])
                S.op("dve", lambda e: e.tensor_copy(out=VN0[:, 1, 1:65], in_=VW[0:TS, NT, 0, 0:64]), reads=[tVW], writes=[tVN0])
                mk_idxl2()
                for b in range(2):
                    stgv = STG[:, b, :].rearrange("p (j c) -> p j c", c=288)
                    S.op("pool", lambda e, stgv=stgv: e.memset(stgv[:, :, 256:288], 1.0), writes=[tSTG[b]])
                ptsn = [0]
                NBK = 2 * NPG
                CS = 65 + NSELS
                import os
                for sq in range(NS if not os.environ.get('NOSAMP') else 0):
                    sc = slice(sq * DS, (sq + 1) * DS)
                    qcs = slice(T + sq * DS, T + (sq + 1) * DS)

                    def pts_next():
                        i_ = ptsn[0] % 3
                        ptsn[0] += 1
                        return PTS[:, sq, i_, :].rearrange("p (h t) -> p h t", h=4), tPTS[sq][i_]

                    def score_tile(pS, tS_, nk, lhsT_k, tk_, g, bias=None):
                        gp_ = slice(64 * g, 64 * g + 64)
                        S.op("pe", lambda e: e.matmul(pS[0:nk, 0:4 * DS], lhsT=lhsT_k, rhs=QT[gp_, :, qcs], start=True, stop=(bias is None)), reads=[tk_, tQT], writes=[tS_])
                        if bias is not None:
                            bl, br_, tb_ = bias
                            S.op("pe", lambda e: e.matmul(pS[0:nk, 0:4 * DS], lhsT=bl, rhs=br_, start=False, stop=True), reads=[tK, tb_], writes=[tS_])
                        pts, tpts = pts_next()
                        S.op("act", lambda e: e.activation(out=pts[0:nk, :, sc], in_=pS[0:nk, 0:4 * DS].rearrange("p (h t) -> p h t", h=4), func=AF.Exp, scale=SCALE), reads=[tS_], writes=[tpts])
                        return pts, tpts

                    for g in range(2):
                        gp = slice(64 * g, 64 * g + 64)
                        accs = [sacc(), sacc()]
                        for kt in range(NKS):
                            nk = min(128, NCS - kt * 128)
                            pS, tS_ = ps()
                            pts, tpts = score_tile(pS, tS_, nk, KCS[gp, 1 + sq, kt * 128:kt * 128 + nk], tKCS, g)
                            def fin_scmp(kt=kt, nk=nk, pts=pts, tpts=tpts):
                              for h in range(4):
                                  ac, tac = accs[h // 2]
                                  c0_ = (h % 2) * CS
                                  S.op("pe", lambda e, h=h, ac=ac, c0_=c0_: e.matmul(ac[0:TS, c0_:c0_ + 64], lhsT=pts[0:nk, h, :], rhs=VCS[0:nk, 1 + sq, kt, gp],
                                                                                    start=(kt == 0 and h % 2 == 0), stop=(kt == NKS - 1), skip_group_check=True), reads=[tpts, tVCS], writes=[tac])
                                  S.op("pe", lambda e, h=h, ac=ac, c0_=c0_: e.matmul(ac[0:TS, c0_ + 64:c0_ + CS], lhsT=pts[0:nk, h, :], rhs=OVS[0:nk, kt, :],
                                                                                    start=False, stop=(kt == NKS - 1), skip_group_check=True), reads=[tpts, tOV], writes=[tac])
                            push(fin_scmp)
                        flush()
                        score = SC2[0:TS, 0, 0:NSELS]
                        for hb in range(2):
                            ac, tac = accs[hb]
                            attn_epilogue(ac, tac, g, 0, NT, False, CS, rows=TS, nh=2, h0=2 * hb, dest=OACS, tdest=tOACS)
                            hv = ac[0:TS, 0:2 * CS].rearrange("p (h c) -> p h c", c=CS)
                            for h in range(2):
                                if hb == 0 and h == 0:
                                    S.op("dve", lambda e: e.tensor_scalar(out=score, in0=hv[:, 0, 65:CS], scalar1=ST1[0:TS, 40:41], scalar2=None, op0=ALU.mult), reads=[tac, tST1], writes=[tSC2])
                                else:
                                    S.op("dve", lambda e, h=h: e.scalar_tensor_tensor(out=score, in0=hv[:, h, 65:CS], scalar=ST1[0:TS, 40 + h:41 + h], in1=score, op0=ALU.mult, op1=ALU.add),
                                         reads=[tac, tST1, tSC2], writes=[tSC2])
                        S.op("dve", lambda e: e.tensor_tensor(out=score, in0=score, in1=FBS[:, :], op=ALU.add), reads=[tSC2, tK], writes=[tSC2])
                        select(score, NSELS, TS)
                        yield 1
                        pB, tB = ps()
                        S.op("pe", lambda e: e.transpose(pB[0:NBK, 0:TS], SC2[0:TS, 0, 0:NBK], IDF[0:TS, 0:TS]), reads=[tSC2, tK], writes=[tB])
                        for hf_ in range((NBK + 63) // 64):
                            r0, r1 = hf_ * 64, min(NBK, hf_ * 64 + 64)
                            S.op("act", lambda e, hf_=hf_, r0=r0, r1=r1: e.activation(out=BTS[r0:r1, hf_, g, :], in_=pB[r0:r1, 0:TS], func=AF.Identity), reads=[tB], writes=[tBTS])

                    def past_pass(npages, loadfn, biasfn, KN, tKN, VNg1, tVNg1, vn0_idx, br):
                        accg = [sacc(), sacc()]
                        first = [True, True]
                        for pg0 in range(0, npages, 4):
                            npc = min(4, npages - pg0)
                            b = (pg0 // 4) % 2
                            stgv = STG[:, b, :].rearrange("p (j c) -> p j c", c=288)
                            loadfn(stgv, tSTG[b], pg0, npc)
                            yield 4
                            pt_, tt_ = ps()
                            ptv = pt_[:].bitcast(BF16)
                            for j in range(npc):
                                S.op("pe", lambda e, j=j: e.transpose(ptv[:, j * 128:(j + 1) * 128], stgv[:, j, 0:128], IDB[:]), reads=[tSTG[b], tK], writes=[tt_])
                            S.op("dve", lambda e: e.tensor_copy(out=KTC[:, 0:npc * 128], in_=ptv[:, 0:npc * 128]), reads=[tt_], writes=[tRAWT])
                            SLCV = int(os.environ.get('SLCV', '9'))
                            KK_ = int(os.environ.get('KK', '128'))
                            for g in range(2 if SLCV >= 2 else 0):
                                gp = slice(64 * g, 64 * g + 64)
                                ac, tac = accg[g]
                                if os.environ.get('OB') == '1':
                                    ac, tac = ps()
                                for j in range(npc):
                                    pg = pg0 + j
                                    pS, tS_ = ps()
                                    pts, tpts = score_tile(pS, tS_, 128, KTC[gp, j * 128:(j + 1) * 128], tRAWT, g, bias=biasfn(pg, g))
                                    vr = stgv[:, j, 128:192] if g == 0 else stgv[:, j, 192:257]
                                    onec = stgv[:, j, 256:257]
                                    if os.environ.get('VRT') == '1':
                                        vr = stgv[:, j, 144:209] if g == 0 else stgv[:, j, 208:273]
                                    if os.environ.get('VRT') == '2':
                                        vr = VS[:, 0, g, :]
                                    if os.environ.get('VRT') == '3':
                                        vr = VS[:, 0, g, 0:64]
                                    if os.environ.get('E4') == '1':
                                        for hb in range(2):
                                            S.op("pe", lambda e, hb=hb, vr=vr, ac=ac: e.matmul(ac[0:2 * TS, hb * 65:(hb + 1) * 65], lhsT=pts[:, 2 * hb:2 * hb + 2, :], rhs=vr, start=(first[g] and hb == 0), stop=False, skip_group_check=True),
                                                 reads=[tpts, tSTG[b]], writes=[tac])
                                    def fin_pp(pts=pts, tpts=tpts, vr=vr, onec=onec, ac=ac, tac=tac, fg=first[g], b=b, g=g):
                                        for h in range(4):
                                            oc = h * 65 + (1 if g == 0 else 0)
                                            S.op("pe", lambda e, h=h, oc=oc: e.matmul(ac[0:TS, oc:oc + vr.shape[-1]], lhsT=pts[:, h, :], rhs=vr, start=(fg and h == 0), stop=False, skip_group_check=True),
                                                 reads=[tpts, tSTG[b]], writes=[tac])
                                            if g == 0:
                                                S.op("pe", lambda e, h=h: e.matmul(ac[0:TS, h * 65:h * 65 + 1], lhsT=pts[:, h, :], rhs=onec, start=False, stop=False, skip_group_check=True),
                                                     reads=[tpts, tSTG[b]], writes=[tac])
                                    push(fin_pp)
                                    first[g] = False
                        flush()
                        yield 1
                        for g in range(2 if SLCV >= 4 else 0):
                            gp = slice(64 * g, 64 * g + 64)
                            ac, tac = accg[g]
                            pS, tS_ = ps()
                            pts, tpts = score_tile(pS, tS_, TS, KN[gp, T:T + TS], tKN, g, bias=(IDB[:, 0:TS], NEWB[:, sq, :], tK))
                            vr = VN0[0:TS, vn0_idx, :] if g == 0 else VNg1[0:TS, NT, 1, :]
                            for h in range(4):
                                S.op("pe", lambda e, h=h, vr=vr, ac=ac: e.matmul(ac[0:TS, h * 65:(h + 1) * 65], lhsT=pts[0:TS, h, :], rhs=vr, start=False, stop=True, skip_group_check=True),
                                     reads=[tpts, tVN0, tVNg1], writes=[tac])
                            attn_epilogue(ac, tac, g, br, NT, False, 65, rows=TS, ocol=(1 if g == 0 else 0), scol=(0 if g == 0 else 64), dest=OACS, tdest=tOACS)

                    def slc_load(stgv, tstg, pg0, npc):
                        for j in range(npc):
                            gather2(stgv[:, j, 0:256], sq, pg0 + j, tstg, join=(j > 0))

                    def slc_bias(pg, g):
                        half = pg // 32
                        return (E64[:, pg % 32, :], BTS[:, half, g, sc].unsqueeze(1).to_broadcast([128, 4, DS]), tBTS)

                    if int(os.environ.get('SST', '9')) >= 2:
                        yield from past_pass(NPG, slc_load, slc_bias, KST, tKST, VS, tVS, 0, 1)

                    def win_load(stgv, tstg, pg0, npc):
                        S.dma("pool", lambda e: e.dma_start(out=stgv[:, 0:npc, 0:128], in_=st_win[l, sq, pg0 * 128:(pg0 + npc) * 128, 0:128].rearrange("(j p) c -> p j c", p=128)), writes=[tstg])
                        S.dma("pool", lambda e: e.dma_start(out=stgv[:, 0:npc, 128:256], in_=st_win[l, sq, pg0 * 128:(pg0 + npc) * 128, 128:256].rearrange("(j p) c -> p j c", p=128)), writes=[tstg], join=True)

                    def win_bias(pg, g):
                        if pg == 0:
                            return (IDB[:], WINB0[:, :], tK)
                        return None

                    if int(os.environ.get('SST', '9')) >= 3:
                        yield from past_pass(4, win_load, win_bias, KWT, tKWT, VW, tVW, 1, 2)

            MARKS.append(("Bp", l, dict(S.cnt)))
            def attn_epilogue(ac, tac, g, br, qt, first, stride, rows=128, nh=4, h0=0, ocol=0, scol=64, dest=None, tdest=None):
                dest = OACC if dest is None else dest
                tdest = tOACC if tdest is None else tdest
                hv = ac[0:rows, 0:nh * stride].rearrange("p (h c) -> p h c", c=stride)
                S.op("dve", lambda e: e.tensor_scalar(out=ST1[0:rows, 40:40 + nh], in0=hv[:, :, scol], scalar1=1e-30, scalar2=None, op0=ALU.add), reads=[tac], writes=[tST1])
                S.op("dve", lambda e: e.reciprocal(out=ST1[0:rows, 40:40 + nh], in_=ST1[0:rows, 40:40 + nh]), reads=[tST1], writes=[tST1])
                S.op("dve", lambda e: e.tensor_tensor(out=ST1[0:rows, 44:44 + nh], in0=ST1[0:rows, 40:40 + nh],
                                                      in1=GSIG[0:rows, qt, :].rearrange("p (h b) -> p h b", b=3)[:, 4 * g + h0:4 * g + h0 + nh, br], op=ALU.mult), reads=[tST1, tGS[qt]], writes=[tST1])
                for h in range(nh):
                    hh_ = 4 * g + h0 + h
                    if first:
                        S.op("dve", lambda e, h=h, hh_=hh_: e.tensor_scalar(out=dest[0:rows, hh_ * 64:(hh_ + 1) * 64], in0=hv[:, h, ocol:ocol + 64], scalar1=ST1[0:rows, 44 + h:45 + h], scalar2=None, op0=ALU.mult),
                             reads=[tac, tST1], writes=[tdest])
                    else:
                        S.op("dve", lambda e, h=h, hh_=hh_: e.scalar_tensor_tensor(out=dest[0:rows, hh_ * 64:(hh_ + 1) * 64], in0=hv[:, h, ocol:ocol + 64], scalar=ST1[0:rows, 44 + h:45 + h],
                                                                                  in1=dest[0:rows, hh_ * 64:(hh_ + 1) * 64], op0=ALU.mult, op1=ALU.add), reads=[tac, tST1, tdest], writes=[tdest])

            def select(score_ap, nsel, rows):
                if nsel > 16:
                    S.op("dve", lambda e: e.max(out=SC2[0:rows, 2, 0:8], in_=score_ap), reads=[tSC2], writes=[tSC2])
                    S.op("dve", lambda e: e.match_replace(out=SC2[0:rows, 1, 0:nsel], in_to_replace=SC2[0:rows, 2, 0:8], in_values=score_ap, imm_value=-3e38), reads=[tSC2], writes=[tSC2])
                    S.op("dve", lambda e: e.max(out=SC2[0:rows, 2, 8:16], in_=SC2[0:rows, 1, 0:nsel]), reads=[tSC2], writes=[tSC2])
                    S.op("dve", lambda e: e.tensor_scalar(out=SC2[0:rows, 1, 0:nsel], in0=score_ap, scalar1=SC2[0:rows, 2, 15:16], scalar2=None, op0=ALU.is_ge), reads=[tSC2], writes=[tSC2])
                    S.op("dve", lambda e: e.tensor_scalar(out=score_ap, in0=score_ap, scalar1=-5e29, scalar2=None, op0=ALU.is_gt), reads=[tSC2], writes=[tSC2])
                    S.op("dve", lambda e: e.tensor_tensor(out=score_ap, in0=score_ap, in1=SC2[0:rows, 1, 0:nsel], op=ALU.mult), reads=[tSC2], writes=[tSC2])
                else:
                    S.op("dve", lambda e: e.tensor_scalar(out=score_ap, in0=score_ap, scalar1=-5e29, scalar2=None, op0=ALU.is_gt), reads=[tSC2], writes=[tSC2])
                S.op("dve", lambda e: e.tensor_scalar(out=score_ap, in0=score_ap, scalar1=-1.0, scalar2=-NEGB, op0=ALU.add, op1=ALU.mult), reads=[tSC2], writes=[tSC2])

            def cmp_branch(qt, g):
                qc = slice(qt * 128, (qt + 1) * 128)
                gp = slice(64 * g, 64 * g + 64)
                pS, tS_ = ps()
                S.op("pe", lambda e: e.matmul(pS[0:NCP, :], lhsT=KCS[gp, 0, 0:NCP], rhs=QT[gp, :, qc], start=True, stop=False), reads=[tKCS, tQT], writes=[tS_])
                S.op("pe", lambda e: e.matmul(pS[0:NCP, :], lhsT=IDB[0:NCP, 0:NCP], rhs=CMPB[0:NCP, qc].unsqueeze(1).to_broadcast([NCP, 4, 128]), start=False, stop=True),
                     reads=[tK], writes=[tS_])
                pt, tpt = ptbuf()
                S.op("act", lambda e: e.activation(out=pt[0:NCP, :], in_=pS[0:NCP, :], func=AF.Exp, scale=SCALE), reads=[tS_], writes=[tpt])
                ac, tac = acc()

                def fin_cmp():
                    for h in range(4):
                        S.op("pe", lambda e, h=h: e.matmul(ac[:, h * 97:h * 97 + 64], lhsT=pt[0:NCP, h * 128:(h + 1) * 128], rhs=VCS[0:NCP, 0, 0, gp], start=True, stop=True),
                             reads=[tpt, tVCS], writes=[tac])
                        S.op("pe", lambda e, h=h: e.matmul(ac[:, h * 97 + 64:h * 97 + 65 + NSELP], lhsT=pt[0:NCP, h * 128:(h + 1) * 128], rhs=OVP[0:NCP, :], start=True, stop=True),
                             reads=[tpt, tOV], writes=[tac])
                    attn_epilogue(ac, tac, g, 0, qt, True, 97)
                    hv = ac[:, 0:388].rearrange("p (h c) -> p h c", c=97)
                    score = SC2[:, 0, 0:NSELP]
                    for h in range(4):
                        if h == 0:
                            S.op("dve", lambda e: e.tensor_scalar(out=score, in0=hv[:, 0, 65:65 + NSELP], scalar1=ST1[:, 40:41], scalar2=None, op0=ALU.mult), reads=[tac, tST1], writes=[tSC2])
                        else:
                            S.op("dve", lambda e, h=h: e.scalar_tensor_tensor(out=score, in0=hv[:, h, 65:65 + NSELP], scalar=ST1[:, 40 + h:41 + h], in1=score, op0=ALU.mult, op1=ALU.add),
                                 reads=[tac, tST1, tSC2], writes=[tSC2])
                    S.op("dve", lambda e: e.tensor_tensor(out=score, in0=score, in1=FBP[:, qt, :], op=ALU.add), reads=[tSC2, tK], writes=[tSC2])
                    select(score, NSELP, 128)
                    pB, tB = ps()
                    S.op("pe", lambda e: e.transpose(pB[0:NSELP, 0:128], score, IDF[:]), reads=[tSC2, tK], writes=[tB])
                    S.op("act", lambda e: e.activation(out=BT[0:NSELP, g, :], in_=pB[0:NSELP, 0:128], func=AF.Identity), reads=[tB], writes=[tBT])
                push(fin_cmp)

            def kv_branch(qt, g, br):
                qc = slice(qt * 128, (qt + 1) * 128)
                gp = slice(64 * g, 64 * g + 64)
                if br == 1:
                    KT_, VV, tKK, tVV, kts = KST, VS, tKST, tVS, list(range(0, qt + 1))
                else:
                    KT_, VV, tKK, tVV, kts = KWT, VW, tKWT, tVW, list(range(max(0, qt - 4), qt + 1))
                if True:
                    ac, tac = acc()
                    for ki, kt in enumerate(kts):
                        pS, tS_ = ps()
                        need_b = (kt == qt) or (br == 1) or (br == 2 and kt == qt - 4)
                        S.op("pe", lambda e, kt=kt: e.matmul(pS[:, :], lhsT=KT_[gp, kt * 128:(kt + 1) * 128], rhs=QT[gp, :, qc], start=True, stop=not need_b), reads=[tKK, tQT], writes=[tS_])
                        if kt == qt:
                            S.op("pe", lambda e: e.matmul(pS[:, :], lhsT=IDB[:], rhs=CAUS[:].unsqueeze(1).to_broadcast([128, 4, 128]), start=False, stop=True), reads=[tK], writes=[tS_])
                        elif br == 1:
                            S.op("pe", lambda e, kt=kt: e.matmul(pS[:, :], lhsT=E64[:, kt, :], rhs=BT[:, g, :].unsqueeze(1).to_broadcast([128, 4, 128]), start=False, stop=True),
                                 reads=[tK, tBT], writes=[tS_])
                        elif kt == qt - 4:
                            S.op("pe", lambda e: e.matmul(pS[:, :], lhsT=IDB[:], rhs=ANTI[:].unsqueeze(1).to_broadcast([128, 4, 128]), start=False, stop=True), reads=[tK], writes=[tS_])
                        pt, tpt = ptbuf()
                        S.op("act", lambda e: e.activation(out=pt, in_=pS[:, :], func=AF.Exp, scale=SCALE), reads=[tS_], writes=[tpt])
                        def fin_kv(ki=ki, kt=kt, pt=pt, tpt=tpt):
                            for h in range(4):
                                S.op("pe", lambda e, h=h: e.matmul(ac[:, h * 65:(h + 1) * 65], lhsT=pt[:, h * 128:(h + 1) * 128], rhs=VV[:, kt, g, :],
                                                                   start=(ki == 0 and h == 0), stop=(ki == len(kts) - 1), skip_group_check=True), reads=[tpt, tVV], writes=[tac])
                            if ki == len(kts) - 1:
                                attn_epilogue(ac, tac, g, br, qt, False, 65)
                        push(fin_kv)

            gen = sample_gen()
            n_steps = 2 * (2 + NS * 8 * ((NPG + 7) // 8)) + NS * (2 + 4 * ((NPG + 3) // 4) + 1 + 4 + 1)
            wts = [2 + (qt + 1) + min(qt + 1, 5) for qt in range(NT)]
            done_steps = [0]

            def advance(target):
                while done_steps[0] < target:
                    try:
                        done_steps[0] += (next(gen) or 1)
                    except StopIteration:
                        done_steps[0] = 10 ** 9
                        return

            cum = 0
            for qt in range(NT):
                for g in range(2):
                    cmp_branch(qt, g)
                advance(int(n_steps * (cum + 0.3 * wts[qt]) / sum(wts)))
                for g in range(2):
                    kv_branch(qt, g, 2)
                advance(int(n_steps * (cum + 0.6 * wts[qt]) / sum(wts)))
                for g in range(2):
                    kv_branch(qt, g, 1)
                flush()
                cum += wts[qt]
                advance(int(n_steps * cum / sum(wts)))
                S.op("act", lambda e: e.activation(out=QN[:, 0:512], in_=OACC[:, :], func=AF.Identity), reads=[tOACC], writes=[tQN])
                pt_, tt_ = ps()
                ptv = pt_[:].bitcast(BF16)
                for k in range(4):
                    S.op("pe", lambda e, k=k: e.transpose(ptv[:, k * 128:(k + 1) * 128], QN[:, k * 128:(k + 1) * 128], IDB[:]), reads=[tQN, tK], writes=[tt_])
                S.op("dve", lambda e: e.tensor_copy(out=OT[:], in_=ptv[:, 0:512].rearrange("p (k t) -> p k t", k=4)), reads=[tt_], writes=[tOT])
                for half in range(2):
                    po, to = ps()
                    for k in range(4):
                        S.op("pe", lambda e, k=k, half=half: e.matmul(po[:, :], lhsT=OT[:, k, :], rhs=WOA[:, k, half * 512:(half + 1) * 512], start=(k == 0), stop=(k == 3)),
                             reads=[tOT, tWOA], writes=[to])
                    resid_add(qt, po, to, half, 0)


            for _ in gen:
                pass
            flush()
            S.op("act", lambda e: e.activation(out=QN[0:TS, 0:512], in_=OACS[:, :], func=AF.Identity), reads=[tOACS], writes=[tQN])
            pt_, tt_ = ps()
            ptv = pt_[:].bitcast(BF16)
            for k in range(4):
                S.op("pe", lambda e, k=k: e.transpose(ptv[:, k * 128:k * 128 + TS], QN[0:TS, k * 128:(k + 1) * 128], IDB[0:TS, 0:TS]), reads=[tQN, tK], writes=[tt_])
            S.op("dve", lambda e: e.tensor_copy(out=OT[:, :, 0:TS], in_=ptv[:, 0:512].rearrange("p (k t) -> p k t", k=4)[:, :, 0:TS]), reads=[tt_], writes=[tOT])
            for half in range(2):
                po, to = ps()
                for k in range(4):
                    S.op("pe", lambda e, k=k, half=half: e.matmul(po[0:TS, :], lhsT=OT[:, k, 0:TS], rhs=WOA[:, k, half * 512:(half + 1) * 512], start=(k == 0), stop=(k == 3)),
                         reads=[tOT, tWOA], writes=[to])
                resid_add(NT, po, to, half, 0)

            S.barrier()
            MARKS.append(("C", l, dict(S.cnt)))
            NPSR[0] = 5
            for i in range(NTT):
                norm_to_HT([i], 2, 3, H2T, tH2T, i * 128)
            colgroups = [(c0, min(512, T - c0), list(range(c0 // 128, (c0 + min(512, T - c0)) // 128))) for c0 in range(0, T, 512)] + [(T, TS, [NT])]
            NCH = FFN_H // 128
            blocks = [(j0, min(4, NCH - j0)) for j0 in range(0, NCH, 4)]
            for bi, (j0, nb) in enumerate(blocks):
                sl = bi % 2
                cast_load(WUP[sl][:, :, 0:nb * 128], w_up[l, :, j0 * 128:(j0 + nb) * 128].rearrange("(k p) c -> p k c", p=128), [tWF[sl]] + (allWA if bi < 2 else []))
                cast_load(WUP[sl][:, :, 512:512 + nb * 128], w_up[l, :, FFN_H + j0 * 128:FFN_H + (j0 + nb) * 128].rearrange("(k p) c -> p k c", p=128), [tWF[sl]])
                cast_load(WDN[sl][:, 0:nb, :], w_down[l, j0 * 128:(j0 + nb) * 128, :].rearrange("(j p) c -> p j c", p=128), [tWF[sl]])
                for (c0, ncol, tiles) in colgroups:
                    for j in range(nb):
                        pa, ta = ps()
                        for k in range(8):
                            S.op("pe", lambda e, k=k, j=j, pa=pa, c0=c0, ncol=ncol, sl=sl: e.matmul(pa[:, 0:ncol], lhsT=WUP[sl][:, k, j * 128:(j + 1) * 128], rhs=H2T[:, k, c0:c0 + ncol],
                                                                                               start=(k == 0), stop=(k == 7)), reads=[tWF[sl], tH2T], writes=[ta])
                        pb2, tb2 = ps()
                        for k in range(8):
                            S.op("pe", lambda e, k=k, j=j, pb2=pb2, c0=c0, ncol=ncol, sl=sl: e.matmul(pb2[:, 0:ncol], lhsT=WUP[sl][:, k, 512 + j * 128:512 + (j + 1) * 128], rhs=H2T[:, k, c0:c0 + ncol],
                                                                                                 start=(k == 0), stop=(k == 7)), reads=[tWF[sl], tH2T], writes=[tb2])
                        S.op("act", lambda e, pa=pa, ncol=ncol: e.activation(out=SCR[:, 0:ncol], in_=pa[:, 0:ncol], func=AF.Silu), reads=[ta], writes=[tSCR])
                        S.op("dve", lambda e, pb2=pb2, j=j, ncol=ncol: e.tensor_tensor(out=UT[:, j, 0:ncol], in0=SCR[:, 0:ncol], in1=pb2[:, 0:ncol], op=ALU.mult), reads=[tSCR, tb2], writes=[tUT[0]])
                    for ti, i in enumerate(tiles):
                        rows = 128 if i < NT else TS
                        for half in range(2):
                            po, to = ps()
                            for j in range(nb):
                                S.op("pe", lambda e, j=j, po=po, ti=ti, rows=rows, half=half, sl=sl: e.matmul(po[0:rows, :], lhsT=UT[:, j, ti * 128:ti * 128 + rows], rhs=WDN[sl][:, j, half * 512:(half + 1) * 512],
                                                                                                       start=(j == 0), stop=(j == nb - 1)), reads=[tUT[0], tWF[sl]], writes=[to])
                            resid_add(i, po, to, half, 1)

        for l in range(DEPTH):
            S.barrier()
            MARKS.append(("L", l, dict(S.cnt)))
            ld("sp", SPR[:], spar[l], [tSPR])
            pb, tp = ps()
            S.op("pe", lambda e, pb=pb: e.transpose(pb[:, 0:80], SPR[:], IDF[0:80, 0:80]), reads=[tSPR, tK], writes=[tp])
            S.op("dve", lambda e, pb=pb: e.tensor_copy(out=SPT[:], in_=pb[:, 0:80]), reads=[tp], writes=[tSPT])
            ld("sp", KGB[:].rearrange("p a c -> p (a c)"), kgbc[l].rearrange("a c -> (a c)").partition_broadcast(128), [tKGB])
            for n in range(12):
                kind, half = n // 2, n % 2
                wb = n % 4
                cast_load(WADA[:, wb], w_ada[l, :, n * 512:(n + 1) * 512].rearrange("(k p) c -> p k c", p=128), [tWADA[wb]] + ([tWIN] if False else []),
                          r=[])
                if kind in (2, 5):
                    gi = 0 if kind == 2 else 1
                    ld("sp", BST[:], b_ada[l, n * 512:(n + 1) * 512].partition_broadcast(128), [tBST])
                    for gsel, (c0, cw) in enumerate(((0, 128), (128, TS))):
                        pb, tp = ps()
                        for k in range(8):
                            S.op("pe", lambda e, k=k, pb=pb, c0=c0, cw=cw, wb=wb: e.matmul(pb[0:cw, :], lhsT=CTS[:, k, c0:c0 + cw], rhs=WADA[:, wb, k, :],
                                                                                            start=(k == 0), stop=(k == 7)), reads=[tK, tWADA[wb]], writes=[tp])
                        S.op("dve", lambda e, pb=pb, cw=cw, gi=gi, gsel=gsel, half=half: e.tensor_tensor(
                            out=GATE[0:cw, gi, gsel, half * 512:(half + 1) * 512], in0=pb[0:cw, :], in1=BST[0:cw, :], op=ALU.add),
                             reads=[tp, tBST], writes=[tGATE[gi][gsel]])
                else:
                    mk = {0: 0, 1: 1, 3: 2, 4: 3}[kind]
                    pb, tp = ps()
                    for j in range(4):
                        for k in range(8):
                            S.op("pe", lambda e, k=k, j=j, pb=pb, wb=wb: e.matmul(pb[:, j * 8:j * 8 + 1 + NS], lhsT=WADA[:, wb, k, j * 128:(j + 1) * 128], rhs=CT5[:, k, :],
                                                                                   start=(k == 0), stop=(k == 7)), reads=[tK, tWADA[wb]], writes=[tp])
                    for j in range(4):
                        c = half * 4 + j
                        bcol = SPT[:, n * 4 + j:n * 4 + j + 1]
                        if kind in (0, 3):
                            S.op("dve", lambda e, j=j, pb=pb, c=c, bcol=bcol, mk=mk: e.tensor_scalar(out=MODC[:, mk, c, :], in0=pb[:, j * 8:j * 8 + 1 + NS], scalar1=bcol, scalar2=None, op0=ALU.add),
                                 reads=[tp, tSPT], writes=[tMODC])
                        else:
                            ncol = SPT[:, 48 + c:49 + c] if kind == 1 else SPT[:, 56 + c:57 + c]
                            S.op("dve", lambda e, j=j, pb=pb, c=c, bcol=bcol, mk=mk: e.tensor_scalar(out=MODC[:, mk, c, :], in0=pb[:, j * 8:j * 8 + 1 + NS], scalar1=bcol, scalar2=1.0, op0=ALU.add, op1=ALU.add),
                                 reads=[tp, tSPT], writes=[tMODC])
                            S.op("dve", lambda e, c=c, ncol=ncol, mk=mk: e.tensor_scalar(out=MODC[:, mk, c, :], in0=MODC[:, mk, c, :], scalar1=ncol, scalar2=None, op0=ALU.mult),
                                 reads=[tMODC, tSPT], writes=[tMODC])
            LAYER_BODY(l)
        for i in range(NT):
            S.dma("sp", lambda e, i=i: e.dma_start(out=y_p[i * 128:(i + 1) * 128, :], in_=X[:, i, :]), reads=[tX[i]])
        S.dma("sp", lambda e: e.dma_start(out=y_s, in_=X[0:TS, NT, :]), reads=[tX[NT]])
        S.final_wait("sp")
        print('CNT', S.cnt, {k: v for k, v in S.dval.items() if v > 1000})

        sems = {e: es.enter_context(nc.semaphore("s_" + e)) for e in ENGS}
        for q, n in S.NDS.items():
            for i in range(n):
                sems[("d", q, i)] = es.enter_context(nc.semaphore(f"d_{q}_{i}"))

        def run(e, lst):
            for waits, fn, key, inc in lst:
                for k, v in waits:
                    e.wait_ge(sems[k], v)
                if fn is not None:
                    name, a, k = fn
                    getattr(e, name)(*a, **k).then_inc(sems[key], inc)

        with nc.Block() as block:
            @block.sync
            def _(e):
                run(e, S.ops["sp"])

            @block.scalar
            def _(e):
                run(e, S.ops["act"])

            @block.vector
            def _(e):
                run(e, S.ops["dve"])

            @block.gpsimd
            def _(e):
                run(e, S.ops["pool"])

            @block.tensor
            def _(e):
                run(e, S.ops["pe"])
    return nc


def _consts(cfg):
    T, NPG, NS, DS = cfg["T"], cfg["NPG"], cfg["NS"], cfg["DS"]
    NT = T // 128; TS = NS * DS; PAST = NPG * 128
    NCS = (PAST + DS - 32) // 16 + 1; NSELP = T // 64; NSELS = -(-(PAST + DS) // 64); NKS = (NCS + 127) // 128
    bf = ml_dtypes.bfloat16
    k = {}
    k["k_idf"] = np.eye(128, dtype=np.float32); k["k_idb"] = np.eye(128).astype(bf)
    kk, qq = np.meshgrid(np.arange(128), np.arange(128), indexing="ij")
    k["k_caus"] = np.where(kk > qq, NEGB, 0.0).astype(bf); k["k_anti"] = np.where(kk <= qq, NEGB, 0.0).astype(bf)
    c = np.arange(128)[:, None]; t = np.arange(T)[None, :]
    k["k_cmpb"] = np.where(16 * c + 31 <= t, 0.0, NEGB).astype(bf)
    e64 = np.zeros((128, 32, 128), np.float32)
    for m in range(32):
        for key in range(128):
            j = 2 * m + key // 64
            e64[j % 64, m, key] = 1.0
            e64[64 + j % 64, m, key] = 1.0
    k["k_e64"] = e64.astype(bf)
    fbp = np.zeros((128, NT, NSELP), np.float32)
    for i in range(NT):
        for p in range(128):
            cur = (i * 128 + p) // 64
            for j in range(NSELP):
                if j > cur: fbp[p, i, j] = -1e30
                elif j == 0 or j == cur or j == cur - 1: fbp[p, i, j] = 1e4
    k["k_fbp"] = fbp.astype(bf)
    fbs = np.zeros((TS, NSELS), np.float32); cur = NSELS - 1
    fbs[:, 0] = 1e4; fbs[:, cur] = 1e4; fbs[:, cur - 1] = 1e4
    k["k_fbs"] = fbs
    newb = np.full((TS, NS, 4, DS), NEGB, np.float32)
    for s in range(NS):
        for t2 in range(DS):
            for tq in range(DS):
                if t2 <= tq: newb[s * DS + t2, s, :, tq] = 0.0
    k["k_newb"] = newb.reshape(TS, NS, 4 * DS).astype(bf)
    wb0 = np.zeros((128, 4, DS), np.float32)
    for i in range(128):
        for tq in range(DS):
            if not (i > tq): wb0[i, :, tq] = NEGB
    k["k_winb0"] = wb0.reshape(128, 4 * DS).astype(bf)

    def ov(ncmp, nsel):
        m = np.zeros((ncmp, nsel), np.float32); i = np.arange(ncmp)
        for part in range(2):
            j = np.minimum((i + part) * 16 // 64, nsel - 1); np.add.at(m, (i, j), 1.0)
        return m
    NCP = (T - 32) // 16 + 1
    ovp = np.zeros((128, 1 + NSELP), np.float32); ovp[:, 0] = 1.0; ovp[:NCP, 1:] = ov(NCP, NSELP)
    k["k_ovp"] = ovp.astype(bf)
    ovs = np.zeros((NKS * 128, 1 + NSELS), np.float32); ovs[:, 0] = 1.0; ovs[:NCS, 1:] = ov(NCS, NSELS)
    k["k_ovs"] = ovs.reshape(NKS, 128, 1 + NSELS).transpose(1, 0, 2).astype(bf)
    rc = np.zeros((128, 2, 16), np.float32)
    for ch in range(2):
        for p in range(128):
            w = (2, 4, 8, 16)[ch * 2 + p // 64]
            rc[p, ch, :] = 1.0 / np.minimum(w, np.arange(16) + 1)
    k["k_rc"] = rc
    bo = np.zeros((128, 128), np.float32); bo[:64, :64] = 1; bo[64:, 64:] = 1
    k["k_bones"] = bo.astype(bf)
    sel = np.zeros((1 + NS, 128 + TS), np.float32); sel[0, :128] = 1
    for s in range(NS): sel[1 + s, 128 + s * DS:128 + (s + 1) * DS] = 1
    k["k_cts"] = sel
    return k


_NC_CACHE = {}


def kernel(x_prompt, x_sample, cache_nsa_kv, state_win_kv, state_conv, state_pool, page_table,
           c_prompt, c_sample, norm_mix, norm_ffn, w_ada, b_ada, w_in, w_out, q_norm, k_norm,
           cmp_pe, cmp_w1, cmp_w2, conv_w, conv_bias, pool_w, pool_scale, w_up, w_down, cfg=None):
    cfg = dict(CFG) if cfg is None else cfg
    T, NPG, DEPTH, NS, DS = cfg["T"], cfg["NPG"], cfg["DEPTH"], cfg["NS"], cfg["DS"]
    f = lambda a: np.ascontiguousarray(np.asarray(a))
    NB = x_prompt.shape[0]; TS = NS * DS
    key = tuple(sorted(cfg.items()))
    if key not in _NC_CACHE:
        _NC_CACHE[key] = build(cfg)
    nc = _NC_CACHE[key]
    consts = _consts(cfg)
    spar = np.zeros((DEPTH, 80, 128), np.float32)
    spar[:, 0:48] = f(b_ada).reshape(DEPTH, 48, 128)
    spar[:, 48:56] = f(norm_mix).reshape(DEPTH, 8, 128); spar[:, 56:64] = f(norm_ffn).reshape(DEPTH, 8, 128)
    spar[:, 64:70] = f(conv_w).reshape(DEPTH, 6, 128); spar[:, 70:72] = f(conv_bias).reshape(DEPTH, 2, 128)
    spar[:, 72:74] = f(pool_scale).reshape(DEPTH, 2, 128)
    spar[:, 74, 0:64] = f(q_norm); spar[:, 74, 64:128] = f(q_norm)
    spar[:, 75:78, 0:64] = f(k_norm); spar[:, 75:78, 64:128] = f(k_norm)
    kgbc = np.concatenate([f(k_norm)[:, 1:3], f(k_norm)[:, 1:3]], axis=-1).astype(np.float32)
    w1 = f(cmp_w1).reshape(DEPTH, 2, 32, 64, 64)
    w1bd = np.zeros((DEPTH, 2, 128, 32, 128), np.float32)
    w1bd[:, :, 0:64, :, 0:64] = w1.transpose(0, 1, 3, 2, 4); w1bd[:, :, 64:, :, 64:] = w1.transpose(0, 1, 3, 2, 4)
    w2bd = np.zeros((DEPTH, 2, 128, 128), np.float32)
    w2bd[:, :, :64, :64] = f(cmp_w2); w2bd[:, :, 64:, 64:] = f(cmp_w2)
    pw = f(pool_w); pwbd = np.zeros((DEPTH, 2, 128, 128), np.float32)
    pwbd[:, 0, :64, :64] = pw[:, 0]; pwbd[:, 0, 64:, 64:] = pw[:, 1]; pwbd[:, 1, :64, :64] = pw[:, 2]; pwbd[:, 1, 64:, 64:] = pw[:, 3]
    pe_t = f(cmp_pe).transpose(0, 1, 3, 2)
    pet = np.concatenate([pe_t, pe_t], axis=2).astype(np.float32)
    cache2 = f(cache_nsa_kv).reshape(DEPTH, -1, 512)
    in_maps = []
    for c in range(NB):
        sl = slice(c * NS, (c + 1) * NS)
        st_cp = np.concatenate([f(state_pool)[:, sl].reshape(DEPTH, NS * 15, 256), f(state_conv)[:, sl].reshape(DEPTH, NS * 2, 256)], axis=1)
        m = dict(x_p=f(x_prompt[c]), x_s=f(x_sample[sl]).reshape(TS, D), cache=cache2,
                 st_win=f(state_win_kv)[:, sl].reshape(DEPTH, NS, 512, 256), st_cp=st_cp,
                 ptab=f(page_table[sl]).astype(np.int32), c_all=np.concatenate([f(c_prompt)[c:c + 1], f(c_sample)[sl]], 0),
                 spar=spar, kgbc=kgbc, w_ada=f(w_ada), b_ada=f(b_ada), w_in=f(w_in), w_out=f(w_out), w1bd=w1bd, w2bd=w2bd,
                 pwbd=pwbd, pet=pet, w_up=f(w_up), w_down=f(w_down))
        m.update(consts)
        in_maps.append(m)
    res = run_bass_kernel_spmd(nc, in_maps, core_ids=list(range(NB)))
    R_ = res.results
    cat = lambda k, ax=0: np.stack([r[k] for r in R_], axis=ax)
    yp = cat("y_p"); ys = np.concatenate([r["y_s"].reshape(NS, DS, D) for r in R_], 0)
    kvp = cat("kv_p", 1).reshape(DEPTH, NB, T, 4, 2, 64)
    kvs = np.concatenate([r["kv_s"].reshape(DEPTH, NS, DS, 4, 2, 64) for r in R_], 1)
    wp = cat("win_p", 1).reshape(DEPTH, NB, -1, 2, 2, 64)
    ws = np.concatenate([r["win_s"].reshape(DEPTH, NS, 512, 2, 2, 64) for r in R_], 1)
    cp = cat("conv_p", 1); cs = np.concatenate([r["conv_s"].reshape(DEPTH, NS, 2, 256) for r in R_], 1)
    pp = cat("pool_p", 1); pls = np.concatenate([r["pool_s"].reshape(DEPTH, NS, 15, 256) for r in R_], 1)
    return (yp, ys, kvp, kvs, wp, ws, cp, cs, pp, pls)
```
